# Optimizing a Trainium2 kernel written in Bass

```python
import math
import jax, jax.numpy as jnp
from jax import lax
import numpy as np

D_MODEL = 1024
BATCH = 2
SEQ = 16384
DEPTH = 4

N_MIXERS = 4
EPS = 1e-6
CONV_WIDTH = 1024
CONV_K = 3
FOX_HEADS = 8
FOX_HEAD_DIM = 128
FOX_WIDTH = FOX_HEADS * FOX_HEAD_DIM
Q_BLOCK = 128
GDN_HEADS = 8
GDN_HEAD_DIM = 128
GDN_WIDTH = GDN_HEADS * GDN_HEAD_DIM
GDN_CONV_K = 4
GDN_CHUNK = 64
SWA_HEADS = 16
SWA_KV_HEADS = 4
SWA_GROUP = SWA_HEADS // SWA_KV_HEADS
SWA_HEAD_DIM = 64
SWA_WIDTH = SWA_HEADS * SWA_HEAD_DIM
SWA_KV_WIDTH = SWA_KV_HEADS * SWA_HEAD_DIM
WINDOW = 128
ROPE_THETA = 10000.0

N_CONV_LAYERS = (DEPTH + 3) // 4
N_FOX_LAYERS = (DEPTH + 2) // 4
N_GDN_LAYERS = (DEPTH + 1) // 4
N_SWA_LAYERS = DEPTH // 4

kernel_name = "hybrid_interleaved_mixers"


def rms_norm(x, w):
    x32 = x.astype(jnp.float32)
    y = x32 * lax.rsqrt(jnp.mean(x32 * x32, axis=-1, keepdims=True) + EPS)
    return (y * w.astype(jnp.float32)).astype(x.dtype)


def causal_conv(u, w):
    K = w.shape[0]
    S = u.shape[1]
    up = jnp.pad(u, ((0, 0), (K - 1, 0), (0, 0)))
    out = up[:, 0:S] * w[0]
    for k in range(1, K):
        out = out + up[:, k:k + S] * w[k]
    return out


def rope(x, positions):
    d = x.shape[-1]
    half = d // 2
    inv_freq = ROPE_THETA ** (-jnp.arange(half, dtype=jnp.float32) / half)
    ang = positions.astype(jnp.float32)[..., None] * inv_freq
    cos = jnp.cos(ang)[:, :, None, :]
    sin = jnp.sin(ang)[:, :, None, :]
    x32 = x.astype(jnp.float32)
    x1, x2 = x32[..., :half], x32[..., half:]
    return jnp.concatenate([x1 * cos - x2 * sin, x2 * cos + x1 * sin], axis=-1).astype(x.dtype)


def conv_mixer(h, w_in, w_conv, w_out):
    proj = h @ w_in
    b_gate, c_gate, v, z = jnp.split(proj, 4, axis=-1)
    y = b_gate * causal_conv(c_gate * v, w_conv)
    return (y * jax.nn.silu(z)) @ w_out


def fox_mixer(h, w_in, b_f, w_out):
    B_, S, _ = h.shape
    H, d = FOX_HEADS, FOX_HEAD_DIM
    proj = h @ w_in
    q, k, v, z, f_logit = jnp.split(proj, [FOX_WIDTH, 2 * FOX_WIDTH, 3 * FOX_WIDTH, 4 * FOX_WIDTH], axis=-1)
    q = q.reshape(B_, S, H, d)
    k = k.reshape(B_, S, H, d)
    v = v.reshape(B_, S, H, d)
    log_f = jax.nn.log_sigmoid((f_logit + b_f).astype(jnp.float32))
    c = jnp.cumsum(log_f, axis=1)
    c_keys = c.transpose(0, 2, 1)
    nblk = S // Q_BLOCK
    q_blocks = q.reshape(B_, nblk, Q_BLOCK, H, d).transpose(1, 0, 2, 3, 4)
    c_blocks = c.reshape(B_, nblk, Q_BLOCK, H).transpose(1, 0, 3, 2)
    scale = 1.0 / math.sqrt(d)
    k_pos = jnp.arange(S)

    def block(args):
        i, q_i, c_i = args
        s = jnp.einsum('bqhd,bkhd->bhqk', q_i, k).astype(jnp.float32) * scale
        decay = c_i[..., :, None] - c_keys[:, :, None, :]
        q_pos = i * Q_BLOCK + jnp.arange(Q_BLOCK)
        mask = k_pos[None, :] <= q_pos[:, None]
        logits = jnp.where(mask, s + decay, -jnp.inf)
        p = jax.nn.softmax(logits, axis=-1)
        return jnp.einsum('bhqk,bkhd->bqhd', p.astype(v.dtype), v)

    o = lax.map(block, (jnp.arange(nblk), q_blocks, c_blocks))
    o = o.transpose(1, 0, 2, 3, 4).reshape(B_, S, FOX_WIDTH)
    return (o * jax.nn.silu(z)) @ w_out


def chunk_gated_delta(q, k, v, g, beta):
    B_, S, H, dk = q.shape
    dv = v.shape[-1]
    C = GDN_CHUNK
    n = S // C
    f32 = jnp.float32

    def chunks(t):
        return t.astype(f32).reshape(B_, n, C, H, -1).transpose(0, 3, 1, 2, 4)

    q, k, v = chunks(q), chunks(k), chunks(v)
    beta = beta.astype(f32).reshape(B_, n, C, H).transpose(0, 3, 1, 2)
    g_cum = jnp.cumsum(g.astype(f32).reshape(B_, n, C, H).transpose(0, 3, 1, 2), axis=-1)
    k_beta = k * beta[..., None]
    v_beta = v * beta[..., None]
    idx = jnp.arange(C)
    causal = idx[:, None] >= idx[None, :]
    strict = idx[:, None] > idx[None, :]
    diff = g_cum[..., :, None] - g_cum[..., None, :]
    decay = jnp.where(causal, jnp.exp(jnp.where(causal, diff, 0.0)), 0.0)
    L = jnp.where(strict, jnp.einsum('bhnid,bhnjd->bhnij', k_beta, k) * decay, 0.0)
    eye = jnp.eye(C, dtype=f32)
    T = lax.linalg.triangular_solve(eye + L, jnp.broadcast_to(eye, L.shape), left_side=True, lower=True, unit_diagonal=True)
    u = T @ v_beta
    w = T @ (k_beta * jnp.exp(g_cum)[..., None])
    qk = jnp.where(causal, jnp.einsum('bhnid,bhnjd->bhnij', q, k) * decay, 0.0)
    g_last = g_cum[..., -1]
    k_dec = k * jnp.exp(g_last[..., None] - g_cum)[..., None]
    q_dec = q * jnp.exp(g_cum)[..., None]

    def to_scan(t):
        return jnp.moveaxis(t, 2, 0)

    def step(state, xs):
        qd, kd, u_c, w_c, qk_c, gl = xs
        v_new = u_c - w_c @ state
        o = qd @ state + qk_c @ v_new
        state = state * jnp.exp(gl)[..., None, None] + jnp.einsum('bhck,bhcv->bhkv', kd, v_new)
        return state, o

    state0 = jnp.zeros((B_, H, dk, dv), f32)
    xs = (to_scan(q_dec), to_scan(k_dec), to_scan(u), to_scan(w), to_scan(qk), to_scan(g_last))
    _, o = lax.scan(step, state0, xs)
    return o.transpose(1, 0, 3, 2, 4).reshape(B_, S, H, dv)


def gdn_mixer(h, w_in, w_conv, a_log, dt_bias, w_onorm, w_out):
    B_, S, _ = h.shape
    H, d = GDN_HEADS, GDN_HEAD_DIM
    proj = h @ w_in
    qkv, z, beta_logit, a_logit = jnp.split(proj, [3 * GDN_WIDTH, 4 * GDN_WIDTH, 4 * GDN_WIDTH + H], axis=-1)
    qkv = jax.nn.silu(causal_conv(qkv, w_conv))
    q, k, v = jnp.split(qkv, 3, axis=-1)
    q = q.reshape(B_, S, H, d).astype(jnp.float32)
    k = k.reshape(B_, S, H, d).astype(jnp.float32)
    v = v.reshape(B_, S, H, d)
    q = q * lax.rsqrt(jnp.sum(q * q, axis=-1, keepdims=True) + EPS) * (1.0 / math.sqrt(d))
    k = k * lax.rsqrt(jnp.sum(k * k, axis=-1, keepdims=True) + EPS)
    beta = jax.nn.sigmoid(beta_logit.astype(jnp.float32))
    g = -jnp.exp(a_log.astype(jnp.float32)) * jax.nn.softplus(a_logit.astype(jnp.float32) + dt_bias.astype(jnp.float32))
    o = chunk_gated_delta(q, k, v, g, beta)
    o = o * lax.rsqrt(jnp.mean(o * o, axis=-1, keepdims=True) + EPS) * w_onorm.astype(jnp.float32)
    o = o.reshape(B_, S, GDN_WIDTH).astype(h.dtype)
    return (o * jax.nn.silu(z)) @ w_out


def swa_mixer(h, positions, w_in, sinks, w_out):
    B_, S, _ = h.shape
    Hk, G, d = SWA_KV_HEADS, SWA_GROUP, SWA_HEAD_DIM
    proj = h @ w_in
    q, k, v, z = jnp.split(proj, [SWA_WIDTH, SWA_WIDTH + SWA_KV_WIDTH, SWA_WIDTH + 2 * SWA_KV_WIDTH], axis=-1)
    q = rope(q.reshape(B_, S, SWA_HEADS, d), positions)
    k = rope(k.reshape(B_, S, Hk, d), positions)
    v = v.reshape(B_, S, Hk, d)
    W = WINDOW
    nblk = S // W
    qb = q.reshape(B_, nblk, W, Hk, G, d)

    def band(t):
        tb = t.reshape(B_, nblk, W, Hk, d)
        prev = jnp.pad(tb, ((0, 0), (1, 0), (0, 0), (0, 0), (0, 0)))[:, :-1]
        return jnp.concatenate([prev, tb], axis=2)

    kk, vv = band(k), band(v)
    s = jnp.einsum('bnqhgd,bnkhd->bnhgqk', qb, kk).astype(jnp.float32) * (1.0 / math.sqrt(d))
    q_idx = jnp.arange(W)[:, None] + W
    k_idx = jnp.arange(2 * W)[None, :]
    rel = q_idx - k_idx
    blk = jnp.arange(nblk)[:, None, None]
    mask = (rel >= 0) & (rel < WINDOW) & (blk * W + k_idx - W >= 0)
    logits = jnp.where(mask[None, :, None, None, :, :], s, -jnp.inf)
    sink = sinks.astype(jnp.float32).reshape(1, 1, Hk, G, 1, 1)
    m = jnp.maximum(jnp.max(logits, axis=-1, keepdims=True), sink)
    e = jnp.exp(logits - m)
    p = e / (jnp.sum(e, axis=-1, keepdims=True) + jnp.exp(sink - m))
    o = jnp.einsum('bnhgqk,bnkhd->bnqhgd', p.astype(vv.dtype), vv).reshape(B_, S, SWA_WIDTH)
    return (o * jax.nn.silu(z)) @ w_out


def setup_inputs(seed: int = 0) -> dict:
    key = jax.random.key(seed)
    ks = jax.random.split(key, 24)
    f32 = jnp.float32
    D = D_MODEL

    def lin(k, shape):
        return jax.random.normal(k, shape, f32) * (shape[-2] ** -0.5)

    x = jax.random.normal(ks[0], (BATCH, SEQ, D), f32)
    offset = jax.random.randint(ks[1], (BATCH, 1), 0, 1024, dtype=jnp.int32)
    positions = offset + jnp.arange(SEQ, dtype=jnp.int32)[None, :]
    norm_w = 1.0 + 0.1 * jax.random.normal(ks[2], (DEPTH, D), f32)
    final_norm_w = 1.0 + 0.1 * jax.random.normal(ks[3], (D,), f32)
    conv_w_in = lin(ks[4], (N_CONV_LAYERS, D, 4 * CONV_WIDTH))
    conv_w_conv = jax.random.normal(ks[5], (N_CONV_LAYERS, CONV_K, CONV_WIDTH), f32) * (CONV_K ** -0.5)
    conv_w_out = lin(ks[6], (N_CONV_LAYERS, CONV_WIDTH, D))
    fox_w_in = lin(ks[7], (N_FOX_LAYERS, D, 4 * FOX_WIDTH + FOX_HEADS))
    fox_b_f = jax.random.uniform(ks[8], (N_FOX_LAYERS, FOX_HEADS), f32, 1.0, 5.0)
    fox_w_out = lin(ks[9], (N_FOX_LAYERS, FOX_WIDTH, D))
    gdn_w_in = lin(ks[10], (N_GDN_LAYERS, D, 4 * GDN_WIDTH + 2 * GDN_HEADS))
    gdn_w_conv = jax.random.normal(ks[11], (N_GDN_LAYERS, GDN_CONV_K, 3 * GDN_WIDTH), f32) * (GDN_CONV_K ** -0.5)
    gdn_a_log = jnp.log(jax.random.uniform(ks[12], (N_GDN_LAYERS, GDN_HEADS), f32, 1.0, 16.0))
    dt = jnp.exp(jax.random.uniform(ks[13], (N_GDN_LAYERS, GDN_HEADS), f32, math.log(1e-3), math.log(1e-1)))
    gdn_dt_bias = dt + jnp.log(-jnp.expm1(-dt))
    gdn_norm_w = 1.0 + 0.1 * jax.random.normal(ks[14], (N_GDN_LAYERS, GDN_HEAD_DIM), f32)
    gdn_w_out = lin(ks[15], (N_GDN_LAYERS, GDN_WIDTH, D))
    swa_w_in = lin(ks[16], (N_SWA_LAYERS, D, 2 * SWA_WIDTH + 2 * SWA_KV_WIDTH))
    swa_sinks = 0.5 * jax.random.normal(ks[17], (N_SWA_LAYERS, SWA_HEADS), f32)
    swa_w_out = lin(ks[18], (N_SWA_LAYERS, SWA_WIDTH, D))
    return {"x": x, "positions": positions, "norm_w": norm_w, "final_norm_w": final_norm_w,
            "conv_w_in": conv_w_in, "conv_w_conv": conv_w_conv, "conv_w_out": conv_w_out,
            "fox_w_in": fox_w_in, "fox_b_f": fox_b_f, "fox_w_out": fox_w_out,
            "gdn_w_in": gdn_w_in, "gdn_w_conv": gdn_w_conv, "gdn_a_log": gdn_a_log, "gdn_dt_bias": gdn_dt_bias,
            "gdn_norm_w": gdn_norm_w, "gdn_w_out": gdn_w_out,
            "swa_w_in": swa_w_in, "swa_sinks": swa_sinks, "swa_w_out": swa_w_out}


def reference(x, positions, norm_w, final_norm_w, conv_w_in, conv_w_conv, conv_w_out,
              fox_w_in, fox_b_f, fox_w_out, gdn_w_in, gdn_w_conv, gdn_a_log, gdn_dt_bias,
              gdn_norm_w, gdn_w_out, swa_w_in, swa_sinks, swa_w_out):
    for i in range(DEPTH):
        h = rms_norm(x, norm_w[i])
        kind, j = i % N_MIXERS, i // N_MIXERS
        if kind == 0:
            y = conv_mixer(h, conv_w_in[j], conv_w_conv[j], conv_w_out[j])
        elif kind == 1:
            y = fox_mixer(h, fox_w_in[j], fox_b_f[j], fox_w_out[j])
        elif kind == 2:
            y = gdn_mixer(h, gdn_w_in[j], gdn_w_conv[j], gdn_a_log[j], gdn_dt_bias[j], gdn_norm_w[j], gdn_w_out[j])
        else:
            y = swa_mixer(h, positions, swa_w_in[j], swa_sinks[j], swa_w_out[j])
        x = x + y.astype(x.dtype)
    return rms_norm(x, final_norm_w)
```

```python
from contextlib import ExitStack
import numpy as np
import concourse.bass as bass
import concourse.mybir as mybir

F32 = mybir.dt.float32
BF16 = mybir.dt.bfloat16
I32 = mybir.dt.int32
ALU = mybir.AluOpType
AF = mybir.ActivationFunctionType

ENGS = ("pe", "act", "dve", "pool", "sp")


class Buf:
    __slots__ = ("name", "w", "r", "sem", "semval", "excl", "cls")

    def __init__(self, name):
        self.name = name
        self.w = {}
        self.r = {}
        self.sem = None
        self.semval = 0
        self.excl = False
        self.cls = None


class _Op:
    __slots__ = ("fn", "deps", "dma", "signal", "sigord", "tag")

    def __init__(self, fn, deps, dma=None):
        self.fn = fn
        self.deps = deps
        self.dma = dma
        self.signal = False
        self.sigord = 0
        self.tag = _Op.cur_tag


_Op.cur_tag = None


def _merge(d, s):
    for k, v in s.items():
        if d.get(k, 0) < v:
            d[k] = v


class Prog:
    def __init__(self, nc, stack):
        self.nc = nc
        self.stack = stack
        self.semstack = stack
        self.free_sems = {"hw": [], "sw": [], "cc": []}
        self.chans = []
        self.jx = {}
        self.ops = {e: [] for e in ENGS}
        self.esem = {}
        for e in ("pe", "act", "dve", "pool"):
            self.esem[e] = stack.enter_context(nc.semaphore("es_" + e))
        self.dsems = {}
        self.nsem = 4
        self.final = {}
        self.uid = 0
        self.need_rank = False
        self.pidx = 0
        self.scopes = False
        self.fill_queue = []

    def sb(self, name, shape, dt):
        return self.stack.enter_context(self.nc.sbuf_tensor("p%d_%s" % (self.pidx, name), list(shape), dt))

    def ps(self, name, shape, dt=F32):
        return self.stack.enter_context(self.nc.psum_tensor("p%d_%s" % (self.pidx, name), list(shape), dt))

    def buf(self, name=None):
        self.uid += 1
        return Buf(name or f"b{self.uid}")

    def _chan(self, b, cls="hw"):
        if b.sem is None:
            b.cls = cls
            if self.free_sems[cls]:
                b.sem, b.semval = self.free_sems[cls].pop()
            else:
                b.sem = self.semstack.enter_context(self.nc.semaphore("ds_%d" % self.nsem))
                self.dsems[id(b.sem)] = b.sem
                self.nsem += 1
            self.chans.append(b)
        return b.sem

    def tag(self, name):
        _Op.cur_tag = name

    def fill_step(self, k, q="pool", maxkey=99):
        for _ in range(k):
            if not self.fill_queue or self.fill_queue[0][0] > maxkey:
                return
            self.fill_queue.pop(0)[1](q)

    def fill_until(self, key, q="pool"):
        while self.fill_queue and self.fill_queue[0][0] <= key:
            self.fill_queue.pop(0)[1](q)

    def phase_begin(self):
        self.pidx += 1
        self._pstack = ExitStack()
        self._outer = self.stack
        self.stack = self._pstack

    def phase_end(self):
        self.barrier()
        self._pstack.close()
        self.stack = self._outer

    def barrier(self):
        deps = {}
        for e in ("pe", "act", "dve", "pool"):
            lst = self.ops[e]
            for i in range(len(lst) - 1, -1, -1):
                if lst[i].dma is None and lst[i].fn is not None:
                    deps[("e", e)] = i + 1
                    break
        for b in self.chans:
            if b.semval:
                deps[("d", id(b.sem))] = b.semval
        for e in ENGS:
            self.ops[e].append(_Op(None, dict(deps)))
        for b in self.chans:
            self.free_sems[b.cls].append((b.sem, b.semval))
            b.sem = None
        self.chans = []

    def coll(self, kind, groups, in_ap, out_ap, writes=(), reads=()):
        b = self.buf()
        sem = self._chan(b, "cc")
        b.semval += 1
        v = b.semval
        for w_ in writes:
            w_.w[("d", id(sem))] = v
        deps = {}
        for r_ in reads:
            _merge(deps, r_.w)
        self.ops["pool"].append(_Op(lambda e: e.collective_compute(kind, ALU.add, replica_groups=groups, ins=[in_ap.opt()], outs=[out_ap.opt()]),
                                    deps, dma=(sem, v, 1)))

    def _deps(self, reads, writes):
        d = {}
        for b in reads:
            _merge(d, b.w)
        for b in writes:
            _merge(d, b.w)
            _merge(d, b.r)
        return d

    def op(self, eng, fn, reads=(), writes=()):
        if any(b.excl for b in reads):
            writes = list(writes) + [b for b in reads if b.excl]
            reads = [b for b in reads if not b.excl]
        deps = self._deps(reads, writes)
        if eng == "pe":
            deps.pop(("e", "pe"), None)
        lst = self.ops[eng]
        lst.append(_Op(fn, deps))
        key = ("e", eng)
        v = len(lst)
        for b in reads:
            b.r[key] = v
        for b in writes:
            b.w[key] = v

    def dma(self, q, out, in_, reads=(), writes=(), chan=None, store=False, final=False, defer=False, **kw):
        if chan is None:
            chan = (reads[0] if store else writes[0])
        sem = self._chan(chan, "sw" if q == "pool" else "hw")
        assert chan.cls == ("sw" if q == "pool" else "hw"), "a DMA channel must stay on one kind of queue"
        if defer and chan in self.chans:
            self.chans.remove(chan)
        key = ("d", id(sem))
        deps = self._deps(reads, writes)
        if store and chan.semval:
            deps[key] = max(deps.get(key, 0), chan.semval)
        chan.semval += 16
        v = chan.semval
        def _fn(e):
            o = out() if callable(out) else out
            i = in_() if callable(in_) else in_
            try:
                return e.dma_start(out=o, in_=i, **kw)
            except Exception:
                print("DMA FAIL out=", o, " in=", i)
                raise
        self.ops[q].append(_Op(_fn, deps, dma=(sem, v, 16)))
        for b in reads:
            b.r[key] = v
        for b in writes:
            b.w[key] = v
        if final:
            self.final[key] = v

    def emit(self):
        nc = self.nc
        for e in ENGS:
            for o in self.ops[e]:
                for (kind, k), v in o.deps.items():
                    if kind == "e":
                        self.ops[k][v - 1].signal = True
        for e in ENGS:
            n = 0
            for o in self.ops[e]:
                if o.signal:
                    n += 1
                o.sigord = n
        if self.final:
            self.ops["sp"].append(_Op(None, dict(self.final)))
        eng_handle = {"pe": "tensor", "act": "scalar", "dve": "vector", "pool": "gpsimd", "sp": "sync"}
        stats = {}
        with nc.Block() as block:
            for e in ENGS:
                ops = self.ops[e]
                if not ops:
                    continue

                def body(eng, ops=ops, e=e):
                    if e in ("sp", "pool") and self.need_rank:
                        self.jx[e] = eng.partition_id() % 4
                    seen = {}
                    nw = 0
                    cur = None
                    sid = None
                    for o in ops:
                        if self.scopes and o.tag != cur:
                            if cur is not None:
                                nc.leave_named_scope(cur, sid, False)
                            cur = o.tag
                            if cur is not None:
                                sid, _ = nc.enter_named_scope(cur, False)
                        for (kind, k), v in o.deps.items():
                            if kind == "e":
                                sem = self.esem[k]
                                val = self.ops[k][v - 1].sigord
                            else:
                                sem = self.dsems[k]
                                val = v
                            sk = (kind, k)
                            if seen.get(sk, 0) >= val:
                                continue
                            seen[sk] = val
                            eng.wait_ge(sem, val)
                            nw += 1
                        if o.fn is None:
                            continue
                        inst = o.fn(eng)
                        if o.dma is not None:
                            if o.dma[2] == 16:
                                inst.then_inc(o.dma[0], 16)
                            else:
                                inst.then_inc(o.dma[0])
                        elif o.signal:
                            inst.then_inc(self.esem[e], 1)
                    if self.scopes and cur is not None:
                        nc.leave_named_scope(cur, sid, False)
                    stats[e] = (len(ops), nw)

                getattr(block, eng_handle[e])(body)
        self.stats = stats
        return stats


D = 1024
KC = 8
TT = 512


def load_weight_bf16(P, wdram, ncols, wb, Bwb, stg, Bstg, scale_col=None, q="sp", Bscale=None):
    piece = 1024
    n = 0
    for kc in range(KC):
        for c0 in range(0, ncols, piece):
            c1 = min(ncols, c0 + piece)
            s = n % 2
            n += 1
            P.dma(q, stg[s][:, 0:c1 - c0], wdram[kc * 128:(kc + 1) * 128, c0:c1], writes=[Bstg[s]])
            eng = ("dve", "pool")[n % 2]
            if scale_col is not None:
                P.op(eng, lambda e, s=s, kc=kc, c0=c0, c1=c1: e.tensor_scalar(
                    out=wb[:, kc, c0:c1], in0=stg[s][:, 0:c1 - c0], scalar1=scale_col[:, kc:kc + 1], scalar2=None,
                    op0=ALU.mult), reads=[Bstg[s], Bscale], writes=[Bwb])
            else:
                P.op(eng, lambda e, s=s, kc=kc, c0=c0, c1=c1: e.tensor_copy(
                    out=wb[:, kc, c0:c1], in_=stg[s][:, 0:c1 - c0]), reads=[Bstg[s]], writes=[Bwb])


class Norm:
    def __init__(self, P, ones, Bones, nmax=TT):
        self.P = P
        self.ones = ones
        self.Bones = Bones
        self.sq = P.sb("nsq", [128, KC, nmax], BF16)
        self.Bsq = P.buf()
        self.ssp = P.ps("nss", [128, nmax])
        self.Bss = P.buf()
        self.sd = P.sb("nsd", [128, nmax], F32)
        self.Bsd = P.buf()

    def run(self, xs, Bxs, n, rstd, Brstd):
        P = self.P
        P.op("act", lambda e: e.activation(out=self.sq[:, :, 0:n], in_=xs[:, :, 0:n], func=AF.Square),
             reads=[Bxs], writes=[self.Bsq])
        for kc in range(KC):
            P.op("pe", lambda e, kc=kc: e.matmul(self.ssp[:, 0:n], lhsT=self.ones[:], rhs=self.sq[:, kc, 0:n],
                                                  start=(kc == 0), stop=(kc == KC - 1)),
                 reads=[self.Bsq, self.Bones], writes=[self.Bss])
        P.op("act", lambda e: e.activation(out=self.sd[:, 0:n], in_=self.ssp[:, 0:n], func=AF.Sqrt,
                                           scale=1.0 / D, bias=self.epsc[:, 0:1]),
             reads=[self.Bss, self.Bones], writes=[self.Bsd])
        P.op("dve", lambda e: e.reciprocal(out=rstd[:, 0:n], in_=self.sd[:, 0:n]), reads=[self.Bsd], writes=[Brstd])


def build_stage0(P, io, NTOK=4096):
    NT = NTOK // TT
    xT, nw, w_in, w_cv, w_out, xo = io["l0_xT"], io["l0_nw"], io["l0_w_in"], io["l0_w_cv"], io["l0_w_out"], io["x1T"]
    xTv = xT.rearrange("(c p) t -> p c t", p=128)
    xov = xo.rearrange("(c p) t -> p c t", p=128)

    ones = P.sb("ones", [128, 128], BF16)
    Bones = P.buf()
    epsc = P.sb("epsc", [128, 1], F32)
    P.op("dve", lambda e: e.memset(ones[:], 1.0), writes=[Bones])
    P.op("dve", lambda e: e.memset(epsc[:], 1e-6), writes=[Bones])
    nws = P.sb("nws", [128, KC], F32)
    Bnw = P.buf()
    P.dma("sp", nws[:], nw[:, :], writes=[Bnw])
    wcs = P.sb("wcs", [128, KC * 3], F32)
    Bwc = P.buf()
    P.dma("sp", wcs[:], w_cv[:, :], writes=[Bwc])

    stg = [P.sb("stg%d" % i, [128, 1024], F32) for i in range(2)]
    Bstg = [P.buf(), P.buf()]
    wib = P.sb("wib", [128, KC, 4096], BF16)
    Bwib = P.buf()
    wob = P.sb("wob", [128, KC, D], BF16)
    Bwob = P.buf()

    xs = [P.sb("xs%d" % i, [128, KC, TT], F32) for i in range(2)]
    Bxs = [P.buf(), P.buf()]
    hT = [P.sb("hT%d" % i, [128, KC, TT], BF16) for i in range(2)]
    BhT = [P.buf(), P.buf()]
    rstd = [P.sb("rstd%d" % i, [128, TT], F32) for i in range(2)]
    Brstd = [P.buf(), P.buf()]
    xh = P.sb("xh", [128, KC, 2], F32)
    Bxh = P.buf()
    hh = P.sb("hh", [128, KC, 2], BF16)
    Bhh = P.buf()
    rh = P.sb("rh", [128, 2], F32)
    Brh = P.buf()
    norm = Norm(P, ones, Bones)
    norm.epsc = epsc

    ubuf = P.sb("ubuf", [128, KC, TT + 2], F32)
    Bu = [P.buf() for _ in range(KC)]
    vsb = P.sb("vsb", [128, TT], F32)
    Bv = P.buf()
    ycv = P.sb("ycv", [128, TT], F32)
    By = P.buf()
    szb = P.sb("szb", [128, TT], F32)
    Bsz = P.buf()
    tb = P.sb("tb", [128, TT], F32)
    Bt = P.buf()
    og = P.sb("og", [128, KC, TT], BF16)
    Bog = P.buf()
    xn = [P.sb("xn%d" % i, [128, TT], F32) for i in range(2)]
    Bxn = [P.buf(), P.buf()]
    pb, pc, pv, pz = [P.ps("pp%d" % i, [128, TT]) for i in range(4)]
    Bpb, Bpc, Bpv, Bpz = [P.buf() for _ in range(4)]
    py = [P.ps("py%d" % i, [128, TT]) for i in range(2)]
    Bpy = [P.buf(), P.buf()]

    load_weight_bf16(P, w_in, 4096, wib, Bwib, stg, Bstg, scale_col=nws, Bscale=Bnw)
    load_weight_bf16(P, w_out, D, wob, Bwob, stg, Bstg)

    def proj(ps, Bps, oc, h, Bh, n):
        for kc in range(KC):
            P.op("pe", lambda e, kc=kc: e.matmul(ps[:, 0:n], lhsT=wib[:, kc, oc * 128:(oc + 1) * 128],
                                                  rhs=h[:, kc, 0:n], start=(kc == 0), stop=(kc == KC - 1)),
                 reads=[Bwib, Bh], writes=[Bps])

    def make_h(x_, Bx_, n, r_, Br_, h_, Bh_):
        norm.run(x_, Bx_, n, r_, Br_)
        for kc in range(KC):
            eng = "dve" if kc % 2 == 0 else "pool"
            P.op(eng, lambda e, kc=kc: e.tensor_tensor(out=h_[:, kc, 0:n], in0=x_[:, kc, 0:n], in1=r_[:, 0:n],
                                                        op=ALU.mult), reads=[Bx_, Br_], writes=[Bh_])

    P.dma("sp", xh[:], xTv[:, :, 0:2], writes=[Bxh])
    make_h(xh, Bxh, 2, rh, Brh, hh, Bhh)
    for ci in range(KC):
        proj(pc, Bpc, 8 + ci, hh, Bhh, 2)
        proj(pv, Bpv, 16 + ci, hh, Bhh, 2)
        P.op("act", lambda e: e.activation(out=vsb[:, 0:2], in_=pv[:, 0:2], func=AF.Copy), reads=[Bpv], writes=[Bv])
        P.op("dve", lambda e, ci=ci: e.tensor_tensor(out=ubuf[:, ci, 0:2], in0=pc[:, 0:2], in1=vsb[:, 0:2], op=ALU.mult),
             reads=[Bpc, Bv], writes=[Bu[ci]])

    def prep(i):
        s = i % 2
        P.dma("sp", xs[s][:], xTv[:, :, 2 + i * TT:2 + (i + 1) * TT], writes=[Bxs[s]])
        make_h(xs[s], Bxs[s], TT, rstd[s], Brstd[s], hT[s], BhT[s])

    def main(i):
        s = i % 2
        h, Bh = hT[s], BhT[s]
        for ci in range(KC):
            proj(pc, Bpc, 8 + ci, h, Bh, TT)
            proj(pv, Bpv, 16 + ci, h, Bh, TT)
            proj(pb, Bpb, ci, h, Bh, TT)
            proj(pz, Bpz, 24 + ci, h, Bh, TT)
            P.op("act", lambda e: e.activation(out=vsb[:], in_=pv[:], func=AF.Copy), reads=[Bpv], writes=[Bv])
            P.op("dve", lambda e, ci=ci: e.tensor_tensor(out=ubuf[:, ci, 2:TT + 2], in0=pc[:], in1=vsb[:], op=ALU.mult),
                 reads=[Bpc, Bv], writes=[Bu[ci]])
            P.op("dve", lambda e, ci=ci: e.tensor_scalar(out=ycv[:], in0=ubuf[:, ci, 2:TT + 2],
                                                          scalar1=wcs[:, ci * 3 + 2:ci * 3 + 3], scalar2=None, op0=ALU.mult),
                 reads=[Bu[ci], Bwc], writes=[By])
            for k in (1, 0):
                P.op("dve", lambda e, ci=ci, k=k: e.scalar_tensor_tensor(
                    out=ycv[:], in0=ubuf[:, ci, k:k + TT], scalar=wcs[:, ci * 3 + k:ci * 3 + k + 1], in1=ycv[:],
                    op0=ALU.mult, op1=ALU.add), reads=[Bu[ci], Bwc, By], writes=[By])
            P.op("pool", lambda e, ci=ci: e.tensor_copy(out=ubuf[:, ci, 0:2], in_=ubuf[:, ci, TT:TT + 2]),
                 reads=[Bu[ci]], writes=[Bu[ci]])
            P.op("act", lambda e: e.activation(out=szb[:], in_=pz[:], func=AF.Silu), reads=[Bpz], writes=[Bsz])
            P.op("dve", lambda e: e.tensor_tensor(out=tb[:], in0=pb[:], in1=ycv[:], op=ALU.mult),
                 reads=[Bpb, By], writes=[Bt])
            P.op("pool", lambda e, ci=ci: e.tensor_tensor(out=og[:, ci, :], in0=tb[:], in1=szb[:], op=ALU.mult),
                 reads=[Bt, Bsz], writes=[Bog])
        for dc in range(KC):
            b = dc % 2
            for ci in range(KC):
                P.op("pe", lambda e, ci=ci, dc=dc, b=b: e.matmul(py[b][:], lhsT=wob[:, ci, dc * 128:(dc + 1) * 128],
                                                                rhs=og[:, ci, :], start=(ci == 0), stop=(ci == KC - 1)),
                     reads=[Bwob, Bog], writes=[Bpy[b]])
            P.op("dve", lambda e, dc=dc, b=b: e.tensor_tensor(out=xn[b][:], in0=py[b][:], in1=xs[s][:, dc, :], op=ALU.add),
                 reads=[Bpy[b], Bxs[s]], writes=[Bxn[b]])
            P.dma("sp", xov[:, dc, i * TT:(i + 1) * TT], xn[b][:], reads=[Bxn[b]], store=True)

    prep(0)
    for i in range(NT):
        P.fill_step(4)
        if i + 1 < NT:
            prep(i + 1)
        main(i)
    return P


import math
from concourse.bass import ds


class Rot:
    def __init__(self, P, name, shape, dt, n):
        self.t = [P.sb("%s%d" % (name, i), shape, dt) for i in range(n)]
        self.b = [P.buf() for _ in range(n)]
        self.i = 0

    def next(self):
        k = self.i % len(self.t)
        self.i += 1
        return self.t[k], self.b[k]


class Common:
    def __init__(self, P):
        self.P = P
        self.ones = P.sb("ones", [128, 128], BF16)
        self.Bc = P.buf()
        self.epsc = P.sb("epsc", [128, 1], F32)
        P.op("dve", lambda e: e.memset(self.ones[:], 1.0), writes=[self.Bc])
        P.op("dve", lambda e: e.memset(self.epsc[:], 1e-6), writes=[self.Bc])
        self.norm = Norm(P, self.ones, self.Bc)
        self.norm.epsc = self.epsc
        self.stg = [P.sb("stg%d" % i, [128, 1024], F32) for i in range(2)]
        self.Bstg = [P.buf(), P.buf()]
        self.xs = [P.sb("xs%d" % i, [128, KC, TT], F32) for i in range(2)]
        self.Bxs = [P.buf(), P.buf()]
        self.hT = [P.sb("hT%d" % i, [128, KC, TT], BF16) for i in range(2)]
        self.BhT = [P.buf(), P.buf()]
        self.rstd = [P.sb("rstd%d" % i, [128, TT], F32) for i in range(2)]
        self.Brstd = [P.buf(), P.buf()]
        self.f32r = Rot(P, "f32r", [128, TT], F32, 4)
        self.bf16r = Rot(P, "bf16r", [128, TT], BF16, 4)
        self.pp = [P.ps("pp%d" % i, [128, TT]) for i in range(6)]
        self.Bpp = [P.buf() for _ in range(6)]
        self.ppi = 0

    def psum(self):
        k = self.ppi % len(self.pp)
        self.ppi += 1
        return self.pp[k], self.Bpp[k]

    def make_h(self, s, n=TT):
        P = self.P
        x_, Bx_, r_, Br_, h_, Bh_ = self.xs[s], self.Bxs[s], self.rstd[s], self.Brstd[s], self.hT[s], self.BhT[s]
        self.norm.run(x_, Bx_, n, r_, Br_)
        for kc in range(KC):
            eng = "dve" if kc % 2 == 0 else "pool"
            P.op(eng, lambda e, kc=kc: e.tensor_tensor(out=h_[:, kc, 0:n], in0=x_[:, kc, 0:n], in1=r_[:, 0:n],
                                                        op=ALU.mult), reads=[Bx_, Br_], writes=[Bh_])


def load_small(P, dram, shape, name, dt=F32):
    t = P.sb(name, shape, dt)
    B = P.buf()
    P.dma("sp", t[:], dram, writes=[B])
    return t, B


class StageC:
    def __init__(self, P, cm, io, NTOK):
        self.P, self.cm = P, cm
        v = lambda a: a.rearrange("(c p) t -> p c t", p=128)
        self.szT, self.xT, self.w_out, self.xo = v(io["c_szT"]), v(io["c_xT"]), io["c_w_out"], v(io["xo"])
        SUB = NTOK // 4
        self.SUB = SUB
        self.oTs = [v(io["c_oT"][s_ * D:(s_ + 1) * D, :]) for s_ in range(4)]
        self.Bo = io["c_oT_bufs"]
        self.wob = P.sb("wob", [128, KC, D], BF16)
        self.Bwob = P.buf()
        self.og = P.sb("og", [128, KC, TT], BF16)
        self.Bog = P.buf()
        self.lo = Rot(P, "c_lo", [128, TT], F32, 2)
        self.ls = Rot(P, "c_ls", [128, TT], F32, 2)
        self.lx = Rot(P, "c_lx", [128, TT], F32, 2)

    def load_weights(self):
        load_weight_bf16(self.P, self.w_out, D, self.wob, self.Bwob, self.cm.stg, self.cm.Bstg)

    def tile(self, i, s):
        P, cm = self.P, self.cm
        sl = slice(i * TT, (i + 1) * TT)
        for ci in range(KC):
            to, Bo = self.lo.next()
            ts, Bs = self.ls.next()
            s_ = (i * TT) // self.SUB
            lo_ = i * TT - s_ * self.SUB
            P.dma("sp", to[:], self.oTs[s_][:, ci, lo_:lo_ + TT], reads=[self.Bo[s_]], writes=[Bo])
            P.dma("sp", ts[:], self.szT[:, ci, sl], writes=[Bs])
            eng = "dve" if ci % 2 == 0 else "pool"
            P.op(eng, lambda e, ci=ci, to=to, ts=ts: e.tensor_tensor(out=self.og[:, ci, :], in0=to[:], in1=ts[:], op=ALU.mult),
                 reads=[Bo, Bs], writes=[self.Bog])
        for dc in range(KC):
            ps, Bps = cm.psum()
            tx, Bx = self.lx.next()
            P.dma("sp", tx[:], self.xT[:, dc, sl], writes=[Bx])
            for ci in range(KC):
                P.op("pe", lambda e, ci=ci, dc=dc, ps=ps: e.matmul(ps[:], lhsT=self.wob[:, ci, dc * 128:(dc + 1) * 128],
                                                                  rhs=self.og[:, ci, :], start=(ci == 0), stop=(ci == KC - 1)),
                     reads=[self.Bwob, self.Bog], writes=[Bps])
            P.op("dve", lambda e, dc=dc, ps=ps, tx=tx: e.tensor_tensor(out=cm.xs[s][:, dc, :], in0=ps[:], in1=tx[:], op=ALU.add),
                 reads=[Bps, Bx], writes=[cm.Bxs[s]])
        P.dma("sp", self.xo[:, :, sl], cm.xs[s][:], reads=[cm.Bxs[s]], store=True, chan=cm.Bxs[s])


class StageLoadX:
    def __init__(self, P, cm, io, NTOK):
        self.P, self.cm = P, cm
        self.xT = io["c_xT"].rearrange("(c p) t -> p c t", p=128)

    def load_weights(self):
        pass

    def tile(self, i, s):
        self.P.dma("sp", self.cm.xs[s][:], self.xT[:, :, i * TT:(i + 1) * TT], writes=[self.cm.Bxs[s]])


class StageAFox:
    NCOL = 4104

    def __init__(self, P, cm, io, NTOK):
        self.P, self.cm = P, cm
        self.NTOK = NTOK
        self.nw, self.w_in = io["a_nw"], io["a_w_in"]
        self.xb, self.xf = io["xb"], io["xf"]
        self.szT = io["szT"].rearrange("(c p) t -> p c t", p=128)
        self.wib = P.sb("wib", [128, KC, self.NCOL], BF16)
        self.Bwib = P.buf()
        self.flr = Rot(P, "a_fl", [8, TT], F32, 2)

    def load_weights(self):
        self.nws, self.Bnw = load_small(self.P, self.nw[:, :], [128, KC], "a_nws")
        load_weight_bf16(self.P, self.w_in, self.NCOL, self.wib, self.Bwib, self.cm.stg, self.cm.Bstg, scale_col=self.nws, Bscale=self.Bnw)

    def fm_proj(self, col0, s, M=128):
        P, cm = self.P, self.cm
        ps, Bps = cm.psum()
        for kc in range(KC):
            P.op("pe", lambda e, kc=kc, ps=ps: e.matmul(ps[0:M, :], lhsT=self.wib[:, kc, col0:col0 + M], rhs=cm.hT[s][:, kc, :],
                                                        start=(kc == 0), stop=(kc == KC - 1)),
                 reads=[self.Bwib, cm.BhT[s]], writes=[Bps])
        return ps, Bps

    def tm_proj(self, col0, ncol, blk, s):
        P, cm = self.P, self.cm
        ps, Bps = cm.psum()
        for kc in range(KC):
            P.op("pe", lambda e, kc=kc, ps=ps: e.matmul(ps[:, 0:ncol], lhsT=cm.hT[s][:, kc, blk * 128:(blk + 1) * 128],
                                                        rhs=self.wib[:, kc, col0:col0 + ncol],
                                                        start=(kc == 0), stop=(kc == KC - 1)),
                 reads=[self.Bwib, cm.BhT[s]], writes=[Bps])
        return ps, Bps

    def tile(self, i, s):
        P, cm = self.P, self.cm
        sl = slice(i * TT, (i + 1) * TT)
        S_ = 4 * self.NTOK
        xb, xf = self.xb, self.xf
        n = 0
        for which in (0, 1):
            for c in range(KC):
                ps, Bps = self.fm_proj(which * 1024 + c * 128, s)
                t, Bt = cm.bf16r.next()
                if n % 2 == 0:
                    P.op("act", lambda e, t=t, ps=ps: e.activation(out=t[:], in_=ps[:], func=AF.Copy), reads=[Bps], writes=[Bt])
                else:
                    P.op("dve", lambda e, t=t, ps=ps: e.tensor_copy(out=t[:], in_=ps[:]), reads=[Bps], writes=[Bt])
                n += 1
                row0 = (c // 2) * 768 + which * 256 + (c % 2) * 128
                P.dma("sp", xb[row0:row0 + 128, sl], t[:], reads=[Bt], store=True)
        for c in range(KC):
            ps, Bps = self.fm_proj(3072 + c * 128, s)
            t, Bt = cm.f32r.next()
            P.op("act", lambda e, t=t, ps=ps: e.activation(out=t[:], in_=ps[:], func=AF.Silu), reads=[Bps], writes=[Bt])
            P.dma("sp", self.szT[:, c, sl], t[:], reads=[Bt], store=True)
        for blk in range(TT // 128):
            for half in range(2):
                ps, Bps = self.tm_proj(2048 + half * 512, 512, blk, s)
                t, Bt = cm.bf16r.next()
                P.op("dve", lambda e, t=t, ps=ps: e.tensor_copy(out=t[:], in_=ps[:]), reads=[Bps], writes=[Bt])
                for pr in range(2):
                    g = half * 2 + pr
                    tok0 = i * TT + blk * 128
                    r0 = g * 768 + 512
                    vview = xb[r0:r0 + 256, :].rearrange("(h r) (q d) -> h (r q) d", h=2, d=128)
                    P.dma("sp", vview[:, tok0:tok0 + 128, :].rearrange("h t d -> t h d"),
                          t[:, pr * 256:(pr + 1) * 256].rearrange("p (h d) -> p h d", d=128), reads=[Bt], store=True)
        ps, Bps = cm.psum()
        for kc in range(KC):
            P.op("pe", lambda e, kc=kc, ps=ps: e.matmul(ps[0:8, :], lhsT=self.wib[:, kc, 4096:4104], rhs=cm.hT[s][:, kc, :],
                                                        start=(kc == 0), stop=(kc == KC - 1)), reads=[self.Bwib, cm.BhT[s]], writes=[Bps])
        t, Bt = self.flr.next()
        P.op("dve", lambda e, t=t, ps=ps: e.tensor_copy(out=t[:], in_=ps[0:8, :]), reads=[Bps], writes=[Bt])
        P.dma("sp", xf[0:8, sl], t[:], reads=[Bt], store=True)


def build_CA(P, io, Ccls, Acls, NTOK=4096):
    cm = Common(P)
    C = Ccls(P, cm, io, NTOK)
    A = Acls(P, cm, io, NTOK)
    C.load_weights()
    A.load_weights()
    NT = NTOK // TT

    def prep(i):
        C.tile(i, i % 2)
        if Acls is not StageANull:
            cm.make_h(i % 2)

    prep(0)
    for i in range(NT):
        P.fill_step(3)
        if i + 1 < NT:
            prep(i + 1)
        A.tile(i, i % 2)
    return P


class StageAGdn(StageAFox):
    NCOL = 4112

    def __init__(self, P, cm, io, NTOK):
        self.P, self.cm = P, cm
        self.NTOK = NTOK
        self.nw, self.w_in = io["a_nw"], io["a_w_in"]
        self.xb, self.xf = io["xb"], io["xf"]
        self.szT = io["szT"].rearrange("(c p) t -> p c t", p=128)
        self.wib = P.sb("wib", [128, KC, self.NCOL], BF16)
        self.Bwib = P.buf()
        self.flr = Rot(P, "a_fl", [16, TT], F32, 2)
        self.lbufs = io["l_bufs"]
        self.piece_done = io["piece_done"]

    def load_weights(self):
        StageAFox.load_weights(self)
        P = self.P
        self.wba = P.sb("wba", [128, KC, 16], BF16)
        P.op("dve", lambda e: e.tensor_copy(out=self.wba[:].rearrange("p k (g h w) -> p k g h w", g=4, h=2, w=2),
                                            in_=self.wib[:, :, 4096:4112].rearrange("p k (w g h) -> p k g h w", w=2, g=4, h=2)),
             reads=[self.Bwib], writes=[self.Bwib])

    def tile(self, i, s):
        P, cm = self.P, self.cm
        sl = slice(i * TT, (i + 1) * TT)
        xb, xf = self.xb, self.xf
        for c in range(24):
            ps, Bps = self.fm_proj(c * 128, s)
            t, Bt = cm.f32r.next()
            if c % 2 == 0:
                P.op("act", lambda e, t=t, ps=ps: e.activation(out=t[:], in_=ps[:], func=AF.Copy), reads=[Bps], writes=[Bt])
            else:
                P.op("dve", lambda e, t=t, ps=ps: e.tensor_copy(out=t[:], in_=ps[:]), reads=[Bps], writes=[Bt])
            which, h = c // 8, c % 8
            row0 = (h // 2) * 768 + ((h % 2) * 3 + which) * 128
            P.dma("sp", xb[row0:row0 + 128, sl], t[:], reads=[Bt], writes=[self.lbufs[i // 2]], store=True, chan=Bt)
        for c in range(KC):
            ps, Bps = self.fm_proj(3072 + c * 128, s)
            t, Bt = cm.f32r.next()
            P.op("act", lambda e, t=t, ps=ps: e.activation(out=t[:], in_=ps[:], func=AF.Silu), reads=[Bps], writes=[Bt])
            P.dma("sp", self.szT[:, c, sl], t[:], reads=[Bt], store=True)
        ps, Bps = cm.psum()
        for kc in range(KC):
            P.op("pe", lambda e, kc=kc, ps=ps: e.matmul(ps[0:16, :], lhsT=self.wba[:, kc, :], rhs=cm.hT[s][:, kc, :],
                                                        start=(kc == 0), stop=(kc == KC - 1)), reads=[self.Bwib, cm.BhT[s]], writes=[Bps])
        t, Bt = self.flr.next()
        P.op("dve", lambda e, t=t, ps=ps: e.tensor_copy(out=t[:], in_=ps[0:16, :]), reads=[Bps], writes=[Bt])
        P.dma("sp", xf[0:16, sl], t[:], reads=[Bt], store=True)
        if i % 2 == 1:
            self.piece_done(i // 2)


class StageANull:
    def __init__(self, P, cm, io, NTOK):
        pass

    def load_weights(self):
        pass

    def tile(self, i, s):
        pass


import math
import numpy as np
from concourse.bass import ds

S = 16384
NB = S // 128
QT = 512
NEG = -30000.0


def fox_consts():
    k = np.arange(128)
    U = (k[:, None] <= k[None, :]).astype(np.float32)
    SU = (k[:, None] < k[None, :]).astype(np.float32)
    E127 = np.zeros((128, 128), np.float32)
    E127[127, :] = 1.0
    ident = np.eye(128, dtype=np.float32)
    masks = np.zeros((4, 128, 512), np.float32)
    for r in range(4):
        for rp in range(4):
            blk = masks[r][:, rp * 128:(rp + 1) * 128]
            if rp < r:
                blk[:] = NEG
            elif rp == r:
                blk[:] = np.where(k[:, None] <= k[None, :], 0.0, NEG)
    return {"cU": U, "cSU": SU, "cE127": E127, "cI": ident, "cMask": np.ascontiguousarray(masks.transpose(1, 0, 2).reshape(128, 2048))}


def build_foxB(P, io, NH=2, S_=S):
    NBk = S_ // 128
    NQ = S_ // QT
    NTOK = S_ // 4
    scale = 1.0 / math.sqrt(128.0)
    yb = io["yb"]
    By = io["yb_bufs"]
    qTq = [yb[c_ * 768:c_ * 768 + 256, :].rearrange("(h p) s -> h p s", p=128) for c_ in range(4)]
    kTq = [yb[c_ * 768 + 256:c_ * 768 + 512, :].rearrange("(h p) s -> h p s", p=128) for c_ in range(4)]
    vvq = [yb[c_ * 768 + 512:c_ * 768 + 768, :].rearrange("(h r) (q d) -> h (r q) d", h=2, d=128) for c_ in range(4)]
    fl = io["yf"]
    bfd = io["fox_bf"]
    cdr = {n: io[n] for n in ("cU", "cSU", "cE127", "cI", "cMask")}
    xo = io["xo"]

    def const(name, w):
        t = P.sb("k" + name, [128, w], F32)
        B = P.buf()
        P.dma("sp", t[:], cdr[name][:, :], writes=[B])
        return t, B

    U, BU = const("cU", 128)
    SU, BSU = const("cSU", 128)
    E127, BE = const("cE127", 128)
    I32f, BI = const("cI", 128)
    Mf, BM = const("cMask", 2048)
    Bk = P.buf()
    ones_f = P.sb("ones_f", [128, 128], F32)
    ones_b = P.sb("ones_b", [128, 128], BF16)
    ident_b = P.sb("ident_b", [128, 128], BF16)
    mask_b = P.sb("mask_b", [128, 2048], BF16)
    P.op("dve", lambda e: e.memset(ones_f[:], 1.0), writes=[Bk])
    P.op("dve", lambda e: e.memset(ones_b[:], 1.0), writes=[Bk])
    P.op("dve", lambda e: e.tensor_copy(out=ident_b[:], in_=I32f[:]), reads=[BI], writes=[Bk])
    P.op("dve", lambda e: e.tensor_copy(out=mask_b[:], in_=Mf[:]), reads=[BM], writes=[Bk])
    bfs = P.sb("bfs", [128, NH], F32)
    nbf = P.sb("nbf", [128, NH], F32)
    Bbf = P.buf()
    P.dma("sp", bfs[:], bfd[:, 0:NH], writes=[Bbf])
    P.op("dve", lambda e: e.tensor_scalar(out=nbf[:], in0=bfs[:], scalar1=-1.0, scalar2=None, op0=ALU.mult),
         reads=[Bbf], writes=[Bbf])
    flr = P.sb("flr", [128, NH, 128], F32)
    Bflr = P.buf()
    P.dma("sp", flr[:], fl.rearrange("h (b s) -> b h s", s=128), writes=[Bflr])
    fls = P.sb("fls", [128, NH * NBk], F32)
    Bfl = P.buf()

    ks = [P.sb("ks%d" % i, [128, S_], BF16) for i in range(2)]
    Bks = [[P.buf() for _ in range(4)] for _ in range(2)]
    vs = [P.sb("vs%d" % i, [128, NBk, 128], BF16) for i in range(2)]
    Bvs = [[P.buf() for _ in range(4)] for _ in range(2)]
    cpos = [P.sb("cpos%d" % i, [128, NBk], F32) for i in range(2)]
    clast = [P.sb("clast%d" % i, [128, NBk], F32) for i in range(2)]
    Bcp = [P.buf(), P.buf()]
    l1 = P.sb("l1", [128, NBk], F32)
    Bl1 = P.buf()
    totT = P.sb("totT", [128, 128], F32)
    Btot = P.buf()
    qs = [P.sb("qs%d" % i, [128, QT], BF16) for i in range(2)]
    Bqs = [P.buf(), P.buf()]
    biasM = [P.sb("biasM%d" % i, [128, NBk], F32) for i in range(2)]
    BbM = [P.buf(), P.buf()]
    Rm = [P.sb("Rm%d" % i, [128, QT], BF16) for i in range(2)]
    BRm = [P.buf(), P.buf()]
    NS = 3
    pst = [P.ps("pst%d" % i, [128, QT]) for i in range(NS)]
    Bpst = [P.buf() for _ in range(NS)]
    pts = [P.sb("pts%d" % i, [128, QT], BF16) for i in range(NS)]
    Bpts = [P.buf() for _ in range(NS)]
    po = [P.ps("po%d" % i, [128, QT]) for i in range(2)]
    Bpo = [P.buf(), P.buf()]
    pr = [P.ps("pr%d" % i, [128, QT]) for i in range(2)]
    Bpr = [P.buf(), P.buf()]
    racc = [[P.sb("racc%d%d" % (i, z), [128, QT], F32) for z in range(2)] for i in range(2)]
    Bracc = [[P.buf(), P.buf()] for _ in range(2)]
    rinv = P.sb("rinv", [128, QT], F32)
    Brinv = P.buf()
    osb = [P.sb("osb%d" % i, [128, QT], F32) for i in range(2)]
    Bosb = [P.buf(), P.buf()]
    pmisc = P.ps("pmisc", [128, 512])
    Bpm = P.buf()

    def load_kv(h, c_):
        hs = h % 2
        P.dma("sp", ks[hs][:, c_ * NTOK:(c_ + 1) * NTOK], kTq[c_][h, :, :], reads=[By[c_]], writes=[Bks[hs][c_]])
        P.dma("sp", vs[hs][:, c_ * (NTOK // 128):(c_ + 1) * (NTOK // 128), :], vvq[c_][h].rearrange("(b s) d -> s b d", s=128),
              reads=[By[c_]], writes=[Bvs[hs][c_]])

    def head_prep(h):
        hs = h % 2
        for c_ in range(4 if h > 0 else 1):
            load_kv(h, c_)
        P.op("pe", lambda e: e.transpose(pmisc[:, 384:384 + NBk], flr[:, h, :], I32f[:]), reads=[Bflr, BI], writes=[Bpm])
        P.op("dve", lambda e: e.tensor_copy(out=fls[:, h * NBk:(h + 1) * NBk], in_=pmisc[:, 384:384 + NBk]), reads=[Bpm], writes=[Bfl])
        f_h = fls[:, h * NBk:(h + 1) * NBk]
        P.op("act", lambda e: e.activation(out=l1[:], in_=f_h, func=AF.Exp, scale=-1.0, bias=nbf[:, h:h + 1]),
             reads=[Bfl, Bbf], writes=[Bl1])
        P.op("act", lambda e: e.activation(out=l1[:], in_=l1[:], func=AF.Ln, scale=1.0, bias=ones_f[:, 0:1]),
             reads=[Bl1, Bk], writes=[Bl1])
        P.op("pe", lambda e: e.matmul(pmisc[0:NBk, 0:128], lhsT=l1[:, 0:NBk], rhs=ones_f[:], start=True, stop=True),
             reads=[Bl1, Bk], writes=[Bpm])
        P.op("dve", lambda e: e.tensor_copy(out=totT[0:NBk, :], in_=pmisc[0:NBk, 0:128]), reads=[Bpm], writes=[Btot])
        P.op("pe", lambda e: e.matmul(pmisc[:, 128:128 + NBk], lhsT=U[:], rhs=l1[:, 0:NBk], start=True, stop=False),
             reads=[Bl1, BU], writes=[Bpm])
        P.op("pe", lambda e: e.matmul(pmisc[:, 128:128 + NBk], lhsT=totT[0:NBk, :], rhs=SU[0:NBk, 0:NBk], start=False, stop=True),
             reads=[Btot, BSU], writes=[Bpm])
        P.op("dve", lambda e: e.tensor_copy(out=cpos[hs][:], in_=pmisc[:, 128:128 + NBk]), reads=[Bpm], writes=[Bcp[hs]])
        P.op("pe", lambda e: e.matmul(pmisc[:, 256:256 + NBk], lhsT=E127[:], rhs=cpos[hs][:], start=True, stop=True),
             reads=[Bcp[hs], BE], writes=[Bpm])
        P.op("dve", lambda e: e.tensor_copy(out=clast[hs][:], in_=pmisc[:, 256:256 + NBk]), reads=[Bpm], writes=[Bcp[hs]])

    items = []
    for h in range(NH):
        for j in range(NQ):
            nkb = 4 * j + 4
            for kb in range(nkb):
                items.append((h, j, kb, nkb))
    qcount = [0]

    def qtile_prep(h, j):
        P.fill_step(2 if j >= 8 else 0, q="sp", maxkey=3)
        hs = h % 2
        s = qcount[0] % 2
        qcount[0] += 1
        nkb = 4 * j + 4
        cq = (j * QT) // NTOK
        if h == 0 and cq > 0 and (j * QT) % NTOK == 0:
            load_kv(h, cq)
        P.dma("sp", qs[s][:], qTq[cq][h, :, j * QT - cq * NTOK:(j + 1) * QT - cq * NTOK], reads=[By[cq]], writes=[Bqs[s]])
        P.op("dve", lambda e: e.tensor_scalar(out=biasM[s][:, 0:nkb], in0=cpos[hs][:, 0:nkb],
                                              scalar1=clast[hs][:, nkb - 1:nkb], scalar2=None, op0=ALU.subtract),
             reads=[Bcp[hs]], writes=[BbM[s]])
        for r in range(4):
            P.op("dve", lambda e, r=r: e.tensor_scalar(out=Rm[s][:, r * 128:(r + 1) * 128], in0=I32f[:],
                                                         scalar1=biasM[s][:, 4 * j + r:4 * j + r + 1], scalar2=-math.sqrt(128.0),
                                                         op0=ALU.mult, op1=ALU.mult),
                 reads=[BI, BbM[s]], writes=[BRm[s]])
        return s

    qslot = {}

    def QK(n):
        h, j, kb, nkb = items[n]
        hs = h % 2
        if kb == 0:
            if j == 0:
                head_prep(h)
            qslot[(h, j)] = qtile_prep(h, j)
        s = qslot[(h, j)]
        b = n % NS
        diag = kb >= 4 * j
        P.op("pe", lambda e: e.matmul(pst[b][:], lhsT=ks[hs][:, kb * 128:(kb + 1) * 128], rhs=qs[s][:], start=True, stop=False),
             reads=[Bks[hs][(kb * 128) // NTOK], Bqs[s]], writes=[Bpst[b]])
        P.op("pe", lambda e: e.matmul(pst[b][:], lhsT=ones_b[:], rhs=Rm[s][:], start=False, stop=not diag),
             reads=[Bk, BRm[s]], writes=[Bpst[b]])
        if diag:
            r = kb - 4 * j
            P.op("pe", lambda e: e.matmul(pst[b][:], lhsT=ident_b[:], rhs=mask_b[:, r * 512:(r + 1) * 512], start=False, stop=True),
                 reads=[Bk], writes=[Bpst[b]])

    def PV(n):
        h, j, kb, nkb = items[n]
        hs = h % 2
        s = qslot[(h, j)]
        b = n % NS
        a = (h * NQ + j) % 2
        P.op("act", lambda e: e.activation(out=pts[b][:], in_=pst[b][:], func=AF.Exp, scale=scale, bias=biasM[s][:, kb:kb + 1]),
             reads=[Bpst[b], BbM[s]], writes=[Bpts[b]])
        P.op("pe", lambda e: e.matmul(po[a][:], lhsT=vs[hs][:, kb, :], rhs=pts[b][:], start=(kb == 0), stop=(kb == nkb - 1)),
             reads=[Bvs[hs][(kb * 128) // NTOK], Bpts[b]], writes=[Bpo[a]])
        ra, Bra = racc[a][kb % 2], Bracc[a][kb % 2]
        if kb < 2:
            P.op("dve", lambda e: e.tensor_copy(out=ra[:], in_=pts[b][:]), reads=[Bpts[b]], writes=[Bra])
        else:
            P.op("dve", lambda e: e.tensor_tensor(out=ra[:], in0=ra[:], in1=pts[b][:], op=ALU.add), reads=[Bpts[b], Bra], writes=[Bra])
        if kb == nkb - 1:
            for z_ in range(2):
                P.op("pe", lambda e, z_=z_: e.matmul(pr[a][:], lhsT=ones_f[:], rhs=racc[a][z_][:], start=(z_ == 0), stop=(z_ == 1)),
                     reads=[Bk, Bracc[a][z_]], writes=[Bpr[a]])
            P.op("dve", lambda e: e.reciprocal(out=rinv[:], in_=pr[a][:]), reads=[Bpr[a]], writes=[Brinv])
            P.op("dve", lambda e: e.tensor_tensor(out=osb[a][:], in0=po[a][:], in1=rinv[:], op=ALU.mult),
                 reads=[Bpo[a], Brinv], writes=[Bosb[a]])
            t0 = j * QT
            r0_ = (t0 // NTOK) * 256 + h * 128
            sub_ = (t0 % NTOK) // (NTOK // 4)
            P.dma("sp", xo[r0_:r0_ + 128, (t0 % NTOK):(t0 % NTOK) + QT], osb[a][:], reads=[Bosb[a]], writes=[io["lo_bufs"][sub_]], store=True, chan=Bosb[a])
            if h == NH - 1 and t0 >= 3 * NTOK and ((t0 + QT) % (NTOK // 4)) == 0:
                io["sub_done"](sub_)

    LA = 2
    for n in range(len(items) + LA):
        if n < len(items):
            QK(n)
        if n - LA >= 0:
            PV(n - LA)
    return P


import math
import numpy as np
from concourse.bass import ds

NEG = -30000.0
TT = 512


def gdn_consts():
    k = np.arange(128)
    same = (k[:, None] // 64) == (k[None, :] // 64)
    c = {}
    c["gU"] = ((k[:, None] <= k[None, :]) & same).astype(np.float32)
    c["gSL"] = ((k[:, None] > k[None, :]) & same).astype(np.float32)
    c["gSame"] = same.astype(np.float32)
    h0 = np.zeros((128, 128), np.float32)
    h0[:64, :] = 1.0
    c["gH0"] = h0
    c["gH1"] = 1.0 - h0
    c["gI"] = np.eye(128, dtype=np.float32)
    c["gMS"] = np.where((k[:, None] > k[None, :]) & same, 0.0, NEG).astype(np.float32)
    c["gMT"] = np.where((k[None, :] >= k[:, None]) & same, 0.0, NEG).astype(np.float32)
    return c


CN = ("gU", "gSL", "gSame", "gH0", "gH1", "gI", "gMS", "gMT")


def build_gdnB(P, io, NH=2, S_=16384, dbg=False):
    NB = S_ // 128
    NT = S_ // TT
    NTOK = S_ // 4
    qkvq = [io["yb"][c_ * 768:(c_ + 1) * 768, :].rearrange("(h w p) s -> h w p s", h=NH, w=3) for c_ in range(4)]
    By = io["yb_bufs"]
    wcv = io["gdn_wcv"]
    ba = io["yf"]
    hp = io["gdn_hp"]
    onw = io["gdn_onw"]
    cdr = {n: io[n] for n in CN}
    xo = io["xo"]

    def ld(name, dram, w):
        t = P.sb(name, [128, w], F32)
        B = P.buf()
        P.dma("sp", t[:], dram, writes=[B])
        return t, B

    K = {}
    BK = P.buf()
    for n in CN:
        K[n] = P.sb("k" + n, [128, 128], F32)
        P.dma("sp", K[n][:], cdr[n][:, :], writes=[BK])
    wcs, Bw = ld("wcs", wcv[:, 0:NH * 12], NH * 12)
    hps, Bhp = ld("hps", hp[:, 0:NH * 2], NH * 2)
    bar = P.sb("bar", [128, NH * 2, 128], F32)
    Bbar = P.buf()
    P.dma("sp", bar[:], ba.rearrange("r (b s) -> b r s", s=128), writes=[Bbar])
    bas = P.sb("bas", [128, NH * 2 * NB], F32)
    Bba = P.buf()
    onws, Bon = ld("onws", onw[:, :], 128)
    ones_b = P.sb("ones_b", [128, 128], BF16)
    ident_b = P.sb("ident_b", [128, 128], BF16)
    epsc = P.sb("epsc", [128, 1], F32)
    one_c = P.sb("one_c", [128, 1], F32)
    P.op("dve", lambda e: e.memset(ones_b[:], 1.0), writes=[BK])
    P.op("dve", lambda e: e.memset(epsc[:], 1e-6), writes=[BK])
    P.op("dve", lambda e: e.memset(one_c[:], 1.0), writes=[BK])
    P.op("dve", lambda e: e.tensor_copy(out=ident_b[:], in_=K["gI"][:]), reads=[BK], writes=[BK])

    pf = [P.ps("pf%d" % i, [128, 512]) for i in range(6)]
    Bpf = [P.buf() for _ in range(6)]
    for b_ in Bpf:
        b_.excl = True
    pfslots = [(pf[i % 6][:, ((i // 6) % 4) * 128:((i // 6) % 4 + 1) * 128], Bpf[i % 6]) for i in range(24)]
    pbt = P.ps("pbt", [128, 1024], BF16)
    Bpbt = P.buf()
    Bpbt.excl = True
    pbslots = [(pbt[:, i * 128:(i + 1) * 128], Bpbt) for i in range(8)]
    pbig = P.ps("pbig", [128, 512])
    Bpbig = P.buf()
    Bpbig.excl = True
    for r_ in range(NH * 2):
        P.op("pe", lambda e, r_=r_: e.transpose(pbig[:, (r_ % 4) * 128:(r_ % 4) * 128 + NB], bar[:, r_, :], K["gI"][:]), reads=[Bbar, BK], writes=[Bpbig])
        P.op("dve", lambda e, r_=r_: e.tensor_copy(out=bas[:, r_ * NB:(r_ + 1) * NB], in_=pbig[:, (r_ % 4) * 128:(r_ % 4) * 128 + NB]),
             reads=[Bpbig], writes=[Bba])
    cnt = {"f": 0, "b": 0, 0: 0, 1: 0}
    cur_head = [0]

    def psf():
        h_ = cur_head[0]
        cnt[h_] += 1
        k_ = cnt[h_] % 12
        bank = h_ * 3 + (k_ % 3)
        return pf[bank][:, (k_ // 3) * 128:(k_ // 3 + 1) * 128], Bpf[bank]

    def psb():
        cnt["b"] += 1
        return pbslots[cnt["b"] % 8]

    class Head:
        pass

    heads = []
    for hh in range(NH):
        H = Head()
        H.hh = hh
        n = "h%d_" % hh
        mk = lambda nm, w, dt=F32: P.sb(n + nm, [128, w], dt)
        H.tab = {t: mk("t" + t, NB) for t in ("beta", "nbeta", "g", "gc", "egc", "ekd", "bgc", "eg0", "eg1", "tmp")}
        H.Btab = P.buf()
        H.raw = [P.sb(n + "raw%d" % i_, [128, 3, TT + 3], F32) for i_ in range(2)]
        H.Braw = [P.buf(), P.buf()]
        H.cv = [mk("cv%d" % w, TT) for w in range(3)]
        H.Bcv = [P.buf() for _ in range(3)]
        H.sq = mk("sq", TT, BF16)
        H.Bsq = P.buf()
        H.rs = mk("rs", TT)
        H.Brs = P.buf()
        H.qn = mk("qn", TT, BF16)
        H.kn = mk("kn", TT, BF16)
        H.Bqn, H.Bkn = P.buf(), P.buf()
        H.S32 = mk("S32", 128)
        H.Sb = [mk("Sb%d" % i, 128, BF16) for i in range(2)]
        H.BS32 = P.buf()
        H.BSb = [P.buf(), P.buf()]
        H.scur = 0
        H.osb = mk("osb", TT)
        H.Bosb = P.buf()
        H.t = {}
        H.B = {}
        for nm, dt in (("kbg", BF16), ("kdec", BF16), ("vb32", F32), ("vb16", BF16), ("gU", F32), ("Ds", F32), ("DT", F32),
                       ("X", BF16), ("XT", BF16), ("X2", BF16), ("XT2", BF16), ("N", BF16), ("N2", BF16),
                       ("AqkT", BF16), ("u32", F32), ("wT", BF16), ("vnew", BF16), ("aq", F32), ("o32", F32),
                       ("on32", F32), ("junk", F32)):
            H.t[nm] = mk("b_" + nm, 128, dt)
            H.B[nm] = P.buf()
        H.ss = mk("ss", 1)
        H.rstd = mk("rstd", 1)
        H.Bss = P.buf()
        heads.append(H)

    def head_tables(H):
        hh = H.hh
        T = H.tab
        bl = bas[:, (hh * 2) * NB:(hh * 2 + 1) * NB]
        al = bas[:, (hh * 2 + 1) * NB:(hh * 2 + 2) * NB]
        rw = [Bba, Bhp, BK, H.Btab]
        P.op("act", lambda e: e.activation(out=T["beta"][:], in_=bl, func=AF.Sigmoid), reads=rw, writes=[H.Btab])
        P.op("dve", lambda e: e.tensor_scalar(out=T["nbeta"][:], in0=T["beta"][:], scalar1=-1.0, scalar2=None, op0=ALU.mult),
             reads=rw, writes=[H.Btab])
        P.op("act", lambda e: e.activation(out=T["tmp"][:], in_=al, func=AF.Exp, bias=hps[:, hh * 2 + 1:hh * 2 + 2], scale=1.0),
             reads=rw, writes=[H.Btab])
        P.op("act", lambda e: e.activation(out=T["tmp"][:], in_=T["tmp"][:], func=AF.Ln, bias=one_c[:, 0:1], scale=1.0),
             reads=rw, writes=[H.Btab])
        P.op("act", lambda e: e.activation(out=H.ss[:], in_=hps[:, hh * 2:hh * 2 + 1], func=AF.Exp), reads=rw, writes=[H.Bss])
        P.op("dve", lambda e: e.tensor_scalar(out=T["g"][:], in0=T["tmp"][:], scalar1=H.ss[:, 0:1], scalar2=-1.0,
                                              op0=ALU.mult, op1=ALU.mult), reads=rw + [H.Bss], writes=[H.Btab])
        P.op("pe", lambda e: e.matmul(pbig[:, 0:NB], lhsT=K["gU"][:], rhs=T["g"][:], start=True, stop=True), reads=rw, writes=[Bpbig])
        P.op("pe", lambda e: e.matmul(pbig[:, 128:128 + NB], lhsT=K["gSame"][:], rhs=T["g"][:], start=True, stop=True), reads=rw, writes=[Bpbig])
        P.op("pe", lambda e: e.matmul(pbig[:, 256:256 + NB], lhsT=K["gH0"][:], rhs=T["g"][:], start=True, stop=True), reads=rw, writes=[Bpbig])
        P.op("pe", lambda e: e.matmul(pbig[:, 384:384 + NB], lhsT=K["gH1"][:], rhs=T["g"][:], start=True, stop=True), reads=rw, writes=[Bpbig])
        P.op("dve", lambda e: e.tensor_copy(out=T["gc"][:], in_=pbig[:, 0:NB]), reads=[Bpbig], writes=[H.Btab])
        P.op("act", lambda e: e.activation(out=T["egc"][:], in_=pbig[:, 0:NB], func=AF.Exp), reads=[Bpbig], writes=[H.Btab])
        P.op("dve", lambda e: e.tensor_tensor(out=T["tmp"][:], in0=pbig[:, 128:128 + NB], in1=T["gc"][:], op=ALU.subtract),
             reads=[Bpbig, H.Btab], writes=[H.Btab])
        P.op("act", lambda e: e.activation(out=T["ekd"][:], in_=T["tmp"][:], func=AF.Exp), reads=[H.Btab], writes=[H.Btab])
        P.op("act", lambda e: e.activation(out=T["eg0"][:], in_=pbig[:, 256:256 + NB], func=AF.Exp), reads=[Bpbig], writes=[H.Btab])
        P.op("act", lambda e: e.activation(out=T["eg1"][:], in_=pbig[:, 384:384 + NB], func=AF.Exp), reads=[Bpbig], writes=[H.Btab])
        P.op("dve", lambda e: e.tensor_tensor(out=T["bgc"][:], in0=T["beta"][:], in1=T["egc"][:], op=ALU.mult),
             reads=[H.Btab], writes=[H.Btab])
        P.op("dve", lambda e: e.memset(H.S32[:], 0.0), writes=[H.BS32])
        P.op("dve", lambda e: e.memset(H.Sb[0][:], 0.0), writes=[H.BSb[0]])
        P.op("dve", lambda e: e.memset(H.raw[0][:, :, 0:3], 0.0), writes=[H.Braw[0]])

    def load_raw(H, ti):
        hh = H.hh
        s_ = ti % 2
        c_ = (ti * TT) // NTOK
        lo = ti * TT - c_ * NTOK
        P.dma("sp", H.raw[s_][:, :, 3:TT + 3], qkvq[c_][hh, :, :, lo:lo + TT].rearrange("w p t -> p w t"), reads=[By[c_]], writes=[H.Braw[s_]])

    def carry(H, ti):
        s_ = ti % 2
        if ti > 0:
            P.op("dve", lambda e: e.tensor_copy(out=H.raw[s_][:, :, 0:3], in_=H.raw[1 - s_][:, :, TT:TT + 3]), reads=[H.Braw[1 - s_]], writes=[H.Braw[s_]])

    def conv_phase(H, ti):
        hh = H.hh
        s_ = ti % 2
        raw, Braw = H.raw[s_], H.Braw[s_]
        for w in range(3):
            cv, Bc = H.cv[w], H.Bcv[w]
            wcl = [wcs[:, hh * 12 + w * 4 + k:hh * 12 + w * 4 + k + 1] for k in range(4)]
            P.op("dve", lambda e, cv=cv, w=w, wcl=wcl: e.tensor_scalar(out=cv[:], in0=raw[:, w, 0:TT], scalar1=wcl[0], scalar2=None, op0=ALU.mult),
                 reads=[Braw, Bw], writes=[Bc])
            for k in (1, 2, 3):
                P.op("dve", lambda e, cv=cv, k=k, w=w, wcl=wcl: e.scalar_tensor_tensor(out=cv[:], in0=raw[:, w, k:k + TT], scalar=wcl[k], in1=cv[:],
                                                                                    op0=ALU.mult, op1=ALU.add), reads=[Braw, Bw, Bc], writes=[Bc])
            P.op("act", lambda e, cv=cv: e.activation(out=cv[:], in_=cv[:], func=AF.Silu), reads=[Bc], writes=[Bc])
        for w, dst, Bd, sc in ((0, H.qn, H.Bqn, 1.0 / math.sqrt(128.0)), (1, H.kn, H.Bkn, 1.0)):
            cv, Bc = H.cv[w], H.Bcv[w]
            P.op("act", lambda e, cv=cv: e.activation(out=H.sq[:], in_=cv[:], func=AF.Square), reads=[Bc], writes=[H.Bsq])
            P.op("pe", lambda e: e.matmul(pbig[:], lhsT=ones_b[:], rhs=H.sq[:], start=True, stop=True), reads=[H.Bsq, BK], writes=[Bpbig])
            P.op("act", lambda e: e.activation(out=H.rs[:], in_=pbig[:], func=AF.Ln, bias=epsc[:, 0:1], scale=1.0), reads=[Bpbig, BK], writes=[H.Brs])
            P.op("act", lambda e: e.activation(out=H.rs[:], in_=H.rs[:], func=AF.Exp, scale=-0.5), reads=[H.Brs], writes=[H.Brs])
            P.op("dve", lambda e, cv=cv, dst=dst, sc=sc: e.scalar_tensor_tensor(out=dst[:], in0=cv[:], scalar=sc, in1=H.rs[:],
                                                                              op0=ALU.mult, op1=ALU.mult), reads=[Bc, H.Brs], writes=[Bd])

    def evac(eng, dst, Bdst, src, Bsrc, extra_reads=()):
        if eng == "act":
            P.op("act", lambda e: e.activation(out=dst, in_=src, func=AF.Copy), reads=[Bsrc] + list(extra_reads), writes=[Bdst])
        else:
            P.op("dve", lambda e: e.tensor_copy(out=dst, in_=src), reads=[Bsrc] + list(extra_reads), writes=[Bdst])

    def pre_scan(H, blk, bi):
        T, t, B = H.tab, H.t, H.B
        cur_head[0] = H.hh
        cs = slice(bi * 128, (bi + 1) * 128)
        col = lambda nm: T[nm][:, blk:blk + 1]
        kn, qn = H.kn[:, cs], H.qn[:, cs]
        pk, Bpk = psb()
        P.op("pe", lambda e: e.transpose(pk, kn, ident_b[:]), reads=[H.Bkn, BK], writes=[Bpk])
        yield
        P.op("act", lambda e: e.activation(out=t["kbg"][:], in_=pk, func=AF.Copy, scale=col("bgc")), reads=[Bpk, H.Btab], writes=[B["kbg"]])
        yield
        P.op("dve", lambda e: e.tensor_scalar(out=t["kdec"][:], in0=pk, scalar1=col("ekd"), scalar2=None, op0=ALU.mult),
             reads=[Bpk, H.Btab], writes=[B["kdec"]])
        yield
        pv, Bpv = psf()
        P.op("pe", lambda e: e.transpose(pv, H.cv[2][:, cs], K["gI"][:]), reads=[H.Bcv[2], BK], writes=[Bpv])
        yield
        P.op("dve", lambda e: e.tensor_scalar(out=t["vb32"][:], in0=pv, scalar1=col("beta"), scalar2=None, op0=ALU.mult),
             reads=[Bpv, H.Btab], writes=[B["vb32"]])
        yield
        P.op("act", lambda e: e.activation(out=t["vb16"][:], in_=pv, func=AF.Copy, scale=col("beta")), reads=[Bpv, H.Btab], writes=[B["vb16"]])
        yield
        P.op("act", lambda e: e.activation(out=t["gU"][:], in_=K["gU"][:], func=AF.Copy, scale=col("g")),
             reads=[BK, H.Btab], writes=[B["gU"]])
        yield
        pd, Bpd = psf()
        P.op("pe", lambda e: e.matmul(pd, lhsT=t["gU"][:], rhs=K["gSL"][:], start=True, stop=False), reads=[B["gU"], BK], writes=[Bpd])
        P.op("pe", lambda e: e.matmul(pd, lhsT=K["gI"][:], rhs=K["gMS"][:], start=False, stop=True), reads=[BK], writes=[Bpd])
        yield
        P.op("act", lambda e: e.activation(out=t["Ds"][:], in_=pd, func=AF.Exp), reads=[Bpd], writes=[B["Ds"]])
        yield
        pdt, Bpdt = psf()
        P.op("pe", lambda e: e.matmul(pdt, lhsT=K["gSL"][:], rhs=t["gU"][:], start=True, stop=False), reads=[B["gU"], BK], writes=[Bpdt])
        P.op("pe", lambda e: e.matmul(pdt, lhsT=K["gI"][:], rhs=K["gMT"][:], start=False, stop=True), reads=[BK], writes=[Bpdt])
        yield
        P.op("act", lambda e: e.activation(out=t["DT"][:], in_=pdt, func=AF.Exp), reads=[Bpdt], writes=[B["DT"]])
        yield
        pg, Bpg = psf()
        P.op("pe", lambda e: e.matmul(pg, lhsT=kn, rhs=kn, start=True, stop=True), reads=[H.Bkn], writes=[Bpg])
        yield
        P.op("dve", lambda e: e.scalar_tensor_tensor(out=t["X"][:], in0=pg, scalar=col("nbeta"), in1=t["Ds"][:], op0=ALU.mult, op1=ALU.mult),
             reads=[Bpg, H.Btab, B["Ds"]], writes=[B["X"]])
        yield
        pq, Bpq = psf()
        P.op("pe", lambda e: e.matmul(pq, lhsT=kn, rhs=qn, start=True, stop=True), reads=[H.Bkn, H.Bqn], writes=[Bpq])
        yield
        P.op("dve", lambda e: e.tensor_tensor(out=t["AqkT"][:], in0=pq, in1=t["DT"][:], op=ALU.mult), reads=[Bpq, B["DT"]], writes=[B["AqkT"]])
        yield
        px, Bpx = psb()
        P.op("pe", lambda e: e.transpose(px, t["X"][:], ident_b[:]), reads=[B["X"], BK], writes=[Bpx])
        yield
        evac("act", t["XT"][:], B["XT"], px, Bpx)
        yield
        evac("dve", t["N"][:], B["N"], px, Bpx)
        yield
        X, XT, X2, XT2, N, N2 = "X", "XT", "X2", "XT2", "N", "N2"
        for lvl in range(5):
            p1, Bp1 = psf()
            P.op("pe", lambda e, p1=p1, X=X, XT=XT: e.matmul(p1, lhsT=t[XT][:], rhs=t[X][:], start=True, stop=True),
                 reads=[B[X], B[XT]], writes=[Bp1])
            yield
            p2, Bp2 = psf()
            P.op("pe", lambda e, p2=p2, X=X, XT=XT: e.matmul(p2, lhsT=t[X][:], rhs=t[XT][:], start=True, stop=True),
                 reads=[B[X], B[XT]], writes=[Bp2])
            yield
            evac("act", t[X2][:], B[X2], p1, Bp1)
            yield
            evac("dve", t[XT2][:], B[XT2], p2, Bp2)
            yield
            p3, Bp3 = psf()
            P.op("pe", lambda e, p3=p3, X2=X2, N=N: e.matmul(p3, lhsT=t[X2][:], rhs=t[N][:], start=True, stop=False),
                 reads=[B[X2], B[N]], writes=[Bp3])
            P.op("pe", lambda e, p3=p3, N=N: e.matmul(p3, lhsT=ident_b[:], rhs=t[N][:], start=False, stop=False),
                 reads=[BK, B[N]], writes=[Bp3])
            P.op("pe", lambda e, p3=p3, XT2=XT2: e.matmul(p3, lhsT=ident_b[:], rhs=t[XT2][:], start=False, stop=True),
                 reads=[BK, B[XT2]], writes=[Bp3])
            yield
            evac("act" if lvl % 2 else "dve", t[N2][:], B[N2], p3, Bp3)
            yield
            X, X2 = X2, X
            XT, XT2 = XT2, XT
            N, N2 = N2, N
        pu, Bpu = psf()
        P.op("pe", lambda e, N=N: e.matmul(pu, lhsT=t[N][:], rhs=t["vb16"][:], start=True, stop=True), reads=[B[N], B["vb16"]], writes=[Bpu])
        yield
        P.op("dve", lambda e: e.tensor_tensor(out=t["u32"][:], in0=pu, in1=t["vb32"][:], op=ALU.add), reads=[Bpu, B["vb32"]], writes=[B["u32"]])
        yield
        pw, Bpw = psf()
        P.op("pe", lambda e, N=N: e.matmul(pw, lhsT=t["kbg"][:], rhs=t[N][:], start=True, stop=False), reads=[B["kbg"], B[N]], writes=[Bpw])
        P.op("pe", lambda e: e.matmul(pw, lhsT=t["kbg"][:], rhs=ident_b[:], start=False, stop=True), reads=[B["kbg"], BK], writes=[Bpw])
        yield
        evac("act", t["wT"][:], B["wT"], pw, Bpw)
        yield

    def scan_steps(H, blk, bi):
        T, t, B = H.tab, H.t, H.B
        cs0 = bi * 128
        steps = []
        pws, Bpws = psf()
        pqs, Bpqs = psf()
        for c in (0, 1):
            r = slice(c * 64, (c + 1) * 64)
            egl = T["eg%d" % c][:, blk:blk + 1]

            def s1(c=c, r=r):
                cur = H.scur
                P.op("pe", lambda e: e.matmul(pws[r, :], lhsT=t["wT"][:, r], rhs=H.Sb[cur][:], start=True, stop=True),
                     reads=[B["wT"], H.BSb[cur]], writes=[Bpws])
                P.op("pe", lambda e: e.matmul(pqs[r, :], lhsT=H.qn[:, cs0 + c * 64:cs0 + (c + 1) * 64], rhs=H.Sb[cur][:], start=True, stop=True),
                     reads=[H.Bqn, H.BSb[cur]], writes=[Bpqs])

            def s2(c=c, r=r):
                P.op("dve", lambda e: e.tensor_tensor(out=t["vnew"][r, :], in0=t["u32"][r, :], in1=pws[r, :], op=ALU.subtract),
                     reads=[B["u32"], Bpws], writes=[B["vnew"]])

            def s3(c=c, r=r, egl=egl):
                cur = H.scur
                nxt = 1 - cur
                pds, Bpds = psf()
                P.op("pe", lambda e: e.matmul(pds, lhsT=t["kdec"][r, :], rhs=t["vnew"][r, :], start=True, stop=True),
                     reads=[B["kdec"], B["vnew"]], writes=[Bpds])
                P.op("dve", lambda e: e.scalar_tensor_tensor(out=H.Sb[nxt][:], in0=H.S32[:], scalar=egl, in1=pds, op0=ALU.mult, op1=ALU.add),
                     reads=[H.BS32, H.Btab, Bpds], writes=[H.BSb[nxt]])
                P.op("dve", lambda e: e.scalar_tensor_tensor(out=H.S32[:], in0=H.S32[:], scalar=egl, in1=pds, op0=ALU.mult, op1=ALU.add),
                     reads=[H.BS32, H.Btab, Bpds], writes=[H.BS32])
                H.scur = nxt

            steps += [s1, s2, s3]

        def fin():
            pa, Bpa = psf()
            P.op("pe", lambda e: e.matmul(pa, lhsT=t["AqkT"][:], rhs=t["vnew"][:], start=True, stop=True), reads=[B["AqkT"], B["vnew"]], writes=[Bpa])
            evac("act", t["aq"][:], B["aq"], pa, Bpa)
            P.op("dve", lambda e: e.scalar_tensor_tensor(out=t["o32"][:], in0=pqs, scalar=T["egc"][:, blk:blk + 1], in1=t["aq"][:],
                                                         op0=ALU.mult, op1=ALU.add), reads=[Bpqs, H.Btab, B["aq"]], writes=[B["o32"]])
            P.op("act", lambda e: e.activation(out=t["junk"][:], in_=t["o32"][:], func=AF.Square, accum_out=H.ss[:, 0:1]),
                 reads=[B["o32"]], writes=[B["junk"], H.Bss])
            P.op("act", lambda e: e.activation(out=H.rstd[:], in_=H.ss[:], func=AF.Sqrt, scale=1.0 / 128.0, bias=epsc[:, 0:1]),
                 reads=[H.Bss, BK], writes=[H.Bss])
            P.op("dve", lambda e: e.reciprocal(out=H.rstd[:], in_=H.rstd[:]), reads=[H.Bss], writes=[H.Bss])
            P.op("dve", lambda e: e.scalar_tensor_tensor(out=t["on32"][:], in0=t["o32"][:], scalar=H.rstd[:, 0:1], in1=onws[:],
                                                         op0=ALU.mult, op1=ALU.mult), reads=[B["o32"], H.Bss, Bon], writes=[B["on32"]])
            po, Bpo = psf()
            P.op("pe", lambda e: e.transpose(po, t["on32"][:], K["gI"][:]), reads=[B["on32"], BK], writes=[Bpo])
            evac("act", H.osb[:, bi * 128:(bi + 1) * 128], H.Bosb, po, Bpo)

        steps.append(fin)
        return steps

    dbgn = []
    if dbg:
        dbgT = nc.dram_tensor("dbg", [128, 40 * 128], F32, kind="ExternalOutput").ap()
        dstg = P.sb("dstg", [128, 128], F32)
        Bdstg = P.buf()

        def dump(name, ap, Bs, w=128):
            i = len(dbgn)
            dbgn.append(name)
            P.op("dve", lambda e: e.tensor_copy(out=dstg[:, 0:w], in_=ap), reads=Bs, writes=[Bdstg])
            P.dma("sp", dbgT[:, i * 128:i * 128 + w], dstg[:, 0:w], reads=[Bdstg], store=True, final=True)
    P.dbgn = dbgn
    for H in heads:
        head_tables(H)
    for H in heads:
        load_raw(H, 0)
    for ti in range(NT):
        P.fill_step(1, q="sp")
        P.tag("ph5_gdnB_q%d" % (ti // 8))
        for H in heads:
            carry(H, ti)
        if ti + 1 < NT:
            for H in heads:
                load_raw(H, ti + 1)
        for H in heads:
            conv_phase(H, ti)
        for bi in range(TT // 128):
            blk = ti * 4 + bi
            gens = [pre_scan(H, blk, bi) for H in heads]
            gen_head = {id(gn): H.hh for gn, H in zip(gens, heads)}
            alive = list(gens)
            while alive:
                for gi_, gen in enumerate(list(alive)):
                    try:
                        cur_head[0] = gen_head[id(gen)]
                        next(gen)
                    except StopIteration:
                        alive.remove(gen)
            lists = []
            for H in heads:
                cur_head[0] = H.hh
                lists.append(scan_steps(H, blk, bi))
            for k in range(len(lists[0])):
                for hi_, L in enumerate(lists):
                    cur_head[0] = heads[hi_].hh
                    L[k]()
            if dbg and blk == 0:
                H = heads[0]
                for nm in ("g", "gc", "beta", "egc", "ekd", "eg0", "eg1", "bgc"):
                    dump("t_" + nm, H.tab[nm][:, 0:NB], [H.Btab], w=NB)
                dump("kn", H.kn[:, 0:128], [H.Bkn])
                dump("qn", H.qn[:, 0:128], [H.Bqn])
                dump("v", H.cv[2][:, 0:128], [H.Bcv[2]])
                for nm in H.t:
                    dump(nm, H.t[nm][:], [H.B[nm]])
                dump("S32", H.S32[:], [H.BS32])
        for H in heads:
            t0 = ti * TT
            r0_ = (t0 // NTOK) * 256 + H.hh * 128
            sub_ = (t0 % NTOK) // (NTOK // 4)
            P.dma("sp", xo[r0_:r0_ + 128, (t0 % NTOK):(t0 % NTOK) + TT], H.osb[:], reads=[H.Bosb], writes=[io["lo_bufs"][sub_]], store=True, chan=H.Bosb)
        if t0 >= 3 * NTOK and ((t0 + TT) % (NTOK // 4)) == 0:
            io["sub_done"]((t0 % NTOK) // (NTOK // 4))
    return P


import math
import numpy as np

HL = 128
NEG = -30000.0
TWO_PI = 2.0 * math.pi
C1 = 6.28125
C2 = TWO_PI - C1
MAGIC = 12582912.0
QA, QR, KA, KR, VV, ZZ, NCOL = 0, 1024, 2048, 2560, 3072, 3328, 4352


def swa_consts():
    k = np.arange(128)
    mprev = np.where(k[:, None] > k[None, :], 0.0, NEG).astype(np.float32)
    mcur = np.where(k[:, None] <= k[None, :], 0.0, NEG).astype(np.float32)
    mask = np.concatenate([mprev, mprev, mcur, mcur], axis=1)
    invf = (np.float32(10000.0) ** (-(np.arange(32, dtype=np.float32)) / np.float32(32))).astype(np.float32)
    invf = np.tile(invf, 4).reshape(128, 1)
    onesP = np.zeros((128, 2, 128), np.float32)
    onesP[:, 0, 0:64] = 1.0
    onesP[:, 1, 64:128] = 1.0
    return {"sMask": mask, "sInvf": invf, "sI": np.eye(128, dtype=np.float32), "sOnesP": onesP.reshape(128, 256)}


def build_swa(P, io, NTOK=4096):
    NT = NTOK // TT
    NC_ = HL + NTOK
    fmv = lambda a: a.rearrange("(c p) t -> p c t", p=128)
    x3T = fmv(io["x3T"])
    xH = fmv(io["yh"])
    posd, nw, fnw, w_in, w_out = io["swa_pos"], io["swa_nw"], io["swa_fnw"], io["swa_w_in"], io["swa_w_out"]
    skd, hmd, cM, cF, cI, cO = io["swa_sk"], io["swa_hmask"], io["sMask"], io["sInvf"], io["sI"], io["sOnesP"]
    outT = fmv(io["outT"])

    BK = P.buf()
    ones = P.sb("ones", [128, 128], BF16)
    epsc = P.sb("epsc", [128, 1], F32)
    P.op("dve", lambda e: e.memset(ones[:], 1.0), writes=[BK])
    P.op("dve", lambda e: e.memset(epsc[:], 1e-6), writes=[BK])
    norm = Norm(P, ones, BK)
    norm.epsc = epsc
    stg = [P.sb("stg%d" % i, [128, 1024], F32) for i in range(2)]
    Bstg = [P.buf(), P.buf()]
    nws, Bnw = load_small(P, nw[:, :], [128, KC], "nws")
    fnws, Bfnw = load_small(P, fnw[:, :], [128, KC], "fnws")
    sks, Bsk = load_small(P, skd[:, :], [128, 8], "sks")
    hms, Bhm = load_small(P, hmd[:, :], [128, 1], "hms")
    invf, Binvf = load_small(P, cF[:, :], [128, 1], "invf")
    esk = P.sb("esk", [128, 8], F32)
    P.op("act", lambda e: e.activation(out=esk[:], in_=sks[:], func=AF.Exp), reads=[Bsk], writes=[Bsk])
    mask_b = P.sb("mask_b", [128, 512], BF16)
    ident_b = P.sb("ident_b", [128, 128], BF16)
    onesP = P.sb("onesP", [128, 2, 128], BF16)
    P.dma("sp", stg[0][:, 0:512], cM[:, :], writes=[Bstg[0]])
    P.op("dve", lambda e: e.tensor_copy(out=mask_b[:], in_=stg[0][:, 0:512]), reads=[Bstg[0]], writes=[BK])
    P.dma("sp", stg[1][:, 0:128], cI[:, :], writes=[Bstg[1]])
    P.op("dve", lambda e: e.tensor_copy(out=ident_b[:], in_=stg[1][:, 0:128]), reads=[Bstg[1]], writes=[BK])
    P.dma("sp", stg[0][:, 0:256], cO[:, :], writes=[Bstg[0]])
    P.op("dve", lambda e: e.tensor_copy(out=onesP[:].rearrange("p a b -> p (a b)"), in_=stg[0][:, 0:256]), reads=[Bstg[0]], writes=[BK])

    wib = P.sb("wib", [128, KC, NCOL], BF16)
    Bwib = P.buf()
    wob = P.sb("wob", [128, KC, D], BF16)
    Bwob = P.buf()
    nld = [0]

    def stage_load(src, ncols):
        s = nld[0] % 2
        nld[0] += 1
        P.dma("sp", stg[s][:, 0:ncols], src, writes=[Bstg[s]])
        return stg[s], Bstg[s]

    def cast(dst, src, Bs, kc, sign=1.0, eng=None):
        eng = eng or ("dve", "pool")[nld[0] % 2]
        P.op(eng, lambda e: e.tensor_scalar(out=dst, in0=src, scalar1=nws[:, kc:kc + 1], scalar2=sign, op0=ALU.mult, op1=ALU.mult),
             reads=[Bs, Bnw], writes=[Bwib])

    for kc in range(KC):
        rows = slice(kc * 128, (kc + 1) * 128)
        s_, Bs = stage_load(w_in[rows, 0:1024], 1024)
        cast(wib[:, kc, QA:QA + 1024], s_[:, 0:1024], Bs, kc)
        sv = s_[:, 0:1024].rearrange("p (h t d) -> p h t d", t=2, d=32)
        dv = wib[:, kc, QR:QR + 1024].rearrange("p (h t d) -> p h t d", t=2, d=32)
        cast(dv[:, :, 0, :], sv[:, :, 1, :], Bs, kc, sign=-1.0, eng="dve")
        cast(dv[:, :, 1, :], sv[:, :, 0, :], Bs, kc, eng="pool")
        s_, Bs = stage_load(w_in[rows, 1024:1536], 512)
        ksrc = s_[:, 0:256].rearrange("p (g d) -> p g d", d=64)
        ksr2 = s_[:, 0:256].rearrange("p (g t d) -> p g t d", t=2, d=32)
        kad = wib[:, kc, KA:KA + 512].rearrange("p (g u d) -> p g u d", u=2, d=64)
        krd = wib[:, kc, KR:KR + 512].rearrange("p (g u t d) -> p g u t d", u=2, t=2, d=32)
        for u in range(2):
            cast(kad[:, :, u, :], ksrc, Bs, kc, eng=("dve", "pool")[u])
            cast(krd[:, :, u, 0, :], ksr2[:, :, 1, :], Bs, kc, sign=-1.0, eng="dve")
            cast(krd[:, :, u, 1, :], ksr2[:, :, 0, :], Bs, kc, eng="pool")
        cast(wib[:, kc, VV:VV + 256], s_[:, 256:512], Bs, kc, eng="dve")
        s_, Bs = stage_load(w_in[rows, 1536:2560], 1024)
        cast(wib[:, kc, ZZ:ZZ + 1024], s_[:, 0:1024], Bs, kc)
    for kc in range(KC):
        s_, Bs = stage_load(w_out[kc * 128:(kc + 1) * 128, :], 1024)
        P.op(("dve", "pool")[kc % 2], lambda e, kc=kc, s_=s_: e.tensor_copy(out=wob[:, kc, :], in_=s_[:, 0:1024]), reads=[Bs], writes=[Bwob])

    xs = P.sb("xs", [128, KC, TT], F32)
    Bxs = P.buf()
    hT = P.sb("hT", [128, KC, TT], BF16)
    BhT = P.buf()
    rstd = P.sb("rstd", [128, TT], F32)
    Brstd = P.buf()
    posi = P.sb("posi", [128, TT], I32)
    ang = P.sb("ang", [128, TT], F32)
    kf = P.sb("kf", [128, TT], F32)
    cosT = P.sb("cosT", [128, TT], F32)
    sinT = P.sb("sinT", [128, TT], F32)
    Brope = P.buf()
    Bcs = P.buf()
    QP = P.sb("QP", [128, 8, TT], BF16)
    BQP = P.buf()
    KP = P.sb("KP", [128, 4, HL + TT], BF16)
    BKP = P.buf()
    Vp = P.sb("Vp", [128, 5, 4, 2, 128], BF16)
    BVp = P.buf()
    P.op("pool", lambda e: e.memset(Vp[:].rearrange("p a b c d -> p (a b c d)"), 0.0), writes=[BVp])
    szs = P.sb("szs", [128, 8, TT], F32)
    Bszs = P.buf()
    og = P.sb("og", [128, 8, TT], BF16)
    Bog = P.buf()
    t1r = Rot(P, "t1r", [128, TT], F32, 2)
    ptr = Rot(P, "ptr", [128, TT], BF16, 4)
    smr = Rot(P, "smr", [128, 128], F32, 4)
    f32r = Rot(P, "f32r", [128, TT], F32, 2)
    pp = [P.ps("pp%d" % i, [128, TT]) for i in range(7)]
    Bpp = [P.buf() for _ in range(7)]
    ppi = [0]

    def psum():
        k = ppi[0] % 7
        ppi[0] += 1
        return pp[k], Bpp[k]

    def fm_proj(col0, n):
        ps, Bps = psum()
        for kc in range(KC):
            P.op("pe", lambda e, kc=kc: e.matmul(ps[:, 0:n], lhsT=wib[:, kc, col0:col0 + 128], rhs=hT[:, kc, 0:n],
                                                  start=(kc == 0), stop=(kc == KC - 1)), reads=[Bwib, BhT], writes=[Bps])
        return ps, Bps

    def rope_tables(c0, n):
        P.dma("sp", posi[:, 0:n], posd[:, c0:c0 + n], writes=[Brope])
        P.op("dve", lambda e: e.tensor_copy(out=ang[:, 0:n], in_=posi[:, 0:n]), reads=[Brope], writes=[Brope])
        P.op("dve", lambda e: e.tensor_scalar(out=ang[:, 0:n], in0=ang[:, 0:n], scalar1=invf[:, 0:1], scalar2=None, op0=ALU.mult),
             reads=[Brope, Binvf], writes=[Brope])
        P.op("dve", lambda e: e.tensor_scalar(out=kf[:, 0:n], in0=ang[:, 0:n], scalar1=1.0 / TWO_PI, scalar2=MAGIC, op0=ALU.mult, op1=ALU.add),
             reads=[Brope], writes=[Brope])
        P.op("dve", lambda e: e.tensor_scalar(out=kf[:, 0:n], in0=kf[:, 0:n], scalar1=MAGIC, scalar2=None, op0=ALU.subtract),
             reads=[Brope], writes=[Brope])
        P.op("dve", lambda e: e.scalar_tensor_tensor(out=ang[:, 0:n], in0=kf[:, 0:n], scalar=-C1, in1=ang[:, 0:n], op0=ALU.mult, op1=ALU.add),
             reads=[Brope], writes=[Brope])
        P.op("dve", lambda e: e.scalar_tensor_tensor(out=ang[:, 0:n], in0=kf[:, 0:n], scalar=-C2, in1=ang[:, 0:n], op0=ALU.mult, op1=ALU.add),
             reads=[Brope], writes=[Brope])
        P.op("dve", lambda e: e.tensor_scalar(out=ang[:, 0:n], in0=ang[:, 0:n], scalar1=3.14159, scalar2=-3.14159, op0=ALU.min, op1=ALU.max),
             reads=[Brope], writes=[Brope])
        P.op("act", lambda e: e.activation(out=sinT[:, 0:n], in_=ang[:, 0:n], func=AF.Sin), reads=[Brope], writes=[Bcs])
        P.op("act", lambda e: e.activation(out=kf[:, 0:n], in_=ang[:, 0:n], func=AF.Sin, scale=0.5), reads=[Brope], writes=[Brope])
        P.op("dve", lambda e: e.tensor_tensor(out=kf[:, 0:n], in0=kf[:, 0:n], in1=kf[:, 0:n], op=ALU.mult), reads=[Brope], writes=[Brope])
        P.op("dve", lambda e: e.tensor_scalar(out=cosT[:, 0:n], in0=kf[:, 0:n], scalar1=-2.0, scalar2=1.0, op0=ALU.mult, op1=ALU.add),
             reads=[Brope], writes=[Bcs])

    def roped(colA, colR, n, dst, Bdst):
        psA, BA = fm_proj(colA, n)
        psR, BR = fm_proj(colR, n)
        t1, Bt1 = t1r.next()
        P.op("dve", lambda e: e.tensor_tensor(out=t1[:, 0:n], in0=psA[:, 0:n], in1=cosT[:, 0:n], op=ALU.mult), reads=[BA, Bcs], writes=[Bt1])
        t2, Bt2 = t1r.next()
        P.op("dve", lambda e: e.tensor_tensor(out=t2[:, 0:n], in0=psR[:, 0:n], in1=sinT[:, 0:n], op=ALU.mult), reads=[BR, Bcs], writes=[Bt2])
        P.op("pool", lambda e: e.tensor_tensor(out=dst, in0=t1[:, 0:n], in1=t2[:, 0:n], op=ALU.add), reads=[Bt1, Bt2], writes=[Bdst])

    def kv_part(c0, n, koff, vslot0):
        for g in range(4):
            roped(KA + g * 128, KR + g * 128, n, KP[:, g, koff:koff + n], BKP)
        for blk in range(n // 128):
            ps, Bps = psum()
            for kc in range(KC):
                P.op("pe", lambda e, kc=kc, blk=blk, ps=ps: e.matmul(ps[:, 0:256], lhsT=hT[:, kc, blk * 128:(blk + 1) * 128], rhs=wib[:, kc, VV:VV + 256],
                                                              start=(kc == 0), stop=(kc == KC - 1)), reads=[Bwib, BhT], writes=[Bps])
            src = ps[:, 0:256].rearrange("p (g d) -> p g d", d=64)
            P.op("dve", lambda e, blk=blk, src=src: e.tensor_copy(out=Vp[:, vslot0 + blk, :, 0, 0:64], in_=src), reads=[Bps], writes=[BVp])
            P.op("dve", lambda e, blk=blk, src=src: e.tensor_copy(out=Vp[:, vslot0 + blk, :, 1, 64:128], in_=src), reads=[Bps], writes=[BVp])

    def load_and_norm(c0, n):
        if c0 == 0:
            P.dma("sp", xs[:, :, 0:n], xH[:, :, 0:n], writes=[Bxs])
        else:
            P.dma("sp", xs[:, :, 0:n], x3T[:, :, c0 - HL:c0 - HL + n], writes=[Bxs])
        norm.run(xs, Bxs, n, rstd, Brstd)
        for kc in range(KC):
            eng = "dve" if kc % 2 == 0 else "pool"
            P.op(eng, lambda e, kc=kc: e.tensor_tensor(out=hT[:, kc, 0:n], in0=xs[:, kc, 0:n], in1=rstd[:, 0:n], op=ALU.mult),
                 reads=[Bxs, Brstd], writes=[BhT])

    scale = 1.0 / math.sqrt(64.0)

    load_and_norm(0, HL)
    rope_tables(0, HL)
    kv_part(0, HL, 0, 0)

    for i in range(NT):
        c0 = HL + i * TT
        load_and_norm(c0, TT)
        rope_tables(c0, TT)
        kv_part(c0, TT, HL, 1)
        for p in range(8):
            roped(QA + p * 128, QR + p * 128, TT, QP[:, p, :], BQP)
        for c in range(8):
            ps, Bps = fm_proj(ZZ + c * 128, TT)
            P.op("act", lambda e, c=c, ps=ps: e.activation(out=szs[:, c, :], in_=ps[:], func=AF.Silu), reads=[Bps], writes=[Bszs])
        def attn(i, n, g):
            qc = slice(n * 128, (n + 1) * 128)
            kprev = slice(n * 128, (n + 1) * 128)
            kcur = slice((n + 1) * 128, (n + 2) * 128)
            pts = []
            for half in range(2):
                pr = slice(half * 64, half * 64 + 64)
                pst, Bpst = psum()
                rhs = QP[pr, 2 * g:2 * g + 2, qc]
                P.op("pe", lambda e, pst=pst, pr=pr, rhs=rhs: e.matmul(pst[:, 0:256], lhsT=KP[pr, g, kprev], rhs=rhs, start=True, stop=False),
                     reads=[BKP, BQP], writes=[Bpst])
                P.op("pe", lambda e, pst=pst, pr=pr, rhs=rhs: e.matmul(pst[:, 256:512], lhsT=KP[pr, g, kcur], rhs=rhs, start=False, stop=False),
                     reads=[BKP, BQP], writes=[Bpst])
                P.op("pe", lambda e, pst=pst: e.matmul(pst[:, :], lhsT=ident_b[:], rhs=mask_b[:], start=False, stop=True),
                     reads=[BK], writes=[Bpst])
                pt, Bpt = ptr.next()
                if i == 0 and n == 0:
                    P.op("act", lambda e, pt=pt, pst=pst: e.activation(out=pt[:, 0:256], in_=pst[:, 0:256], func=AF.Exp, scale=scale, bias=hms[:, 0:1]),
                         reads=[Bpst, Bhm], writes=[Bpt])
                    P.op("act", lambda e, pt=pt, pst=pst: e.activation(out=pt[:, 256:512], in_=pst[:, 256:512], func=AF.Exp, scale=scale),
                         reads=[Bpst], writes=[Bpt])
                else:
                    P.op("act", lambda e, pt=pt, pst=pst: e.activation(out=pt[:], in_=pst[:], func=AF.Exp, scale=scale), reads=[Bpst], writes=[Bpt])
                pts.append((pt, Bpt))
            po, Bpo = psum()
            prs, Bprs = psum()
            seq = [(0, n, slice(0, 256)), (0, n + 1, slice(256, 512)), (1, n, slice(0, 256)), (1, n + 1, slice(256, 512))]
            for idx, (half, vs_, cols) in enumerate(seq):
                pt, Bpt = pts[half]
                P.op("pe", lambda e, half=half, vs_=vs_, cols=cols, pt=pt, idx=idx, po=po: e.matmul(
                    po[:, 0:256], lhsT=Vp[:, vs_, g, half, :], rhs=pt[:, cols], start=(idx == 0), stop=(idx == 3)),
                    reads=[BVp, Bpt], writes=[Bpo])
            for idx, (half, vs_, cols) in enumerate(seq):
                pt, Bpt = pts[half]
                P.op("pe", lambda e, half=half, cols=cols, pt=pt, idx=idx, prs=prs: e.matmul(
                    prs[:, 0:256], lhsT=onesP[:, half, :], rhs=pt[:, cols], start=(idx == 0), stop=(idx == 3)),
                    reads=[BK, Bpt], writes=[Bprs])
            for c in range(2):
                pair = 2 * g + c
                cc = slice(c * 128, (c + 1) * 128)
                sm, Bsm = smr.next()
                P.op("act", lambda e, sm=sm, prs=prs, cc=cc, pair=pair: e.activation(out=sm[:], in_=prs[:, cc], func=AF.Ln, bias=esk[:, pair:pair + 1], scale=1.0),
                     reads=[Bprs, Bsk], writes=[Bsm])
                P.op("act", lambda e, sm=sm: e.activation(out=sm[:], in_=sm[:], func=AF.Exp, scale=-1.0), reads=[Bsm], writes=[Bsm])
                sm2, Bsm2 = smr.next()
                P.op("dve", lambda e, sm=sm, sm2=sm2, po=po, cc=cc: e.tensor_tensor(out=sm2[:], in0=po[:, cc], in1=sm[:], op=ALU.mult),
                     reads=[Bpo, Bsm], writes=[Bsm2])
                P.op("pool", lambda e, sm2=sm2, pair=pair: e.tensor_tensor(out=og[:, pair, qc], in0=sm2[:], in1=szs[:, pair, qc], op=ALU.mult),
                     reads=[Bsm2, Bszs], writes=[Bog])

        for n in range(4):
            for g in range(4):
                attn(i, n, g)
        P.op("pool", lambda e: e.tensor_copy(out=KP[:, :, 0:HL], in_=KP[:, :, TT:TT + HL]), reads=[BKP], writes=[BKP])
        P.op("pool", lambda e: e.tensor_copy(out=Vp[:, 0].rearrange("p a b c -> p (a b c)"), in_=Vp[:, 4].rearrange("p a b c -> p (a b c)")),
             reads=[BVp], writes=[BVp])
        for dc in range(KC):
            ps, Bps = psum()
            for pr_ in range(8):
                P.op("pe", lambda e, pr_=pr_, dc=dc, ps=ps: e.matmul(ps[:], lhsT=wob[:, pr_, dc * 128:(dc + 1) * 128], rhs=og[:, pr_, :],
                                                                    start=(pr_ == 0), stop=(pr_ == 7)), reads=[Bwob, Bog], writes=[Bps])
            P.op("dve", lambda e, dc=dc, ps=ps: e.tensor_tensor(out=xs[:, dc, :], in0=ps[:], in1=xs[:, dc, :], op=ALU.add),
                 reads=[Bps, Bxs], writes=[Bxs])
        norm.run(xs, Bxs, TT, rstd, Brstd)
        for dc in range(KC):
            t, Bt = f32r.next()
            P.op("dve", lambda e, dc=dc, t=t: e.scalar_tensor_tensor(out=t[:], in0=xs[:, dc, :], scalar=fnws[:, dc:dc + 1], in1=rstd[:],
                                                                    op0=ALU.mult, op1=ALU.mult), reads=[Bxs, Bfnw, Brstd], writes=[Bt])
            P.dma("sp", outT[:, dc, i * TT:(i + 1) * TT], t[:], reads=[Bt], store=True, final=True)
    return P


from concourse.bass import ds

SEQ = 16384
TOKC = 4096
GROUPS = [[0, 1, 2, 3], [4, 5, 6, 7]]


def build_fused(nc, st):
    P = Prog(nc, st)
    P.need_rank = True
    io = {}

    def ext(name, shape, dt=F32):
        io[name] = nc.dram_tensor(name, list(shape), dt, kind="ExternalInput").ap()

    def itn(name, shape, dt=F32):
        io[name] = nc.dram_tensor(name, list(shape), dt).ap()

    ext("l0_xT", [D, TOKC + 2]); ext("l0_nw", [128, KC]); ext("l0_w_in", [D, 4096]); ext("l0_w_cv", [128, 24]); ext("l0_w_out", [D, D])
    ext("f_nw", [128, KC]); ext("f_w_in", [D, 4104]); ext("f_w_out", [D, D]); ext("fox_bf", [128, 2])
    for n, w in (("cU", 128), ("cSU", 128), ("cE127", 128), ("cI", 128), ("cMask", 2048)):
        ext(n, [128, w])
    ext("g_nw", [128, KC]); ext("g_w_in", [D, 4112]); ext("g_w_out", [D, D]); ext("gdn_wcv", [128, 24]); ext("gdn_hp", [128, 4]); ext("gdn_onw", [128, 128])
    for n in CN:
        ext(n, [128, 128])
    ext("swa_pos", [128, 128 + TOKC], I32); ext("swa_nw", [128, KC]); ext("swa_fnw", [128, KC]); ext("swa_w_in", [D, 2560]); ext("swa_w_out", [D, D])
    ext("swa_sk", [128, 8]); ext("swa_hmask", [128, 1]); ext("sMask", [128, 512]); ext("sInvf", [128, 1]); ext("sI", [128, 128]); ext("sOnesP", [128, 256])
    io["outT"] = nc.dram_tensor("outT", [D, TOKC], F32, kind="ExternalOutput").ap()
    for n in ("x1T", "x2T", "x3T", "szT1", "szT2"):
        itn(n, [D, TOKC])
    itn("XB1", [4 * 3072, TOKC], BF16); itn("YB1", [4 * 768, TOKC], BF16); itn("XF1", [8, SEQ]); itn("YF1", [2, SEQ])
    itn("XO1", [4 * 4 * D, TOKC // 4]); itn("YO1", [4 * D, TOKC // 4])
    itn("XB2", [4 * 3072, TOKC]); itn("YB2", [4 * 768, TOKC]); itn("XF2", [16, SEQ]); itn("YF2", [4, SEQ])
    itn("XO2", [4 * 4 * D, TOKC // 4]); itn("YO2", [4 * D, TOKC // 4])
    itn("XH", [4 * D, 128]); itn("YH", [D, 128])
    itn("L1", [3072, TOKC], BF16); itn("LF1", [8, TOKC]); itn("LO1", [D, TOKC])
    itn("L2", [3072, TOKC]); itn("LF2", [16, TOKC]); itn("LO2", [D, TOKC])

    ZW = 1024
    zt32 = P.sb("zt32", [128, ZW], F32)
    zt16 = P.sb("zt16", [128, ZW], BF16)
    Bzt = P.buf()
    P.op("pool", lambda e: e.memset(zt32[:], 0.0), writes=[Bzt])
    P.op("pool", lambda e: e.memset(zt16[:], 0.0), writes=[Bzt])
    fills = {}

    def zfill(name, key):
        ap = io[name]
        rows, cols = ap.shape
        zt = zt16 if name == "XB1" else zt32
        tot = rows * cols
        per = 128 * ZW
        flat = ap.rearrange("r c -> (r c)")
        KB = 8
        o = 0
        while o < tot:
            k = min(KB, (tot - o) // per)
            if k >= 1:
                n = k * per
                src = bass.AP(zt[:].tensor, 0, [[ZW, 128], [0, k], [1, ZW]])
                P.fill_queue.append((key, lambda q, dst=flat[o:o + n].rearrange("(k p w) -> p k w", p=128, w=ZW), src=src, key=key:
                                     P.dma(q, dst, src, reads=[Bzt], writes=[fills.setdefault((key, q), P.buf())], chan=fills[(key, q)], defer=True)))
            else:
                n = tot - o
                w = n // 128
                P.fill_queue.append((key, lambda q, dst=flat[o:o + n].rearrange("(p w) -> p w", p=128), src=zt[:, 0:w], key=key:
                                     P.dma(q, dst, src, reads=[Bzt], writes=[fills.setdefault((key, q), P.buf())], chan=fills[(key, q)], defer=True)))
            o += n

    def exchange(xin, yout):
        P.coll("ReduceScatter", GROUPS, io[xin], io[yout])

    def dyn_copy(q, dst, dst_pat, dst_off, src, src_pat, src_off, jscale, fkey=None, extra=()):
        b = P.buf()
        P.dma(q, (lambda: bass.AP(io[dst].tensor, P.jx[q] * jscale + dst_off, [list(p) for p in dst_pat])),
              bass.AP(io[src].tensor, src_off, [list(p) for p in src_pat]), reads=[fb for (k_, q_), fb in fills.items() if k_ == fkey] + list(extra), writes=[b], chan=b)
        return b

    def place_tok(lname, xname, fl_l, fl_x, nfl, with_v, fkey):
        if True:
            for h_ in range(2):
                q = ("sp", "pool")[h_]
                dyn_copy(q, xname, [[TOKC, 1536], [1, TOKC]], h_ * 1536 * TOKC, lname, [[TOKC, 1536], [1, TOKC]], h_ * 1536 * TOKC, 3072 * TOKC, fkey=fkey)
        dyn_copy("sp", fl_x, [[SEQ, nfl], [1, TOKC]], 0, fl_l, [[TOKC, nfl], [1, TOKC]], 0, TOKC, fkey=fkey)

    def o_exchange_hooks(lname, xname, yname, fkey):
        SUB = TOKC // 4
        Blo = [P.buf() for _ in range(4)]

        def done(s_):
            P.fill_until(fkey)
            b = dyn_copy("pool", xname, [[D * SUB, 4], [SUB, 256], [1, SUB]], s_ * 4 * D * SUB,
                         lname, [[256 * TOKC, 4], [TOKC, 256], [1, SUB]], s_ * SUB, 256 * SUB, fkey=fkey, extra=[Blo[s_]])
            P.coll("ReduceScatter", GROUPS, io[xname][s_ * 4 * D:(s_ + 1) * 4 * D, :], io[yname][s_ * D:(s_ + 1) * D, :], reads=[b])
        return Blo, done

    P.tag('fill')
    zfill("XB1", 1); zfill("XF1", 1); zfill("XO1", 2); zfill("XB2", 3); zfill("XF2", 3); zfill("XO2", 4); zfill("XH", 5)
    P.tag('ph1_conv')
    P.phase_begin()
    build_stage0(P, io)
    P.phase_end()

    P.tag('ph2_foxA')
    P.phase_begin()
    io2 = dict(io, c_xT=io["x1T"], a_nw=io["f_nw"], a_w_in=io["f_w_in"], xb=io["L1"], xf=io["LF1"], szT=io["szT1"])
    build_CA(P, io2, StageLoadX, StageAFox)
    P.phase_end()
    P.tag("x1_place")
    P.fill_until(1)
    place_tok("L1", "XB1", "LF1", "XF1", 8, True, 1)
    P.barrier()
    P.tag("x1_rs")
    exchange("XF1", "YF1")
    P.barrier()

    P.tag('ph3_foxB')
    P.phase_begin()
    By1 = [P.buf() for _ in range(4)]
    for c_ in range(4):
        P.coll("ReduceScatter", GROUPS, io["XB1"][c_ * 3072:(c_ + 1) * 3072, :], io["YB1"][c_ * 768:(c_ + 1) * 768, :], writes=[By1[c_]])
    Blo1, done1 = o_exchange_hooks("LO1", "XO1", "YO1", 2)
    build_foxB(P, dict(io, yb=io["YB1"], yb_bufs=By1, yf=io["YF1"], xo=io["LO1"], lo_bufs=Blo1, sub_done=done1), NH=2, S_=SEQ)
    P.phase_end()

    P.tag('ph4_foxC_gdnA')
    P.phase_begin()
    Bl2 = [P.buf() for _ in range(4)]

    def piece_done2(p_):
        P.fill_until(3)
        dyn_copy("sp", "XB2", [[TOKC, 3072], [1, 1024]], p_ * 1024, "L2", [[TOKC, 3072], [1, 1024]], p_ * 1024, 3072 * TOKC, fkey=3, extra=[Bl2[p_]])
    io4 = dict(io, c_oT=io["YO1"], c_oT_bufs=[P.buf() for _ in range(4)], c_szT=io["szT1"], c_xT=io["x1T"], c_w_out=io["f_w_out"], xo=io["x2T"],
               a_nw=io["g_nw"], a_w_in=io["g_w_in"], xb=io["L2"], xf=io["LF2"], szT=io["szT2"], l_bufs=Bl2, piece_done=piece_done2)
    build_CA(P, io4, StageC, StageAGdn)
    P.phase_end()
    P.tag("x3_place")
    P.fill_until(3)
    dyn_copy("sp", "XF2", [[SEQ, 16], [1, TOKC]], 0, "LF2", [[TOKC, 16], [1, TOKC]], 0, TOKC, fkey=3)
    P.barrier()
    P.tag("x3_rs")
    exchange("XF2", "YF2")
    P.barrier()

    P.tag('ph5_gdnB')
    P.phase_begin()
    By = [P.buf() for _ in range(4)]
    for c_ in range(4):
        P.coll("ReduceScatter", GROUPS, io["XB2"][c_ * 3072:(c_ + 1) * 3072, :], io["YB2"][c_ * 768:(c_ + 1) * 768, :], writes=[By[c_]])
    Blo2, done2 = o_exchange_hooks("LO2", "XO2", "YO2", 4)
    build_gdnB(P, dict(io, yb=io["YB2"], yb_bufs=By, yf=io["YF2"], xo=io["LO2"], lo_bufs=Blo2, sub_done=done2), NH=2, S_=SEQ)
    P.phase_end()

    P.tag('ph6_gdnC')
    P.phase_begin()
    io6 = dict(io, c_oT=io["YO2"], c_oT_bufs=[P.buf() for _ in range(4)], c_szT=io["szT2"], c_xT=io["x2T"], c_w_out=io["g_w_out"], xo=io["x3T"])
    build_CA(P, io6, StageC, StageANull)
    P.phase_end()
    P.tag('x5_halo')
    P.fill_until(5)
    Bh = P.buf()
    P.dma("sp", (lambda: io["XH"][ds(((P.jx["sp"] + 1) % 4) * D, D), :]), io["x3T"][:, TOKC - 128:TOKC], reads=[fb for (k_, q_), fb in fills.items() if k_ == 5], writes=[Bh], chan=Bh)
    P.barrier()
    exchange("XH", "YH")
    P.barrier()

    P.tag('ph7_swa')
    P.phase_begin()
    build_swa(P, dict(io, yh=io["YH"]))
    P.phase_end()
    return P


from concourse.bass_utils import run_bass_kernel_spmd

NCORE = 8


def _nwT(v):
    return np.ascontiguousarray(np.asarray(v, np.float32).reshape(8, 128).T)


def kernel(x, positions, norm_w, final_norm_w, conv_w_in, conv_w_conv, conv_w_out,
           fox_w_in, fox_b_f, fox_w_out, gdn_w_in, gdn_w_conv, gdn_a_log, gdn_dt_bias,
           gdn_norm_w, gdn_w_out, swa_w_in, swa_sinks, swa_w_out):
    x = np.asarray(x, np.float32)
    positions = np.asarray(positions, np.int32)
    f = lambda a: np.ascontiguousarray(np.asarray(a, np.float32))
    nc = bass.Bass("TRN2", target_bir_lowering=False)
    with ExitStack() as st:
        P = build_fused(nc, st)
        P.emit()
    shared = {"l0_nw": _nwT(norm_w[0]), "l0_w_in": f(conv_w_in[0]),
              "l0_w_cv": np.ascontiguousarray(f(conv_w_conv[0]).reshape(3, 8, 128).transpose(2, 1, 0).reshape(128, 24)),
              "l0_w_out": f(conv_w_out[0]),
              "f_nw": _nwT(norm_w[1]), "f_w_in": f(fox_w_in[0]), "f_w_out": f(fox_w_out[0]),
              "g_nw": _nwT(norm_w[2]), "g_w_in": f(gdn_w_in[0]), "g_w_out": f(gdn_w_out[0]),
              "gdn_onw": np.ascontiguousarray(np.broadcast_to(f(gdn_norm_w[0])[None, :], (128, 128))),
              "swa_nw": _nwT(norm_w[3]), "swa_fnw": _nwT(final_norm_w), "swa_w_in": f(swa_w_in[0]), "swa_w_out": f(swa_w_out[0])}
    wc = f(gdn_w_conv[0])
    wcv_all = np.stack([wc[k, w * 1024 + h * 128: w * 1024 + (h + 1) * 128]
                        for h in range(8) for w in range(3) for k in range(4)], axis=1)
    hp_all = np.stack([f(gdn_a_log[0]), f(gdn_dt_bias[0])], axis=1).reshape(1, 16)
    sinks = f(swa_sinks[0])
    sk = np.zeros((128, 8), np.float32)
    for p in range(8):
        sk[:64, p] = sinks[2 * p]
        sk[64:, p] = sinks[2 * p + 1]
    shared["swa_sk"] = sk
    shared.update(fox_consts())
    shared.update(gdn_consts())
    shared.update(swa_consts())
    ins = []
    for c in range(NCORE):
        b, t0 = c // 4, (c % 4) * TOKC
        xt = np.zeros((1024, TOKC + 2), np.float32)
        pos = np.zeros((128 + TOKC,), np.int32)
        if t0 > 0:
            xt[:, 0:2] = x[b, t0 - 2:t0].T
            pos[:128] = positions[b, t0 - 128:t0]
        xt[:, 2:] = x[b, t0:t0 + TOKC].T
        pos[128:] = positions[b, t0:t0 + TOKC]
        d = dict(shared)
        d["l0_xT"] = xt
        d["swa_pos"] = np.ascontiguousarray(np.broadcast_to(pos[None, :], (128, pos.shape[0])))
        d["swa_hmask"] = np.full((128, 1), -30000.0 if t0 == 0 else 0.0, np.float32)
        g = c % 4
        d["fox_bf"] = np.ascontiguousarray(np.broadcast_to(f(fox_b_f[0])[None, 2 * g:2 * g + 2], (128, 2)))
        d["gdn_wcv"] = np.ascontiguousarray(wcv_all[:, g * 24:(g + 1) * 24])
        d["gdn_hp"] = np.ascontiguousarray(np.broadcast_to(hp_all[:, g * 4:(g + 1) * 4], (128, 4)))
        ins.append(d)
    res = run_bass_kernel_spmd(nc, ins, core_ids=list(range(NCORE)))
    out = np.zeros((2, SEQ, 1024), np.float32)
    for c in range(NCORE):
        b, t0 = c // 4, (c % 4) * TOKC
        out[b, t0:t0 + TOKC] = np.asarray(res.results[c]["outT"]).T
    return out
```

```python
from contextlib import ExitStack
import numpy as np
import concourse.bass as bass
import concourse.mybir as mybir

F32 = mybir.dt.float32
BF16 = mybir.dt.bfloat16
I32 = mybir.dt.int32
ALU = mybir.AluOpType
AF = mybir.ActivationFunctionType

ENGS = ("pe", "act", "dve", "pool", "sp")


class Buf:
    __slots__ = ("name", "w", "r", "sem", "semval", "excl", "cls")

    def __init__(self, name):
        self.name = name
        self.w = {}
        self.r = {}
        self.sem = None
        self.semval = 0
        self.excl = False
        self.cls = None


class _Op:
    __slots__ = ("fn", "deps", "dma", "signal", "sigord", "tag")

    def __init__(self, fn, deps, dma=None):
        self.fn = fn
        self.deps = deps
        self.dma = dma
        self.signal = False
        self.sigord = 0
        self.tag = _Op.cur_tag


_Op.cur_tag = None


def _merge(d, s):
    for k, v in s.items():
        if d.get(k, 0) < v:
            d[k] = v


class Prog:
    def __init__(self, nc, stack):
        self.nc = nc
        self.stack = stack
        self.semstack = stack
        self.free_sems = {"hw": [], "sw": [], "cc": []}
        self.chans = []
        self.jx = {}
        self.ops = {e: [] for e in ENGS}
        self.esem = {}
        for e in ("pe", "act", "dve", "pool"):
            self.esem[e] = stack.enter_context(nc.semaphore("es_" + e))
        self.dsems = {}
        self.nsem = 4
        self.final = {}
        self.uid = 0
        self.need_rank = False
        self.pidx = 0
        self.scopes = False
        self.fill_queue = []

    def sb(self, name, shape, dt):
        return self.stack.enter_context(self.nc.sbuf_tensor("p%d_%s" % (self.pidx, name), list(shape), dt))

    def ps(self, name, shape, dt=F32):
        return self.stack.enter_context(self.nc.psum_tensor("p%d_%s" % (self.pidx, name), list(shape), dt))

    def buf(self, name=None):
        self.uid += 1
        return Buf(name or f"b{self.uid}")

    def _chan(self, b, cls="hw"):
        if b.sem is None:
            b.cls = cls
            if self.free_sems[cls]:
                b.sem, b.semval = self.free_sems[cls].pop()
            else:
                b.sem = self.semstack.enter_context(self.nc.semaphore("ds_%d" % self.nsem))
                self.dsems[id(b.sem)] = b.sem
                self.nsem += 1
            self.chans.append(b)
        return b.sem

    def tag(self, name):
        _Op.cur_tag = name

    def fill_step(self, k, q="pool", maxkey=99):
        for _ in range(k):
            if not self.fill_queue or self.fill_queue[0][0] > maxkey:
                return
            self.fill_queue.pop(0)[1](q)

    def fill_until(self, key, q="pool"):
        while self.fill_queue and self.fill_queue[0][0] <= key:
            self.fill_queue.pop(0)[1](q)

    def phase_begin(self):
        self.pidx += 1
        self._pstack = ExitStack()
        self._outer = self.stack
        self.stack = self._pstack

    def phase_end(self):
        self.barrier()
        self._pstack.close()
        self.stack = self._outer

    def barrier(self):
        deps = {}
        for e in ("pe", "act", "dve", "pool"):
            lst = self.ops[e]
            for i in range(len(lst) - 1, -1, -1):
                if lst[i].dma is None and lst[i].fn is not None:
                    deps[("e", e)] = i + 1
                    break
        for b in self.chans:
            if b.semval:
                deps[("d", id(b.sem))] = b.semval
        for e in ENGS:
            self.ops[e].append(_Op(None, dict(deps)))
        for b in self.chans:
            self.free_sems[b.cls].append((b.sem, b.semval))
            b.sem = None
        self.chans = []

    def coll(self, kind, groups, in_ap, out_ap, writes=(), reads=()):
        b = self.buf()
        sem = self._chan(b, "cc")
        b.semval += 1
        v = b.semval
        for w_ in writes:
            w_.w[("d", id(sem))] = v
        deps = {}
        for r_ in reads:
            _merge(deps, r_.w)
        self.ops["pool"].append(_Op(lambda e: e.collective_compute(kind, ALU.add, replica_groups=groups, ins=[in_ap.opt()], outs=[out_ap.opt()]),
                                    deps, dma=(sem, v, 1)))

    def _deps(self, reads, writes):
        d = {}
        for b in reads:
            _merge(d, b.w)
        for b in writes:
            _merge(d, b.w)
            _merge(d, b.r)
        return d

    def op(self, eng, fn, reads=(), writes=()):
        if any(b.excl for b in reads):
            writes = list(writes) + [b for b in reads if b.excl]
            reads = [b for b in reads if not b.excl]
        deps = self._deps(reads, writes)
        if eng == "pe":
            deps.pop(("e", "pe"), None)
        lst = self.ops[eng]
        lst.append(_Op(fn, deps))
        key = ("e", eng)
        v = len(lst)
        for b in reads:
            b.r[key] = v
        for b in writes:
            b.w[key] = v

    def dma(self, q, out, in_, reads=(), writes=(), chan=None, store=False, final=False, defer=False, **kw):
        if chan is None:
            chan = (reads[0] if store else writes[0])
        sem = self._chan(chan, "sw" if q == "pool" else "hw")
        assert chan.cls == ("sw" if q == "pool" else "hw"), "a DMA channel must stay on one kind of queue"
        if defer and chan in self.chans:
            self.chans.remove(chan)
        key = ("d", id(sem))
        deps = self._deps(reads, writes)
        if store and chan.semval:
            deps[key] = max(deps.get(key, 0), chan.semval)
        chan.semval += 16
        v = chan.semval
        def _fn(e):
            o = out() if callable(out) else out
            i = in_() if callable(in_) else in_
            try:
                return e.dma_start(out=o, in_=i, **kw)
            except Exception:
                print("DMA FAIL out=", o, " in=", i)
                raise
        self.ops[q].append(_Op(_fn, deps, dma=(sem, v, 16)))
        for b in reads:
            b.r[key] = v
        for b in writes:
            b.w[key] = v
        if final:
            self.final[key] = v

    def emit(self):
        nc = self.nc
        for e in ENGS:
            for o in self.ops[e]:
                for (kind, k), v in o.deps.items():
                    if kind == "e":
                        self.ops[k][v - 1].signal = True
        for e in ENGS:
            n = 0
            for o in self.ops[e]:
                if o.signal:
                    n += 1
                o.sigord = n
        if self.final:
            self.ops["sp"].append(_Op(None, dict(self.final)))
        eng_handle = {"pe": "tensor", "act": "scalar", "dve": "vector", "pool": "gpsimd", "sp": "sync"}
        stats = {}
        with nc.Block() as block:
            for e in ENGS:
                ops = self.ops[e]
                if not ops:
                    continue

                def body(eng, ops=ops, e=e):
                    if e in ("sp", "pool") and self.need_rank:
                        self.jx[e] = eng.partition_id() % 4
                    seen = {}
                    nw = 0
                    cur = None
                    sid = None
                    for o in ops:
                        if self.scopes and o.tag != cur:
                            if cur is not None:
                                nc.leave_named_scope(cur, sid, False)
                            cur = o.tag
                            if cur is not None:
                                sid, _ = nc.enter_named_scope(cur, False)
                        for (kind, k), v in o.deps.items():
                            if kind == "e":
                                sem = self.esem[k]
                                val = self.ops[k][v - 1].sigord
                            else:
                                sem = self.dsems[k]
                                val = v
                            sk = (kind, k)
                            if seen.get(sk, 0) >= val:
                                continue
                            seen[sk] = val
                            eng.wait_ge(sem, val)
                            nw += 1
                        if o.fn is None:
                            continue
                        inst = o.fn(eng)
                        if o.dma is not None:
                            if o.dma[2] == 16:
                                inst.then_inc(o.dma[0], 16)
                            else:
                                inst.then_inc(o.dma[0])
                        elif o.signal:
                            inst.then_inc(self.esem[e], 1)
                    if self.scopes and cur is not None:
                        nc.leave_named_scope(cur, sid, False)
                    stats[e] = (len(ops), nw)

                getattr(block, eng_handle[e])(body)
        self.stats = stats
        return stats


D = 1024
KC = 8
TT = 512


def load_weight_bf16(P, wdram, ncols, wb, Bwb, stg, Bstg, scale_col=None, q="sp", Bscale=None):
    piece = 1024
    n = 0
    for kc in range(KC):
        for c0 in range(0, ncols, piece):
            c1 = min(ncols, c0 + piece)
            s = n % 2
            n += 1
            P.dma(q, stg[s][:, 0:c1 - c0], wdram[kc * 128:(kc + 1) * 128, c0:c1], writes=[Bstg[s]])
            eng = ("dve", "pool")[n % 2]
            if scale_col is not None:
                P.op(eng, lambda e, s=s, kc=kc, c0=c0, c1=c1: e.tensor_scalar(
                    out=wb[:, kc, c0:c1], in0=stg[s][:, 0:c1 - c0], scalar1=scale_col[:, kc:kc + 1], scalar2=None,
                    op0=ALU.mult), reads=[Bstg[s], Bscale], writes=[Bwb])
            else:
                P.op(eng, lambda e, s=s, kc=kc, c0=c0, c1=c1: e.tensor_copy(
                    out=wb[:, kc, c0:c1], in_=stg[s][:, 0:c1 - c0]), reads=[Bstg[s]], writes=[Bwb])


class Norm:
    def __init__(self, P, ones, Bones, nmax=TT):
        self.P = P
        self.ones = ones
        self.Bones = Bones
        self.sq = P.sb("nsq", [128, KC, nmax], BF16)
        self.Bsq = P.buf()
        self.ssp = P.ps("nss", [128, nmax])
        self.Bss = P.buf()
        self.sd = P.sb("nsd", [128, nmax], F32)
        self.Bsd = P.buf()

    def run(self, xs, Bxs, n, rstd, Brstd):
        P = self.P
        P.op("act", lambda e: e.activation(out=self.sq[:, :, 0:n], in_=xs[:, :, 0:n], func=AF.Square),
             reads=[Bxs], writes=[self.Bsq])
        for kc in range(KC):
            P.op("pe", lambda e, kc=kc: e.matmul(self.ssp[:, 0:n], lhsT=self.ones[:], rhs=self.sq[:, kc, 0:n],
                                                  start=(kc == 0), stop=(kc == KC - 1)),
                 reads=[self.Bsq, self.Bones], writes=[self.Bss])
        P.op("act", lambda e: e.activation(out=self.sd[:, 0:n], in_=self.ssp[:, 0:n], func=AF.Sqrt,
                                           scale=1.0 / D, bias=self.epsc[:, 0:1]),
             reads=[self.Bss, self.Bones], writes=[self.Bsd])
        P.op("dve", lambda e: e.reciprocal(out=rstd[:, 0:n], in_=self.sd[:, 0:n]), reads=[self.Bsd], writes=[Brstd])


def build_stage0(P, io, NTOK=4096):
    NT = NTOK // TT
    xT, nw, w_in, w_cv, w_out, xo = io["l0_xT"], io["l0_nw"], io["l0_w_in"], io["l0_w_cv"], io["l0_w_out"], io["x1T"]
    xTv = xT.rearrange("(c p) t -> p c t", p=128)
    xov = xo.rearrange("(c p) t -> p c t", p=128)

    ones = P.sb("ones", [128, 128], BF16)
    Bones = P.buf()
    epsc = P.sb("epsc", [128, 1], F32)
    P.op("dve", lambda e: e.memset(ones[:], 1.0), writes=[Bones])
    P.op("dve", lambda e: e.memset(epsc[:], 1e-6), writes=[Bones])
    nws = P.sb("nws", [128, KC], F32)
    Bnw = P.buf()
    P.dma("sp", nws[:], nw[:, :], writes=[Bnw])
    wcs = P.sb("wcs", [128, KC * 3], F32)
    Bwc = P.buf()
    P.dma("sp", wcs[:], w_cv[:, :], writes=[Bwc])

    stg = [P.sb("stg%d" % i, [128, 1024], F32) for i in range(2)]
    Bstg = [P.buf(), P.buf()]
    wib = P.sb("wib", [128, KC, 4096], BF16)
    Bwib = P.buf()
    wob = P.sb("wob", [128, KC, D], BF16)
    Bwob = P.buf()

    xs = [P.sb("xs%d" % i, [128, KC, TT], F32) for i in range(2)]
    Bxs = [P.buf(), P.buf()]
    hT = [P.sb("hT%d" % i, [128, KC, TT], BF16) for i in range(2)]
    BhT = [P.buf(), P.buf()]
    rstd = [P.sb("rstd%d" % i, [128, TT], F32) for i in range(2)]
    Brstd = [P.buf(), P.buf()]
    xh = P.sb("xh", [128, KC, 2], F32)
    Bxh = P.buf()
    hh = P.sb("hh", [128, KC, 2], BF16)
    Bhh = P.buf()
    rh = P.sb("rh", [128, 2], F32)
    Brh = P.buf()
    norm = Norm(P, ones, Bones)
    norm.epsc = epsc

    ubuf = P.sb("ubuf", [128, KC, TT + 2], F32)
    Bu = [P.buf() for _ in range(KC)]
    vsb = P.sb("vsb", [128, TT], F32)
    Bv = P.buf()
    ycv = P.sb("ycv", [128, TT], F32)
    By = P.buf()
    szb = P.sb("szb", [128, TT], F32)
    Bsz = P.buf()
    tb = P.sb("tb", [128, TT], F32)
    Bt = P.buf()
    og = P.sb("og", [128, KC, TT], BF16)
    Bog = P.buf()
    xn = [P.sb("xn%d" % i, [128, TT], F32) for i in range(2)]
    Bxn = [P.buf(), P.buf()]
    pb, pc, pv, pz = [P.ps("pp%d" % i, [128, TT]) for i in range(4)]
    Bpb, Bpc, Bpv, Bpz = [P.buf() for _ in range(4)]
    py = [P.ps("py%d" % i, [128, TT]) for i in range(2)]
    Bpy = [P.buf(), P.buf()]

    load_weight_bf16(P, w_in, 4096, wib, Bwib, stg, Bstg, scale_col=nws, Bscale=Bnw)
    load_weight_bf16(P, w_out, D, wob, Bwob, stg, Bstg)

    def proj(ps, Bps, oc, h, Bh, n):
        for kc in range(KC):
            P.op("pe", lambda e, kc=kc: e.matmul(ps[:, 0:n], lhsT=wib[:, kc, oc * 128:(oc + 1) * 128],
                                                  rhs=h[:, kc, 0:n], start=(kc == 0), stop=(kc == KC - 1)),
                 reads=[Bwib, Bh], writes=[Bps])

    def make_h(x_, Bx_, n, r_, Br_, h_, Bh_):
        norm.run(x_, Bx_, n, r_, Br_)
        for kc in range(KC):
            eng = "dve" if kc % 2 == 0 else "pool"
            P.op(eng, lambda e, kc=kc: e.tensor_tensor(out=h_[:, kc, 0:n], in0=x_[:, kc, 0:n], in1=r_[:, 0:n],
                                                        op=ALU.mult), reads=[Bx_, Br_], writes=[Bh_])

    P.dma("sp", xh[:], xTv[:, :, 0:2], writes=[Bxh])
    make_h(xh, Bxh, 2, rh, Brh, hh, Bhh)
    for ci in range(KC):
        proj(pc, Bpc, 8 + ci, hh, Bhh, 2)
        proj(pv, Bpv, 16 + ci, hh, Bhh, 2)
        P.op("act", lambda e: e.activation(out=vsb[:, 0:2], in_=pv[:, 0:2], func=AF.Copy), reads=[Bpv], writes=[Bv])
        P.op("dve", lambda e, ci=ci: e.tensor_tensor(out=ubuf[:, ci, 0:2], in0=pc[:, 0:2], in1=vsb[:, 0:2], op=ALU.mult),
             reads=[Bpc, Bv], writes=[Bu[ci]])

    def prep(i):
        s = i % 2
        P.dma("sp", xs[s][:], xTv[:, :, 2 + i * TT:2 + (i + 1) * TT], writes=[Bxs[s]])
        make_h(xs[s], Bxs[s], TT, rstd[s], Brstd[s], hT[s], BhT[s])

    def main(i):
        s = i % 2
        h, Bh = hT[s], BhT[s]
        for ci in range(KC):
            proj(pc, Bpc, 8 + ci, h, Bh, TT)
            proj(pv, Bpv, 16 + ci, h, Bh, TT)
            proj(pb, Bpb, ci, h, Bh, TT)
            proj(pz, Bpz, 24 + ci, h, Bh, TT)
            P.op("act", lambda e: e.activation(out=vsb[:], in_=pv[:], func=AF.Copy), reads=[Bpv], writes=[Bv])
            P.op("dve", lambda e, ci=ci: e.tensor_tensor(out=ubuf[:, ci, 2:TT + 2], in0=pc[:], in1=vsb[:], op=ALU.mult),
                 reads=[Bpc, Bv], writes=[Bu[ci]])
            P.op("dve", lambda e, ci=ci: e.tensor_scalar(out=ycv[:], in0=ubuf[:, ci, 2:TT + 2],
                                                          scalar1=wcs[:, ci * 3 + 2:ci * 3 + 3], scalar2=None, op0=ALU.mult),
                 reads=[Bu[ci], Bwc], writes=[By])
            for k in (1, 0):
                P.op("dve", lambda e, ci=ci, k=k: e.scalar_tensor_tensor(
                    out=ycv[:], in0=ubuf[:, ci, k:k + TT], scalar=wcs[:, ci * 3 + k:ci * 3 + k + 1], in1=ycv[:],
                    op0=ALU.mult, op1=ALU.add), reads=[Bu[ci], Bwc, By], writes=[By])
            P.op("pool", lambda e, ci=ci: e.tensor_copy(out=ubuf[:, ci, 0:2], in_=ubuf[:, ci, TT:TT + 2]),
                 reads=[Bu[ci]], writes=[Bu[ci]])
            P.op("act", lambda e: e.activation(out=szb[:], in_=pz[:], func=AF.Silu), reads=[Bpz], writes=[Bsz])
            P.op("dve", lambda e: e.tensor_tensor(out=tb[:], in0=pb[:], in1=ycv[:], op=ALU.mult),
                 reads=[Bpb, By], writes=[Bt])
            P.op("pool", lambda e, ci=ci: e.tensor_tensor(out=og[:, ci, :], in0=tb[:], in1=szb[:], op=ALU.mult),
                 reads=[Bt, Bsz], writes=[Bog])
        for dc in range(KC):
            b = dc % 2
            for ci in range(KC):
                P.op("pe", lambda e, ci=ci, dc=dc, b=b: e.matmul(py[b][:], lhsT=wob[:, ci, dc * 128:(dc + 1) * 128],
                                                                rhs=og[:, ci, :], start=(ci == 0), stop=(ci == KC - 1)),
                     reads=[Bwob, Bog], writes=[Bpy[b]])
            P.op("dve", lambda e, dc=dc, b=b: e.tensor_tensor(out=xn[b][:], in0=py[b][:], in1=xs[s][:, dc, :], op=ALU.add),
                 reads=[Bpy[b], Bxs[s]], writes=[Bxn[b]])
            P.dma("sp", xov[:, dc, i * TT:(i + 1) * TT], xn[b][:], reads=[Bxn[b]], store=True)

    prep(0)
    for i in range(NT):
        P.fill_step(4)
        if i + 1 < NT:
            prep(i + 1)
        main(i)
    return P


import math
from concourse.bass import ds


class Rot:
    def __init__(self, P, name, shape, dt, n):
        self.t = [P.sb("%s%d" % (name, i), shape, dt) for i in range(n)]
        self.b = [P.buf() for _ in range(n)]
        self.i = 0

    def next(self):
        k = self.i % len(self.t)
        self.i += 1
        return self.t[k], self.b[k]


class Common:
    def __init__(self, P):
        self.P = P
        self.ones = P.sb("ones", [128, 128], BF16)
        self.Bc = P.buf()
        self.epsc = P.sb("epsc", [128, 1], F32)
        P.op("dve", lambda e: e.memset(self.ones[:], 1.0), writes=[self.Bc])
        P.op("dve", lambda e: e.memset(self.epsc[:], 1e-6), writes=[self.Bc])
        self.norm = Norm(P, self.ones, self.Bc)
        self.norm.epsc = self.epsc
        self.stg = [P.sb("stg%d" % i, [128, 1024], F32) for i in range(2)]
        self.Bstg = [P.buf(), P.buf()]
        self.xs = [P.sb("xs%d" % i, [128, KC, TT], F32) for i in range(2)]
        self.Bxs = [P.buf(), P.buf()]
        self.hT = [P.sb("hT%d" % i, [128, KC, TT], BF16) for i in range(2)]
        self.BhT = [P.buf(), P.buf()]
        self.rstd = [P.sb("rstd%d" % i, [128, TT], F32) for i in range(2)]
        self.Brstd = [P.buf(), P.buf()]
        self.f32r = Rot(P, "f32r", [128, TT], F32, 4)
        self.bf16r = Rot(P, "bf16r", [128, TT], BF16, 4)
        self.pp = [P.ps("pp%d" % i, [128, TT]) for i in range(6)]
        self.Bpp = [P.buf() for _ in range(6)]
        self.ppi = 0

    def psum(self):
        k = self.ppi % len(self.pp)
        self.ppi += 1
        return self.pp[k], self.Bpp[k]

    def make_h(self, s, n=TT):
        P = self.P
        x_, Bx_, r_, Br_, h_, Bh_ = self.xs[s], self.Bxs[s], self.rstd[s], self.Brstd[s], self.hT[s], self.BhT[s]
        self.norm.run(x_, Bx_, n, r_, Br_)
        for kc in range(KC):
            eng = "dve" if kc % 2 == 0 else "pool"
            P.op(eng, lambda e, kc=kc: e.tensor_tensor(out=h_[:, kc, 0:n], in0=x_[:, kc, 0:n], in1=r_[:, 0:n],
                                                        op=ALU.mult), reads=[Bx_, Br_], writes=[Bh_])


def load_small(P, dram, shape, name, dt=F32):
    t = P.sb(name, shape, dt)
    B = P.buf()
    P.dma("sp", t[:], dram, writes=[B])
    return t, B


class StageC:
    def __init__(self, P, cm, io, NTOK):
        self.P, self.cm = P, cm
        v = lambda a: a.rearrange("(c p) t -> p c t", p=128)
        self.szT, self.xT, self.w_out, self.xo = v(io["c_szT"]), v(io["c_xT"]), io["c_w_out"], v(io["xo"])
        SUB = NTOK // 8
        self.SUB = SUB
        self.oTs = [v(io["c_oT"][s_ * D:(s_ + 1) * D, :]) for s_ in range(8)]
        self.Bo = io["c_oT_bufs"]
        self.wob = P.sb("wob", [128, KC, D], BF16)
        self.Bwob = P.buf()
        self.og = P.sb("og", [128, KC, TT], BF16)
        self.Bog = P.buf()
        self.lo = Rot(P, "c_lo", [128, TT], F32, 2)
        self.ls = Rot(P, "c_ls", [128, TT], F32, 2)
        self.lx = Rot(P, "c_lx", [128, TT], F32, 2)

    def load_weights(self):
        load_weight_bf16(self.P, self.w_out, D, self.wob, self.Bwob, self.cm.stg, self.cm.Bstg)

    def tile(self, i, s):
        P, cm = self.P, self.cm
        sl = slice(i * TT, (i + 1) * TT)
        for ci in range(KC):
            to, Bo = self.lo.next()
            ts, Bs = self.ls.next()
            s_ = (i * TT) // self.SUB
            lo_ = i * TT - s_ * self.SUB
            P.dma("sp", to[:], self.oTs[s_][:, ci, lo_:lo_ + TT], reads=[self.Bo[s_]], writes=[Bo])
            P.dma("sp", ts[:], self.szT[:, ci, sl], writes=[Bs])
            eng = "dve" if ci % 2 == 0 else "pool"
            P.op(eng, lambda e, ci=ci, to=to, ts=ts: e.tensor_tensor(out=self.og[:, ci, :], in0=to[:], in1=ts[:], op=ALU.mult),
                 reads=[Bo, Bs], writes=[self.Bog])
        for dc in range(KC):
            ps, Bps = cm.psum()
            tx, Bx = self.lx.next()
            P.dma("sp", tx[:], self.xT[:, dc, sl], writes=[Bx])
            for ci in range(KC):
                P.op("pe", lambda e, ci=ci, dc=dc, ps=ps: e.matmul(ps[:], lhsT=self.wob[:, ci, dc * 128:(dc + 1) * 128],
                                                                  rhs=self.og[:, ci, :], start=(ci == 0), stop=(ci == KC - 1)),
                     reads=[self.Bwob, self.Bog], writes=[Bps])
            P.op("dve", lambda e, dc=dc, ps=ps, tx=tx: e.tensor_tensor(out=cm.xs[s][:, dc, :], in0=ps[:], in1=tx[:], op=ALU.add),
                 reads=[Bps, Bx], writes=[cm.Bxs[s]])
        P.dma("sp", self.xo[:, :, sl], cm.xs[s][:], reads=[cm.Bxs[s]], store=True, chan=cm.Bxs[s])


class StageLoadX:
    def __init__(self, P, cm, io, NTOK):
        self.P, self.cm = P, cm
        self.xT = io["c_xT"].rearrange("(c p) t -> p c t", p=128)

    def load_weights(self):
        pass

    def tile(self, i, s):
        self.P.dma("sp", self.cm.xs[s][:], self.xT[:, :, i * TT:(i + 1) * TT], writes=[self.cm.Bxs[s]])


class StageAFox:
    NCOL = 4104

    def __init__(self, P, cm, io, NTOK):
        self.P, self.cm = P, cm
        self.NTOK = NTOK
        self.nw, self.w_in = io["a_nw"], io["a_w_in"]
        self.xb, self.xf = io["xb"], io["xf"]
        self.szT = io["szT"].rearrange("(c p) t -> p c t", p=128)
        self.wib = P.sb("wib", [128, KC, self.NCOL], BF16)
        self.Bwib = P.buf()
        self.flr = Rot(P, "a_fl", [8, TT], F32, 2)

    def load_weights(self):
        self.nws, self.Bnw = load_small(self.P, self.nw[:, :], [128, KC], "a_nws")
        load_weight_bf16(self.P, self.w_in, self.NCOL, self.wib, self.Bwib, self.cm.stg, self.cm.Bstg, scale_col=self.nws, Bscale=self.Bnw)

    def fm_proj(self, col0, s, M=128):
        P, cm = self.P, self.cm
        ps, Bps = cm.psum()
        for kc in range(KC):
            P.op("pe", lambda e, kc=kc, ps=ps: e.matmul(ps[0:M, :], lhsT=self.wib[:, kc, col0:col0 + M], rhs=cm.hT[s][:, kc, :],
                                                        start=(kc == 0), stop=(kc == KC - 1)),
                 reads=[self.Bwib, cm.BhT[s]], writes=[Bps])
        return ps, Bps

    def tm_proj(self, col0, ncol, blk, s):
        P, cm = self.P, self.cm
        ps, Bps = cm.psum()
        for kc in range(KC):
            P.op("pe", lambda e, kc=kc, ps=ps: e.matmul(ps[:, 0:ncol], lhsT=cm.hT[s][:, kc, blk * 128:(blk + 1) * 128],
                                                        rhs=self.wib[:, kc, col0:col0 + ncol],
                                                        start=(kc == 0), stop=(kc == KC - 1)),
                 reads=[self.Bwib, cm.BhT[s]], writes=[Bps])
        return ps, Bps

    def tile(self, i, s):
        P, cm = self.P, self.cm
        sl = slice(i * TT, (i + 1) * TT)
        S_ = 4 * self.NTOK
        xb, xf = self.xb, self.xf
        n = 0
        for which in (0, 1):
            for c in range(KC):
                ps, Bps = self.fm_proj(which * 1024 + c * 128, s)
                t, Bt = cm.bf16r.next()
                if n % 2 == 0:
                    P.op("act", lambda e, t=t, ps=ps: e.activation(out=t[:], in_=ps[:], func=AF.Copy), reads=[Bps], writes=[Bt])
                else:
                    P.op("dve", lambda e, t=t, ps=ps: e.tensor_copy(out=t[:], in_=ps[:]), reads=[Bps], writes=[Bt])
                n += 1
                row0 = (c // 2) * 768 + which * 256 + (c % 2) * 128
                P.dma("sp", xb[row0:row0 + 128, sl], t[:], reads=[Bt], store=True)
        for c in range(KC):
            ps, Bps = self.fm_proj(3072 + c * 128, s)
            t, Bt = cm.f32r.next()
            P.op("act", lambda e, t=t, ps=ps: e.activation(out=t[:], in_=ps[:], func=AF.Silu), reads=[Bps], writes=[Bt])
            P.dma("sp", self.szT[:, c, sl], t[:], reads=[Bt], store=True)
        for blk in range(TT // 128):
            for half in range(2):
                ps, Bps = self.tm_proj(2048 + half * 512, 512, blk, s)
                t, Bt = cm.bf16r.next()
                P.op("dve", lambda e, t=t, ps=ps: e.tensor_copy(out=t[:], in_=ps[:]), reads=[Bps], writes=[Bt])
                for pr in range(2):
                    g = half * 2 + pr
                    tok0 = i * TT + blk * 128
                    r0 = g * 768 + 512
                    vview = xb[r0:r0 + 256, :].rearrange("(h r) (q d) -> h (r q) d", h=2, d=128)
                    P.dma("sp", vview[:, tok0:tok0 + 128, :].rearrange("h t d -> t h d"),
                          t[:, pr * 256:(pr + 1) * 256].rearrange("p (h d) -> p h d", d=128), reads=[Bt], store=True)
        ps, Bps = cm.psum()
        for kc in range(KC):
            P.op("pe", lambda e, kc=kc, ps=ps: e.matmul(ps[0:8, :], lhsT=self.wib[:, kc, 4096:4104], rhs=cm.hT[s][:, kc, :],
                                                        start=(kc == 0), stop=(kc == KC - 1)), reads=[self.Bwib, cm.BhT[s]], writes=[Bps])
        t, Bt = self.flr.next()
        P.op("dve", lambda e, t=t, ps=ps: e.tensor_copy(out=t[:], in_=ps[0:8, :]), reads=[Bps], writes=[Bt])
        P.dma("sp", xf[0:8, sl], t[:], reads=[Bt], store=True)


def build_CA(P, io, Ccls, Acls, NTOK=4096):
    cm = Common(P)
    C = Ccls(P, cm, io, NTOK)
    A = Acls(P, cm, io, NTOK)
    C.load_weights()
    A.load_weights()
    NT = NTOK // TT

    def prep(i):
        C.tile(i, i % 2)
        if Acls is not StageANull:
            cm.make_h(i % 2)

    prep(0)
    for i in range(NT):
        P.fill_step(3)
        if i + 1 < NT:
            prep(i + 1)
        A.tile(i, i % 2)
    return P


class StageAGdn(StageAFox):
    NCOL = 4112

    def __init__(self, P, cm, io, NTOK):
        self.P, self.cm = P, cm
        self.NTOK = NTOK
        self.nw, self.w_in = io["a_nw"], io["a_w_in"]
        self.xb, self.xf = io["xb"], io["xf"]
        self.szT = io["szT"].rearrange("(c p) t -> p c t", p=128)
        self.wib = P.sb("wib", [128, KC, self.NCOL], BF16)
        self.Bwib = P.buf()
        self.flr = Rot(P, "a_fl", [16, TT], F32, 2)

    def load_weights(self):
        StageAFox.load_weights(self)
        P = self.P
        self.wba = P.sb("wba", [128, KC, 16], BF16)
        P.op("dve", lambda e: e.tensor_copy(out=self.wba[:].rearrange("p k (g h w) -> p k g h w", g=4, h=2, w=2),
                                            in_=self.wib[:, :, 4096:4112].rearrange("p k (w g h) -> p k g h w", w=2, g=4, h=2)),
             reads=[self.Bwib], writes=[self.Bwib])

    def tile(self, i, s):
        P, cm = self.P, self.cm
        sl = slice(i * TT, (i + 1) * TT)
        xb, xf = self.xb, self.xf
        for c in range(24):
            ps, Bps = self.fm_proj(c * 128, s)
            t, Bt = cm.f32r.next()
            if c % 2 == 0:
                P.op("act", lambda e, t=t, ps=ps: e.activation(out=t[:], in_=ps[:], func=AF.Copy), reads=[Bps], writes=[Bt])
            else:
                P.op("dve", lambda e, t=t, ps=ps: e.tensor_copy(out=t[:], in_=ps[:]), reads=[Bps], writes=[Bt])
            which, h = c // 8, c % 8
            row0 = (h // 2) * 768 + ((h % 2) * 3 + which) * 128
            P.dma("sp", xb[row0:row0 + 128, sl], t[:], reads=[Bt], store=True)
        for c in range(KC):
            ps, Bps = self.fm_proj(3072 + c * 128, s)
            t, Bt = cm.f32r.next()
            P.op("act", lambda e, t=t, ps=ps: e.activation(out=t[:], in_=ps[:], func=AF.Silu), reads=[Bps], writes=[Bt])
            P.dma("sp", self.szT[:, c, sl], t[:], reads=[Bt], store=True)
        ps, Bps = cm.psum()
        for kc in range(KC):
            P.op("pe", lambda e, kc=kc, ps=ps: e.matmul(ps[0:16, :], lhsT=self.wba[:, kc, :], rhs=cm.hT[s][:, kc, :],
                                                        start=(kc == 0), stop=(kc == KC - 1)), reads=[self.Bwib, cm.BhT[s]], writes=[Bps])
        t, Bt = self.flr.next()
        P.op("dve", lambda e, t=t, ps=ps: e.tensor_copy(out=t[:], in_=ps[0:16, :]), reads=[Bps], writes=[Bt])
        P.dma("sp", xf[0:16, sl], t[:], reads=[Bt], store=True)


class StageANull:
    def __init__(self, P, cm, io, NTOK):
        pass

    def load_weights(self):
        pass

    def tile(self, i, s):
        pass


import math
import numpy as np
from concourse.bass import ds

S = 16384
NB = S // 128
QT = 512
NEG = -30000.0


def fox_consts():
    k = np.arange(128)
    U = (k[:, None] <= k[None, :]).astype(np.float32)
    SU = (k[:, None] < k[None, :]).astype(np.float32)
    E127 = np.zeros((128, 128), np.float32)
    E127[127, :] = 1.0
    ident = np.eye(128, dtype=np.float32)
    masks = np.zeros((4, 128, 512), np.float32)
    for r in range(4):
        for rp in range(4):
            blk = masks[r][:, rp * 128:(rp + 1) * 128]
            if rp < r:
                blk[:] = NEG
            elif rp == r:
                blk[:] = np.where(k[:, None] <= k[None, :], 0.0, NEG)
    return {"cU": U, "cSU": SU, "cE127": E127, "cI": ident, "cMask": np.ascontiguousarray(masks.transpose(1, 0, 2).reshape(128, 2048))}


def build_foxB(P, io, NH=2, S_=S):
    NBk = S_ // 128
    NQ = S_ // QT
    NTOK = S_ // 4
    scale = 1.0 / math.sqrt(128.0)
    yb = io["yb"]
    By = io["yb_bufs"]
    qTq = [yb[c_ * 768:c_ * 768 + 256, :].rearrange("(h p) s -> h p s", p=128) for c_ in range(4)]
    kTq = [yb[c_ * 768 + 256:c_ * 768 + 512, :].rearrange("(h p) s -> h p s", p=128) for c_ in range(4)]
    vvq = [yb[c_ * 768 + 512:c_ * 768 + 768, :].rearrange("(h r) (q d) -> h (r q) d", h=2, d=128) for c_ in range(4)]
    fl = io["yf"]
    bfd = io["fox_bf"]
    cdr = {n: io[n] for n in ("cU", "cSU", "cE127", "cI", "cMask")}
    xo = io["xo"]

    def const(name, w):
        t = P.sb("k" + name, [128, w], F32)
        B = P.buf()
        P.dma("sp", t[:], cdr[name][:, :], writes=[B])
        return t, B

    U, BU = const("cU", 128)
    SU, BSU = const("cSU", 128)
    E127, BE = const("cE127", 128)
    I32f, BI = const("cI", 128)
    Mf, BM = const("cMask", 2048)
    Bk = P.buf()
    ones_f = P.sb("ones_f", [128, 128], F32)
    ones_b = P.sb("ones_b", [128, 128], BF16)
    ident_b = P.sb("ident_b", [128, 128], BF16)
    mask_b = P.sb("mask_b", [128, 2048], BF16)
    P.op("dve", lambda e: e.memset(ones_f[:], 1.0), writes=[Bk])
    P.op("dve", lambda e: e.memset(ones_b[:], 1.0), writes=[Bk])
    P.op("dve", lambda e: e.tensor_copy(out=ident_b[:], in_=I32f[:]), reads=[BI], writes=[Bk])
    P.op("dve", lambda e: e.tensor_copy(out=mask_b[:], in_=Mf[:]), reads=[BM], writes=[Bk])
    bfs = P.sb("bfs", [128, NH], F32)
    nbf = P.sb("nbf", [128, NH], F32)
    Bbf = P.buf()
    P.dma("sp", bfs[:], bfd[:, 0:NH], writes=[Bbf])
    P.op("dve", lambda e: e.tensor_scalar(out=nbf[:], in0=bfs[:], scalar1=-1.0, scalar2=None, op0=ALU.mult),
         reads=[Bbf], writes=[Bbf])
    flr = P.sb("flr", [128, NH, 128], F32)
    Bflr = P.buf()
    P.dma("sp", flr[:], fl.rearrange("h (b s) -> b h s", s=128), writes=[Bflr])
    fls = P.sb("fls", [128, NH * NBk], F32)
    Bfl = P.buf()

    ks = [P.sb("ks%d" % i, [128, S_], BF16) for i in range(2)]
    Bks = [[P.buf() for _ in range(4)] for _ in range(2)]
    vs = [P.sb("vs%d" % i, [128, NBk, 128], BF16) for i in range(2)]
    Bvs = [[P.buf() for _ in range(4)] for _ in range(2)]
    cpos = [P.sb("cpos%d" % i, [128, NBk], F32) for i in range(2)]
    clast = [P.sb("clast%d" % i, [128, NBk], F32) for i in range(2)]
    Bcp = [P.buf(), P.buf()]
    l1 = P.sb("l1", [128, NBk], F32)
    Bl1 = P.buf()
    totT = P.sb("totT", [128, 128], F32)
    Btot = P.buf()
    qs = [P.sb("qs%d" % i, [128, QT], BF16) for i in range(2)]
    Bqs = [P.buf(), P.buf()]
    biasM = [P.sb("biasM%d" % i, [128, NBk], F32) for i in range(2)]
    BbM = [P.buf(), P.buf()]
    Rm = [P.sb("Rm%d" % i, [128, QT], BF16) for i in range(2)]
    BRm = [P.buf(), P.buf()]
    NS = 3
    pst = [P.ps("pst%d" % i, [128, QT]) for i in range(NS)]
    Bpst = [P.buf() for _ in range(NS)]
    pts = [P.sb("pts%d" % i, [128, QT], BF16) for i in range(NS)]
    Bpts = [P.buf() for _ in range(NS)]
    po = [P.ps("po%d" % i, [128, QT]) for i in range(2)]
    Bpo = [P.buf(), P.buf()]
    pr = [P.ps("pr%d" % i, [128, QT]) for i in range(2)]
    Bpr = [P.buf(), P.buf()]
    racc = [[P.sb("racc%d%d" % (i, z), [128, QT], F32) for z in range(2)] for i in range(2)]
    Bracc = [[P.buf(), P.buf()] for _ in range(2)]
    rinv = P.sb("rinv", [128, QT], F32)
    Brinv = P.buf()
    osb = [P.sb("osb%d" % i, [128, QT], F32) for i in range(2)]
    Bosb = [P.buf(), P.buf()]
    pmisc = P.ps("pmisc", [128, 512])
    Bpm = P.buf()

    def load_kv(h, c_):
        hs = h % 2
        P.dma("sp", ks[hs][:, c_ * NTOK:(c_ + 1) * NTOK], kTq[c_][h, :, :], reads=[By[c_]], writes=[Bks[hs][c_]])
        P.dma("sp", vs[hs][:, c_ * (NTOK // 128):(c_ + 1) * (NTOK // 128), :], vvq[c_][h].rearrange("(b s) d -> s b d", s=128),
              reads=[By[c_]], writes=[Bvs[hs][c_]])

    def head_prep(h):
        hs = h % 2
        for c_ in range(4 if h > 0 else 1):
            load_kv(h, c_)
        P.op("pe", lambda e: e.transpose(pmisc[:, 384:384 + NBk], flr[:, h, :], I32f[:]), reads=[Bflr, BI], writes=[Bpm])
        P.op("dve", lambda e: e.tensor_copy(out=fls[:, h * NBk:(h + 1) * NBk], in_=pmisc[:, 384:384 + NBk]), reads=[Bpm], writes=[Bfl])
        f_h = fls[:, h * NBk:(h + 1) * NBk]
        P.op("act", lambda e: e.activation(out=l1[:], in_=f_h, func=AF.Exp, scale=-1.0, bias=nbf[:, h:h + 1]),
             reads=[Bfl, Bbf], writes=[Bl1])
        P.op("act", lambda e: e.activation(out=l1[:], in_=l1[:], func=AF.Ln, scale=1.0, bias=ones_f[:, 0:1]),
             reads=[Bl1, Bk], writes=[Bl1])
        P.op("pe", lambda e: e.matmul(pmisc[0:NBk, 0:128], lhsT=l1[:, 0:NBk], rhs=ones_f[:], start=True, stop=True),
             reads=[Bl1, Bk], writes=[Bpm])
        P.op("dve", lambda e: e.tensor_copy(out=totT[0:NBk, :], in_=pmisc[0:NBk, 0:128]), reads=[Bpm], writes=[Btot])
        P.op("pe", lambda e: e.matmul(pmisc[:, 128:128 + NBk], lhsT=U[:], rhs=l1[:, 0:NBk], start=True, stop=False),
             reads=[Bl1, BU], writes=[Bpm])
        P.op("pe", lambda e: e.matmul(pmisc[:, 128:128 + NBk], lhsT=totT[0:NBk, :], rhs=SU[0:NBk, 0:NBk], start=False, stop=True),
             reads=[Btot, BSU], writes=[Bpm])
        P.op("dve", lambda e: e.tensor_copy(out=cpos[hs][:], in_=pmisc[:, 128:128 + NBk]), reads=[Bpm], writes=[Bcp[hs]])
        P.op("pe", lambda e: e.matmul(pmisc[:, 256:256 + NBk], lhsT=E127[:], rhs=cpos[hs][:], start=True, stop=True),
             reads=[Bcp[hs], BE], writes=[Bpm])
        P.op("dve", lambda e: e.tensor_copy(out=clast[hs][:], in_=pmisc[:, 256:256 + NBk]), reads=[Bpm], writes=[Bcp[hs]])

    items = []
    for h in range(NH):
        for j in range(NQ):
            nkb = 4 * j + 4
            for kb in range(nkb):
                items.append((h, j, kb, nkb))
    qcount = [0]

    def qtile_prep(h, j):
        P.fill_step(2 if j >= 8 else 0, q="sp", maxkey=3)
        hs = h % 2
        s = qcount[0] % 2
        qcount[0] += 1
        nkb = 4 * j + 4
        cq = (j * QT) // NTOK
        if h == 0 and cq > 0 and (j * QT) % NTOK == 0:
            load_kv(h, cq)
        P.dma("sp", qs[s][:], qTq[cq][h, :, j * QT - cq * NTOK:(j + 1) * QT - cq * NTOK], reads=[By[cq]], writes=[Bqs[s]])
        P.op("dve", lambda e: e.tensor_scalar(out=biasM[s][:, 0:nkb], in0=cpos[hs][:, 0:nkb],
                                              scalar1=clast[hs][:, nkb - 1:nkb], scalar2=None, op0=ALU.subtract),
             reads=[Bcp[hs]], writes=[BbM[s]])
        for r in range(4):
            P.op("dve", lambda e, r=r: e.tensor_scalar(out=Rm[s][:, r * 128:(r + 1) * 128], in0=I32f[:],
                                                         scalar1=biasM[s][:, 4 * j + r:4 * j + r + 1], scalar2=-math.sqrt(128.0),
                                                         op0=ALU.mult, op1=ALU.mult),
                 reads=[BI, BbM[s]], writes=[BRm[s]])
        return s

    qslot = {}

    def QK(n):
        h, j, kb, nkb = items[n]
        hs = h % 2
        if kb == 0:
            if j == 0:
                head_prep(h)
            qslot[(h, j)] = qtile_prep(h, j)
        s = qslot[(h, j)]
        b = n % NS
        diag = kb >= 4 * j
        P.op("pe", lambda e: e.matmul(pst[b][:], lhsT=ks[hs][:, kb * 128:(kb + 1) * 128], rhs=qs[s][:], start=True, stop=False),
             reads=[Bks[hs][(kb * 128) // NTOK], Bqs[s]], writes=[Bpst[b]])
        P.op("pe", lambda e: e.matmul(pst[b][:], lhsT=ones_b[:], rhs=Rm[s][:], start=False, stop=not diag),
             reads=[Bk, BRm[s]], writes=[Bpst[b]])
        if diag:
            r = kb - 4 * j
            P.op("pe", lambda e: e.matmul(pst[b][:], lhsT=ident_b[:], rhs=mask_b[:, r * 512:(r + 1) * 512], start=False, stop=True),
                 reads=[Bk], writes=[Bpst[b]])

    def PV(n):
        h, j, kb, nkb = items[n]
        hs = h % 2
        s = qslot[(h, j)]
        b = n % NS
        a = (h * NQ + j) % 2
        P.op("act", lambda e: e.activation(out=pts[b][:], in_=pst[b][:], func=AF.Exp, scale=scale, bias=biasM[s][:, kb:kb + 1]),
             reads=[Bpst[b], BbM[s]], writes=[Bpts[b]])
        P.op("pe", lambda e: e.matmul(po[a][:], lhsT=vs[hs][:, kb, :], rhs=pts[b][:], start=(kb == 0), stop=(kb == nkb - 1)),
             reads=[Bvs[hs][(kb * 128) // NTOK], Bpts[b]], writes=[Bpo[a]])
        ra, Bra = racc[a][kb % 2], Bracc[a][kb % 2]
        if kb < 2:
            P.op("dve", lambda e: e.tensor_copy(out=ra[:], in_=pts[b][:]), reads=[Bpts[b]], writes=[Bra])
        else:
            P.op("dve", lambda e: e.tensor_tensor(out=ra[:], in0=ra[:], in1=pts[b][:], op=ALU.add), reads=[Bpts[b], Bra], writes=[Bra])
        if kb == nkb - 1:
            for z_ in range(2):
                P.op("pe", lambda e, z_=z_: e.matmul(pr[a][:], lhsT=ones_f[:], rhs=racc[a][z_][:], start=(z_ == 0), stop=(z_ == 1)),
                     reads=[Bk, Bracc[a][z_]], writes=[Bpr[a]])
            P.op("dve", lambda e: e.reciprocal(out=rinv[:], in_=pr[a][:]), reads=[Bpr[a]], writes=[Brinv])
            P.op("dve", lambda e: e.tensor_tensor(out=osb[a][:], in0=po[a][:], in1=rinv[:], op=ALU.mult),
                 reads=[Bpo[a], Brinv], writes=[Bosb[a]])
            t0 = j * QT
            r0_ = (t0 // NTOK) * 256 + h * 128
            sub_ = (t0 % NTOK) // (NTOK // 8)
            P.dma("sp", xo[r0_:r0_ + 128, (t0 % NTOK):(t0 % NTOK) + QT], osb[a][:], reads=[Bosb[a]], writes=[io["lo_bufs"][sub_]], store=True, chan=Bosb[a])
            if h == NH - 1 and t0 >= 3 * NTOK and ((t0 + QT) % (NTOK // 8)) == 0:
                io["sub_done"](sub_)

    LA = 2
    for n in range(len(items) + LA):
        if n < len(items):
            QK(n)
        if n - LA >= 0:
            PV(n - LA)
    return P


import math
import numpy as np
from concourse.bass import ds

NEG = -30000.0
TT = 512


def gdn_consts():
    k = np.arange(128)
    same = (k[:, None] // 64) == (k[None, :] // 64)
    c = {}
    c["gU"] = ((k[:, None] <= k[None, :]) & same).astype(np.float32)
    c["gSL"] = ((k[:, None] > k[None, :]) & same).astype(np.float32)
    c["gSame"] = same.astype(np.float32)
    h0 = np.zeros((128, 128), np.float32)
    h0[:64, :] = 1.0
    c["gH0"] = h0
    c["gH1"] = 1.0 - h0
    c["gI"] = np.eye(128, dtype=np.float32)
    c["gMS"] = np.where((k[:, None] > k[None, :]) & same, 0.0, NEG).astype(np.float32)
    c["gMT"] = np.where((k[None, :] >= k[:, None]) & same, 0.0, NEG).astype(np.float32)
    return c


CN = ("gU", "gSL", "gSame", "gH0", "gH1", "gI", "gMS", "gMT")


def build_gdnB(P, io, NH=2, S_=16384, dbg=False):
    NB = S_ // 128
    NT = S_ // TT
    NTOK = S_ // 4
    qkvq = [io["yb"][c_ * 768:(c_ + 1) * 768, :].rearrange("(h w p) s -> h w p s", h=NH, w=3) for c_ in range(4)]
    By = io["yb_bufs"]
    wcv = io["gdn_wcv"]
    ba = io["yf"]
    hp = io["gdn_hp"]
    onw = io["gdn_onw"]
    cdr = {n: io[n] for n in CN}
    xo = io["xo"]

    def ld(name, dram, w):
        t = P.sb(name, [128, w], F32)
        B = P.buf()
        P.dma("sp", t[:], dram, writes=[B])
        return t, B

    K = {}
    BK = P.buf()
    for n in CN:
        K[n] = P.sb("k" + n, [128, 128], F32)
        P.dma("sp", K[n][:], cdr[n][:, :], writes=[BK])
    wcs, Bw = ld("wcs", wcv[:, 0:NH * 12], NH * 12)
    hps, Bhp = ld("hps", hp[:, 0:NH * 2], NH * 2)
    bar = P.sb("bar", [128, NH * 2, 128], F32)
    Bbar = P.buf()
    P.dma("sp", bar[:], ba.rearrange("r (b s) -> b r s", s=128), writes=[Bbar])
    bas = P.sb("bas", [128, NH * 2 * NB], F32)
    Bba = P.buf()
    onws, Bon = ld("onws", onw[:, :], 128)
    ones_b = P.sb("ones_b", [128, 128], BF16)
    ident_b = P.sb("ident_b", [128, 128], BF16)
    epsc = P.sb("epsc", [128, 1], F32)
    one_c = P.sb("one_c", [128, 1], F32)
    P.op("dve", lambda e: e.memset(ones_b[:], 1.0), writes=[BK])
    P.op("dve", lambda e: e.memset(epsc[:], 1e-6), writes=[BK])
    P.op("dve", lambda e: e.memset(one_c[:], 1.0), writes=[BK])
    P.op("dve", lambda e: e.tensor_copy(out=ident_b[:], in_=K["gI"][:]), reads=[BK], writes=[BK])

    pf = [P.ps("pf%d" % i, [128, 512]) for i in range(6)]
    Bpf = [P.buf() for _ in range(6)]
    for b_ in Bpf:
        b_.excl = True
    pfslots = [(pf[i % 6][:, ((i // 6) % 4) * 128:((i // 6) % 4 + 1) * 128], Bpf[i % 6]) for i in range(24)]
    pbt = P.ps("pbt", [128, 1024], BF16)
    Bpbt = P.buf()
    Bpbt.excl = True
    pbslots = [(pbt[:, i * 128:(i + 1) * 128], Bpbt) for i in range(8)]
    pbig = P.ps("pbig", [128, 512])
    Bpbig = P.buf()
    Bpbig.excl = True
    for r_ in range(NH * 2):
        P.op("pe", lambda e, r_=r_: e.transpose(pbig[:, (r_ % 4) * 128:(r_ % 4) * 128 + NB], bar[:, r_, :], K["gI"][:]), reads=[Bbar, BK], writes=[Bpbig])
        P.op("dve", lambda e, r_=r_: e.tensor_copy(out=bas[:, r_ * NB:(r_ + 1) * NB], in_=pbig[:, (r_ % 4) * 128:(r_ % 4) * 128 + NB]),
             reads=[Bpbig], writes=[Bba])
    cnt = {"f": 0, "b": 0, 0: 0, 1: 0}
    cur_head = [0]

    def psf():
        h_ = cur_head[0]
        cnt[h_] += 1
        k_ = cnt[h_] % 12
        bank = h_ * 3 + (k_ % 3)
        return pf[bank][:, (k_ // 3) * 128:(k_ // 3 + 1) * 128], Bpf[bank]

    def psb():
        cnt["b"] += 1
        return pbslots[cnt["b"] % 8]

    class Head:
        pass

    heads = []
    for hh in range(NH):
        H = Head()
        H.hh = hh
        n = "h%d_" % hh
        mk = lambda nm, w, dt=F32: P.sb(n + nm, [128, w], dt)
        H.tab = {t: mk("t" + t, NB) for t in ("beta", "nbeta", "g", "gc", "egc", "ekd", "bgc", "eg0", "eg1", "tmp")}
        H.Btab = P.buf()
        H.raw = [P.sb(n + "raw%d" % i_, [128, 3, TT + 3], F32) for i_ in range(2)]
        H.Braw = [P.buf(), P.buf()]
        H.cv = [mk("cv%d" % w, TT) for w in range(3)]
        H.Bcv = [P.buf() for _ in range(3)]
        H.sq = mk("sq", TT, BF16)
        H.Bsq = P.buf()
        H.rs = mk("rs", TT)
        H.Brs = P.buf()
        H.qn = mk("qn", TT, BF16)
        H.kn = mk("kn", TT, BF16)
        H.Bqn, H.Bkn = P.buf(), P.buf()
        H.S32 = mk("S32", 128)
        H.Sb = [mk("Sb%d" % i, 128, BF16) for i in range(2)]
        H.BS32 = P.buf()
        H.BSb = [P.buf(), P.buf()]
        H.scur = 0
        H.osb = mk("osb", TT)
        H.Bosb = P.buf()
        H.t = {}
        H.B = {}
        for nm, dt in (("kbg", BF16), ("kdec", BF16), ("vb32", F32), ("vb16", BF16), ("gU", F32), ("Ds", F32), ("DT", F32),
                       ("X", BF16), ("XT", BF16), ("X2", BF16), ("XT2", BF16), ("N", BF16), ("N2", BF16),
                       ("AqkT", BF16), ("u32", F32), ("wT", BF16), ("vnew", BF16), ("aq", F32), ("o32", F32),
                       ("on32", F32), ("junk", F32)):
            H.t[nm] = mk("b_" + nm, 128, dt)
            H.B[nm] = P.buf()
        H.ss = mk("ss", 1)
        H.rstd = mk("rstd", 1)
        H.Bss = P.buf()
        heads.append(H)

    def head_tables(H):
        hh = H.hh
        T = H.tab
        bl = bas[:, (hh * 2) * NB:(hh * 2 + 1) * NB]
        al = bas[:, (hh * 2 + 1) * NB:(hh * 2 + 2) * NB]
        rw = [Bba, Bhp, BK, H.Btab]
        P.op("act", lambda e: e.activation(out=T["beta"][:], in_=bl, func=AF.Sigmoid), reads=rw, writes=[H.Btab])
        P.op("dve", lambda e: e.tensor_scalar(out=T["nbeta"][:], in0=T["beta"][:], scalar1=-1.0, scalar2=None, op0=ALU.mult),
             reads=rw, writes=[H.Btab])
        P.op("act", lambda e: e.activation(out=T["tmp"][:], in_=al, func=AF.Exp, bias=hps[:, hh * 2 + 1:hh * 2 + 2], scale=1.0),
             reads=rw, writes=[H.Btab])
        P.op("act", lambda e: e.activation(out=T["tmp"][:], in_=T["tmp"][:], func=AF.Ln, bias=one_c[:, 0:1], scale=1.0),
             reads=rw, writes=[H.Btab])
        P.op("act", lambda e: e.activation(out=H.ss[:], in_=hps[:, hh * 2:hh * 2 + 1], func=AF.Exp), reads=rw, writes=[H.Bss])
        P.op("dve", lambda e: e.tensor_scalar(out=T["g"][:], in0=T["tmp"][:], scalar1=H.ss[:, 0:1], scalar2=-1.0,
                                              op0=ALU.mult, op1=ALU.mult), reads=rw + [H.Bss], writes=[H.Btab])
        P.op("pe", lambda e: e.matmul(pbig[:, 0:NB], lhsT=K["gU"][:], rhs=T["g"][:], start=True, stop=True), reads=rw, writes=[Bpbig])
        P.op("pe", lambda e: e.matmul(pbig[:, 128:128 + NB], lhsT=K["gSame"][:], rhs=T["g"][:], start=True, stop=True), reads=rw, writes=[Bpbig])
        P.op("pe", lambda e: e.matmul(pbig[:, 256:256 + NB], lhsT=K["gH0"][:], rhs=T["g"][:], start=True, stop=True), reads=rw, writes=[Bpbig])
        P.op("pe", lambda e: e.matmul(pbig[:, 384:384 + NB], lhsT=K["gH1"][:], rhs=T["g"][:], start=True, stop=True), reads=rw, writes=[Bpbig])
        P.op("dve", lambda e: e.tensor_copy(out=T["gc"][:], in_=pbig[:, 0:NB]), reads=[Bpbig], writes=[H.Btab])
        P.op("act", lambda e: e.activation(out=T["egc"][:], in_=pbig[:, 0:NB], func=AF.Exp), reads=[Bpbig], writes=[H.Btab])
        P.op("dve", lambda e: e.tensor_tensor(out=T["tmp"][:], in0=pbig[:, 128:128 + NB], in1=T["gc"][:], op=ALU.subtract),
             reads=[Bpbig, H.Btab], writes=[H.Btab])
        P.op("act", lambda e: e.activation(out=T["ekd"][:], in_=T["tmp"][:], func=AF.Exp), reads=[H.Btab], writes=[H.Btab])
        P.op("act", lambda e: e.activation(out=T["eg0"][:], in_=pbig[:, 256:256 + NB], func=AF.Exp), reads=[Bpbig], writes=[H.Btab])
        P.op("act", lambda e: e.activation(out=T["eg1"][:], in_=pbig[:, 384:384 + NB], func=AF.Exp), reads=[Bpbig], writes=[H.Btab])
        P.op("dve", lambda e: e.tensor_tensor(out=T["bgc"][:], in0=T["beta"][:], in1=T["egc"][:], op=ALU.mult),
             reads=[H.Btab], writes=[H.Btab])
        P.op("dve", lambda e: e.memset(H.S32[:], 0.0), writes=[H.BS32])
        P.op("dve", lambda e: e.memset(H.Sb[0][:], 0.0), writes=[H.BSb[0]])
        P.op("dve", lambda e: e.memset(H.raw[0][:, :, 0:3], 0.0), writes=[H.Braw[0]])

    def load_raw(H, ti):
        hh = H.hh
        s_ = ti % 2
        c_ = (ti * TT) // NTOK
        lo = ti * TT - c_ * NTOK
        P.dma("sp", H.raw[s_][:, :, 3:TT + 3], qkvq[c_][hh, :, :, lo:lo + TT].rearrange("w p t -> p w t"), reads=[By[c_]], writes=[H.Braw[s_]])

    def carry(H, ti):
        s_ = ti % 2
        if ti > 0:
            P.op("dve", lambda e: e.tensor_copy(out=H.raw[s_][:, :, 0:3], in_=H.raw[1 - s_][:, :, TT:TT + 3]), reads=[H.Braw[1 - s_]], writes=[H.Braw[s_]])

    def conv_phase(H, ti):
        hh = H.hh
        s_ = ti % 2
        raw, Braw = H.raw[s_], H.Braw[s_]
        for w in range(3):
            cv, Bc = H.cv[w], H.Bcv[w]
            wcl = [wcs[:, hh * 12 + w * 4 + k:hh * 12 + w * 4 + k + 1] for k in range(4)]
            P.op("dve", lambda e, cv=cv, w=w, wcl=wcl: e.tensor_scalar(out=cv[:], in0=raw[:, w, 0:TT], scalar1=wcl[0], scalar2=None, op0=ALU.mult),
                 reads=[Braw, Bw], writes=[Bc])
            for k in (1, 2, 3):
                P.op("dve", lambda e, cv=cv, k=k, w=w, wcl=wcl: e.scalar_tensor_tensor(out=cv[:], in0=raw[:, w, k:k + TT], scalar=wcl[k], in1=cv[:],
                                                                                    op0=ALU.mult, op1=ALU.add), reads=[Braw, Bw, Bc], writes=[Bc])
            P.op("act", lambda e, cv=cv: e.activation(out=cv[:], in_=cv[:], func=AF.Silu), reads=[Bc], writes=[Bc])
        for w, dst, Bd, sc in ((0, H.qn, H.Bqn, 1.0 / math.sqrt(128.0)), (1, H.kn, H.Bkn, 1.0)):
            cv, Bc = H.cv[w], H.Bcv[w]
            P.op("act", lambda e, cv=cv: e.activation(out=H.sq[:], in_=cv[:], func=AF.Square), reads=[Bc], writes=[H.Bsq])
            P.op("pe", lambda e: e.matmul(pbig[:], lhsT=ones_b[:], rhs=H.sq[:], start=True, stop=True), reads=[H.Bsq, BK], writes=[Bpbig])
            P.op("act", lambda e: e.activation(out=H.rs[:], in_=pbig[:], func=AF.Ln, bias=epsc[:, 0:1], scale=1.0), reads=[Bpbig, BK], writes=[H.Brs])
            P.op("act", lambda e: e.activation(out=H.rs[:], in_=H.rs[:], func=AF.Exp, scale=-0.5), reads=[H.Brs], writes=[H.Brs])
            P.op("dve", lambda e, cv=cv, dst=dst, sc=sc: e.scalar_tensor_tensor(out=dst[:], in0=cv[:], scalar=sc, in1=H.rs[:],
                                                                              op0=ALU.mult, op1=ALU.mult), reads=[Bc, H.Brs], writes=[Bd])

    def evac(eng, dst, Bdst, src, Bsrc, extra_reads=()):
        if eng == "act":
            P.op("act", lambda e: e.activation(out=dst, in_=src, func=AF.Copy), reads=[Bsrc] + list(extra_reads), writes=[Bdst])
        else:
            P.op("dve", lambda e: e.tensor_copy(out=dst, in_=src), reads=[Bsrc] + list(extra_reads), writes=[Bdst])

    def pre_scan(H, blk, bi):
        T, t, B = H.tab, H.t, H.B
        cur_head[0] = H.hh
        cs = slice(bi * 128, (bi + 1) * 128)
        col = lambda nm: T[nm][:, blk:blk + 1]
        kn, qn = H.kn[:, cs], H.qn[:, cs]
        pk, Bpk = psb()
        P.op("pe", lambda e: e.transpose(pk, kn, ident_b[:]), reads=[H.Bkn, BK], writes=[Bpk])
        yield
        P.op("act", lambda e: e.activation(out=t["kbg"][:], in_=pk, func=AF.Copy, scale=col("bgc")), reads=[Bpk, H.Btab], writes=[B["kbg"]])
        yield
        P.op("dve", lambda e: e.tensor_scalar(out=t["kdec"][:], in0=pk, scalar1=col("ekd"), scalar2=None, op0=ALU.mult),
             reads=[Bpk, H.Btab], writes=[B["kdec"]])
        yield
        pv, Bpv = psf()
        P.op("pe", lambda e: e.transpose(pv, H.cv[2][:, cs], K["gI"][:]), reads=[H.Bcv[2], BK], writes=[Bpv])
        yield
        P.op("dve", lambda e: e.tensor_scalar(out=t["vb32"][:], in0=pv, scalar1=col("beta"), scalar2=None, op0=ALU.mult),
             reads=[Bpv, H.Btab], writes=[B["vb32"]])
        yield
        P.op("act", lambda e: e.activation(out=t["vb16"][:], in_=pv, func=AF.Copy, scale=col("beta")), reads=[Bpv, H.Btab], writes=[B["vb16"]])
        yield
        P.op("act", lambda e: e.activation(out=t["gU"][:], in_=K["gU"][:], func=AF.Copy, scale=col("g")),
             reads=[BK, H.Btab], writes=[B["gU"]])
        yield
        pd, Bpd = psf()
        P.op("pe", lambda e: e.matmul(pd, lhsT=t["gU"][:], rhs=K["gSL"][:], start=True, stop=False), reads=[B["gU"], BK], writes=[Bpd])
        P.op("pe", lambda e: e.matmul(pd, lhsT=K["gI"][:], rhs=K["gMS"][:], start=False, stop=True), reads=[BK], writes=[Bpd])
        yield
        P.op("act", lambda e: e.activation(out=t["Ds"][:], in_=pd, func=AF.Exp), reads=[Bpd], writes=[B["Ds"]])
        yield
        pdt, Bpdt = psf()
        P.op("pe", lambda e: e.matmul(pdt, lhsT=K["gSL"][:], rhs=t["gU"][:], start=True, stop=False), reads=[B["gU"], BK], writes=[Bpdt])
        P.op("pe", lambda e: e.matmul(pdt, lhsT=K["gI"][:], rhs=K["gMT"][:], start=False, stop=True), reads=[BK], writes=[Bpdt])
        yield
        P.op("act", lambda e: e.activation(out=t["DT"][:], in_=pdt, func=AF.Exp), reads=[Bpdt], writes=[B["DT"]])
        yield
        pg, Bpg = psf()
        P.op("pe", lambda e: e.matmul(pg, lhsT=kn, rhs=kn, start=True, stop=True), reads=[H.Bkn], writes=[Bpg])
        yield
        P.op("dve", lambda e: e.scalar_tensor_tensor(out=t["X"][:], in0=pg, scalar=col("nbeta"), in1=t["Ds"][:], op0=ALU.mult, op1=ALU.mult),
             reads=[Bpg, H.Btab, B["Ds"]], writes=[B["X"]])
        yield
        pq, Bpq = psf()
        P.op("pe", lambda e: e.matmul(pq, lhsT=kn, rhs=qn, start=True, stop=True), reads=[H.Bkn, H.Bqn], writes=[Bpq])
        yield
        P.op("dve", lambda e: e.tensor_tensor(out=t["AqkT"][:], in0=pq, in1=t["DT"][:], op=ALU.mult), reads=[Bpq, B["DT"]], writes=[B["AqkT"]])
        yield
        px, Bpx = psb()
        P.op("pe", lambda e: e.transpose(px, t["X"][:], ident_b[:]), reads=[B["X"], BK], writes=[Bpx])
        yield
        evac("act", t["XT"][:], B["XT"], px, Bpx)
        yield
        evac("dve", t["N"][:], B["N"], px, Bpx)
        yield
        X, XT, X2, XT2, N, N2 = "X", "XT", "X2", "XT2", "N", "N2"
        for lvl in range(5):
            p1, Bp1 = psf()
            P.op("pe", lambda e, p1=p1, X=X, XT=XT: e.matmul(p1, lhsT=t[XT][:], rhs=t[X][:], start=True, stop=True),
                 reads=[B[X], B[XT]], writes=[Bp1])
            yield
            p2, Bp2 = psf()
            P.op("pe", lambda e, p2=p2, X=X, XT=XT: e.matmul(p2, lhsT=t[X][:], rhs=t[XT][:], start=True, stop=True),
                 reads=[B[X], B[XT]], writes=[Bp2])
            yield
            evac("act", t[X2][:], B[X2], p1, Bp1)
            yield
            evac("dve", t[XT2][:], B[XT2], p2, Bp2)
            yield
            p3, Bp3 = psf()
            P.op("pe", lambda e, p3=p3, X2=X2, N=N: e.matmul(p3, lhsT=t[X2][:], rhs=t[N][:], start=True, stop=False),
                 reads=[B[X2], B[N]], writes=[Bp3])
            P.op("pe", lambda e, p3=p3, N=N: e.matmul(p3, lhsT=ident_b[:], rhs=t[N][:], start=False, stop=False),
                 reads=[BK, B[N]], writes=[Bp3])
            P.op("pe", lambda e, p3=p3, XT2=XT2: e.matmul(p3, lhsT=ident_b[:], rhs=t[XT2][:], start=False, stop=True),
                 reads=[BK, B[XT2]], writes=[Bp3])
            yield
            evac("act" if lvl % 2 else "dve", t[N2][:], B[N2], p3, Bp3)
            yield
            X, X2 = X2, X
            XT, XT2 = XT2, XT
            N, N2 = N2, N
        pu, Bpu = psf()
        P.op("pe", lambda e, N=N: e.matmul(pu, lhsT=t[N][:], rhs=t["vb16"][:], start=True, stop=True), reads=[B[N], B["vb16"]], writes=[Bpu])
        yield
        P.op("dve", lambda e: e.tensor_tensor(out=t["u32"][:], in0=pu, in1=t["vb32"][:], op=ALU.add), reads=[Bpu, B["vb32"]], writes=[B["u32"]])
        yield
        pw, Bpw = psf()
        P.op("pe", lambda e, N=N: e.matmul(pw, lhsT=t["kbg"][:], rhs=t[N][:], start=True, stop=False), reads=[B["kbg"], B[N]], writes=[Bpw])
        P.op("pe", lambda e: e.matmul(pw, lhsT=t["kbg"][:], rhs=ident_b[:], start=False, stop=True), reads=[B["kbg"], BK], writes=[Bpw])
        yield
        evac("act", t["wT"][:], B["wT"], pw, Bpw)
        yield

    def scan_steps(H, blk, bi):
        T, t, B = H.tab, H.t, H.B
        cs0 = bi * 128
        steps = []
        pws, Bpws = psf()
        pqs, Bpqs = psf()
        for c in (0, 1):
            r = slice(c * 64, (c + 1) * 64)
            egl = T["eg%d" % c][:, blk:blk + 1]

            def s1(c=c, r=r):
                cur = H.scur
                P.op("pe", lambda e: e.matmul(pws[r, :], lhsT=t["wT"][:, r], rhs=H.Sb[cur][:], start=True, stop=True),
                     reads=[B["wT"], H.BSb[cur]], writes=[Bpws])
                P.op("pe", lambda e: e.matmul(pqs[r, :], lhsT=H.qn[:, cs0 + c * 64:cs0 + (c + 1) * 64], rhs=H.Sb[cur][:], start=True, stop=True),
                     reads=[H.Bqn, H.BSb[cur]], writes=[Bpqs])

            def s2(c=c, r=r):
                P.op("dve", lambda e: e.tensor_tensor(out=t["vnew"][r, :], in0=t["u32"][r, :], in1=pws[r, :], op=ALU.subtract),
                     reads=[B["u32"], Bpws], writes=[B["vnew"]])

            def s3(c=c, r=r, egl=egl):
                cur = H.scur
                nxt = 1 - cur
                pds, Bpds = psf()
                P.op("pe", lambda e: e.matmul(pds, lhsT=t["kdec"][r, :], rhs=t["vnew"][r, :], start=True, stop=True),
                     reads=[B["kdec"], B["vnew"]], writes=[Bpds])
                P.op("dve", lambda e: e.scalar_tensor_tensor(out=H.Sb[nxt][:], in0=H.S32[:], scalar=egl, in1=pds, op0=ALU.mult, op1=ALU.add),
                     reads=[H.BS32, H.Btab, Bpds], writes=[H.BSb[nxt]])
                P.op("dve", lambda e: e.scalar_tensor_tensor(out=H.S32[:], in0=H.S32[:], scalar=egl, in1=pds, op0=ALU.mult, op1=ALU.add),
                     reads=[H.BS32, H.Btab, Bpds], writes=[H.BS32])
                H.scur = nxt

            steps += [s1, s2, s3]

        def fin():
            pa, Bpa = psf()
            P.op("pe", lambda e: e.matmul(pa, lhsT=t["AqkT"][:], rhs=t["vnew"][:], start=True, stop=True), reads=[B["AqkT"], B["vnew"]], writes=[Bpa])
            evac("act", t["aq"][:], B["aq"], pa, Bpa)
            P.op("dve", lambda e: e.scalar_tensor_tensor(out=t["o32"][:], in0=pqs, scalar=T["egc"][:, blk:blk + 1], in1=t["aq"][:],
                                                         op0=ALU.mult, op1=ALU.add), reads=[Bpqs, H.Btab, B["aq"]], writes=[B["o32"]])
            P.op("act", lambda e: e.activation(out=t["junk"][:], in_=t["o32"][:], func=AF.Square, accum_out=H.ss[:, 0:1]),
                 reads=[B["o32"]], writes=[B["junk"], H.Bss])
            P.op("act", lambda e: e.activation(out=H.rstd[:], in_=H.ss[:], func=AF.Sqrt, scale=1.0 / 128.0, bias=epsc[:, 0:1]),
                 reads=[H.Bss, BK], writes=[H.Bss])
            P.op("dve", lambda e: e.reciprocal(out=H.rstd[:], in_=H.rstd[:]), reads=[H.Bss], writes=[H.Bss])
            P.op("dve", lambda e: e.scalar_tensor_tensor(out=t["on32"][:], in0=t["o32"][:], scalar=H.rstd[:, 0:1], in1=onws[:],
                                                         op0=ALU.mult, op1=ALU.mult), reads=[B["o32"], H.Bss, Bon], writes=[B["on32"]])
            po, Bpo = psf()
            P.op("pe", lambda e: e.transpose(po, t["on32"][:], K["gI"][:]), reads=[B["on32"], BK], writes=[Bpo])
            evac("act", H.osb[:, bi * 128:(bi + 1) * 128], H.Bosb, po, Bpo)

        steps.append(fin)
        return steps

    dbgn = []
    if dbg:
        dbgT = nc.dram_tensor("dbg", [128, 40 * 128], F32, kind="ExternalOutput").ap()
        dstg = P.sb("dstg", [128, 128], F32)
        Bdstg = P.buf()

        def dump(name, ap, Bs, w=128):
            i = len(dbgn)
            dbgn.append(name)
            P.op("dve", lambda e: e.tensor_copy(out=dstg[:, 0:w], in_=ap), reads=Bs, writes=[Bdstg])
            P.dma("sp", dbgT[:, i * 128:i * 128 + w], dstg[:, 0:w], reads=[Bdstg], store=True, final=True)
    P.dbgn = dbgn
    for H in heads:
        head_tables(H)
    for H in heads:
        load_raw(H, 0)
    for ti in range(NT):
        P.fill_step(1, q="sp")
        P.tag("ph5_gdnB_q%d" % (ti // 8))
        for H in heads:
            carry(H, ti)
        if ti + 1 < NT:
            for H in heads:
                load_raw(H, ti + 1)
        for H in heads:
            conv_phase(H, ti)
        for bi in range(TT // 128):
            blk = ti * 4 + bi
            gens = [pre_scan(H, blk, bi) for H in heads]
            gen_head = {id(gn): H.hh for gn, H in zip(gens, heads)}
            alive = list(gens)
            while alive:
                for gi_, gen in enumerate(list(alive)):
                    try:
                        cur_head[0] = gen_head[id(gen)]
                        next(gen)
                    except StopIteration:
                        alive.remove(gen)
            lists = []
            for H in heads:
                cur_head[0] = H.hh
                lists.append(scan_steps(H, blk, bi))
            for k in range(len(lists[0])):
                for hi_, L in enumerate(lists):
                    cur_head[0] = heads[hi_].hh
                    L[k]()
            if dbg and blk == 0:
                H = heads[0]
                for nm in ("g", "gc", "beta", "egc", "ekd", "eg0", "eg1", "bgc"):
                    dump("t_" + nm, H.tab[nm][:, 0:NB], [H.Btab], w=NB)
                dump("kn", H.kn[:, 0:128], [H.Bkn])
                dump("qn", H.qn[:, 0:128], [H.Bqn])
                dump("v", H.cv[2][:, 0:128], [H.Bcv[2]])
                for nm in H.t:
                    dump(nm, H.t[nm][:], [H.B[nm]])
                dump("S32", H.S32[:], [H.BS32])
        for H in heads:
            t0 = ti * TT
            r0_ = (t0 // NTOK) * 256 + H.hh * 128
            sub_ = (t0 % NTOK) // (NTOK // 8)
            P.dma("sp", xo[r0_:r0_ + 128, (t0 % NTOK):(t0 % NTOK) + TT], H.osb[:], reads=[H.Bosb], writes=[io["lo_bufs"][sub_]], store=True, chan=H.Bosb)
        if t0 >= 3 * NTOK and ((t0 + TT) % (NTOK // 8)) == 0:
            io["sub_done"]((t0 % NTOK) // (NTOK // 8))
    return P


import math
import numpy as np

HL = 128
NEG = -30000.0
TWO_PI = 2.0 * math.pi
C1 = 6.28125
C2 = TWO_PI - C1
MAGIC = 12582912.0
QA, QR, KA, KR, VV, ZZ, NCOL = 0, 1024, 2048, 2560, 3072, 3328, 4352


def swa_consts():
    k = np.arange(128)
    mprev = np.where(k[:, None] > k[None, :], 0.0, NEG).astype(np.float32)
    mcur = np.where(k[:, None] <= k[None, :], 0.0, NEG).astype(np.float32)
    mask = np.concatenate([mprev, mprev, mcur, mcur], axis=1)
    invf = (np.float32(10000.0) ** (-(np.arange(32, dtype=np.float32)) / np.float32(32))).astype(np.float32)
    invf = np.tile(invf, 4).reshape(128, 1)
    onesP = np.zeros((128, 2, 128), np.float32)
    onesP[:, 0, 0:64] = 1.0
    onesP[:, 1, 64:128] = 1.0
    return {"sMask": mask, "sInvf": invf, "sI": np.eye(128, dtype=np.float32), "sOnesP": onesP.reshape(128, 256)}


def build_swa(P, io, NTOK=4096):
    NT = NTOK // TT
    NC_ = HL + NTOK
    fmv = lambda a: a.rearrange("(c p) t -> p c t", p=128)
    x3T = fmv(io["x3T"])
    xH = fmv(io["yh"])
    posd, nw, fnw, w_in, w_out = io["swa_pos"], io["swa_nw"], io["swa_fnw"], io["swa_w_in"], io["swa_w_out"]
    skd, hmd, cM, cF, cI, cO = io["swa_sk"], io["swa_hmask"], io["sMask"], io["sInvf"], io["sI"], io["sOnesP"]
    outT = fmv(io["outT"])

    BK = P.buf()
    ones = P.sb("ones", [128, 128], BF16)
    epsc = P.sb("epsc", [128, 1], F32)
    P.op("dve", lambda e: e.memset(ones[:], 1.0), writes=[BK])
    P.op("dve", lambda e: e.memset(epsc[:], 1e-6), writes=[BK])
    norm = Norm(P, ones, BK)
    norm.epsc = epsc
    stg = [P.sb("stg%d" % i, [128, 1024], F32) for i in range(2)]
    Bstg = [P.buf(), P.buf()]
    nws, Bnw = load_small(P, nw[:, :], [128, KC], "nws")
    fnws, Bfnw = load_small(P, fnw[:, :], [128, KC], "fnws")
    sks, Bsk = load_small(P, skd[:, :], [128, 8], "sks")
    hms, Bhm = load_small(P, hmd[:, :], [128, 1], "hms")
    invf, Binvf = load_small(P, cF[:, :], [128, 1], "invf")
    esk = P.sb("esk", [128, 8], F32)
    P.op("act", lambda e: e.activation(out=esk[:], in_=sks[:], func=AF.Exp), reads=[Bsk], writes=[Bsk])
    mask_b = P.sb("mask_b", [128, 512], BF16)
    ident_b = P.sb("ident_b", [128, 128], BF16)
    onesP = P.sb("onesP", [128, 2, 128], BF16)
    P.dma("sp", stg[0][:, 0:512], cM[:, :], writes=[Bstg[0]])
    P.op("dve", lambda e: e.tensor_copy(out=mask_b[:], in_=stg[0][:, 0:512]), reads=[Bstg[0]], writes=[BK])
    P.dma("sp", stg[1][:, 0:128], cI[:, :], writes=[Bstg[1]])
    P.op("dve", lambda e: e.tensor_copy(out=ident_b[:], in_=stg[1][:, 0:128]), reads=[Bstg[1]], writes=[BK])
    P.dma("sp", stg[0][:, 0:256], cO[:, :], writes=[Bstg[0]])
    P.op("dve", lambda e: e.tensor_copy(out=onesP[:].rearrange("p a b -> p (a b)"), in_=stg[0][:, 0:256]), reads=[Bstg[0]], writes=[BK])

    wib = P.sb("wib", [128, KC, NCOL], BF16)
    Bwib = P.buf()
    wob = P.sb("wob", [128, KC, D], BF16)
    Bwob = P.buf()
    nld = [0]

    def stage_load(src, ncols):
        s = nld[0] % 2
        nld[0] += 1
        P.dma("sp", stg[s][:, 0:ncols], src, writes=[Bstg[s]])
        return stg[s], Bstg[s]

    def cast(dst, src, Bs, kc, sign=1.0, eng=None):
        eng = eng or ("dve", "pool")[nld[0] % 2]
        P.op(eng, lambda e: e.tensor_scalar(out=dst, in0=src, scalar1=nws[:, kc:kc + 1], scalar2=sign, op0=ALU.mult, op1=ALU.mult),
             reads=[Bs, Bnw], writes=[Bwib])

    for kc in range(KC):
        rows = slice(kc * 128, (kc + 1) * 128)
        s_, Bs = stage_load(w_in[rows, 0:1024], 1024)
        cast(wib[:, kc, QA:QA + 1024], s_[:, 0:1024], Bs, kc)
        sv = s_[:, 0:1024].rearrange("p (h t d) -> p h t d", t=2, d=32)
        dv = wib[:, kc, QR:QR + 1024].rearrange("p (h t d) -> p h t d", t=2, d=32)
        cast(dv[:, :, 0, :], sv[:, :, 1, :], Bs, kc, sign=-1.0, eng="dve")
        cast(dv[:, :, 1, :], sv[:, :, 0, :], Bs, kc, eng="pool")
        s_, Bs = stage_load(w_in[rows, 1024:1536], 512)
        ksrc = s_[:, 0:256].rearrange("p (g d) -> p g d", d=64)
        ksr2 = s_[:, 0:256].rearrange("p (g t d) -> p g t d", t=2, d=32)
        kad = wib[:, kc, KA:KA + 512].rearrange("p (g u d) -> p g u d", u=2, d=64)
        krd = wib[:, kc, KR:KR + 512].rearrange("p (g u t d) -> p g u t d", u=2, t=2, d=32)
        for u in range(2):
            cast(kad[:, :, u, :], ksrc, Bs, kc, eng=("dve", "pool")[u])
            cast(krd[:, :, u, 0, :], ksr2[:, :, 1, :], Bs, kc, sign=-1.0, eng="dve")
            cast(krd[:, :, u, 1, :], ksr2[:, :, 0, :], Bs, kc, eng="pool")
        cast(wib[:, kc, VV:VV + 256], s_[:, 256:512], Bs, kc, eng="dve")
        s_, Bs = stage_load(w_in[rows, 1536:2560], 1024)
        cast(wib[:, kc, ZZ:ZZ + 1024], s_[:, 0:1024], Bs, kc)
    for kc in range(KC):
        s_, Bs = stage_load(w_out[kc * 128:(kc + 1) * 128, :], 1024)
        P.op(("dve", "pool")[kc % 2], lambda e, kc=kc, s_=s_: e.tensor_copy(out=wob[:, kc, :], in_=s_[:, 0:1024]), reads=[Bs], writes=[Bwob])

    xs = P.sb("xs", [128, KC, TT], F32)
    Bxs = P.buf()
    hT = P.sb("hT", [128, KC, TT], BF16)
    BhT = P.buf()
    rstd = P.sb("rstd", [128, TT], F32)
    Brstd = P.buf()
    posi = P.sb("posi", [128, TT], I32)
    ang = P.sb("ang", [128, TT], F32)
    kf = P.sb("kf", [128, TT], F32)
    cosT = P.sb("cosT", [128, TT], F32)
    sinT = P.sb("sinT", [128, TT], F32)
    Brope = P.buf()
    Bcs = P.buf()
    QP = P.sb("QP", [128, 8, TT], BF16)
    BQP = P.buf()
    KP = P.sb("KP", [128, 4, HL + TT], BF16)
    BKP = P.buf()
    Vp = P.sb("Vp", [128, 5, 4, 2, 128], BF16)
    BVp = P.buf()
    P.op("pool", lambda e: e.memset(Vp[:].rearrange("p a b c d -> p (a b c d)"), 0.0), writes=[BVp])
    szs = P.sb("szs", [128, 8, TT], F32)
    Bszs = P.buf()
    og = P.sb("og", [128, 8, TT], BF16)
    Bog = P.buf()
    t1r = Rot(P, "t1r", [128, TT], F32, 2)
    ptr = Rot(P, "ptr", [128, TT], BF16, 4)
    smr = Rot(P, "smr", [128, 128], F32, 4)
    f32r = Rot(P, "f32r", [128, TT], F32, 2)
    pp = [P.ps("pp%d" % i, [128, TT]) for i in range(7)]
    Bpp = [P.buf() for _ in range(7)]
    ppi = [0]

    def psum():
        k = ppi[0] % 7
        ppi[0] += 1
        return pp[k], Bpp[k]

    def fm_proj(col0, n):
        ps, Bps = psum()
        for kc in range(KC):
            P.op("pe", lambda e, kc=kc: e.matmul(ps[:, 0:n], lhsT=wib[:, kc, col0:col0 + 128], rhs=hT[:, kc, 0:n],
                                                  start=(kc == 0), stop=(kc == KC - 1)), reads=[Bwib, BhT], writes=[Bps])
        return ps, Bps

    def rope_tables(c0, n):
        P.dma("sp", posi[:, 0:n], posd[:, c0:c0 + n], writes=[Brope])
        P.op("dve", lambda e: e.tensor_copy(out=ang[:, 0:n], in_=posi[:, 0:n]), reads=[Brope], writes=[Brope])
        P.op("dve", lambda e: e.tensor_scalar(out=ang[:, 0:n], in0=ang[:, 0:n], scalar1=invf[:, 0:1], scalar2=None, op0=ALU.mult),
             reads=[Brope, Binvf], writes=[Brope])
        P.op("dve", lambda e: e.tensor_scalar(out=kf[:, 0:n], in0=ang[:, 0:n], scalar1=1.0 / TWO_PI, scalar2=MAGIC, op0=ALU.mult, op1=ALU.add),
             reads=[Brope], writes=[Brope])
        P.op("dve", lambda e: e.tensor_scalar(out=kf[:, 0:n], in0=kf[:, 0:n], scalar1=MAGIC, scalar2=None, op0=ALU.subtract),
             reads=[Brope], writes=[Brope])
        P.op("dve", lambda e: e.scalar_tensor_tensor(out=ang[:, 0:n], in0=kf[:, 0:n], scalar=-C1, in1=ang[:, 0:n], op0=ALU.mult, op1=ALU.add),
             reads=[Brope], writes=[Brope])
        P.op("dve", lambda e: e.scalar_tensor_tensor(out=ang[:, 0:n], in0=kf[:, 0:n], scalar=-C2, in1=ang[:, 0:n], op0=ALU.mult, op1=ALU.add),
             reads=[Brope], writes=[Brope])
        P.op("dve", lambda e: e.tensor_scalar(out=ang[:, 0:n], in0=ang[:, 0:n], scalar1=3.14159, scalar2=-3.14159, op0=ALU.min, op1=ALU.max),
             reads=[Brope], writes=[Brope])
        P.op("act", lambda e: e.activation(out=sinT[:, 0:n], in_=ang[:, 0:n], func=AF.Sin), reads=[Brope], writes=[Bcs])
        P.op("act", lambda e: e.activation(out=kf[:, 0:n], in_=ang[:, 0:n], func=AF.Sin, scale=0.5), reads=[Brope], writes=[Brope])
        P.op("dve", lambda e: e.tensor_tensor(out=kf[:, 0:n], in0=kf[:, 0:n], in1=kf[:, 0:n], op=ALU.mult), reads=[Brope], writes=[Brope])
        P.op("dve", lambda e: e.tensor_scalar(out=cosT[:, 0:n], in0=kf[:, 0:n], scalar1=-2.0, scalar2=1.0, op0=ALU.mult, op1=ALU.add),
             reads=[Brope], writes=[Bcs])

    def roped(colA, colR, n, dst, Bdst):
        psA, BA = fm_proj(colA, n)
        psR, BR = fm_proj(colR, n)
        t1, Bt1 = t1r.next()
        P.op("dve", lambda e: e.tensor_tensor(out=t1[:, 0:n], in0=psA[:, 0:n], in1=cosT[:, 0:n], op=ALU.mult), reads=[BA, Bcs], writes=[Bt1])
        t2, Bt2 = t1r.next()
        P.op("dve", lambda e: e.tensor_tensor(out=t2[:, 0:n], in0=psR[:, 0:n], in1=sinT[:, 0:n], op=ALU.mult), reads=[BR, Bcs], writes=[Bt2])
        P.op("pool", lambda e: e.tensor_tensor(out=dst, in0=t1[:, 0:n], in1=t2[:, 0:n], op=ALU.add), reads=[Bt1, Bt2], writes=[Bdst])

    def kv_part(c0, n, koff, vslot0):
        for g in range(4):
            roped(KA + g * 128, KR + g * 128, n, KP[:, g, koff:koff + n], BKP)
        for blk in range(n // 128):
            ps, Bps = psum()
            for kc in range(KC):
                P.op("pe", lambda e, kc=kc, blk=blk, ps=ps: e.matmul(ps[:, 0:256], lhsT=hT[:, kc, blk * 128:(blk + 1) * 128], rhs=wib[:, kc, VV:VV + 256],
                                                              start=(kc == 0), stop=(kc == KC - 1)), reads=[Bwib, BhT], writes=[Bps])
            src = ps[:, 0:256].rearrange("p (g d) -> p g d", d=64)
            P.op("dve", lambda e, blk=blk, src=src: e.tensor_copy(out=Vp[:, vslot0 + blk, :, 0, 0:64], in_=src), reads=[Bps], writes=[BVp])
            P.op("dve", lambda e, blk=blk, src=src: e.tensor_copy(out=Vp[:, vslot0 + blk, :, 1, 64:128], in_=src), reads=[Bps], writes=[BVp])

    def load_and_norm(c0, n):
        if c0 == 0:
            P.dma("sp", xs[:, :, 0:n], xH[:, :, 0:n], writes=[Bxs])
        else:
            P.dma("sp", xs[:, :, 0:n], x3T[:, :, c0 - HL:c0 - HL + n], writes=[Bxs])
        norm.run(xs, Bxs, n, rstd, Brstd)
        for kc in range(KC):
            eng = "dve" if kc % 2 == 0 else "pool"
            P.op(eng, lambda e, kc=kc: e.tensor_tensor(out=hT[:, kc, 0:n], in0=xs[:, kc, 0:n], in1=rstd[:, 0:n], op=ALU.mult),
                 reads=[Bxs, Brstd], writes=[BhT])

    scale = 1.0 / math.sqrt(64.0)

    load_and_norm(0, HL)
    rope_tables(0, HL)
    kv_part(0, HL, 0, 0)

    for i in range(NT):
        c0 = HL + i * TT
        load_and_norm(c0, TT)
        rope_tables(c0, TT)
        kv_part(c0, TT, HL, 1)
        for p in range(8):
            roped(QA + p * 128, QR + p * 128, TT, QP[:, p, :], BQP)
        for c in range(8):
            ps, Bps = fm_proj(ZZ + c * 128, TT)
            P.op("act", lambda e, c=c, ps=ps: e.activation(out=szs[:, c, :], in_=ps[:], func=AF.Silu), reads=[Bps], writes=[Bszs])
        def attn(i, n, g):
            qc = slice(n * 128, (n + 1) * 128)
            kprev = slice(n * 128, (n + 1) * 128)
            kcur = slice((n + 1) * 128, (n + 2) * 128)
            pts = []
            for half in range(2):
                pr = slice(half * 64, half * 64 + 64)
                pst, Bpst = psum()
                rhs = QP[pr, 2 * g:2 * g + 2, qc]
                P.op("pe", lambda e, pst=pst, pr=pr, rhs=rhs: e.matmul(pst[:, 0:256], lhsT=KP[pr, g, kprev], rhs=rhs, start=True, stop=False),
                     reads=[BKP, BQP], writes=[Bpst])
                P.op("pe", lambda e, pst=pst, pr=pr, rhs=rhs: e.matmul(pst[:, 256:512], lhsT=KP[pr, g, kcur], rhs=rhs, start=False, stop=False),
                     reads=[BKP, BQP], writes=[Bpst])
                P.op("pe", lambda e, pst=pst: e.matmul(pst[:, :], lhsT=ident_b[:], rhs=mask_b[:], start=False, stop=True),
                     reads=[BK], writes=[Bpst])
                pt, Bpt = ptr.next()
                if i == 0 and n == 0:
                    P.op("act", lambda e, pt=pt, pst=pst: e.activation(out=pt[:, 0:256], in_=pst[:, 0:256], func=AF.Exp, scale=scale, bias=hms[:, 0:1]),
                         reads=[Bpst, Bhm], writes=[Bpt])
                    P.op("act", lambda e, pt=pt, pst=pst: e.activation(out=pt[:, 256:512], in_=pst[:, 256:512], func=AF.Exp, scale=scale),
                         reads=[Bpst], writes=[Bpt])
                else:
                    P.op("act", lambda e, pt=pt, pst=pst: e.activation(out=pt[:], in_=pst[:], func=AF.Exp, scale=scale), reads=[Bpst], writes=[Bpt])
                pts.append((pt, Bpt))
            po, Bpo = psum()
            prs, Bprs = psum()
            seq = [(0, n, slice(0, 256)), (0, n + 1, slice(256, 512)), (1, n, slice(0, 256)), (1, n + 1, slice(256, 512))]
            for idx, (half, vs_, cols) in enumerate(seq):
                pt, Bpt = pts[half]
                P.op("pe", lambda e, half=half, vs_=vs_, cols=cols, pt=pt, idx=idx, po=po: e.matmul(
                    po[:, 0:256], lhsT=Vp[:, vs_, g, half, :], rhs=pt[:, cols], start=(idx == 0), stop=(idx == 3)),
                    reads=[BVp, Bpt], writes=[Bpo])
            for idx, (half, vs_, cols) in enumerate(seq):
                pt, Bpt = pts[half]
                P.op("pe", lambda e, half=half, cols=cols, pt=pt, idx=idx, prs=prs: e.matmul(
                    prs[:, 0:256], lhsT=onesP[:, half, :], rhs=pt[:, cols], start=(idx == 0), stop=(idx == 3)),
                    reads=[BK, Bpt], writes=[Bprs])
            for c in range(2):
                pair = 2 * g + c
                cc = slice(c * 128, (c + 1) * 128)
                sm, Bsm = smr.next()
                P.op("act", lambda e, sm=sm, prs=prs, cc=cc, pair=pair: e.activation(out=sm[:], in_=prs[:, cc], func=AF.Ln, bias=esk[:, pair:pair + 1], scale=1.0),
                     reads=[Bprs, Bsk], writes=[Bsm])
                P.op("act", lambda e, sm=sm: e.activation(out=sm[:], in_=sm[:], func=AF.Exp, scale=-1.0), reads=[Bsm], writes=[Bsm])
                sm2, Bsm2 = smr.next()
                P.op("dve", lambda e, sm=sm, sm2=sm2, po=po, cc=cc: e.tensor_tensor(out=sm2[:], in0=po[:, cc], in1=sm[:], op=ALU.mult),
                     reads=[Bpo, Bsm], writes=[Bsm2])
                P.op("pool", lambda e, sm2=sm2, pair=pair: e.tensor_tensor(out=og[:, pair, qc], in0=sm2[:], in1=szs[:, pair, qc], op=ALU.mult),
                     reads=[Bsm2, Bszs], writes=[Bog])

        for n in range(4):
            for g in range(4):
                attn(i, n, g)
        P.op("pool", lambda e: e.tensor_copy(out=KP[:, :, 0:HL], in_=KP[:, :, TT:TT + HL]), reads=[BKP], writes=[BKP])
        P.op("pool", lambda e: e.tensor_copy(out=Vp[:, 0].rearrange("p a b c -> p (a b c)"), in_=Vp[:, 4].rearrange("p a b c -> p (a b c)")),
             reads=[BVp], writes=[BVp])
        for dc in range(KC):
            ps, Bps = psum()
            for pr_ in range(8):
                P.op("pe", lambda e, pr_=pr_, dc=dc, ps=ps: e.matmul(ps[:], lhsT=wob[:, pr_, dc * 128:(dc + 1) * 128], rhs=og[:, pr_, :],
                                                                    start=(pr_ == 0), stop=(pr_ == 7)), reads=[Bwob, Bog], writes=[Bps])
            P.op("dve", lambda e, dc=dc, ps=ps: e.tensor_tensor(out=xs[:, dc, :], in0=ps[:], in1=xs[:, dc, :], op=ALU.add),
                 reads=[Bps, Bxs], writes=[Bxs])
        norm.run(xs, Bxs, TT, rstd, Brstd)
        for dc in range(KC):
            t, Bt = f32r.next()
            P.op("dve", lambda e, dc=dc, t=t: e.scalar_tensor_tensor(out=t[:], in0=xs[:, dc, :], scalar=fnws[:, dc:dc + 1], in1=rstd[:],
                                                                    op0=ALU.mult, op1=ALU.mult), reads=[Bxs, Bfnw, Brstd], writes=[Bt])
            P.dma("sp", outT[:, dc, i * TT:(i + 1) * TT], t[:], reads=[Bt], store=True, final=True)
    return P


from concourse.bass import ds

SEQ = 16384
TOKC = 4096
GROUPS = [[0, 1, 2, 3], [4, 5, 6, 7]]


def build_fused(nc, st):
    P = Prog(nc, st)
    P.need_rank = True
    io = {}

    def ext(name, shape, dt=F32):
        io[name] = nc.dram_tensor(name, list(shape), dt, kind="ExternalInput").ap()

    def itn(name, shape, dt=F32):
        io[name] = nc.dram_tensor(name, list(shape), dt).ap()

    ext("l0_xT", [D, TOKC + 2]); ext("l0_nw", [128, KC]); ext("l0_w_in", [D, 4096]); ext("l0_w_cv", [128, 24]); ext("l0_w_out", [D, D])
    ext("f_nw", [128, KC]); ext("f_w_in", [D, 4104]); ext("f_w_out", [D, D]); ext("fox_bf", [128, 2])
    for n, w in (("cU", 128), ("cSU", 128), ("cE127", 128), ("cI", 128), ("cMask", 2048)):
        ext(n, [128, w])
    ext("g_nw", [128, KC]); ext("g_w_in", [D, 4112]); ext("g_w_out", [D, D]); ext("gdn_wcv", [128, 24]); ext("gdn_hp", [128, 4]); ext("gdn_onw", [128, 128])
    for n in CN:
        ext(n, [128, 128])
    ext("swa_pos", [128, 128 + TOKC], I32); ext("swa_nw", [128, KC]); ext("swa_fnw", [128, KC]); ext("swa_w_in", [D, 2560]); ext("swa_w_out", [D, D])
    ext("swa_sk", [128, 8]); ext("swa_hmask", [128, 1]); ext("sMask", [128, 512]); ext("sInvf", [128, 1]); ext("sI", [128, 128]); ext("sOnesP", [128, 256])
    io["outT"] = nc.dram_tensor("outT", [D, TOKC], F32, kind="ExternalOutput").ap()
    for n in ("x1T", "x2T", "x3T", "szT1", "szT2"):
        itn(n, [D, TOKC])
    itn("XB1", [4 * 3072, TOKC], BF16); itn("YB1", [4 * 768, TOKC], BF16); itn("XF1", [8, SEQ]); itn("YF1", [2, SEQ])
    itn("XO1", [8 * 4 * D, TOKC // 8]); itn("YO1", [8 * D, TOKC // 8])
    itn("XB2", [4 * 3072, TOKC]); itn("YB2", [4 * 768, TOKC]); itn("XF2", [16, SEQ]); itn("YF2", [4, SEQ])
    itn("XO2", [8 * 4 * D, TOKC // 8]); itn("YO2", [8 * D, TOKC // 8])
    itn("XH", [4 * D, 128]); itn("YH", [D, 128])
    itn("L1", [3072, TOKC], BF16); itn("LF1", [8, TOKC]); itn("LO1", [D, TOKC])
    itn("L2", [3072, TOKC]); itn("LF2", [16, TOKC]); itn("LO2", [D, TOKC])

    ZW = 1024
    zt32 = P.sb("zt32", [128, ZW], F32)
    zt16 = P.sb("zt16", [128, ZW], BF16)
    Bzt = P.buf()
    P.op("pool", lambda e: e.memset(zt32[:], 0.0), writes=[Bzt])
    P.op("pool", lambda e: e.memset(zt16[:], 0.0), writes=[Bzt])
    fills = {}

    def zfill(name, key):
        ap = io[name]
        rows, cols = ap.shape
        zt = zt16 if name == "XB1" else zt32
        tot = rows * cols
        per = 128 * ZW
        flat = ap.rearrange("r c -> (r c)")
        KB = 8
        o = 0
        while o < tot:
            k = min(KB, (tot - o) // per)
            if k >= 1:
                n = k * per
                src = bass.AP(zt[:].tensor, 0, [[ZW, 128], [0, k], [1, ZW]])
                P.fill_queue.append((key, lambda q, dst=flat[o:o + n].rearrange("(k p w) -> p k w", p=128, w=ZW), src=src, key=key:
                                     P.dma(q, dst, src, reads=[Bzt], writes=[fills.setdefault((key, q), P.buf())], chan=fills[(key, q)], defer=True)))
            else:
                n = tot - o
                w = n // 128
                P.fill_queue.append((key, lambda q, dst=flat[o:o + n].rearrange("(p w) -> p w", p=128), src=zt[:, 0:w], key=key:
                                     P.dma(q, dst, src, reads=[Bzt], writes=[fills.setdefault((key, q), P.buf())], chan=fills[(key, q)], defer=True)))
            o += n

    def exchange(xin, yout):
        P.coll("ReduceScatter", GROUPS, io[xin], io[yout])

    def dyn_copy(q, dst, dst_pat, dst_off, src, src_pat, src_off, jscale, fkey=None, extra=()):
        b = P.buf()
        P.dma(q, (lambda: bass.AP(io[dst].tensor, P.jx[q] * jscale + dst_off, [list(p) for p in dst_pat])),
              bass.AP(io[src].tensor, src_off, [list(p) for p in src_pat]), reads=[fb for (k_, q_), fb in fills.items() if k_ == fkey] + list(extra), writes=[b], chan=b)
        return b

    def place_tok(lname, xname, fl_l, fl_x, nfl, with_v, fkey):
        if True:
            for h_ in range(2):
                q = ("sp", "pool")[h_]
                dyn_copy(q, xname, [[TOKC, 1536], [1, TOKC]], h_ * 1536 * TOKC, lname, [[TOKC, 1536], [1, TOKC]], h_ * 1536 * TOKC, 3072 * TOKC, fkey=fkey)
        dyn_copy("sp", fl_x, [[SEQ, nfl], [1, TOKC]], 0, fl_l, [[TOKC, nfl], [1, TOKC]], 0, TOKC, fkey=fkey)

    def o_exchange_hooks(lname, xname, yname, fkey):
        SUB = TOKC // 8
        Blo = [P.buf() for _ in range(8)]

        def done(s_):
            P.fill_until(fkey)
            b = dyn_copy("pool", xname, [[D * SUB, 4], [SUB, 256], [1, SUB]], s_ * 4 * D * SUB,
                         lname, [[256 * TOKC, 4], [TOKC, 256], [1, SUB]], s_ * SUB, 256 * SUB, fkey=fkey, extra=[Blo[s_]])
            P.coll("ReduceScatter", GROUPS, io[xname][s_ * 4 * D:(s_ + 1) * 4 * D, :], io[yname][s_ * D:(s_ + 1) * D, :], reads=[b])
        return Blo, done

    P.tag('fill')
    zfill("XB1", 1); zfill("XF1", 1); zfill("XO1", 2); zfill("XB2", 3); zfill("XF2", 3); zfill("XO2", 4); zfill("XH", 5)
    P.tag('ph1_conv')
    P.phase_begin()
    build_stage0(P, io)
    P.phase_end()

    P.tag('ph2_foxA')
    P.phase_begin()
    io2 = dict(io, c_xT=io["x1T"], a_nw=io["f_nw"], a_w_in=io["f_w_in"], xb=io["L1"], xf=io["LF1"], szT=io["szT1"])
    build_CA(P, io2, StageLoadX, StageAFox)
    P.phase_end()
    P.tag("x1_place")
    P.fill_until(1)
    place_tok("L1", "XB1", "LF1", "XF1", 8, True, 1)
    P.barrier()
    P.tag("x1_rs")
    exchange("XF1", "YF1")
    P.barrier()

    P.tag('ph3_foxB')
    P.phase_begin()
    By1 = [P.buf() for _ in range(4)]
    for c_ in range(4):
        P.coll("ReduceScatter", GROUPS, io["XB1"][c_ * 3072:(c_ + 1) * 3072, :], io["YB1"][c_ * 768:(c_ + 1) * 768, :], writes=[By1[c_]])
    Blo1, done1 = o_exchange_hooks("LO1", "XO1", "YO1", 2)
    build_foxB(P, dict(io, yb=io["YB1"], yb_bufs=By1, yf=io["YF1"], xo=io["LO1"], lo_bufs=Blo1, sub_done=done1), NH=2, S_=SEQ)
    P.phase_end()

    P.tag('ph4_foxC_gdnA')
    P.phase_begin()
    io4 = dict(io, c_oT=io["YO1"], c_oT_bufs=[P.buf() for _ in range(8)], c_szT=io["szT1"], c_xT=io["x1T"], c_w_out=io["f_w_out"], xo=io["x2T"],
               a_nw=io["g_nw"], a_w_in=io["g_w_in"], xb=io["L2"], xf=io["LF2"], szT=io["szT2"])
    build_CA(P, io4, StageC, StageAGdn)
    P.phase_end()
    P.tag("x3_place")
    P.fill_until(3)
    place_tok("L2", "XB2", "LF2", "XF2", 16, False, 3)
    P.barrier()
    P.tag("x3_rs")
    exchange("XF2", "YF2")
    P.barrier()

    P.tag('ph5_gdnB')
    P.phase_begin()
    By = [P.buf() for _ in range(4)]
    for c_ in range(4):
        P.coll("ReduceScatter", GROUPS, io["XB2"][c_ * 3072:(c_ + 1) * 3072, :], io["YB2"][c_ * 768:(c_ + 1) * 768, :], writes=[By[c_]])
    Blo2, done2 = o_exchange_hooks("LO2", "XO2", "YO2", 4)
    build_gdnB(P, dict(io, yb=io["YB2"], yb_bufs=By, yf=io["YF2"], xo=io["LO2"], lo_bufs=Blo2, sub_done=done2), NH=2, S_=SEQ)
    P.phase_end()

    P.tag('ph6_gdnC')
    P.phase_begin()
    io6 = dict(io, c_oT=io["YO2"], c_oT_bufs=[P.buf() for _ in range(8)], c_szT=io["szT2"], c_xT=io["x2T"], c_w_out=io["g_w_out"], xo=io["x3T"])
    build_CA(P, io6, StageC, StageANull)
    P.phase_end()
    P.tag('x5_halo')
    P.fill_until(5)
    Bh = P.buf()
    P.dma("sp", (lambda: io["XH"][ds(((P.jx["sp"] + 1) % 4) * D, D), :]), io["x3T"][:, TOKC - 128:TOKC], reads=[fb for (k_, q_), fb in fills.items() if k_ == 5], writes=[Bh], chan=Bh)
    P.barrier()
    exchange("XH", "YH")
    P.barrier()

    P.tag('ph7_swa')
    P.phase_begin()
    build_swa(P, dict(io, yh=io["YH"]))
    P.phase_end()
    return P


from concourse.bass_utils import run_bass_kernel_spmd

NCORE = 8


def _nwT(v):
    return np.ascontiguousarray(np.asarray(v, np.float32).reshape(8, 128).T)


def kernel(x, positions, norm_w, final_norm_w, conv_w_in, conv_w_conv, conv_w_out,
           fox_w_in, fox_b_f, fox_w_out, gdn_w_in, gdn_w_conv, gdn_a_log, gdn_dt_bias,
           gdn_norm_w, gdn_w_out, swa_w_in, swa_sinks, swa_w_out):
    x = np.asarray(x, np.float32)
    positions = np.asarray(positions, np.int32)
    f = lambda a: np.ascontiguousarray(np.asarray(a, np.float32))
    nc = bass.Bass("TRN2", target_bir_lowering=False)
    with ExitStack() as st:
        P = build_fused(nc, st)
        P.emit()
    shared = {"l0_nw": _nwT(norm_w[0]), "l0_w_in": f(conv_w_in[0]),
              "l0_w_cv": np.ascontiguousarray(f(conv_w_conv[0]).reshape(3, 8, 128).transpose(2, 1, 0).reshape(128, 24)),
              "l0_w_out": f(conv_w_out[0]),
              "f_nw": _nwT(norm_w[1]), "f_w_in": f(fox_w_in[0]), "f_w_out": f(fox_w_out[0]),
              "g_nw": _nwT(norm_w[2]), "g_w_in": f(gdn_w_in[0]), "g_w_out": f(gdn_w_out[0]),
              "gdn_onw": np.ascontiguousarray(np.broadcast_to(f(gdn_norm_w[0])[None, :], (128, 128))),
              "swa_nw": _nwT(norm_w[3]), "swa_fnw": _nwT(final_norm_w), "swa_w_in": f(swa_w_in[0]), "swa_w_out": f(swa_w_out[0])}
    wc = f(gdn_w_conv[0])
    wcv_all = np.stack([wc[k, w * 1024 + h * 128: w * 1024 + (h + 1) * 128]
                        for h in range(8) for w in range(3) for k in range(4)], axis=1)
    hp_all = np.stack([f(gdn_a_log[0]), f(gdn_dt_bias[0])], axis=1).reshape(1, 16)
    sinks = f(swa_sinks[0])
    sk = np.zeros((128, 8), np.float32)
    for p in range(8):
        sk[:64, p] = sinks[2 * p]
        sk[64:, p] = sinks[2 * p + 1]
    shared["swa_sk"] = sk
    shared.update(fox_consts())
    shared.update(gdn_consts())
    shared.update(swa_consts())
    ins = []
    for c in range(NCORE):
        b, t0 = c // 4, (c % 4) * TOKC
        xt = np.zeros((1024, TOKC + 2), np.float32)
        pos = np.zeros((128 + TOKC,), np.int32)
        if t0 > 0:
            xt[:, 0:2] = x[b, t0 - 2:t0].T
            pos[:128] = positions[b, t0 - 128:t0]
        xt[:, 2:] = x[b, t0:t0 + TOKC].T
        pos[128:] = positions[b, t0:t0 + TOKC]
        d = dict(shared)
        d["l0_xT"] = xt
        d["swa_pos"] = np.ascontiguousarray(np.broadcast_to(pos[None, :], (128, pos.shape[0])))
        d["swa_hmask"] = np.full((128, 1), -30000.0 if t0 == 0 else 0.0, np.float32)
        g = c % 4
        d["fox_bf"] = np.ascontiguousarray(np.broadcast_to(f(fox_b_f[0])[None, 2 * g:2 * g + 2], (128, 2)))
        d["gdn_wcv"] = np.ascontiguousarray(wcv_all[:, g * 24:(g + 1) * 24])
        d["gdn_hp"] = np.ascontiguousarray(np.broadcast_to(hp_all[:, g * 4:(g + 1) * 4], (128, 4)))
        ins.append(d)
    res = run_bass_kernel_spmd(nc, ins, core_ids=list(range(NCORE)))
    out = np.zeros((2, SEQ, 1024), np.float32)
    for c in range(NCORE):
        b, t0 = c // 4, (c % 4) * TOKC
        out[b, t0:t0 + TOKC] = np.asarray(res.results[c]["outT"]).T
    return out
```

```python
from contextlib import ExitStack
import numpy as np
import concourse.bass as bass
import concourse.mybir as mybir

F32 = mybir.dt.float32
BF16 = mybir.dt.bfloat16
I32 = mybir.dt.int32
ALU = mybir.AluOpType
AF = mybir.ActivationFunctionType

ENGS = ("pe", "act", "dve", "pool", "sp")


class Buf:
    __slots__ = ("name", "w", "r", "sem", "semval", "excl", "cls")

    def __init__(self, name):
        self.name = name
        self.w = {}
        self.r = {}
        self.sem = None
        self.semval = 0
        self.excl = False
        self.cls = None


class _Op:
    __slots__ = ("fn", "deps", "dma", "signal", "sigord", "tag")

    def __init__(self, fn, deps, dma=None):
        self.fn = fn
        self.deps = deps
        self.dma = dma
        self.signal = False
        self.sigord = 0
        self.tag = _Op.cur_tag


_Op.cur_tag = None


def _merge(d, s):
    for k, v in s.items():
        if d.get(k, 0) < v:
            d[k] = v


class Prog:
    def __init__(self, nc, stack):
        self.nc = nc
        self.stack = stack
        self.semstack = stack
        self.free_sems = {"hw": [], "sw": [], "cc": []}
        self.chans = []
        self.jx = {}
        self.ops = {e: [] for e in ENGS}
        self.esem = {}
        for e in ("pe", "act", "dve", "pool"):
            self.esem[e] = stack.enter_context(nc.semaphore("es_" + e))
        self.dsems = {}
        self.nsem = 4
        self.final = {}
        self.uid = 0
        self.need_rank = False
        self.pidx = 0
        self.scopes = False
        self.fill_queue = []

    def sb(self, name, shape, dt):
        return self.stack.enter_context(self.nc.sbuf_tensor("p%d_%s" % (self.pidx, name), list(shape), dt))

    def ps(self, name, shape, dt=F32):
        return self.stack.enter_context(self.nc.psum_tensor("p%d_%s" % (self.pidx, name), list(shape), dt))

    def buf(self, name=None):
        self.uid += 1
        return Buf(name or f"b{self.uid}")

    def _chan(self, b, cls="hw"):
        if b.sem is None:
            b.cls = cls
            if self.free_sems[cls]:
                b.sem, b.semval = self.free_sems[cls].pop()
            else:
                b.sem = self.semstack.enter_context(self.nc.semaphore("ds_%d" % self.nsem))
                self.dsems[id(b.sem)] = b.sem
                self.nsem += 1
            self.chans.append(b)
        return b.sem

    def tag(self, name):
        _Op.cur_tag = name

    def fill_step(self, k, q="pool", maxkey=99):
        for _ in range(k):
            if not self.fill_queue or self.fill_queue[0][0] > maxkey:
                return
            self.fill_queue.pop(0)[1](q)

    def fill_until(self, key, q="pool"):
        while self.fill_queue and self.fill_queue[0][0] <= key:
            self.fill_queue.pop(0)[1](q)

    def phase_begin(self):
        self.pidx += 1
        self._pstack = ExitStack()
        self._outer = self.stack
        self.stack = self._pstack

    def phase_end(self):
        self.barrier()
        self._pstack.close()
        self.stack = self._outer

    def barrier(self):
        deps = {}
        for e in ("pe", "act", "dve", "pool"):
            lst = self.ops[e]
            for i in range(len(lst) - 1, -1, -1):
                if lst[i].dma is None and lst[i].fn is not None:
                    deps[("e", e)] = i + 1
                    break
        for b in self.chans:
            if b.semval:
                deps[("d", id(b.sem))] = b.semval
        for e in ENGS:
            self.ops[e].append(_Op(None, dict(deps)))
        for b in self.chans:
            self.free_sems[b.cls].append((b.sem, b.semval))
            b.sem = None
        self.chans = []

    def coll(self, kind, groups, in_ap, out_ap, writes=(), reads=()):
        b = self.buf()
        sem = self._chan(b, "cc")
        b.semval += 1
        v = b.semval
        for w_ in writes:
            w_.w[("d", id(sem))] = v
        deps = {}
        for r_ in reads:
            _merge(deps, r_.w)
        self.ops["pool"].append(_Op(lambda e: e.collective_compute(kind, ALU.add, replica_groups=groups, ins=[in_ap.opt()], outs=[out_ap.opt()]),
                                    deps, dma=(sem, v, 1)))

    def _deps(self, reads, writes):
        d = {}
        for b in reads:
            _merge(d, b.w)
        for b in writes:
            _merge(d, b.w)
            _merge(d, b.r)
        return d

    def op(self, eng, fn, reads=(), writes=()):
        if any(b.excl for b in reads):
            writes = list(writes) + [b for b in reads if b.excl]
            reads = [b for b in reads if not b.excl]
        deps = self._deps(reads, writes)
        if eng == "pe":
            deps.pop(("e", "pe"), None)
        lst = self.ops[eng]
        lst.append(_Op(fn, deps))
        key = ("e", eng)
        v = len(lst)
        for b in reads:
            b.r[key] = v
        for b in writes:
            b.w[key] = v

    def dma(self, q, out, in_, reads=(), writes=(), chan=None, store=False, final=False, defer=False, **kw):
        if chan is None:
            chan = (reads[0] if store else writes[0])
        sem = self._chan(chan, "sw" if q == "pool" else "hw")
        assert chan.cls == ("sw" if q == "pool" else "hw"), "a DMA channel must stay on one kind of queue"
        if defer and chan in self.chans:
            self.chans.remove(chan)
        key = ("d", id(sem))
        deps = self._deps(reads, writes)
        if store and chan.semval:
            deps[key] = max(deps.get(key, 0), chan.semval)
        chan.semval += 16
        v = chan.semval
        def _fn(e):
            o = out() if callable(out) else out
            i = in_() if callable(in_) else in_
            try:
                return e.dma_start(out=o, in_=i, **kw)
            except Exception:
                print("DMA FAIL out=", o, " in=", i)
                raise
        self.ops[q].append(_Op(_fn, deps, dma=(sem, v, 16)))
        for b in reads:
            b.r[key] = v
        for b in writes:
            b.w[key] = v
        if final:
            self.final[key] = v

    def emit(self):
        nc = self.nc
        for e in ENGS:
            for o in self.ops[e]:
                for (kind, k), v in o.deps.items():
                    if kind == "e":
                        self.ops[k][v - 1].signal = True
        for e in ENGS:
            n = 0
            for o in self.ops[e]:
                if o.signal:
                    n += 1
                o.sigord = n
        if self.final:
            self.ops["sp"].append(_Op(None, dict(self.final)))
        eng_handle = {"pe": "tensor", "act": "scalar", "dve": "vector", "pool": "gpsimd", "sp": "sync"}
        stats = {}
        with nc.Block() as block:
            for e in ENGS:
                ops = self.ops[e]
                if not ops:
                    continue

                def body(eng, ops=ops, e=e):
                    if e in ("sp", "pool") and self.need_rank:
                        self.jx[e] = eng.partition_id() % 4
                    seen = {}
                    nw = 0
                    cur = None
                    sid = None
                    for o in ops:
                        if self.scopes and o.tag != cur:
                            if cur is not None:
                                nc.leave_named_scope(cur, sid, False)
                            cur = o.tag
                            if cur is not None:
                                sid, _ = nc.enter_named_scope(cur, False)
                        for (kind, k), v in o.deps.items():
                            if kind == "e":
                                sem = self.esem[k]
                                val = self.ops[k][v - 1].sigord
                            else:
                                sem = self.dsems[k]
                                val = v
                            sk = (kind, k)
                            if seen.get(sk, 0) >= val:
                                continue
                            seen[sk] = val
                            eng.wait_ge(sem, val)
                            nw += 1
                        if o.fn is None:
                            continue
                        inst = o.fn(eng)
                        if o.dma is not None:
                            if o.dma[2] == 16:
                                inst.then_inc(o.dma[0], 16)
                            else:
                                inst.then_inc(o.dma[0])
                        elif o.signal:
                            inst.then_inc(self.esem[e], 1)
                    if self.scopes and cur is not None:
                        nc.leave_named_scope(cur, sid, False)
                    stats[e] = (len(ops), nw)

                getattr(block, eng_handle[e])(body)
        self.stats = stats
        return stats


D = 1024
KC = 8
TT = 512


def load_weight_bf16(P, wdram, ncols, wb, Bwb, stg, Bstg, scale_col=None, q="sp", Bscale=None):
    piece = 1024
    n = 0
    for kc in range(KC):
        for c0 in range(0, ncols, piece):
            c1 = min(ncols, c0 + piece)
            s = n % 2
            n += 1
            P.dma(q, stg[s][:, 0:c1 - c0], wdram[kc * 128:(kc + 1) * 128, c0:c1], writes=[Bstg[s]])
            eng = ("dve", "pool")[n % 2]
            if scale_col is not None:
                P.op(eng, lambda e, s=s, kc=kc, c0=c0, c1=c1: e.tensor_scalar(
                    out=wb[:, kc, c0:c1], in0=stg[s][:, 0:c1 - c0], scalar1=scale_col[:, kc:kc + 1], scalar2=None,
                    op0=ALU.mult), reads=[Bstg[s], Bscale], writes=[Bwb])
            else:
                P.op(eng, lambda e, s=s, kc=kc, c0=c0, c1=c1: e.tensor_copy(
                    out=wb[:, kc, c0:c1], in_=stg[s][:, 0:c1 - c0]), reads=[Bstg[s]], writes=[Bwb])


class Norm:
    def __init__(self, P, ones, Bones, nmax=TT):
        self.P = P
        self.ones = ones
        self.Bones = Bones
        self.sq = P.sb("nsq", [128, KC, nmax], BF16)
        self.Bsq = P.buf()
        self.ssp = P.ps("nss", [128, nmax])
        self.Bss = P.buf()
        self.sd = P.sb("nsd", [128, nmax], F32)
        self.Bsd = P.buf()

    def run(self, xs, Bxs, n, rstd, Brstd):
        P = self.P
        P.op("act", lambda e: e.activation(out=self.sq[:, :, 0:n], in_=xs[:, :, 0:n], func=AF.Square),
             reads=[Bxs], writes=[self.Bsq])
        for kc in range(KC):
            P.op("pe", lambda e, kc=kc: e.matmul(self.ssp[:, 0:n], lhsT=self.ones[:], rhs=self.sq[:, kc, 0:n],
                                                  start=(kc == 0), stop=(kc == KC - 1)),
                 reads=[self.Bsq, self.Bones], writes=[self.Bss])
        P.op("act", lambda e: e.activation(out=self.sd[:, 0:n], in_=self.ssp[:, 0:n], func=AF.Sqrt,
                                           scale=1.0 / D, bias=self.epsc[:, 0:1]),
             reads=[self.Bss, self.Bones], writes=[self.Bsd])
        P.op("dve", lambda e: e.reciprocal(out=rstd[:, 0:n], in_=self.sd[:, 0:n]), reads=[self.Bsd], writes=[Brstd])


def build_stage0(P, io, NTOK=4096):
    NT = NTOK // TT
    xT, nw, w_in, w_cv, w_out, xo = io["l0_xT"], io["l0_nw"], io["l0_w_in"], io["l0_w_cv"], io["l0_w_out"], io["x1T"]
    xTv = xT.rearrange("(c p) t -> p c t", p=128)
    xov = xo.rearrange("(c p) t -> p c t", p=128)

    ones = P.sb("ones", [128, 128], BF16)
    Bones = P.buf()
    epsc = P.sb("epsc", [128, 1], F32)
    P.op("dve", lambda e: e.memset(ones[:], 1.0), writes=[Bones])
    P.op("dve", lambda e: e.memset(epsc[:], 1e-6), writes=[Bones])
    nws = P.sb("nws", [128, KC], F32)
    Bnw = P.buf()
    P.dma("sp", nws[:], nw[:, :], writes=[Bnw])
    wcs = P.sb("wcs", [128, KC * 3], F32)
    Bwc = P.buf()
    P.dma("sp", wcs[:], w_cv[:, :], writes=[Bwc])

    stg = [P.sb("stg%d" % i, [128, 1024], F32) for i in range(2)]
    Bstg = [P.buf(), P.buf()]
    wib = P.sb("wib", [128, KC, 4096], BF16)
    Bwib = P.buf()
    wob = P.sb("wob", [128, KC, D], BF16)
    Bwob = P.buf()

    xs = [P.sb("xs%d" % i, [128, KC, TT], F32) for i in range(2)]
    Bxs = [P.buf(), P.buf()]
    hT = [P.sb("hT%d" % i, [128, KC, TT], BF16) for i in range(2)]
    BhT = [P.buf(), P.buf()]
    rstd = [P.sb("rstd%d" % i, [128, TT], F32) for i in range(2)]
    Brstd = [P.buf(), P.buf()]
    xh = P.sb("xh", [128, KC, 2], F32)
    Bxh = P.buf()
    hh = P.sb("hh", [128, KC, 2], BF16)
    Bhh = P.buf()
    rh = P.sb("rh", [128, 2], F32)
    Brh = P.buf()
    norm = Norm(P, ones, Bones)
    norm.epsc = epsc

    ubuf = P.sb("ubuf", [128, KC, TT + 2], F32)
    Bu = [P.buf() for _ in range(KC)]
    vsb = P.sb("vsb", [128, TT], F32)
    Bv = P.buf()
    ycv = P.sb("ycv", [128, TT], F32)
    By = P.buf()
    szb = P.sb("szb", [128, TT], F32)
    Bsz = P.buf()
    tb = P.sb("tb", [128, TT], F32)
    Bt = P.buf()
    og = P.sb("og", [128, KC, TT], BF16)
    Bog = P.buf()
    xn = [P.sb("xn%d" % i, [128, TT], F32) for i in range(2)]
    Bxn = [P.buf(), P.buf()]
    pb, pc, pv, pz = [P.ps("pp%d" % i, [128, TT]) for i in range(4)]
    Bpb, Bpc, Bpv, Bpz = [P.buf() for _ in range(4)]
    py = [P.ps("py%d" % i, [128, TT]) for i in range(2)]
    Bpy = [P.buf(), P.buf()]

    load_weight_bf16(P, w_in, 4096, wib, Bwib, stg, Bstg, scale_col=nws, Bscale=Bnw)
    load_weight_bf16(P, w_out, D, wob, Bwob, stg, Bstg)

    def proj(ps, Bps, oc, h, Bh, n):
        for kc in range(KC):
            P.op("pe", lambda e, kc=kc: e.matmul(ps[:, 0:n], lhsT=wib[:, kc, oc * 128:(oc + 1) * 128],
                                                  rhs=h[:, kc, 0:n], start=(kc == 0), stop=(kc == KC - 1)),
                 reads=[Bwib, Bh], writes=[Bps])

    def make_h(x_, Bx_, n, r_, Br_, h_, Bh_):
        norm.run(x_, Bx_, n, r_, Br_)
        for kc in range(KC):
            eng = "dve" if kc % 2 == 0 else "pool"
            P.op(eng, lambda e, kc=kc: e.tensor_tensor(out=h_[:, kc, 0:n], in0=x_[:, kc, 0:n], in1=r_[:, 0:n],
                                                        op=ALU.mult), reads=[Bx_, Br_], writes=[Bh_])

    P.dma("sp", xh[:], xTv[:, :, 0:2], writes=[Bxh])
    make_h(xh, Bxh, 2, rh, Brh, hh, Bhh)
    for ci in range(KC):
        proj(pc, Bpc, 8 + ci, hh, Bhh, 2)
        proj(pv, Bpv, 16 + ci, hh, Bhh, 2)
        P.op("act", lambda e: e.activation(out=vsb[:, 0:2], in_=pv[:, 0:2], func=AF.Copy), reads=[Bpv], writes=[Bv])
        P.op("dve", lambda e, ci=ci: e.tensor_tensor(out=ubuf[:, ci, 0:2], in0=pc[:, 0:2], in1=vsb[:, 0:2], op=ALU.mult),
             reads=[Bpc, Bv], writes=[Bu[ci]])

    def prep(i):
        s = i % 2
        P.dma("sp", xs[s][:], xTv[:, :, 2 + i * TT:2 + (i + 1) * TT], writes=[Bxs[s]])
        make_h(xs[s], Bxs[s], TT, rstd[s], Brstd[s], hT[s], BhT[s])

    def main(i):
        s = i % 2
        h, Bh = hT[s], BhT[s]
        for ci in range(KC):
            proj(pc, Bpc, 8 + ci, h, Bh, TT)
            proj(pv, Bpv, 16 + ci, h, Bh, TT)
            proj(pb, Bpb, ci, h, Bh, TT)
            proj(pz, Bpz, 24 + ci, h, Bh, TT)
            P.op("act", lambda e: e.activation(out=vsb[:], in_=pv[:], func=AF.Copy), reads=[Bpv], writes=[Bv])
            P.op("dve", lambda e, ci=ci: e.tensor_tensor(out=ubuf[:, ci, 2:TT + 2], in0=pc[:], in1=vsb[:], op=ALU.mult),
                 reads=[Bpc, Bv], writes=[Bu[ci]])
            P.op("dve", lambda e, ci=ci: e.tensor_scalar(out=ycv[:], in0=ubuf[:, ci, 2:TT + 2],
                                                          scalar1=wcs[:, ci * 3 + 2:ci * 3 + 3], scalar2=None, op0=ALU.mult),
                 reads=[Bu[ci], Bwc], writes=[By])
            for k in (1, 0):
                P.op("dve", lambda e, ci=ci, k=k: e.scalar_tensor_tensor(
                    out=ycv[:], in0=ubuf[:, ci, k:k + TT], scalar=wcs[:, ci * 3 + k:ci * 3 + k + 1], in1=ycv[:],
                    op0=ALU.mult, op1=ALU.add), reads=[Bu[ci], Bwc, By], writes=[By])
            P.op("pool", lambda e, ci=ci: e.tensor_copy(out=ubuf[:, ci, 0:2], in_=ubuf[:, ci, TT:TT + 2]),
                 reads=[Bu[ci]], writes=[Bu[ci]])
            P.op("act", lambda e: e.activation(out=szb[:], in_=pz[:], func=AF.Silu), reads=[Bpz], writes=[Bsz])
            P.op("dve", lambda e: e.tensor_tensor(out=tb[:], in0=pb[:], in1=ycv[:], op=ALU.mult),
                 reads=[Bpb, By], writes=[Bt])
            P.op("pool", lambda e, ci=ci: e.tensor_tensor(out=og[:, ci, :], in0=tb[:], in1=szb[:], op=ALU.mult),
                 reads=[Bt, Bsz], writes=[Bog])
        for dc in range(KC):
            b = dc % 2
            for ci in range(KC):
                P.op("pe", lambda e, ci=ci, dc=dc, b=b: e.matmul(py[b][:], lhsT=wob[:, ci, dc * 128:(dc + 1) * 128],
                                                                rhs=og[:, ci, :], start=(ci == 0), stop=(ci == KC - 1)),
                     reads=[Bwob, Bog], writes=[Bpy[b]])
            P.op("dve", lambda e, dc=dc, b=b: e.tensor_tensor(out=xn[b][:], in0=py[b][:], in1=xs[s][:, dc, :], op=ALU.add),
                 reads=[Bpy[b], Bxs[s]], writes=[Bxn[b]])
            P.dma("sp", xov[:, dc, i * TT:(i + 1) * TT], xn[b][:], reads=[Bxn[b]], store=True)

    prep(0)
    for i in range(NT):
        P.fill_step(4)
        if i + 1 < NT:
            prep(i + 1)
        main(i)
    return P


import math
from concourse.bass import ds


class Rot:
    def __init__(self, P, name, shape, dt, n):
        self.t = [P.sb("%s%d" % (name, i), shape, dt) for i in range(n)]
        self.b = [P.buf() for _ in range(n)]
        self.i = 0

    def next(self):
        k = self.i % len(self.t)
        self.i += 1
        return self.t[k], self.b[k]


class Common:
    def __init__(self, P):
        self.P = P
        self.ones = P.sb("ones", [128, 128], BF16)
        self.Bc = P.buf()
        self.epsc = P.sb("epsc", [128, 1], F32)
        P.op("dve", lambda e: e.memset(self.ones[:], 1.0), writes=[self.Bc])
        P.op("dve", lambda e: e.memset(self.epsc[:], 1e-6), writes=[self.Bc])
        self.norm = Norm(P, self.ones, self.Bc)
        self.norm.epsc = self.epsc
        self.stg = [P.sb("stg%d" % i, [128, 1024], F32) for i in range(2)]
        self.Bstg = [P.buf(), P.buf()]
        self.xs = [P.sb("xs%d" % i, [128, KC, TT], F32) for i in range(2)]
        self.Bxs = [P.buf(), P.buf()]
        self.hT = [P.sb("hT%d" % i, [128, KC, TT], BF16) for i in range(2)]
        self.BhT = [P.buf(), P.buf()]
        self.rstd = [P.sb("rstd%d" % i, [128, TT], F32) for i in range(2)]
        self.Brstd = [P.buf(), P.buf()]
        self.f32r = Rot(P, "f32r", [128, TT], F32, 4)
        self.bf16r = Rot(P, "bf16r", [128, TT], BF16, 4)
        self.pp = [P.ps("pp%d" % i, [128, TT]) for i in range(6)]
        self.Bpp = [P.buf() for _ in range(6)]
        self.ppi = 0

    def psum(self):
        k = self.ppi % len(self.pp)
        self.ppi += 1
        return self.pp[k], self.Bpp[k]

    def make_h(self, s, n=TT):
        P = self.P
        x_, Bx_, r_, Br_, h_, Bh_ = self.xs[s], self.Bxs[s], self.rstd[s], self.Brstd[s], self.hT[s], self.BhT[s]
        self.norm.run(x_, Bx_, n, r_, Br_)
        for kc in range(KC):
            eng = "dve" if kc % 2 == 0 else "pool"
            P.op(eng, lambda e, kc=kc: e.tensor_tensor(out=h_[:, kc, 0:n], in0=x_[:, kc, 0:n], in1=r_[:, 0:n],
                                                        op=ALU.mult), reads=[Bx_, Br_], writes=[Bh_])


def load_small(P, dram, shape, name, dt=F32):
    t = P.sb(name, shape, dt)
    B = P.buf()
    P.dma("sp", t[:], dram, writes=[B])
    return t, B


class StageC:
    def __init__(self, P, cm, io, NTOK):
        self.P, self.cm = P, cm
        v = lambda a: a.rearrange("(c p) t -> p c t", p=128)
        self.szT, self.xT, self.w_out, self.xo = v(io["c_szT"]), v(io["c_xT"]), io["c_w_out"], v(io["xo"])
        SUB = NTOK // 8
        self.SUB = SUB
        self.oTs = [v(io["c_oT"][s_ * D:(s_ + 1) * D, :]) for s_ in range(8)]
        self.Bo = io["c_oT_bufs"]
        self.wob = P.sb("wob", [128, KC, D], BF16)
        self.Bwob = P.buf()
        self.og = P.sb("og", [128, KC, TT], BF16)
        self.Bog = P.buf()
        self.lo = Rot(P, "c_lo", [128, TT], F32, 2)
        self.ls = Rot(P, "c_ls", [128, TT], F32, 2)
        self.lx = Rot(P, "c_lx", [128, TT], F32, 2)

    def load_weights(self):
        load_weight_bf16(self.P, self.w_out, D, self.wob, self.Bwob, self.cm.stg, self.cm.Bstg)

    def tile(self, i, s):
        P, cm = self.P, self.cm
        sl = slice(i * TT, (i + 1) * TT)
        for ci in range(KC):
            to, Bo = self.lo.next()
            ts, Bs = self.ls.next()
            s_ = (i * TT) // self.SUB
            lo_ = i * TT - s_ * self.SUB
            P.dma("sp", to[:], self.oTs[s_][:, ci, lo_:lo_ + TT], reads=[self.Bo[s_]], writes=[Bo])
            P.dma("sp", ts[:], self.szT[:, ci, sl], writes=[Bs])
            eng = "dve" if ci % 2 == 0 else "pool"
            P.op(eng, lambda e, ci=ci, to=to, ts=ts: e.tensor_tensor(out=self.og[:, ci, :], in0=to[:], in1=ts[:], op=ALU.mult),
                 reads=[Bo, Bs], writes=[self.Bog])
        for dc in range(KC):
            ps, Bps = cm.psum()
            tx, Bx = self.lx.next()
            P.dma("sp", tx[:], self.xT[:, dc, sl], writes=[Bx])
            for ci in range(KC):
                P.op("pe", lambda e, ci=ci, dc=dc, ps=ps: e.matmul(ps[:], lhsT=self.wob[:, ci, dc * 128:(dc + 1) * 128],
                                                                  rhs=self.og[:, ci, :], start=(ci == 0), stop=(ci == KC - 1)),
                     reads=[self.Bwob, self.Bog], writes=[Bps])
            P.op("dve", lambda e, dc=dc, ps=ps, tx=tx: e.tensor_tensor(out=cm.xs[s][:, dc, :], in0=ps[:], in1=tx[:], op=ALU.add),
                 reads=[Bps, Bx], writes=[cm.Bxs[s]])
        P.dma("sp", self.xo[:, :, sl], cm.xs[s][:], reads=[cm.Bxs[s]], store=True, chan=cm.Bxs[s])


class StageLoadX:
    def __init__(self, P, cm, io, NTOK):
        self.P, self.cm = P, cm
        self.xT = io["c_xT"].rearrange("(c p) t -> p c t", p=128)

    def load_weights(self):
        pass

    def tile(self, i, s):
        self.P.dma("sp", self.cm.xs[s][:], self.xT[:, :, i * TT:(i + 1) * TT], writes=[self.cm.Bxs[s]])


class StageAFox:
    NCOL = 4104

    def __init__(self, P, cm, io, NTOK):
        self.P, self.cm = P, cm
        self.NTOK = NTOK
        self.nw, self.w_in = io["a_nw"], io["a_w_in"]
        self.xb, self.xf = io["xb"], io["xf"]
        self.szT = io["szT"].rearrange("(c p) t -> p c t", p=128)
        self.wib = P.sb("wib", [128, KC, self.NCOL], BF16)
        self.Bwib = P.buf()
        self.flr = Rot(P, "a_fl", [8, TT], F32, 2)

    def load_weights(self):
        self.nws, self.Bnw = load_small(self.P, self.nw[:, :], [128, KC], "a_nws")
        load_weight_bf16(self.P, self.w_in, self.NCOL, self.wib, self.Bwib, self.cm.stg, self.cm.Bstg, scale_col=self.nws, Bscale=self.Bnw)

    def fm_proj(self, col0, s, M=128):
        P, cm = self.P, self.cm
        ps, Bps = cm.psum()
        for kc in range(KC):
            P.op("pe", lambda e, kc=kc, ps=ps: e.matmul(ps[0:M, :], lhsT=self.wib[:, kc, col0:col0 + M], rhs=cm.hT[s][:, kc, :],
                                                        start=(kc == 0), stop=(kc == KC - 1)),
                 reads=[self.Bwib, cm.BhT[s]], writes=[Bps])
        return ps, Bps

    def tm_proj(self, col0, ncol, blk, s):
        P, cm = self.P, self.cm
        ps, Bps = cm.psum()
        for kc in range(KC):
            P.op("pe", lambda e, kc=kc, ps=ps: e.matmul(ps[:, 0:ncol], lhsT=cm.hT[s][:, kc, blk * 128:(blk + 1) * 128],
                                                        rhs=self.wib[:, kc, col0:col0 + ncol],
                                                        start=(kc == 0), stop=(kc == KC - 1)),
                 reads=[self.Bwib, cm.BhT[s]], writes=[Bps])
        return ps, Bps

    def tile(self, i, s):
        P, cm = self.P, self.cm
        sl = slice(i * TT, (i + 1) * TT)
        S_ = 4 * self.NTOK
        xb, xf = self.xb, self.xf
        n = 0
        for which in (0, 1):
            for c in range(KC):
                ps, Bps = self.fm_proj(which * 1024 + c * 128, s)
                t, Bt = cm.bf16r.next()
                if n % 2 == 0:
                    P.op("act", lambda e, t=t, ps=ps: e.activation(out=t[:], in_=ps[:], func=AF.Copy), reads=[Bps], writes=[Bt])
                else:
                    P.op("dve", lambda e, t=t, ps=ps: e.tensor_copy(out=t[:], in_=ps[:]), reads=[Bps], writes=[Bt])
                n += 1
                row0 = (c // 2) * 768 + which * 256 + (c % 2) * 128
                P.dma("sp", xb[row0:row0 + 128, sl], t[:], reads=[Bt], store=True)
        for c in range(KC):
            ps, Bps = self.fm_proj(3072 + c * 128, s)
            t, Bt = cm.f32r.next()
            P.op("act", lambda e, t=t, ps=ps: e.activation(out=t[:], in_=ps[:], func=AF.Silu), reads=[Bps], writes=[Bt])
            P.dma("sp", self.szT[:, c, sl], t[:], reads=[Bt], store=True)
        for blk in range(TT // 128):
            for half in range(2):
                ps, Bps = self.tm_proj(2048 + half * 512, 512, blk, s)
                t, Bt = cm.bf16r.next()
                P.op("dve", lambda e, t=t, ps=ps: e.tensor_copy(out=t[:], in_=ps[:]), reads=[Bps], writes=[Bt])
                for pr in range(2):
                    g = half * 2 + pr
                    tok0 = i * TT + blk * 128
                    r0 = g * 768 + 512
                    vview = xb[r0:r0 + 256, :].rearrange("(h r) (q d) -> h (r q) d", h=2, d=128)
                    P.dma("sp", vview[:, tok0:tok0 + 128, :].rearrange("h t d -> t h d"),
                          t[:, pr * 256:(pr + 1) * 256].rearrange("p (h d) -> p h d", d=128), reads=[Bt], store=True)
        ps, Bps = cm.psum()
        for kc in range(KC):
            P.op("pe", lambda e, kc=kc, ps=ps: e.matmul(ps[0:8, :], lhsT=self.wib[:, kc, 4096:4104], rhs=cm.hT[s][:, kc, :],
                                                        start=(kc == 0), stop=(kc == KC - 1)), reads=[self.Bwib, cm.BhT[s]], writes=[Bps])
        t, Bt = self.flr.next()
        P.op("dve", lambda e, t=t, ps=ps: e.tensor_copy(out=t[:], in_=ps[0:8, :]), reads=[Bps], writes=[Bt])
        P.dma("sp", xf[0:8, sl], t[:], reads=[Bt], store=True)


def build_CA(P, io, Ccls, Acls, NTOK=4096):
    cm = Common(P)
    C = Ccls(P, cm, io, NTOK)
    A = Acls(P, cm, io, NTOK)
    C.load_weights()
    A.load_weights()
    NT = NTOK // TT

    def prep(i):
        C.tile(i, i % 2)
        if Acls is not StageANull:
            cm.make_h(i % 2)

    prep(0)
    for i in range(NT):
        P.fill_step(3)
        if i + 1 < NT:
            prep(i + 1)
        A.tile(i, i % 2)
    return P


class StageAGdn(StageAFox):
    NCOL = 4112

    def __init__(self, P, cm, io, NTOK):
        self.P, self.cm = P, cm
        self.NTOK = NTOK
        self.nw, self.w_in = io["a_nw"], io["a_w_in"]
        self.xb, self.xf = io["xb"], io["xf"]
        self.szT = io["szT"].rearrange("(c p) t -> p c t", p=128)
        self.wib = P.sb("wib", [128, KC, self.NCOL], BF16)
        self.Bwib = P.buf()
        self.flr = Rot(P, "a_fl", [16, TT], F32, 2)

    def load_weights(self):
        StageAFox.load_weights(self)
        P = self.P
        self.wba = P.sb("wba", [128, KC, 16], BF16)
        P.op("dve", lambda e: e.tensor_copy(out=self.wba[:].rearrange("p k (g h w) -> p k g h w", g=4, h=2, w=2),
                                            in_=self.wib[:, :, 4096:4112].rearrange("p k (w g h) -> p k g h w", w=2, g=4, h=2)),
             reads=[self.Bwib], writes=[self.Bwib])

    def tile(self, i, s):
        P, cm = self.P, self.cm
        sl = slice(i * TT, (i + 1) * TT)
        xb, xf = self.xb, self.xf
        for c in range(24):
            ps, Bps = self.fm_proj(c * 128, s)
            t, Bt = cm.f32r.next()
            if c % 2 == 0:
                P.op("act", lambda e, t=t, ps=ps: e.activation(out=t[:], in_=ps[:], func=AF.Copy), reads=[Bps], writes=[Bt])
            else:
                P.op("dve", lambda e, t=t, ps=ps: e.tensor_copy(out=t[:], in_=ps[:]), reads=[Bps], writes=[Bt])
            which, h = c // 8, c % 8
            row0 = (h // 2) * 768 + ((h % 2) * 3 + which) * 128
            P.dma("sp", xb[row0:row0 + 128, sl], t[:], reads=[Bt], store=True)
        for c in range(KC):
            ps, Bps = self.fm_proj(3072 + c * 128, s)
            t, Bt = cm.f32r.next()
            P.op("act", lambda e, t=t, ps=ps: e.activation(out=t[:], in_=ps[:], func=AF.Silu), reads=[Bps], writes=[Bt])
            P.dma("sp", self.szT[:, c, sl], t[:], reads=[Bt], store=True)
        ps, Bps = cm.psum()
        for kc in range(KC):
            P.op("pe", lambda e, kc=kc, ps=ps: e.matmul(ps[0:16, :], lhsT=self.wba[:, kc, :], rhs=cm.hT[s][:, kc, :],
                                                        start=(kc == 0), stop=(kc == KC - 1)), reads=[self.Bwib, cm.BhT[s]], writes=[Bps])
        t, Bt = self.flr.next()
        P.op("dve", lambda e, t=t, ps=ps: e.tensor_copy(out=t[:], in_=ps[0:16, :]), reads=[Bps], writes=[Bt])
        P.dma("sp", xf[0:16, sl], t[:], reads=[Bt], store=True)


class StageANull:
    def __init__(self, P, cm, io, NTOK):
        pass

    def load_weights(self):
        pass

    def tile(self, i, s):
        pass


import math
import numpy as np
from concourse.bass import ds

S = 16384
NB = S // 128
QT = 512
NEG = -30000.0


def fox_consts():
    k = np.arange(128)
    U = (k[:, None] <= k[None, :]).astype(np.float32)
    SU = (k[:, None] < k[None, :]).astype(np.float32)
    E127 = np.zeros((128, 128), np.float32)
    E127[127, :] = 1.0
    ident = np.eye(128, dtype=np.float32)
    masks = np.zeros((4, 128, 512), np.float32)
    for r in range(4):
        for rp in range(4):
            blk = masks[r][:, rp * 128:(rp + 1) * 128]
            if rp < r:
                blk[:] = NEG
            elif rp == r:
                blk[:] = np.where(k[:, None] <= k[None, :], 0.0, NEG)
    return {"cU": U, "cSU": SU, "cE127": E127, "cI": ident, "cMask": np.ascontiguousarray(masks.transpose(1, 0, 2).reshape(128, 2048))}


def build_foxB(P, io, NH=2, S_=S):
    NBk = S_ // 128
    NQ = S_ // QT
    NTOK = S_ // 4
    scale = 1.0 / math.sqrt(128.0)
    yb = io["yb"]
    By = io["yb_bufs"]
    qTq = [yb[c_ * 768:c_ * 768 + 256, :].rearrange("(h p) s -> h p s", p=128) for c_ in range(4)]
    kTq = [yb[c_ * 768 + 256:c_ * 768 + 512, :].rearrange("(h p) s -> h p s", p=128) for c_ in range(4)]
    vvq = [yb[c_ * 768 + 512:c_ * 768 + 768, :].rearrange("(h r) (q d) -> h (r q) d", h=2, d=128) for c_ in range(4)]
    fl = io["yf"]
    bfd = io["fox_bf"]
    cdr = {n: io[n] for n in ("cU", "cSU", "cE127", "cI", "cMask")}
    xo = io["xo"]

    def const(name, w):
        t = P.sb("k" + name, [128, w], F32)
        B = P.buf()
        P.dma("sp", t[:], cdr[name][:, :], writes=[B])
        return t, B

    U, BU = const("cU", 128)
    SU, BSU = const("cSU", 128)
    E127, BE = const("cE127", 128)
    I32f, BI = const("cI", 128)
    Mf, BM = const("cMask", 2048)
    Bk = P.buf()
    ones_f = P.sb("ones_f", [128, 128], F32)
    ones_b = P.sb("ones_b", [128, 128], BF16)
    ident_b = P.sb("ident_b", [128, 128], BF16)
    mask_b = P.sb("mask_b", [128, 2048], BF16)
    P.op("dve", lambda e: e.memset(ones_f[:], 1.0), writes=[Bk])
    P.op("dve", lambda e: e.memset(ones_b[:], 1.0), writes=[Bk])
    P.op("dve", lambda e: e.tensor_copy(out=ident_b[:], in_=I32f[:]), reads=[BI], writes=[Bk])
    P.op("dve", lambda e: e.tensor_copy(out=mask_b[:], in_=Mf[:]), reads=[BM], writes=[Bk])
    bfs = P.sb("bfs", [128, NH], F32)
    nbf = P.sb("nbf", [128, NH], F32)
    Bbf = P.buf()
    P.dma("sp", bfs[:], bfd[:, 0:NH], writes=[Bbf])
    P.op("dve", lambda e: e.tensor_scalar(out=nbf[:], in0=bfs[:], scalar1=-1.0, scalar2=None, op0=ALU.mult),
         reads=[Bbf], writes=[Bbf])
    flr = P.sb("flr", [128, NH, 128], F32)
    Bflr = P.buf()
    P.dma("sp", flr[:], fl.rearrange("h (b s) -> b h s", s=128), writes=[Bflr])
    fls = P.sb("fls", [128, NH * NBk], F32)
    Bfl = P.buf()

    ks = [P.sb("ks%d" % i, [128, S_], BF16) for i in range(2)]
    Bks = [[P.buf() for _ in range(4)] for _ in range(2)]
    vs = [P.sb("vs%d" % i, [128, NBk, 128], BF16) for i in range(2)]
    Bvs = [[P.buf() for _ in range(4)] for _ in range(2)]
    cpos = [P.sb("cpos%d" % i, [128, NBk], F32) for i in range(2)]
    clast = [P.sb("clast%d" % i, [128, NBk], F32) for i in range(2)]
    Bcp = [P.buf(), P.buf()]
    l1 = P.sb("l1", [128, NBk], F32)
    Bl1 = P.buf()
    totT = P.sb("totT", [128, 128], F32)
    Btot = P.buf()
    qs = [P.sb("qs%d" % i, [128, QT], BF16) for i in range(2)]
    Bqs = [P.buf(), P.buf()]
    biasM = [P.sb("biasM%d" % i, [128, NBk], F32) for i in range(2)]
    BbM = [P.buf(), P.buf()]
    Rm = [P.sb("Rm%d" % i, [128, QT], BF16) for i in range(2)]
    BRm = [P.buf(), P.buf()]
    NS = 3
    pst = [P.ps("pst%d" % i, [128, QT]) for i in range(NS)]
    Bpst = [P.buf() for _ in range(NS)]
    pts = [P.sb("pts%d" % i, [128, QT], BF16) for i in range(NS)]
    Bpts = [P.buf() for _ in range(NS)]
    po = [P.ps("po%d" % i, [128, QT]) for i in range(2)]
    Bpo = [P.buf(), P.buf()]
    pr = [P.ps("pr%d" % i, [128, QT]) for i in range(2)]
    Bpr = [P.buf(), P.buf()]
    racc = [[P.sb("racc%d%d" % (i, z), [128, QT], F32) for z in range(2)] for i in range(2)]
    Bracc = [[P.buf(), P.buf()] for _ in range(2)]
    rinv = P.sb("rinv", [128, QT], F32)
    Brinv = P.buf()
    osb = [P.sb("osb%d" % i, [128, QT], F32) for i in range(2)]
    Bosb = [P.buf(), P.buf()]
    pmisc = P.ps("pmisc", [128, 512])
    Bpm = P.buf()

    def load_kv(h, c_):
        hs = h % 2
        P.dma("sp", ks[hs][:, c_ * NTOK:(c_ + 1) * NTOK], kTq[c_][h, :, :], reads=[By[c_]], writes=[Bks[hs][c_]])
        P.dma("sp", vs[hs][:, c_ * (NTOK // 128):(c_ + 1) * (NTOK // 128), :], vvq[c_][h].rearrange("(b s) d -> s b d", s=128),
              reads=[By[c_]], writes=[Bvs[hs][c_]])

    def head_prep(h):
        hs = h % 2
        for c_ in range(4 if h > 0 else 1):
            load_kv(h, c_)
        P.op("pe", lambda e: e.transpose(pmisc[:, 384:384 + NBk], flr[:, h, :], I32f[:]), reads=[Bflr, BI], writes=[Bpm])
        P.op("dve", lambda e: e.tensor_copy(out=fls[:, h * NBk:(h + 1) * NBk], in_=pmisc[:, 384:384 + NBk]), reads=[Bpm], writes=[Bfl])
        f_h = fls[:, h * NBk:(h + 1) * NBk]
        P.op("act", lambda e: e.activation(out=l1[:], in_=f_h, func=AF.Exp, scale=-1.0, bias=nbf[:, h:h + 1]),
             reads=[Bfl, Bbf], writes=[Bl1])
        P.op("act", lambda e: e.activation(out=l1[:], in_=l1[:], func=AF.Ln, scale=1.0, bias=ones_f[:, 0:1]),
             reads=[Bl1, Bk], writes=[Bl1])
        P.op("pe", lambda e: e.matmul(pmisc[0:NBk, 0:128], lhsT=l1[:, 0:NBk], rhs=ones_f[:], start=True, stop=True),
             reads=[Bl1, Bk], writes=[Bpm])
        P.op("dve", lambda e: e.tensor_copy(out=totT[0:NBk, :], in_=pmisc[0:NBk, 0:128]), reads=[Bpm], writes=[Btot])
        P.op("pe", lambda e: e.matmul(pmisc[:, 128:128 + NBk], lhsT=U[:], rhs=l1[:, 0:NBk], start=True, stop=False),
             reads=[Bl1, BU], writes=[Bpm])
        P.op("pe", lambda e: e.matmul(pmisc[:, 128:128 + NBk], lhsT=totT[0:NBk, :], rhs=SU[0:NBk, 0:NBk], start=False, stop=True),
             reads=[Btot, BSU], writes=[Bpm])
        P.op("dve", lambda e: e.tensor_copy(out=cpos[hs][:], in_=pmisc[:, 128:128 + NBk]), reads=[Bpm], writes=[Bcp[hs]])
        P.op("pe", lambda e: e.matmul(pmisc[:, 256:256 + NBk], lhsT=E127[:], rhs=cpos[hs][:], start=True, stop=True),
             reads=[Bcp[hs], BE], writes=[Bpm])
        P.op("dve", lambda e: e.tensor_copy(out=clast[hs][:], in_=pmisc[:, 256:256 + NBk]), reads=[Bpm], writes=[Bcp[hs]])

    items = []
    for h in range(NH):
        for j in range(NQ):
            nkb = 4 * j + 4
            for kb in range(nkb):
                items.append((h, j, kb, nkb))
    qcount = [0]

    def qtile_prep(h, j):
        P.fill_step(2 if j >= 8 else 0, q="sp", maxkey=3)
        hs = h % 2
        s = qcount[0] % 2
        qcount[0] += 1
        nkb = 4 * j + 4
        cq = (j * QT) // NTOK
        if h == 0 and cq > 0 and (j * QT) % NTOK == 0:
            load_kv(h, cq)
        P.dma("sp", qs[s][:], qTq[cq][h, :, j * QT - cq * NTOK:(j + 1) * QT - cq * NTOK], reads=[By[cq]], writes=[Bqs[s]])
        P.op("dve", lambda e: e.tensor_scalar(out=biasM[s][:, 0:nkb], in0=cpos[hs][:, 0:nkb],
                                              scalar1=clast[hs][:, nkb - 1:nkb], scalar2=None, op0=ALU.subtract),
             reads=[Bcp[hs]], writes=[BbM[s]])
        for r in range(4):
            P.op("dve", lambda e, r=r: e.tensor_scalar(out=Rm[s][:, r * 128:(r + 1) * 128], in0=I32f[:],
                                                         scalar1=biasM[s][:, 4 * j + r:4 * j + r + 1], scalar2=-math.sqrt(128.0),
                                                         op0=ALU.mult, op1=ALU.mult),
                 reads=[BI, BbM[s]], writes=[BRm[s]])
        return s

    qslot = {}

    def QK(n):
        h, j, kb, nkb = items[n]
        hs = h % 2
        if kb == 0:
            if j == 0:
                head_prep(h)
            qslot[(h, j)] = qtile_prep(h, j)
        s = qslot[(h, j)]
        b = n % NS
        diag = kb >= 4 * j
        P.op("pe", lambda e: e.matmul(pst[b][:], lhsT=ks[hs][:, kb * 128:(kb + 1) * 128], rhs=qs[s][:], start=True, stop=False),
             reads=[Bks[hs][(kb * 128) // NTOK], Bqs[s]], writes=[Bpst[b]])
        P.op("pe", lambda e: e.matmul(pst[b][:], lhsT=ones_b[:], rhs=Rm[s][:], start=False, stop=not diag),
             reads=[Bk, BRm[s]], writes=[Bpst[b]])
        if diag:
            r = kb - 4 * j
            P.op("pe", lambda e: e.matmul(pst[b][:], lhsT=ident_b[:], rhs=mask_b[:, r * 512:(r + 1) * 512], start=False, stop=True),
                 reads=[Bk], writes=[Bpst[b]])

    def PV(n):
        h, j, kb, nkb = items[n]
        hs = h % 2
        s = qslot[(h, j)]
        b = n % NS
        a = (h * NQ + j) % 2
        P.op("act", lambda e: e.activation(out=pts[b][:], in_=pst[b][:], func=AF.Exp, scale=scale, bias=biasM[s][:, kb:kb + 1]),
             reads=[Bpst[b], BbM[s]], writes=[Bpts[b]])
        P.op("pe", lambda e: e.matmul(po[a][:], lhsT=vs[hs][:, kb, :], rhs=pts[b][:], start=(kb == 0), stop=(kb == nkb - 1)),
             reads=[Bvs[hs][(kb * 128) // NTOK], Bpts[b]], writes=[Bpo[a]])
        ra, Bra = racc[a][kb % 2], Bracc[a][kb % 2]
        if kb < 2:
            P.op("dve", lambda e: e.tensor_copy(out=ra[:], in_=pts[b][:]), reads=[Bpts[b]], writes=[Bra])
        else:
            P.op("dve", lambda e: e.tensor_tensor(out=ra[:], in0=ra[:], in1=pts[b][:], op=ALU.add), reads=[Bpts[b], Bra], writes=[Bra])
        if kb == nkb - 1:
            for z_ in range(2):
                P.op("pe", lambda e, z_=z_: e.matmul(pr[a][:], lhsT=ones_f[:], rhs=racc[a][z_][:], start=(z_ == 0), stop=(z_ == 1)),
                     reads=[Bk, Bracc[a][z_]], writes=[Bpr[a]])
            P.op("dve", lambda e: e.reciprocal(out=rinv[:], in_=pr[a][:]), reads=[Bpr[a]], writes=[Brinv])
            P.op("dve", lambda e: e.tensor_tensor(out=osb[a][:], in0=po[a][:], in1=rinv[:], op=ALU.mult),
                 reads=[Bpo[a], Brinv], writes=[Bosb[a]])
            t0 = j * QT
            r0_ = (t0 // NTOK) * 256 + h * 128
            sub_ = (t0 % NTOK) // (NTOK // 8)
            P.dma("sp", xo[r0_:r0_ + 128, (t0 % NTOK):(t0 % NTOK) + QT], osb[a][:], reads=[Bosb[a]], writes=[io["lo_bufs"][sub_]], store=True, chan=Bosb[a])
            if h == NH - 1 and t0 >= 3 * NTOK and ((t0 + QT) % (NTOK // 8)) == 0:
                io["sub_done"](sub_)

    LA = 2
    for n in range(len(items) + LA):
        if n < len(items):
            QK(n)
        if n - LA >= 0:
            PV(n - LA)
    return P


import math
import numpy as np
from concourse.bass import ds

NEG = -30000.0
TT = 512


def gdn_consts():
    k = np.arange(128)
    same = (k[:, None] // 64) == (k[None, :] // 64)
    c = {}
    c["gU"] = ((k[:, None] <= k[None, :]) & same).astype(np.float32)
    c["gSL"] = ((k[:, None] > k[None, :]) & same).astype(np.float32)
    c["gSame"] = same.astype(np.float32)
    h0 = np.zeros((128, 128), np.float32)
    h0[:64, :] = 1.0
    c["gH0"] = h0
    c["gH1"] = 1.0 - h0
    c["gI"] = np.eye(128, dtype=np.float32)
    c["gMS"] = np.where((k[:, None] > k[None, :]) & same, 0.0, NEG).astype(np.float32)
    c["gMT"] = np.where((k[None, :] >= k[:, None]) & same, 0.0, NEG).astype(np.float32)
    return c


CN = ("gU", "gSL", "gSame", "gH0", "gH1", "gI", "gMS", "gMT")


def build_gdnB(P, io, NH=2, S_=16384, dbg=False):
    NB = S_ // 128
    NT = S_ // TT
    NTOK = S_ // 4
    qkvq = [io["yb"][c_ * 768:(c_ + 1) * 768, :].rearrange("(h w p) s -> h w p s", h=NH, w=3) for c_ in range(8)]
    By = io["yb_bufs"]
    wcv = io["gdn_wcv"]
    ba = io["yf"]
    hp = io["gdn_hp"]
    onw = io["gdn_onw"]
    cdr = {n: io[n] for n in CN}
    xo = io["xo"]

    def ld(name, dram, w):
        t = P.sb(name, [128, w], F32)
        B = P.buf()
        P.dma("sp", t[:], dram, writes=[B])
        return t, B

    K = {}
    BK = P.buf()
    for n in CN:
        K[n] = P.sb("k" + n, [128, 128], F32)
        P.dma("sp", K[n][:], cdr[n][:, :], writes=[BK])
    wcs, Bw = ld("wcs", wcv[:, 0:NH * 12], NH * 12)
    hps, Bhp = ld("hps", hp[:, 0:NH * 2], NH * 2)
    bar = P.sb("bar", [128, NH * 2, 128], F32)
    Bbar = P.buf()
    P.dma("sp", bar[:], ba.rearrange("r (b s) -> b r s", s=128), writes=[Bbar])
    bas = P.sb("bas", [128, NH * 2 * NB], F32)
    Bba = P.buf()
    onws, Bon = ld("onws", onw[:, :], 128)
    ones_b = P.sb("ones_b", [128, 128], BF16)
    ident_b = P.sb("ident_b", [128, 128], BF16)
    epsc = P.sb("epsc", [128, 1], F32)
    one_c = P.sb("one_c", [128, 1], F32)
    P.op("dve", lambda e: e.memset(ones_b[:], 1.0), writes=[BK])
    P.op("dve", lambda e: e.memset(epsc[:], 1e-6), writes=[BK])
    P.op("dve", lambda e: e.memset(one_c[:], 1.0), writes=[BK])
    P.op("dve", lambda e: e.tensor_copy(out=ident_b[:], in_=K["gI"][:]), reads=[BK], writes=[BK])

    pf = [P.ps("pf%d" % i, [128, 512]) for i in range(6)]
    Bpf = [P.buf() for _ in range(6)]
    for b_ in Bpf:
        b_.excl = True
    pfslots = [(pf[i % 6][:, ((i // 6) % 4) * 128:((i // 6) % 4 + 1) * 128], Bpf[i % 6]) for i in range(24)]
    pbt = P.ps("pbt", [128, 1024], BF16)
    Bpbt = P.buf()
    Bpbt.excl = True
    pbslots = [(pbt[:, i * 128:(i + 1) * 128], Bpbt) for i in range(8)]
    pbig = P.ps("pbig", [128, 512])
    Bpbig = P.buf()
    Bpbig.excl = True
    for r_ in range(NH * 2):
        P.op("pe", lambda e, r_=r_: e.transpose(pbig[:, (r_ % 4) * 128:(r_ % 4) * 128 + NB], bar[:, r_, :], K["gI"][:]), reads=[Bbar, BK], writes=[Bpbig])
        P.op("dve", lambda e, r_=r_: e.tensor_copy(out=bas[:, r_ * NB:(r_ + 1) * NB], in_=pbig[:, (r_ % 4) * 128:(r_ % 4) * 128 + NB]),
             reads=[Bpbig], writes=[Bba])
    cnt = {"f": 0, "b": 0, 0: 0, 1: 0}
    cur_head = [0]

    def psf():
        h_ = cur_head[0]
        cnt[h_] += 1
        k_ = cnt[h_] % 12
        bank = h_ * 3 + (k_ % 3)
        return pf[bank][:, (k_ // 3) * 128:(k_ // 3 + 1) * 128], Bpf[bank]

    def psb():
        cnt["b"] += 1
        return pbslots[cnt["b"] % 8]

    class Head:
        pass

    heads = []
    for hh in range(NH):
        H = Head()
        H.hh = hh
        n = "h%d_" % hh
        mk = lambda nm, w, dt=F32: P.sb(n + nm, [128, w], dt)
        H.tab = {t: mk("t" + t, NB) for t in ("beta", "nbeta", "g", "gc", "egc", "ekd", "bgc", "eg0", "eg1", "tmp")}
        H.Btab = P.buf()
        H.raw = [P.sb(n + "raw%d" % i_, [128, 3, TT + 3], F32) for i_ in range(2)]
        H.Braw = [P.buf(), P.buf()]
        H.cv = [mk("cv%d" % w, TT) for w in range(3)]
        H.Bcv = [P.buf() for _ in range(3)]
        H.sq = mk("sq", TT, BF16)
        H.Bsq = P.buf()
        H.rs = mk("rs", TT)
        H.Brs = P.buf()
        H.qn = mk("qn", TT, BF16)
        H.kn = mk("kn", TT, BF16)
        H.Bqn, H.Bkn = P.buf(), P.buf()
        H.S32 = mk("S32", 128)
        H.Sb = [mk("Sb%d" % i, 128, BF16) for i in range(2)]
        H.BS32 = P.buf()
        H.BSb = [P.buf(), P.buf()]
        H.scur = 0
        H.osb = mk("osb", TT)
        H.Bosb = P.buf()
        H.t = {}
        H.B = {}
        for nm, dt in (("kbg", BF16), ("kdec", BF16), ("vb32", F32), ("vb16", BF16), ("gU", F32), ("Ds", F32), ("DT", F32),
                       ("X", BF16), ("XT", BF16), ("X2", BF16), ("XT2", BF16), ("N", BF16), ("N2", BF16),
                       ("AqkT", BF16), ("u32", F32), ("wT", BF16), ("vnew", BF16), ("aq", F32), ("o32", F32),
                       ("on32", F32), ("junk", F32)):
            H.t[nm] = mk("b_" + nm, 128, dt)
            H.B[nm] = P.buf()
        H.ss = mk("ss", 1)
        H.rstd = mk("rstd", 1)
        H.Bss = P.buf()
        heads.append(H)

    def head_tables(H):
        hh = H.hh
        T = H.tab
        bl = bas[:, (hh * 2) * NB:(hh * 2 + 1) * NB]
        al = bas[:, (hh * 2 + 1) * NB:(hh * 2 + 2) * NB]
        rw = [Bba, Bhp, BK, H.Btab]
        P.op("act", lambda e: e.activation(out=T["beta"][:], in_=bl, func=AF.Sigmoid), reads=rw, writes=[H.Btab])
        P.op("dve", lambda e: e.tensor_scalar(out=T["nbeta"][:], in0=T["beta"][:], scalar1=-1.0, scalar2=None, op0=ALU.mult),
             reads=rw, writes=[H.Btab])
        P.op("act", lambda e: e.activation(out=T["tmp"][:], in_=al, func=AF.Exp, bias=hps[:, hh * 2 + 1:hh * 2 + 2], scale=1.0),
             reads=rw, writes=[H.Btab])
        P.op("act", lambda e: e.activation(out=T["tmp"][:], in_=T["tmp"][:], func=AF.Ln, bias=one_c[:, 0:1], scale=1.0),
             reads=rw, writes=[H.Btab])
        P.op("act", lambda e: e.activation(out=H.ss[:], in_=hps[:, hh * 2:hh * 2 + 1], func=AF.Exp), reads=rw, writes=[H.Bss])
        P.op("dve", lambda e: e.tensor_scalar(out=T["g"][:], in0=T["tmp"][:], scalar1=H.ss[:, 0:1], scalar2=-1.0,
                                              op0=ALU.mult, op1=ALU.mult), reads=rw + [H.Bss], writes=[H.Btab])
        P.op("pe", lambda e: e.matmul(pbig[:, 0:NB], lhsT=K["gU"][:], rhs=T["g"][:], start=True, stop=True), reads=rw, writes=[Bpbig])
        P.op("pe", lambda e: e.matmul(pbig[:, 128:128 + NB], lhsT=K["gSame"][:], rhs=T["g"][:], start=True, stop=True), reads=rw, writes=[Bpbig])
        P.op("pe", lambda e: e.matmul(pbig[:, 256:256 + NB], lhsT=K["gH0"][:], rhs=T["g"][:], start=True, stop=True), reads=rw, writes=[Bpbig])
        P.op("pe", lambda e: e.matmul(pbig[:, 384:384 + NB], lhsT=K["gH1"][:], rhs=T["g"][:], start=True, stop=True), reads=rw, writes=[Bpbig])
        P.op("dve", lambda e: e.tensor_copy(out=T["gc"][:], in_=pbig[:, 0:NB]), reads=[Bpbig], writes=[H.Btab])
        P.op("act", lambda e: e.activation(out=T["egc"][:], in_=pbig[:, 0:NB], func=AF.Exp), reads=[Bpbig], writes=[H.Btab])
        P.op("dve", lambda e: e.tensor_tensor(out=T["tmp"][:], in0=pbig[:, 128:128 + NB], in1=T["gc"][:], op=ALU.subtract),
             reads=[Bpbig, H.Btab], writes=[H.Btab])
        P.op("act", lambda e: e.activation(out=T["ekd"][:], in_=T["tmp"][:], func=AF.Exp), reads=[H.Btab], writes=[H.Btab])
        P.op("act", lambda e: e.activation(out=T["eg0"][:], in_=pbig[:, 256:256 + NB], func=AF.Exp), reads=[Bpbig], writes=[H.Btab])
        P.op("act", lambda e: e.activation(out=T["eg1"][:], in_=pbig[:, 384:384 + NB], func=AF.Exp), reads=[Bpbig], writes=[H.Btab])
        P.op("dve", lambda e: e.tensor_tensor(out=T["bgc"][:], in0=T["beta"][:], in1=T["egc"][:], op=ALU.mult),
             reads=[H.Btab], writes=[H.Btab])
        P.op("dve", lambda e: e.memset(H.S32[:], 0.0), writes=[H.BS32])
        P.op("dve", lambda e: e.memset(H.Sb[0][:], 0.0), writes=[H.BSb[0]])
        P.op("dve", lambda e: e.memset(H.raw[0][:, :, 0:3], 0.0), writes=[H.Braw[0]])

    def load_raw(H, ti):
        hh = H.hh
        s_ = ti % 2
        c_ = (ti * TT) // (NTOK // 2)
        lo = ti * TT - c_ * (NTOK // 2)
        P.dma("sp", H.raw[s_][:, :, 3:TT + 3], qkvq[c_][hh, :, :, lo:lo + TT].rearrange("w p t -> p w t"), reads=[By[c_]], writes=[H.Braw[s_]])

    def carry(H, ti):
        s_ = ti % 2
        if ti > 0:
            P.op("dve", lambda e: e.tensor_copy(out=H.raw[s_][:, :, 0:3], in_=H.raw[1 - s_][:, :, TT:TT + 3]), reads=[H.Braw[1 - s_]], writes=[H.Braw[s_]])

    def conv_phase(H, ti):
        hh = H.hh
        s_ = ti % 2
        raw, Braw = H.raw[s_], H.Braw[s_]
        for w in range(3):
            cv, Bc = H.cv[w], H.Bcv[w]
            wcl = [wcs[:, hh * 12 + w * 4 + k:hh * 12 + w * 4 + k + 1] for k in range(4)]
            P.op("dve", lambda e, cv=cv, w=w, wcl=wcl: e.tensor_scalar(out=cv[:], in0=raw[:, w, 0:TT], scalar1=wcl[0], scalar2=None, op0=ALU.mult),
                 reads=[Braw, Bw], writes=[Bc])
            for k in (1, 2, 3):
                P.op("dve", lambda e, cv=cv, k=k, w=w, wcl=wcl: e.scalar_tensor_tensor(out=cv[:], in0=raw[:, w, k:k + TT], scalar=wcl[k], in1=cv[:],
                                                                                    op0=ALU.mult, op1=ALU.add), reads=[Braw, Bw, Bc], writes=[Bc])
            P.op("act", lambda e, cv=cv: e.activation(out=cv[:], in_=cv[:], func=AF.Silu), reads=[Bc], writes=[Bc])
        for w, dst, Bd, sc in ((0, H.qn, H.Bqn, 1.0 / math.sqrt(128.0)), (1, H.kn, H.Bkn, 1.0)):
            cv, Bc = H.cv[w], H.Bcv[w]
            P.op("act", lambda e, cv=cv: e.activation(out=H.sq[:], in_=cv[:], func=AF.Square), reads=[Bc], writes=[H.Bsq])
            P.op("pe", lambda e: e.matmul(pbig[:], lhsT=ones_b[:], rhs=H.sq[:], start=True, stop=True), reads=[H.Bsq, BK], writes=[Bpbig])
            P.op("act", lambda e: e.activation(out=H.rs[:], in_=pbig[:], func=AF.Ln, bias=epsc[:, 0:1], scale=1.0), reads=[Bpbig, BK], writes=[H.Brs])
            P.op("act", lambda e: e.activation(out=H.rs[:], in_=H.rs[:], func=AF.Exp, scale=-0.5), reads=[H.Brs], writes=[H.Brs])
            P.op("dve", lambda e, cv=cv, dst=dst, sc=sc: e.scalar_tensor_tensor(out=dst[:], in0=cv[:], scalar=sc, in1=H.rs[:],
                                                                              op0=ALU.mult, op1=ALU.mult), reads=[Bc, H.Brs], writes=[Bd])

    def evac(eng, dst, Bdst, src, Bsrc, extra_reads=()):
        if eng == "act":
            P.op("act", lambda e: e.activation(out=dst, in_=src, func=AF.Copy), reads=[Bsrc] + list(extra_reads), writes=[Bdst])
        else:
            P.op("dve", lambda e: e.tensor_copy(out=dst, in_=src), reads=[Bsrc] + list(extra_reads), writes=[Bdst])

    def pre_scan(H, blk, bi):
        T, t, B = H.tab, H.t, H.B
        cur_head[0] = H.hh
        cs = slice(bi * 128, (bi + 1) * 128)
        col = lambda nm: T[nm][:, blk:blk + 1]
        kn, qn = H.kn[:, cs], H.qn[:, cs]
        pk, Bpk = psb()
        P.op("pe", lambda e: e.transpose(pk, kn, ident_b[:]), reads=[H.Bkn, BK], writes=[Bpk])
        yield
        P.op("act", lambda e: e.activation(out=t["kbg"][:], in_=pk, func=AF.Copy, scale=col("bgc")), reads=[Bpk, H.Btab], writes=[B["kbg"]])
        yield
        P.op("dve", lambda e: e.tensor_scalar(out=t["kdec"][:], in0=pk, scalar1=col("ekd"), scalar2=None, op0=ALU.mult),
             reads=[Bpk, H.Btab], writes=[B["kdec"]])
        yield
        pv, Bpv = psf()
        P.op("pe", lambda e: e.transpose(pv, H.cv[2][:, cs], K["gI"][:]), reads=[H.Bcv[2], BK], writes=[Bpv])
        yield
        P.op("dve", lambda e: e.tensor_scalar(out=t["vb32"][:], in0=pv, scalar1=col("beta"), scalar2=None, op0=ALU.mult),
             reads=[Bpv, H.Btab], writes=[B["vb32"]])
        yield
        P.op("act", lambda e: e.activation(out=t["vb16"][:], in_=pv, func=AF.Copy, scale=col("beta")), reads=[Bpv, H.Btab], writes=[B["vb16"]])
        yield
        P.op("act", lambda e: e.activation(out=t["gU"][:], in_=K["gU"][:], func=AF.Copy, scale=col("g")),
             reads=[BK, H.Btab], writes=[B["gU"]])
        yield
        pd, Bpd = psf()
        P.op("pe", lambda e: e.matmul(pd, lhsT=t["gU"][:], rhs=K["gSL"][:], start=True, stop=False), reads=[B["gU"], BK], writes=[Bpd])
        P.op("pe", lambda e: e.matmul(pd, lhsT=K["gI"][:], rhs=K["gMS"][:], start=False, stop=True), reads=[BK], writes=[Bpd])
        yield
        P.op("act", lambda e: e.activation(out=t["Ds"][:], in_=pd, func=AF.Exp), reads=[Bpd], writes=[B["Ds"]])
        yield
        pdt, Bpdt = psf()
        P.op("pe", lambda e: e.matmul(pdt, lhsT=K["gSL"][:], rhs=t["gU"][:], start=True, stop=False), reads=[B["gU"], BK], writes=[Bpdt])
        P.op("pe", lambda e: e.matmul(pdt, lhsT=K["gI"][:], rhs=K["gMT"][:], start=False, stop=True), reads=[BK], writes=[Bpdt])
        yield
        P.op("act", lambda e: e.activation(out=t["DT"][:], in_=pdt, func=AF.Exp), reads=[Bpdt], writes=[B["DT"]])
        yield
        pg, Bpg = psf()
        P.op("pe", lambda e: e.matmul(pg, lhsT=kn, rhs=kn, start=True, stop=True), reads=[H.Bkn], writes=[Bpg])
        yield
        P.op("dve", lambda e: e.scalar_tensor_tensor(out=t["X"][:], in0=pg, scalar=col("nbeta"), in1=t["Ds"][:], op0=ALU.mult, op1=ALU.mult),
             reads=[Bpg, H.Btab, B["Ds"]], writes=[B["X"]])
        yield
        pq, Bpq = psf()
        P.op("pe", lambda e: e.matmul(pq, lhsT=kn, rhs=qn, start=True, stop=True), reads=[H.Bkn, H.Bqn], writes=[Bpq])
        yield
        P.op("dve", lambda e: e.tensor_tensor(out=t["AqkT"][:], in0=pq, in1=t["DT"][:], op=ALU.mult), reads=[Bpq, B["DT"]], writes=[B["AqkT"]])
        yield
        px, Bpx = psb()
        P.op("pe", lambda e: e.transpose(px, t["X"][:], ident_b[:]), reads=[B["X"], BK], writes=[Bpx])
        yield
        evac("act", t["XT"][:], B["XT"], px, Bpx)
        yield
        evac("dve", t["N"][:], B["N"], px, Bpx)
        yield
        X, XT, X2, XT2, N, N2 = "X", "XT", "X2", "XT2", "N", "N2"
        for lvl in range(5):
            p1, Bp1 = psf()
            P.op("pe", lambda e, p1=p1, X=X, XT=XT: e.matmul(p1, lhsT=t[XT][:], rhs=t[X][:], start=True, stop=True),
                 reads=[B[X], B[XT]], writes=[Bp1])
            yield
            p2, Bp2 = psf()
            P.op("pe", lambda e, p2=p2, X=X, XT=XT: e.matmul(p2, lhsT=t[X][:], rhs=t[XT][:], start=True, stop=True),
                 reads=[B[X], B[XT]], writes=[Bp2])
            yield
            evac("act", t[X2][:], B[X2], p1, Bp1)
            yield
            evac("dve", t[XT2][:], B[XT2], p2, Bp2)
            yield
            p3, Bp3 = psf()
            P.op("pe", lambda e, p3=p3, X2=X2, N=N: e.matmul(p3, lhsT=t[X2][:], rhs=t[N][:], start=True, stop=False),
                 reads=[B[X2], B[N]], writes=[Bp3])
            P.op("pe", lambda e, p3=p3, N=N: e.matmul(p3, lhsT=ident_b[:], rhs=t[N][:], start=False, stop=False),
                 reads=[BK, B[N]], writes=[Bp3])
            P.op("pe", lambda e, p3=p3, XT2=XT2: e.matmul(p3, lhsT=ident_b[:], rhs=t[XT2][:], start=False, stop=True),
                 reads=[BK, B[XT2]], writes=[Bp3])
            yield
            evac("act" if lvl % 2 else "dve", t[N2][:], B[N2], p3, Bp3)
            yield
            X, X2 = X2, X
            XT, XT2 = XT2, XT
            N, N2 = N2, N
        pu, Bpu = psf()
        P.op("pe", lambda e, N=N: e.matmul(pu, lhsT=t[N][:], rhs=t["vb16"][:], start=True, stop=True), reads=[B[N], B["vb16"]], writes=[Bpu])
        yield
        P.op("dve", lambda e: e.tensor_tensor(out=t["u32"][:], in0=pu, in1=t["vb32"][:], op=ALU.add), reads=[Bpu, B["vb32"]], writes=[B["u32"]])
        yield
        pw, Bpw = psf()
        P.op("pe", lambda e, N=N: e.matmul(pw, lhsT=t["kbg"][:], rhs=t[N][:], start=True, stop=False), reads=[B["kbg"], B[N]], writes=[Bpw])
        P.op("pe", lambda e: e.matmul(pw, lhsT=t["kbg"][:], rhs=ident_b[:], start=False, stop=True), reads=[B["kbg"], BK], writes=[Bpw])
        yield
        evac("act", t["wT"][:], B["wT"], pw, Bpw)
        yield

    def scan_steps(H, blk, bi):
        T, t, B = H.tab, H.t, H.B
        cs0 = bi * 128
        steps = []
        pws, Bpws = psf()
        pqs, Bpqs = psf()
        for c in (0, 1):
            r = slice(c * 64, (c + 1) * 64)
            egl = T["eg%d" % c][:, blk:blk + 1]

            def s1(c=c, r=r):
                cur = H.scur
                P.op("pe", lambda e: e.matmul(pws[r, :], lhsT=t["wT"][:, r], rhs=H.Sb[cur][:], start=True, stop=True),
                     reads=[B["wT"], H.BSb[cur]], writes=[Bpws])
                P.op("pe", lambda e: e.matmul(pqs[r, :], lhsT=H.qn[:, cs0 + c * 64:cs0 + (c + 1) * 64], rhs=H.Sb[cur][:], start=True, stop=True),
                     reads=[H.Bqn, H.BSb[cur]], writes=[Bpqs])

            def s2(c=c, r=r):
                P.op("dve", lambda e: e.tensor_tensor(out=t["vnew"][r, :], in0=t["u32"][r, :], in1=pws[r, :], op=ALU.subtract),
                     reads=[B["u32"], Bpws], writes=[B["vnew"]])

            def s3(c=c, r=r, egl=egl):
                cur = H.scur
                nxt = 1 - cur
                pds, Bpds = psf()
                P.op("pe", lambda e: e.matmul(pds, lhsT=t["kdec"][r, :], rhs=t["vnew"][r, :], start=True, stop=True),
                     reads=[B["kdec"], B["vnew"]], writes=[Bpds])
                P.op("dve", lambda e: e.scalar_tensor_tensor(out=H.Sb[nxt][:], in0=H.S32[:], scalar=egl, in1=pds, op0=ALU.mult, op1=ALU.add),
                     reads=[H.BS32, H.Btab, Bpds], writes=[H.BSb[nxt]])
                P.op("dve", lambda e: e.scalar_tensor_tensor(out=H.S32[:], in0=H.S32[:], scalar=egl, in1=pds, op0=ALU.mult, op1=ALU.add),
                     reads=[H.BS32, H.Btab, Bpds], writes=[H.BS32])
                H.scur = nxt

            steps += [s1, s2, s3]

        def fin():
            pa, Bpa = psf()
            P.op("pe", lambda e: e.matmul(pa, lhsT=t["AqkT"][:], rhs=t["vnew"][:], start=True, stop=True), reads=[B["AqkT"], B["vnew"]], writes=[Bpa])
            evac("act", t["aq"][:], B["aq"], pa, Bpa)
            P.op("dve", lambda e: e.scalar_tensor_tensor(out=t["o32"][:], in0=pqs, scalar=T["egc"][:, blk:blk + 1], in1=t["aq"][:],
                                                         op0=ALU.mult, op1=ALU.add), reads=[Bpqs, H.Btab, B["aq"]], writes=[B["o32"]])
            P.op("act", lambda e: e.activation(out=t["junk"][:], in_=t["o32"][:], func=AF.Square, accum_out=H.ss[:, 0:1]),
                 reads=[B["o32"]], writes=[B["junk"], H.Bss])
            P.op("act", lambda e: e.activation(out=H.rstd[:], in_=H.ss[:], func=AF.Sqrt, scale=1.0 / 128.0, bias=epsc[:, 0:1]),
                 reads=[H.Bss, BK], writes=[H.Bss])
            P.op("dve", lambda e: e.reciprocal(out=H.rstd[:], in_=H.rstd[:]), reads=[H.Bss], writes=[H.Bss])
            P.op("dve", lambda e: e.scalar_tensor_tensor(out=t["on32"][:], in0=t["o32"][:], scalar=H.rstd[:, 0:1], in1=onws[:],
                                                         op0=ALU.mult, op1=ALU.mult), reads=[B["o32"], H.Bss, Bon], writes=[B["on32"]])
            po, Bpo = psf()
            P.op("pe", lambda e: e.transpose(po, t["on32"][:], K["gI"][:]), reads=[B["on32"], BK], writes=[Bpo])
            evac("act", H.osb[:, bi * 128:(bi + 1) * 128], H.Bosb, po, Bpo)

        steps.append(fin)
        return steps

    dbgn = []
    if dbg:
        dbgT = nc.dram_tensor("dbg", [128, 40 * 128], F32, kind="ExternalOutput").ap()
        dstg = P.sb("dstg", [128, 128], F32)
        Bdstg = P.buf()

        def dump(name, ap, Bs, w=128):
            i = len(dbgn)
            dbgn.append(name)
            P.op("dve", lambda e: e.tensor_copy(out=dstg[:, 0:w], in_=ap), reads=Bs, writes=[Bdstg])
            P.dma("sp", dbgT[:, i * 128:i * 128 + w], dstg[:, 0:w], reads=[Bdstg], store=True, final=True)
    P.dbgn = dbgn
    for H in heads:
        head_tables(H)
    for H in heads:
        load_raw(H, 0)
    for ti in range(NT):
        P.fill_step(1, q="sp")
        P.tag("ph5_gdnB_q%d" % (ti // 8))
        for H in heads:
            carry(H, ti)
        if ti + 1 < NT:
            for H in heads:
                load_raw(H, ti + 1)
        for H in heads:
            conv_phase(H, ti)
        for bi in range(TT // 128):
            blk = ti * 4 + bi
            gens = [pre_scan(H, blk, bi) for H in heads]
            gen_head = {id(gn): H.hh for gn, H in zip(gens, heads)}
            alive = list(gens)
            while alive:
                for gi_, gen in enumerate(list(alive)):
                    try:
                        cur_head[0] = gen_head[id(gen)]
                        next(gen)
                    except StopIteration:
                        alive.remove(gen)
            lists = []
            for H in heads:
                cur_head[0] = H.hh
                lists.append(scan_steps(H, blk, bi))
            for k in range(len(lists[0])):
                for hi_, L in enumerate(lists):
                    cur_head[0] = heads[hi_].hh
                    L[k]()
            if dbg and blk == 0:
                H = heads[0]
                for nm in ("g", "gc", "beta", "egc", "ekd", "eg0", "eg1", "bgc"):
                    dump("t_" + nm, H.tab[nm][:, 0:NB], [H.Btab], w=NB)
                dump("kn", H.kn[:, 0:128], [H.Bkn])
                dump("qn", H.qn[:, 0:128], [H.Bqn])
                dump("v", H.cv[2][:, 0:128], [H.Bcv[2]])
                for nm in H.t:
                    dump(nm, H.t[nm][:], [H.B[nm]])
                dump("S32", H.S32[:], [H.BS32])
        for H in heads:
            t0 = ti * TT
            r0_ = (t0 // NTOK) * 256 + H.hh * 128
            sub_ = (t0 % NTOK) // (NTOK // 8)
            P.dma("sp", xo[r0_:r0_ + 128, (t0 % NTOK):(t0 % NTOK) + TT], H.osb[:], reads=[H.Bosb], writes=[io["lo_bufs"][sub_]], store=True, chan=H.Bosb)
        if t0 >= 3 * NTOK and ((t0 + TT) % (NTOK // 8)) == 0:
            io["sub_done"]((t0 % NTOK) // (NTOK // 8))
    return P


import math
import numpy as np

HL = 128
NEG = -30000.0
TWO_PI = 2.0 * math.pi
C1 = 6.28125
C2 = TWO_PI - C1
MAGIC = 12582912.0
QA, QR, KA, KR, VV, ZZ, NCOL = 0, 1024, 2048, 2560, 3072, 3328, 4352


def swa_consts():
    k = np.arange(128)
    mprev = np.where(k[:, None] > k[None, :], 0.0, NEG).astype(np.float32)
    mcur = np.where(k[:, None] <= k[None, :], 0.0, NEG).astype(np.float32)
    mask = np.concatenate([mprev, mprev, mcur, mcur], axis=1)
    invf = (np.float32(10000.0) ** (-(np.arange(32, dtype=np.float32)) / np.float32(32))).astype(np.float32)
    invf = np.tile(invf, 4).reshape(128, 1)
    onesP = np.zeros((128, 2, 128), np.float32)
    onesP[:, 0, 0:64] = 1.0
    onesP[:, 1, 64:128] = 1.0
    return {"sMask": mask, "sInvf": invf, "sI": np.eye(128, dtype=np.float32), "sOnesP": onesP.reshape(128, 256)}


def build_swa(P, io, NTOK=4096):
    NT = NTOK // TT
    NC_ = HL + NTOK
    fmv = lambda a: a.rearrange("(c p) t -> p c t", p=128)
    x3T = fmv(io["x3T"])
    xH = fmv(io["yh"])
    posd, nw, fnw, w_in, w_out = io["swa_pos"], io["swa_nw"], io["swa_fnw"], io["swa_w_in"], io["swa_w_out"]
    skd, hmd, cM, cF, cI, cO = io["swa_sk"], io["swa_hmask"], io["sMask"], io["sInvf"], io["sI"], io["sOnesP"]
    outT = fmv(io["outT"])

    BK = P.buf()
    ones = P.sb("ones", [128, 128], BF16)
    epsc = P.sb("epsc", [128, 1], F32)
    P.op("dve", lambda e: e.memset(ones[:], 1.0), writes=[BK])
    P.op("dve", lambda e: e.memset(epsc[:], 1e-6), writes=[BK])
    norm = Norm(P, ones, BK)
    norm.epsc = epsc
    stg = [P.sb("stg%d" % i, [128, 1024], F32) for i in range(2)]
    Bstg = [P.buf(), P.buf()]
    nws, Bnw = load_small(P, nw[:, :], [128, KC], "nws")
    fnws, Bfnw = load_small(P, fnw[:, :], [128, KC], "fnws")
    sks, Bsk = load_small(P, skd[:, :], [128, 8], "sks")
    hms, Bhm = load_small(P, hmd[:, :], [128, 1], "hms")
    invf, Binvf = load_small(P, cF[:, :], [128, 1], "invf")
    esk = P.sb("esk", [128, 8], F32)
    P.op("act", lambda e: e.activation(out=esk[:], in_=sks[:], func=AF.Exp), reads=[Bsk], writes=[Bsk])
    mask_b = P.sb("mask_b", [128, 512], BF16)
    ident_b = P.sb("ident_b", [128, 128], BF16)
    onesP = P.sb("onesP", [128, 2, 128], BF16)
    P.dma("sp", stg[0][:, 0:512], cM[:, :], writes=[Bstg[0]])
    P.op("dve", lambda e: e.tensor_copy(out=mask_b[:], in_=stg[0][:, 0:512]), reads=[Bstg[0]], writes=[BK])
    P.dma("sp", stg[1][:, 0:128], cI[:, :], writes=[Bstg[1]])
    P.op("dve", lambda e: e.tensor_copy(out=ident_b[:], in_=stg[1][:, 0:128]), reads=[Bstg[1]], writes=[BK])
    P.dma("sp", stg[0][:, 0:256], cO[:, :], writes=[Bstg[0]])
    P.op("dve", lambda e: e.tensor_copy(out=onesP[:].rearrange("p a b -> p (a b)"), in_=stg[0][:, 0:256]), reads=[Bstg[0]], writes=[BK])

    wib = P.sb("wib", [128, KC, NCOL], BF16)
    Bwib = P.buf()
    wob = P.sb("wob", [128, KC, D], BF16)
    Bwob = P.buf()
    nld = [0]

    def stage_load(src, ncols):
        s = nld[0] % 2
        nld[0] += 1
        P.dma("sp", stg[s][:, 0:ncols], src, writes=[Bstg[s]])
        return stg[s], Bstg[s]

    def cast(dst, src, Bs, kc, sign=1.0, eng=None):
        eng = eng or ("dve", "pool")[nld[0] % 2]
        P.op(eng, lambda e: e.tensor_scalar(out=dst, in0=src, scalar1=nws[:, kc:kc + 1], scalar2=sign, op0=ALU.mult, op1=ALU.mult),
             reads=[Bs, Bnw], writes=[Bwib])

    for kc in range(KC):
        rows = slice(kc * 128, (kc + 1) * 128)
        s_, Bs = stage_load(w_in[rows, 0:1024], 1024)
        cast(wib[:, kc, QA:QA + 1024], s_[:, 0:1024], Bs, kc)
        sv = s_[:, 0:1024].rearrange("p (h t d) -> p h t d", t=2, d=32)
        dv = wib[:, kc, QR:QR + 1024].rearrange("p (h t d) -> p h t d", t=2, d=32)
        cast(dv[:, :, 0, :], sv[:, :, 1, :], Bs, kc, sign=-1.0, eng="dve")
        cast(dv[:, :, 1, :], sv[:, :, 0, :], Bs, kc, eng="pool")
        s_, Bs = stage_load(w_in[rows, 1024:1536], 512)
        ksrc = s_[:, 0:256].rearrange("p (g d) -> p g d", d=64)
        ksr2 = s_[:, 0:256].rearrange("p (g t d) -> p g t d", t=2, d=32)
        kad = wib[:, kc, KA:KA + 512].rearrange("p (g u d) -> p g u d", u=2, d=64)
        krd = wib[:, kc, KR:KR + 512].rearrange("p (g u t d) -> p g u t d", u=2, t=2, d=32)
        for u in range(2):
            cast(kad[:, :, u, :], ksrc, Bs, kc, eng=("dve", "pool")[u])
            cast(krd[:, :, u, 0, :], ksr2[:, :, 1, :], Bs, kc, sign=-1.0, eng="dve")
            cast(krd[:, :, u, 1, :], ksr2[:, :, 0, :], Bs, kc, eng="pool")
        cast(wib[:, kc, VV:VV + 256], s_[:, 256:512], Bs, kc, eng="dve")
        s_, Bs = stage_load(w_in[rows, 1536:2560], 1024)
        cast(wib[:, kc, ZZ:ZZ + 1024], s_[:, 0:1024], Bs, kc)
    for kc in range(KC):
        s_, Bs = stage_load(w_out[kc * 128:(kc + 1) * 128, :], 1024)
        P.op(("dve", "pool")[kc % 2], lambda e, kc=kc, s_=s_: e.tensor_copy(out=wob[:, kc, :], in_=s_[:, 0:1024]), reads=[Bs], writes=[Bwob])

    xs = P.sb("xs", [128, KC, TT], F32)
    Bxs = P.buf()
    hT = P.sb("hT", [128, KC, TT], BF16)
    BhT = P.buf()
    rstd = P.sb("rstd", [128, TT], F32)
    Brstd = P.buf()
    posi = P.sb("posi", [128, TT], I32)
    ang = P.sb("ang", [128, TT], F32)
    kf = P.sb("kf", [128, TT], F32)
    cosT = P.sb("cosT", [128, TT], F32)
    sinT = P.sb("sinT", [128, TT], F32)
    Brope = P.buf()
    Bcs = P.buf()
    QP = P.sb("QP", [128, 8, TT], BF16)
    BQP = P.buf()
    KP = P.sb("KP", [128, 4, HL + TT], BF16)
    BKP = P.buf()
    Vp = P.sb("Vp", [128, 5, 4, 2, 128], BF16)
    BVp = P.buf()
    P.op("pool", lambda e: e.memset(Vp[:].rearrange("p a b c d -> p (a b c d)"), 0.0), writes=[BVp])
    szs = P.sb("szs", [128, 8, TT], F32)
    Bszs = P.buf()
    og = P.sb("og", [128, 8, TT], BF16)
    Bog = P.buf()
    t1r = Rot(P, "t1r", [128, TT], F32, 2)
    ptr = Rot(P, "ptr", [128, TT], BF16, 4)
    smr = Rot(P, "smr", [128, 128], F32, 4)
    f32r = Rot(P, "f32r", [128, TT], F32, 2)
    pp = [P.ps("pp%d" % i, [128, TT]) for i in range(7)]
    Bpp = [P.buf() for _ in range(7)]
    ppi = [0]

    def psum():
        k = ppi[0] % 7
        ppi[0] += 1
        return pp[k], Bpp[k]

    def fm_proj(col0, n):
        ps, Bps = psum()
        for kc in range(KC):
            P.op("pe", lambda e, kc=kc: e.matmul(ps[:, 0:n], lhsT=wib[:, kc, col0:col0 + 128], rhs=hT[:, kc, 0:n],
                                                  start=(kc == 0), stop=(kc == KC - 1)), reads=[Bwib, BhT], writes=[Bps])
        return ps, Bps

    def rope_tables(c0, n):
        P.dma("sp", posi[:, 0:n], posd[:, c0:c0 + n], writes=[Brope])
        P.op("dve", lambda e: e.tensor_copy(out=ang[:, 0:n], in_=posi[:, 0:n]), reads=[Brope], writes=[Brope])
        P.op("dve", lambda e: e.tensor_scalar(out=ang[:, 0:n], in0=ang[:, 0:n], scalar1=invf[:, 0:1], scalar2=None, op0=ALU.mult),
             reads=[Brope, Binvf], writes=[Brope])
        P.op("dve", lambda e: e.tensor_scalar(out=kf[:, 0:n], in0=ang[:, 0:n], scalar1=1.0 / TWO_PI, scalar2=MAGIC, op0=ALU.mult, op1=ALU.add),
             reads=[Brope], writes=[Brope])
        P.op("dve", lambda e: e.tensor_scalar(out=kf[:, 0:n], in0=kf[:, 0:n], scalar1=MAGIC, scalar2=None, op0=ALU.subtract),
             reads=[Brope], writes=[Brope])
        P.op("dve", lambda e: e.scalar_tensor_tensor(out=ang[:, 0:n], in0=kf[:, 0:n], scalar=-C1, in1=ang[:, 0:n], op0=ALU.mult, op1=ALU.add),
             reads=[Brope], writes=[Brope])
        P.op("dve", lambda e: e.scalar_tensor_tensor(out=ang[:, 0:n], in0=kf[:, 0:n], scalar=-C2, in1=ang[:, 0:n], op0=ALU.mult, op1=ALU.add),
             reads=[Brope], writes=[Brope])
        P.op("dve", lambda e: e.tensor_scalar(out=ang[:, 0:n], in0=ang[:, 0:n], scalar1=3.14159, scalar2=-3.14159, op0=ALU.min, op1=ALU.max),
             reads=[Brope], writes=[Brope])
        P.op("act", lambda e: e.activation(out=sinT[:, 0:n], in_=ang[:, 0:n], func=AF.Sin), reads=[Brope], writes=[Bcs])
        P.op("act", lambda e: e.activation(out=kf[:, 0:n], in_=ang[:, 0:n], func=AF.Sin, scale=0.5), reads=[Brope], writes=[Brope])
        P.op("dve", lambda e: e.tensor_tensor(out=kf[:, 0:n], in0=kf[:, 0:n], in1=kf[:, 0:n], op=ALU.mult), reads=[Brope], writes=[Brope])
        P.op("dve", lambda e: e.tensor_scalar(out=cosT[:, 0:n], in0=kf[:, 0:n], scalar1=-2.0, scalar2=1.0, op0=ALU.mult, op1=ALU.add),
             reads=[Brope], writes=[Bcs])

    def roped(colA, colR, n, dst, Bdst):
        psA, BA = fm_proj(colA, n)
        psR, BR = fm_proj(colR, n)
        t1, Bt1 = t1r.next()
        P.op("dve", lambda e: e.tensor_tensor(out=t1[:, 0:n], in0=psA[:, 0:n], in1=cosT[:, 0:n], op=ALU.mult), reads=[BA, Bcs], writes=[Bt1])
        t2, Bt2 = t1r.next()
        P.op("dve", lambda e: e.tensor_tensor(out=t2[:, 0:n], in0=psR[:, 0:n], in1=sinT[:, 0:n], op=ALU.mult), reads=[BR, Bcs], writes=[Bt2])
        P.op("pool", lambda e: e.tensor_tensor(out=dst, in0=t1[:, 0:n], in1=t2[:, 0:n], op=ALU.add), reads=[Bt1, Bt2], writes=[Bdst])

    def kv_part(c0, n, koff, vslot0):
        for g in range(4):
            roped(KA + g * 128, KR + g * 128, n, KP[:, g, koff:koff + n], BKP)
        for blk in range(n // 128):
            ps, Bps = psum()
            for kc in range(KC):
                P.op("pe", lambda e, kc=kc, blk=blk, ps=ps: e.matmul(ps[:, 0:256], lhsT=hT[:, kc, blk * 128:(blk + 1) * 128], rhs=wib[:, kc, VV:VV + 256],
                                                              start=(kc == 0), stop=(kc == KC - 1)), reads=[Bwib, BhT], writes=[Bps])
            src = ps[:, 0:256].rearrange("p (g d) -> p g d", d=64)
            P.op("dve", lambda e, blk=blk, src=src: e.tensor_copy(out=Vp[:, vslot0 + blk, :, 0, 0:64], in_=src), reads=[Bps], writes=[BVp])
            P.op("dve", lambda e, blk=blk, src=src: e.tensor_copy(out=Vp[:, vslot0 + blk, :, 1, 64:128], in_=src), reads=[Bps], writes=[BVp])

    def load_and_norm(c0, n):
        if c0 == 0:
            P.dma("sp", xs[:, :, 0:n], xH[:, :, 0:n], writes=[Bxs])
        else:
            P.dma("sp", xs[:, :, 0:n], x3T[:, :, c0 - HL:c0 - HL + n], writes=[Bxs])
        norm.run(xs, Bxs, n, rstd, Brstd)
        for kc in range(KC):
            eng = "dve" if kc % 2 == 0 else "pool"
            P.op(eng, lambda e, kc=kc: e.tensor_tensor(out=hT[:, kc, 0:n], in0=xs[:, kc, 0:n], in1=rstd[:, 0:n], op=ALU.mult),
                 reads=[Bxs, Brstd], writes=[BhT])

    scale = 1.0 / math.sqrt(64.0)

    load_and_norm(0, HL)
    rope_tables(0, HL)
    kv_part(0, HL, 0, 0)

    for i in range(NT):
        c0 = HL + i * TT
        load_and_norm(c0, TT)
        rope_tables(c0, TT)
        kv_part(c0, TT, HL, 1)
        for p in range(8):
            roped(QA + p * 128, QR + p * 128, TT, QP[:, p, :], BQP)
        for c in range(8):
            ps, Bps = fm_proj(ZZ + c * 128, TT)
            P.op("act", lambda e, c=c, ps=ps: e.activation(out=szs[:, c, :], in_=ps[:], func=AF.Silu), reads=[Bps], writes=[Bszs])
        def attn(i, n, g):
            qc = slice(n * 128, (n + 1) * 128)
            kprev = slice(n * 128, (n + 1) * 128)
            kcur = slice((n + 1) * 128, (n + 2) * 128)
            pts = []
            for half in range(2):
                pr = slice(half * 64, half * 64 + 64)
                pst, Bpst = psum()
                rhs = QP[pr, 2 * g:2 * g + 2, qc]
                P.op("pe", lambda e, pst=pst, pr=pr, rhs=rhs: e.matmul(pst[:, 0:256], lhsT=KP[pr, g, kprev], rhs=rhs, start=True, stop=False),
                     reads=[BKP, BQP], writes=[Bpst])
                P.op("pe", lambda e, pst=pst, pr=pr, rhs=rhs: e.matmul(pst[:, 256:512], lhsT=KP[pr, g, kcur], rhs=rhs, start=False, stop=False),
                     reads=[BKP, BQP], writes=[Bpst])
                P.op("pe", lambda e, pst=pst: e.matmul(pst[:, :], lhsT=ident_b[:], rhs=mask_b[:], start=False, stop=True),
                     reads=[BK], writes=[Bpst])
                pt, Bpt = ptr.next()
                if i == 0 and n == 0:
                    P.op("act", lambda e, pt=pt, pst=pst: e.activation(out=pt[:, 0:256], in_=pst[:, 0:256], func=AF.Exp, scale=scale, bias=hms[:, 0:1]),
                         reads=[Bpst, Bhm], writes=[Bpt])
                    P.op("act", lambda e, pt=pt, pst=pst: e.activation(out=pt[:, 256:512], in_=pst[:, 256:512], func=AF.Exp, scale=scale),
                         reads=[Bpst], writes=[Bpt])
                else:
                    P.op("act", lambda e, pt=pt, pst=pst: e.activation(out=pt[:], in_=pst[:], func=AF.Exp, scale=scale), reads=[Bpst], writes=[Bpt])
                pts.append((pt, Bpt))
            po, Bpo = psum()
            prs, Bprs = psum()
            seq = [(0, n, slice(0, 256)), (0, n + 1, slice(256, 512)), (1, n, slice(0, 256)), (1, n + 1, slice(256, 512))]
            for idx, (half, vs_, cols) in enumerate(seq):
                pt, Bpt = pts[half]
                P.op("pe", lambda e, half=half, vs_=vs_, cols=cols, pt=pt, idx=idx, po=po: e.matmul(
                    po[:, 0:256], lhsT=Vp[:, vs_, g, half, :], rhs=pt[:, cols], start=(idx == 0), stop=(idx == 3)),
                    reads=[BVp, Bpt], writes=[Bpo])
            for idx, (half, vs_, cols) in enumerate(seq):
                pt, Bpt = pts[half]
                P.op("pe", lambda e, half=half, cols=cols, pt=pt, idx=idx, prs=prs: e.matmul(
                    prs[:, 0:256], lhsT=onesP[:, half, :], rhs=pt[:, cols], start=(idx == 0), stop=(idx == 3)),
                    reads=[BK, Bpt], writes=[Bprs])
            for c in range(2):
                pair = 2 * g + c
                cc = slice(c * 128, (c + 1) * 128)
                sm, Bsm = smr.next()
                P.op("act", lambda e, sm=sm, prs=prs, cc=cc, pair=pair: e.activation(out=sm[:], in_=prs[:, cc], func=AF.Ln, bias=esk[:, pair:pair + 1], scale=1.0),
                     reads=[Bprs, Bsk], writes=[Bsm])
                P.op("act", lambda e, sm=sm: e.activation(out=sm[:], in_=sm[:], func=AF.Exp, scale=-1.0), reads=[Bsm], writes=[Bsm])
                sm2, Bsm2 = smr.next()
                P.op("dve", lambda e, sm=sm, sm2=sm2, po=po, cc=cc: e.tensor_tensor(out=sm2[:], in0=po[:, cc], in1=sm[:], op=ALU.mult),
                     reads=[Bpo, Bsm], writes=[Bsm2])
                P.op("pool", lambda e, sm2=sm2, pair=pair: e.tensor_tensor(out=og[:, pair, qc], in0=sm2[:], in1=szs[:, pair, qc], op=ALU.mult),
                     reads=[Bsm2, Bszs], writes=[Bog])

        for n in range(4):
            for g in range(4):
                attn(i, n, g)
        P.op("pool", lambda e: e.tensor_copy(out=KP[:, :, 0:HL], in_=KP[:, :, TT:TT + HL]), reads=[BKP], writes=[BKP])
        P.op("pool", lambda e: e.tensor_copy(out=Vp[:, 0].rearrange("p a b c -> p (a b c)"), in_=Vp[:, 4].rearrange("p a b c -> p (a b c)")),
             reads=[BVp], writes=[BVp])
        for dc in range(KC):
            ps, Bps = psum()
            for pr_ in range(8):
                P.op("pe", lambda e, pr_=pr_, dc=dc, ps=ps: e.matmul(ps[:], lhsT=wob[:, pr_, dc * 128:(dc + 1) * 128], rhs=og[:, pr_, :],
                                                                    start=(pr_ == 0), stop=(pr_ == 7)), reads=[Bwob, Bog], writes=[Bps])
            P.op("dve", lambda e, dc=dc, ps=ps: e.tensor_tensor(out=xs[:, dc, :], in0=ps[:], in1=xs[:, dc, :], op=ALU.add),
                 reads=[Bps, Bxs], writes=[Bxs])
        norm.run(xs, Bxs, TT, rstd, Brstd)
        for dc in range(KC):
            t, Bt = f32r.next()
            P.op("dve", lambda e, dc=dc, t=t: e.scalar_tensor_tensor(out=t[:], in0=xs[:, dc, :], scalar=fnws[:, dc:dc + 1], in1=rstd[:],
                                                                    op0=ALU.mult, op1=ALU.mult), reads=[Bxs, Bfnw, Brstd], writes=[Bt])
            P.dma("sp", outT[:, dc, i * TT:(i + 1) * TT], t[:], reads=[Bt], store=True, final=True)
    return P


from concourse.bass import ds

SEQ = 16384
TOKC = 4096
GROUPS = [[0, 1, 2, 3], [4, 5, 6, 7]]


def build_fused(nc, st):
    P = Prog(nc, st)
    P.need_rank = True
    io = {}

    def ext(name, shape, dt=F32):
        io[name] = nc.dram_tensor(name, list(shape), dt, kind="ExternalInput").ap()

    def itn(name, shape, dt=F32):
        io[name] = nc.dram_tensor(name, list(shape), dt).ap()

    ext("l0_xT", [D, TOKC + 2]); ext("l0_nw", [128, KC]); ext("l0_w_in", [D, 4096]); ext("l0_w_cv", [128, 24]); ext("l0_w_out", [D, D])
    ext("f_nw", [128, KC]); ext("f_w_in", [D, 4104]); ext("f_w_out", [D, D]); ext("fox_bf", [128, 2])
    for n, w in (("cU", 128), ("cSU", 128), ("cE127", 128), ("cI", 128), ("cMask", 2048)):
        ext(n, [128, w])
    ext("g_nw", [128, KC]); ext("g_w_in", [D, 4112]); ext("g_w_out", [D, D]); ext("gdn_wcv", [128, 24]); ext("gdn_hp", [128, 4]); ext("gdn_onw", [128, 128])
    for n in CN:
        ext(n, [128, 128])
    ext("swa_pos", [128, 128 + TOKC], I32); ext("swa_nw", [128, KC]); ext("swa_fnw", [128, KC]); ext("swa_w_in", [D, 2560]); ext("swa_w_out", [D, D])
    ext("swa_sk", [128, 8]); ext("swa_hmask", [128, 1]); ext("sMask", [128, 512]); ext("sInvf", [128, 1]); ext("sI", [128, 128]); ext("sOnesP", [128, 256])
    io["outT"] = nc.dram_tensor("outT", [D, TOKC], F32, kind="ExternalOutput").ap()
    for n in ("x1T", "x2T", "x3T", "szT1", "szT2"):
        itn(n, [D, TOKC])
    itn("XB1", [4 * 3072, TOKC], BF16); itn("YB1", [4 * 768, TOKC], BF16); itn("XF1", [8, SEQ]); itn("YF1", [2, SEQ])
    itn("XO1", [8 * 4 * D, TOKC // 8]); itn("YO1", [8 * D, TOKC // 8])
    itn("XB2", [8 * 3072, TOKC // 2]); itn("YB2", [8 * 768, TOKC // 2]); itn("XF2", [16, SEQ]); itn("YF2", [4, SEQ])
    itn("XO2", [8 * 4 * D, TOKC // 8]); itn("YO2", [8 * D, TOKC // 8])
    itn("XH", [4 * D, 128]); itn("YH", [D, 128])
    itn("L1", [3072, TOKC], BF16); itn("LF1", [8, TOKC]); itn("LO1", [D, TOKC])
    itn("L2", [3072, TOKC]); itn("LF2", [16, TOKC]); itn("LO2", [D, TOKC])

    ZW = 1024
    zt32 = P.sb("zt32", [128, ZW], F32)
    zt16 = P.sb("zt16", [128, ZW], BF16)
    Bzt = P.buf()
    P.op("pool", lambda e: e.memset(zt32[:], 0.0), writes=[Bzt])
    P.op("pool", lambda e: e.memset(zt16[:], 0.0), writes=[Bzt])
    fills = {}

    def zfill(name, key):
        ap = io[name]
        rows, cols = ap.shape
        zt = zt16 if name == "XB1" else zt32
        tot = rows * cols
        per = 128 * ZW
        flat = ap.rearrange("r c -> (r c)")
        KB = 8
        o = 0
        while o < tot:
            k = min(KB, (tot - o) // per)
            if k >= 1:
                n = k * per
                src = bass.AP(zt[:].tensor, 0, [[ZW, 128], [0, k], [1, ZW]])
                P.fill_queue.append((key, lambda q, dst=flat[o:o + n].rearrange("(k p w) -> p k w", p=128, w=ZW), src=src, key=key:
                                     P.dma(q, dst, src, reads=[Bzt], writes=[fills.setdefault((key, q), P.buf())], chan=fills[(key, q)], defer=True)))
            else:
                n = tot - o
                w = n // 128
                P.fill_queue.append((key, lambda q, dst=flat[o:o + n].rearrange("(p w) -> p w", p=128), src=zt[:, 0:w], key=key:
                                     P.dma(q, dst, src, reads=[Bzt], writes=[fills.setdefault((key, q), P.buf())], chan=fills[(key, q)], defer=True)))
            o += n

    def exchange(xin, yout):
        P.coll("ReduceScatter", GROUPS, io[xin], io[yout])

    def dyn_copy(q, dst, dst_pat, dst_off, src, src_pat, src_off, jscale, fkey=None, extra=()):
        b = P.buf()
        P.dma(q, (lambda: bass.AP(io[dst].tensor, P.jx[q] * jscale + dst_off, [list(p) for p in dst_pat])),
              bass.AP(io[src].tensor, src_off, [list(p) for p in src_pat]), reads=[fb for (k_, q_), fb in fills.items() if k_ == fkey] + list(extra), writes=[b], chan=b)
        return b

    def place_tok(lname, xname, fl_l, fl_x, nfl, with_v, fkey):
        if True:
            for h_ in range(2):
                q = ("sp", "pool")[h_]
                dyn_copy(q, xname, [[TOKC, 1536], [1, TOKC]], h_ * 1536 * TOKC, lname, [[TOKC, 1536], [1, TOKC]], h_ * 1536 * TOKC, 3072 * TOKC, fkey=fkey)
        dyn_copy("sp", fl_x, [[SEQ, nfl], [1, TOKC]], 0, fl_l, [[TOKC, nfl], [1, TOKC]], 0, TOKC, fkey=fkey)

    def o_exchange_hooks(lname, xname, yname, fkey):
        SUB = TOKC // 8
        Blo = [P.buf() for _ in range(8)]

        def done(s_):
            P.fill_until(fkey)
            b = dyn_copy("pool", xname, [[D * SUB, 4], [SUB, 256], [1, SUB]], s_ * 4 * D * SUB,
                         lname, [[256 * TOKC, 4], [TOKC, 256], [1, SUB]], s_ * SUB, 256 * SUB, fkey=fkey, extra=[Blo[s_]])
            P.coll("ReduceScatter", GROUPS, io[xname][s_ * 4 * D:(s_ + 1) * 4 * D, :], io[yname][s_ * D:(s_ + 1) * D, :], reads=[b])
        return Blo, done

    P.tag('fill')
    zfill("XB1", 1); zfill("XF1", 1); zfill("XO1", 2); zfill("XB2", 3); zfill("XF2", 3); zfill("XO2", 4); zfill("XH", 5)
    P.tag('ph1_conv')
    P.phase_begin()
    build_stage0(P, io)
    P.phase_end()

    P.tag('ph2_foxA')
    P.phase_begin()
    io2 = dict(io, c_xT=io["x1T"], a_nw=io["f_nw"], a_w_in=io["f_w_in"], xb=io["L1"], xf=io["LF1"], szT=io["szT1"])
    build_CA(P, io2, StageLoadX, StageAFox)
    P.phase_end()
    P.tag("x1_place")
    P.fill_until(1)
    place_tok("L1", "XB1", "LF1", "XF1", 8, True, 1)
    P.barrier()
    P.tag("x1_rs")
    exchange("XF1", "YF1")
    P.barrier()

    P.tag('ph3_foxB')
    P.phase_begin()
    By1 = [P.buf() for _ in range(4)]
    for c_ in range(4):
        P.coll("ReduceScatter", GROUPS, io["XB1"][c_ * 3072:(c_ + 1) * 3072, :], io["YB1"][c_ * 768:(c_ + 1) * 768, :], writes=[By1[c_]])
    Blo1, done1 = o_exchange_hooks("LO1", "XO1", "YO1", 2)
    build_foxB(P, dict(io, yb=io["YB1"], yb_bufs=By1, yf=io["YF1"], xo=io["LO1"], lo_bufs=Blo1, sub_done=done1), NH=2, S_=SEQ)
    P.phase_end()

    P.tag('ph4_foxC_gdnA')
    P.phase_begin()
    io4 = dict(io, c_oT=io["YO1"], c_oT_bufs=[P.buf() for _ in range(8)], c_szT=io["szT1"], c_xT=io["x1T"], c_w_out=io["f_w_out"], xo=io["x2T"],
               a_nw=io["g_nw"], a_w_in=io["g_w_in"], xb=io["L2"], xf=io["LF2"], szT=io["szT2"])
    build_CA(P, io4, StageC, StageAGdn)
    P.phase_end()
    P.tag("x3_place")
    P.fill_until(3)
    HQ = TOKC // 2
    for h_ in range(2):
        dyn_copy(("sp", "pool")[h_], "XB2", [[HQ, 3072], [1, HQ]], h_ * 3072 * HQ, "L2", [[TOKC, 3072], [1, HQ]], h_ * HQ, 2 * 3072 * HQ, fkey=3)
    dyn_copy("sp", "XF2", [[SEQ, 16], [1, TOKC]], 0, "LF2", [[TOKC, 16], [1, TOKC]], 0, TOKC, fkey=3)
    P.barrier()
    P.tag("x3_rs")
    exchange("XF2", "YF2")
    P.barrier()

    P.tag('ph5_gdnB')
    P.phase_begin()
    By = [P.buf() for _ in range(8)]
    for c_ in range(8):
        P.coll("ReduceScatter", GROUPS, io["XB2"][c_ * 3072:(c_ + 1) * 3072, :], io["YB2"][c_ * 768:(c_ + 1) * 768, :], writes=[By[c_]])
    Blo2, done2 = o_exchange_hooks("LO2", "XO2", "YO2", 4)
    build_gdnB(P, dict(io, yb=io["YB2"], yb_bufs=By, yf=io["YF2"], xo=io["LO2"], lo_bufs=Blo2, sub_done=done2), NH=2, S_=SEQ)
    P.phase_end()

    P.tag('ph6_gdnC')
    P.phase_begin()
    io6 = dict(io, c_oT=io["YO2"], c_oT_bufs=[P.buf() for _ in range(8)], c_szT=io["szT2"], c_xT=io["x2T"], c_w_out=io["g_w_out"], xo=io["x3T"])
    build_CA(P, io6, StageC, StageANull)
    P.phase_end()
    P.tag('x5_halo')
    P.fill_until(5)
    Bh = P.buf()
    P.dma("sp", (lambda: io["XH"][ds(((P.jx["sp"] + 1) % 4) * D, D), :]), io["x3T"][:, TOKC - 128:TOKC], reads=[fb for (k_, q_), fb in fills.items() if k_ == 5], writes=[Bh], chan=Bh)
    P.barrier()
    exchange("XH", "YH")
    P.barrier()

    P.tag('ph7_swa')
    P.phase_begin()
    build_swa(P, dict(io, yh=io["YH"]))
    P.phase_end()
    return P


from concourse.bass_utils import run_bass_kernel_spmd

NCORE = 8


def _nwT(v):
    return np.ascontiguousarray(np.asarray(v, np.float32).reshape(8, 128).T)


def kernel(x, positions, norm_w, final_norm_w, conv_w_in, conv_w_conv, conv_w_out,
           fox_w_in, fox_b_f, fox_w_out, gdn_w_in, gdn_w_conv, gdn_a_log, gdn_dt_bias,
           gdn_norm_w, gdn_w_out, swa_w_in, swa_sinks, swa_w_out):
    x = np.asarray(x, np.float32)
    positions = np.asarray(positions, np.int32)
    f = lambda a: np.ascontiguousarray(np.asarray(a, np.float32))
    nc = bass.Bass("TRN2", target_bir_lowering=False)
    with ExitStack() as st:
        P = build_fused(nc, st)
        P.emit()
    shared = {"l0_nw": _nwT(norm_w[0]), "l0_w_in": f(conv_w_in[0]),
              "l0_w_cv": np.ascontiguousarray(f(conv_w_conv[0]).reshape(3, 8, 128).transpose(2, 1, 0).reshape(128, 24)),
              "l0_w_out": f(conv_w_out[0]),
              "f_nw": _nwT(norm_w[1]), "f_w_in": f(fox_w_in[0]), "f_w_out": f(fox_w_out[0]),
              "g_nw": _nwT(norm_w[2]), "g_w_in": f(gdn_w_in[0]), "g_w_out": f(gdn_w_out[0]),
              "gdn_onw": np.ascontiguousarray(np.broadcast_to(f(gdn_norm_w[0])[None, :], (128, 128))),
              "swa_nw": _nwT(norm_w[3]), "swa_fnw": _nwT(final_norm_w), "swa_w_in": f(swa_w_in[0]), "swa_w_out": f(swa_w_out[0])}
    wc = f(gdn_w_conv[0])
    wcv_all = np.stack([wc[k, w * 1024 + h * 128: w * 1024 + (h + 1) * 128]
                        for h in range(8) for w in range(3) for k in range(4)], axis=1)
    hp_all = np.stack([f(gdn_a_log[0]), f(gdn_dt_bias[0])], axis=1).reshape(1, 16)
    sinks = f(swa_sinks[0])
    sk = np.zeros((128, 8), np.float32)
    for p in range(8):
        sk[:64, p] = sinks[2 * p]
        sk[64:, p] = sinks[2 * p + 1]
    shared["swa_sk"] = sk
    shared.update(fox_consts())
    shared.update(gdn_consts())
    shared.update(swa_consts())
    ins = []
    for c in range(NCORE):
        b, t0 = c // 4, (c % 4) * TOKC
        xt = np.zeros((1024, TOKC + 2), np.float32)
        pos = np.zeros((128 + TOKC,), np.int32)
        if t0 > 0:
            xt[:, 0:2] = x[b, t0 - 2:t0].T
            pos[:128] = positions[b, t0 - 128:t0]
        xt[:, 2:] = x[b, t0:t0 + TOKC].T
        pos[128:] = positions[b, t0:t0 + TOKC]
        d = dict(shared)
        d["l0_xT"] = xt
        d["swa_pos"] = np.ascontiguousarray(np.broadcast_to(pos[None, :], (128, pos.shape[0])))
        d["swa_hmask"] = np.full((128, 1), -30000.0 if t0 == 0 else 0.0, np.float32)
        g = c % 4
        d["fox_bf"] = np.ascontiguousarray(np.broadcast_to(f(fox_b_f[0])[None, 2 * g:2 * g + 2], (128, 2)))
        d["gdn_wcv"] = np.ascontiguousarray(wcv_all[:, g * 24:(g + 1) * 24])
        d["gdn_hp"] = np.ascontiguousarray(np.broadcast_to(hp_all[:, g * 4:(g + 1) * 4], (128, 4)))
        ins.append(d)
    res = run_bass_kernel_spmd(nc, ins, core_ids=list(range(NCORE)))
    out = np.zeros((2, SEQ, 1024), np.float32)
    for c in range(NCORE):
        b, t0 = c // 4, (c % 4) * TOKC
        out[b, t0:t0 + TOKC] = np.asarray(res.results[c]["outT"]).T
    return out
```

```python
from contextlib import ExitStack
import numpy as np
import concourse.bass as bass
import concourse.mybir as mybir

F32 = mybir.dt.float32
BF16 = mybir.dt.bfloat16
I32 = mybir.dt.int32
ALU = mybir.AluOpType
AF = mybir.ActivationFunctionType

ENGS = ("pe", "act", "dve", "pool", "sp")


class Buf:
    __slots__ = ("name", "w", "r", "sem", "semval", "excl", "cls")

    def __init__(self, name):
        self.name = name
        self.w = {}
        self.r = {}
        self.sem = None
        self.semval = 0
        self.excl = False
        self.cls = None


class _Op:
    __slots__ = ("fn", "deps", "dma", "signal", "sigord", "tag")

    def __init__(self, fn, deps, dma=None):
        self.fn = fn
        self.deps = deps
        self.dma = dma
        self.signal = False
        self.sigord = 0
        self.tag = _Op.cur_tag


_Op.cur_tag = None


def _merge(d, s):
    for k, v in s.items():
        if d.get(k, 0) < v:
            d[k] = v


class Prog:
    def __init__(self, nc, stack):
        self.nc = nc
        self.stack = stack
        self.semstack = stack
        self.free_sems = {"hw": [], "sw": [], "cc": []}
        self.chans = []
        self.jx = {}
        self.ops = {e: [] for e in ENGS}
        self.esem = {}
        for e in ("pe", "act", "dve", "pool"):
            self.esem[e] = stack.enter_context(nc.semaphore("es_" + e))
        self.dsems = {}
        self.nsem = 4
        self.final = {}
        self.uid = 0
        self.need_rank = False
        self.pidx = 0
        self.scopes = False
        self.fill_queue = []

    def sb(self, name, shape, dt):
        return self.stack.enter_context(self.nc.sbuf_tensor("p%d_%s" % (self.pidx, name), list(shape), dt))

    def ps(self, name, shape, dt=F32):
        return self.stack.enter_context(self.nc.psum_tensor("p%d_%s" % (self.pidx, name), list(shape), dt))

    def buf(self, name=None):
        self.uid += 1
        return Buf(name or f"b{self.uid}")

    def _chan(self, b, cls="hw"):
        if b.sem is None:
            b.cls = cls
            if self.free_sems[cls]:
                b.sem, b.semval = self.free_sems[cls].pop()
            else:
                b.sem = self.semstack.enter_context(self.nc.semaphore("ds_%d" % self.nsem))
                self.dsems[id(b.sem)] = b.sem
                self.nsem += 1
            self.chans.append(b)
        return b.sem

    def tag(self, name):
        _Op.cur_tag = name

    def fill_step(self, k, q="pool", maxkey=99):
        for _ in range(k):
            if not self.fill_queue or self.fill_queue[0][0] > maxkey:
                return
            self.fill_queue.pop(0)[1](q)

    def fill_until(self, key, q="pool"):
        while self.fill_queue and self.fill_queue[0][0] <= key:
            self.fill_queue.pop(0)[1](q)

    def phase_begin(self):
        self.pidx += 1
        self._pstack = ExitStack()
        self._outer = self.stack
        self.stack = self._pstack

    def phase_end(self):
        self.barrier()
        self._pstack.close()
        self.stack = self._outer

    def barrier(self):
        deps = {}
        for e in ("pe", "act", "dve", "pool"):
            lst = self.ops[e]
            for i in range(len(lst) - 1, -1, -1):
                if lst[i].dma is None and lst[i].fn is not None:
                    deps[("e", e)] = i + 1
                    break
        for b in self.chans:
            if b.semval:
                deps[("d", id(b.sem))] = b.semval
        for e in ENGS:
            self.ops[e].append(_Op(None, dict(deps)))
        for b in self.chans:
            self.free_sems[b.cls].append((b.sem, b.semval))
            b.sem = None
        self.chans = []

    def coll(self, kind, groups, in_ap, out_ap, writes=(), reads=()):
        b = self.buf()
        sem = self._chan(b, "cc")
        b.semval += 1
        v = b.semval
        for w_ in writes:
            w_.w[("d", id(sem))] = v
        deps = {}
        for r_ in reads:
            _merge(deps, r_.w)
        self.ops["pool"].append(_Op(lambda e: e.collective_compute(kind, ALU.add, replica_groups=groups, ins=[in_ap.opt()], outs=[out_ap.opt()]),
                                    deps, dma=(sem, v, 1)))

    def _deps(self, reads, writes):
        d = {}
        for b in reads:
            _merge(d, b.w)
        for b in writes:
            _merge(d, b.w)
            _merge(d, b.r)
        return d

    def op(self, eng, fn, reads=(), writes=()):
        if any(b.excl for b in reads):
            writes = list(writes) + [b for b in reads if b.excl]
            reads = [b for b in reads if not b.excl]
        deps = self._deps(reads, writes)
        if eng == "pe":
            deps.pop(("e", "pe"), None)
        lst = self.ops[eng]
        lst.append(_Op(fn, deps))
        key = ("e", eng)
        v = len(lst)
        for b in reads:
            b.r[key] = v
        for b in writes:
            b.w[key] = v

    def dma(self, q, out, in_, reads=(), writes=(), chan=None, store=False, final=False, defer=False, **kw):
        if chan is None:
            chan = (reads[0] if store else writes[0])
        sem = self._chan(chan, "sw" if q == "pool" else "hw")
        assert chan.cls == ("sw" if q == "pool" else "hw"), "a DMA channel must stay on one kind of queue"
        if defer and chan in self.chans:
            self.chans.remove(chan)
        key = ("d", id(sem))
        deps = self._deps(reads, writes)
        if store and chan.semval:
            deps[key] = max(deps.get(key, 0), chan.semval)
        chan.semval += 16
        v = chan.semval
        def _fn(e):
            o = out() if callable(out) else out
            i = in_() if callable(in_) else in_
            try:
                return e.dma_start(out=o, in_=i, **kw)
            except Exception:
                print("DMA FAIL out=", o, " in=", i)
                raise
        self.ops[q].append(_Op(_fn, deps, dma=(sem, v, 16)))
        for b in reads:
            b.r[key] = v
        for b in writes:
            b.w[key] = v
        if final:
            self.final[key] = v

    def emit(self):
        nc = self.nc
        for e in ENGS:
            for o in self.ops[e]:
                for (kind, k), v in o.deps.items():
                    if kind == "e":
                        self.ops[k][v - 1].signal = True
        for e in ENGS:
            n = 0
            for o in self.ops[e]:
                if o.signal:
                    n += 1
                o.sigord = n
        if self.final:
            self.ops["sp"].append(_Op(None, dict(self.final)))
        eng_handle = {"pe": "tensor", "act": "scalar", "dve": "vector", "pool": "gpsimd", "sp": "sync"}
        stats = {}
        with nc.Block() as block:
            for e in ENGS:
                ops = self.ops[e]
                if not ops:
                    continue

                def body(eng, ops=ops, e=e):
                    if e in ("sp", "pool") and self.need_rank:
                        self.jx[e] = eng.partition_id() % 4
                    seen = {}
                    nw = 0
                    cur = None
                    sid = None
                    for o in ops:
                        if self.scopes and o.tag != cur:
                            if cur is not None:
                                nc.leave_named_scope(cur, sid, False)
                            cur = o.tag
                            if cur is not None:
                                sid, _ = nc.enter_named_scope(cur, False)
                        for (kind, k), v in o.deps.items():
                            if kind == "e":
                                sem = self.esem[k]
                                val = self.ops[k][v - 1].sigord
                            else:
                                sem = self.dsems[k]
                                val = v
                            sk = (kind, k)
                            if seen.get(sk, 0) >= val:
                                continue
                            seen[sk] = val
                            eng.wait_ge(sem, val)
                            nw += 1
                        if o.fn is None:
                            continue
                        inst = o.fn(eng)
                        if o.dma is not None:
                            if o.dma[2] == 16:
                                inst.then_inc(o.dma[0], 16)
                            else:
                                inst.then_inc(o.dma[0])
                        elif o.signal:
                            inst.then_inc(self.esem[e], 1)
                    if self.scopes and cur is not None:
                        nc.leave_named_scope(cur, sid, False)
                    stats[e] = (len(ops), nw)

                getattr(block, eng_handle[e])(body)
        self.stats = stats
        return stats


D = 1024
KC = 8
TT = 512


def load_weight_bf16(P, wdram, ncols, wb, Bwb, stg, Bstg, scale_col=None, q="sp", Bscale=None):
    piece = 1024
    n = 0
    for kc in range(KC):
        for c0 in range(0, ncols, piece):
            c1 = min(ncols, c0 + piece)
            s = n % 2
            n += 1
            P.dma(q, stg[s][:, 0:c1 - c0], wdram[kc * 128:(kc + 1) * 128, c0:c1], writes=[Bstg[s]])
            eng = ("dve", "pool")[n % 2]
            if scale_col is not None:
                P.op(eng, lambda e, s=s, kc=kc, c0=c0, c1=c1: e.tensor_scalar(
                    out=wb[:, kc, c0:c1], in0=stg[s][:, 0:c1 - c0], scalar1=scale_col[:, kc:kc + 1], scalar2=None,
                    op0=ALU.mult), reads=[Bstg[s], Bscale], writes=[Bwb])
            else:
                P.op(eng, lambda e, s=s, kc=kc, c0=c0, c1=c1: e.tensor_copy(
                    out=wb[:, kc, c0:c1], in_=stg[s][:, 0:c1 - c0]), reads=[Bstg[s]], writes=[Bwb])


class Norm:
    def __init__(self, P, ones, Bones, nmax=TT):
        self.P = P
        self.ones = ones
        self.Bones = Bones
        self.sq = P.sb("nsq", [128, KC, nmax], BF16)
        self.Bsq = P.buf()
        self.ssp = P.ps("nss", [128, nmax])
        self.Bss = P.buf()
        self.sd = P.sb("nsd", [128, nmax], F32)
        self.Bsd = P.buf()

    def run(self, xs, Bxs, n, rstd, Brstd):
        P = self.P
        P.op("act", lambda e: e.activation(out=self.sq[:, :, 0:n], in_=xs[:, :, 0:n], func=AF.Square),
             reads=[Bxs], writes=[self.Bsq])
        for kc in range(KC):
            P.op("pe", lambda e, kc=kc: e.matmul(self.ssp[:, 0:n], lhsT=self.ones[:], rhs=self.sq[:, kc, 0:n],
                                                  start=(kc == 0), stop=(kc == KC - 1)),
                 reads=[self.Bsq, self.Bones], writes=[self.Bss])
        P.op("act", lambda e: e.activation(out=self.sd[:, 0:n], in_=self.ssp[:, 0:n], func=AF.Sqrt,
                                           scale=1.0 / D, bias=self.epsc[:, 0:1]),
             reads=[self.Bss, self.Bones], writes=[self.Bsd])
        P.op("dve", lambda e: e.reciprocal(out=rstd[:, 0:n], in_=self.sd[:, 0:n]), reads=[self.Bsd], writes=[Brstd])


def build_stage0(P, io, NTOK=4096):
    NT = NTOK // TT
    xT, nw, w_in, w_cv, w_out, xo = io["l0_xT"], io["l0_nw"], io["l0_w_in"], io["l0_w_cv"], io["l0_w_out"], io["x1T"]
    xTv = xT.rearrange("(c p) t -> p c t", p=128)
    xov = xo.rearrange("(c p) t -> p c t", p=128)

    ones = P.sb("ones", [128, 128], BF16)
    Bones = P.buf()
    epsc = P.sb("epsc", [128, 1], F32)
    P.op("dve", lambda e: e.memset(ones[:], 1.0), writes=[Bones])
    P.op("dve", lambda e: e.memset(epsc[:], 1e-6), writes=[Bones])
    nws = P.sb("nws", [128, KC], F32)
    Bnw = P.buf()
    P.dma("sp", nws[:], nw[:, :], writes=[Bnw])
    wcs = P.sb("wcs", [128, KC * 3], F32)
    Bwc = P.buf()
    P.dma("sp", wcs[:], w_cv[:, :], writes=[Bwc])

    stg = [P.sb("stg%d" % i, [128, 1024], F32) for i in range(2)]
    Bstg = [P.buf(), P.buf()]
    wib = P.sb("wib", [128, KC, 4096], BF16)
    Bwib = P.buf()
    wob = P.sb("wob", [128, KC, D], BF16)
    Bwob = P.buf()

    xs = [P.sb("xs%d" % i, [128, KC, TT], F32) for i in range(2)]
    Bxs = [P.buf(), P.buf()]
    hT = [P.sb("hT%d" % i, [128, KC, TT], BF16) for i in range(2)]
    BhT = [P.buf(), P.buf()]
    rstd = [P.sb("rstd%d" % i, [128, TT], F32) for i in range(2)]
    Brstd = [P.buf(), P.buf()]
    xh = P.sb("xh", [128, KC, 2], F32)
    Bxh = P.buf()
    hh = P.sb("hh", [128, KC, 2], BF16)
    Bhh = P.buf()
    rh = P.sb("rh", [128, 2], F32)
    Brh = P.buf()
    norm = Norm(P, ones, Bones)
    norm.epsc = epsc

    ubuf = P.sb("ubuf", [128, KC, TT + 2], F32)
    Bu = [P.buf() for _ in range(KC)]
    vsb = P.sb("vsb", [128, TT], F32)
    Bv = P.buf()
    ycv = P.sb("ycv", [128, TT], F32)
    By = P.buf()
    szb = P.sb("szb", [128, TT], F32)
    Bsz = P.buf()
    tb = P.sb("tb", [128, TT], F32)
    Bt = P.buf()
    og = P.sb("og", [128, KC, TT], BF16)
    Bog = P.buf()
    xn = [P.sb("xn%d" % i, [128, TT], F32) for i in range(2)]
    Bxn = [P.buf(), P.buf()]
    pb, pc, pv, pz = [P.ps("pp%d" % i, [128, TT]) for i in range(4)]
    Bpb, Bpc, Bpv, Bpz = [P.buf() for _ in range(4)]
    py = [P.ps("py%d" % i, [128, TT]) for i in range(2)]
    Bpy = [P.buf(), P.buf()]

    load_weight_bf16(P, w_in, 4096, wib, Bwib, stg, Bstg, scale_col=nws, Bscale=Bnw)
    load_weight_bf16(P, w_out, D, wob, Bwob, stg, Bstg)

    def proj(ps, Bps, oc, h, Bh, n):
        for kc in range(KC):
            P.op("pe", lambda e, kc=kc: e.matmul(ps[:, 0:n], lhsT=wib[:, kc, oc * 128:(oc + 1) * 128],
                                                  rhs=h[:, kc, 0:n], start=(kc == 0), stop=(kc == KC - 1)),
                 reads=[Bwib, Bh], writes=[Bps])

    def make_h(x_, Bx_, n, r_, Br_, h_, Bh_):
        norm.run(x_, Bx_, n, r_, Br_)
        for kc in range(KC):
            eng = "dve" if kc % 2 == 0 else "pool"
            P.op(eng, lambda e, kc=kc: e.tensor_tensor(out=h_[:, kc, 0:n], in0=x_[:, kc, 0:n], in1=r_[:, 0:n],
                                                        op=ALU.mult), reads=[Bx_, Br_], writes=[Bh_])

    P.dma("sp", xh[:], xTv[:, :, 0:2], writes=[Bxh])
    make_h(xh, Bxh, 2, rh, Brh, hh, Bhh)
    for ci in range(KC):
        proj(pc, Bpc, 8 + ci, hh, Bhh, 2)
        proj(pv, Bpv, 16 + ci, hh, Bhh, 2)
        P.op("act", lambda e: e.activation(out=vsb[:, 0:2], in_=pv[:, 0:2], func=AF.Copy), reads=[Bpv], writes=[Bv])
        P.op("dve", lambda e, ci=ci: e.tensor_tensor(out=ubuf[:, ci, 0:2], in0=pc[:, 0:2], in1=vsb[:, 0:2], op=ALU.mult),
             reads=[Bpc, Bv], writes=[Bu[ci]])

    def prep(i):
        s = i % 2
        P.dma("sp", xs[s][:], xTv[:, :, 2 + i * TT:2 + (i + 1) * TT], writes=[Bxs[s]])
        make_h(xs[s], Bxs[s], TT, rstd[s], Brstd[s], hT[s], BhT[s])

    def main(i):
        s = i % 2
        h, Bh = hT[s], BhT[s]
        for ci in range(KC):
            proj(pc, Bpc, 8 + ci, h, Bh, TT)
            proj(pv, Bpv, 16 + ci, h, Bh, TT)
            proj(pb, Bpb, ci, h, Bh, TT)
            proj(pz, Bpz, 24 + ci, h, Bh, TT)
            P.op("act", lambda e: e.activation(out=vsb[:], in_=pv[:], func=AF.Copy), reads=[Bpv], writes=[Bv])
            P.op("dve", lambda e, ci=ci: e.tensor_tensor(out=ubuf[:, ci, 2:TT + 2], in0=pc[:], in1=vsb[:], op=ALU.mult),
                 reads=[Bpc, Bv], writes=[Bu[ci]])
            P.op("dve", lambda e, ci=ci: e.tensor_scalar(out=ycv[:], in0=ubuf[:, ci, 2:TT + 2],
                                                          scalar1=wcs[:, ci * 3 + 2:ci * 3 + 3], scalar2=None, op0=ALU.mult),
                 reads=[Bu[ci], Bwc], writes=[By])
            for k in (1, 0):
                P.op("dve", lambda e, ci=ci, k=k: e.scalar_tensor_tensor(
                    out=ycv[:], in0=ubuf[:, ci, k:k + TT], scalar=wcs[:, ci * 3 + k:ci * 3 + k + 1], in1=ycv[:],
                    op0=ALU.mult, op1=ALU.add), reads=[Bu[ci], Bwc, By], writes=[By])
            P.op("pool", lambda e, ci=ci: e.tensor_copy(out=ubuf[:, ci, 0:2], in_=ubuf[:, ci, TT:TT + 2]),
                 reads=[Bu[ci]], writes=[Bu[ci]])
            P.op("act", lambda e: e.activation(out=szb[:], in_=pz[:], func=AF.Silu), reads=[Bpz], writes=[Bsz])
            P.op("dve", lambda e: e.tensor_tensor(out=tb[:], in0=pb[:], in1=ycv[:], op=ALU.mult),
                 reads=[Bpb, By], writes=[Bt])
            P.op("pool", lambda e, ci=ci: e.tensor_tensor(out=og[:, ci, :], in0=tb[:], in1=szb[:], op=ALU.mult),
                 reads=[Bt, Bsz], writes=[Bog])
        for dc in range(KC):
            b = dc % 2
            for ci in range(KC):
                P.op("pe", lambda e, ci=ci, dc=dc, b=b: e.matmul(py[b][:], lhsT=wob[:, ci, dc * 128:(dc + 1) * 128],
                                                                rhs=og[:, ci, :], start=(ci == 0), stop=(ci == KC - 1)),
                     reads=[Bwob, Bog], writes=[Bpy[b]])
            P.op("dve", lambda e, dc=dc, b=b: e.tensor_tensor(out=xn[b][:], in0=py[b][:], in1=xs[s][:, dc, :], op=ALU.add),
                 reads=[Bpy[b], Bxs[s]], writes=[Bxn[b]])
            P.dma("sp", xov[:, dc, i * TT:(i + 1) * TT], xn[b][:], reads=[Bxn[b]], store=True)

    prep(0)
    for i in range(NT):
        P.fill_step(4)
        if i + 1 < NT:
            prep(i + 1)
        main(i)
    return P


import math
from concourse.bass import ds


class Rot:
    def __init__(self, P, name, shape, dt, n):
        self.t = [P.sb("%s%d" % (name, i), shape, dt) for i in range(n)]
        self.b = [P.buf() for _ in range(n)]
        self.i = 0

    def next(self):
        k = self.i % len(self.t)
        self.i += 1
        return self.t[k], self.b[k]


class Common:
    def __init__(self, P):
        self.P = P
        self.ones = P.sb("ones", [128, 128], BF16)
        self.Bc = P.buf()
        self.epsc = P.sb("epsc", [128, 1], F32)
        P.op("dve", lambda e: e.memset(self.ones[:], 1.0), writes=[self.Bc])
        P.op("dve", lambda e: e.memset(self.epsc[:], 1e-6), writes=[self.Bc])
        self.norm = Norm(P, self.ones, self.Bc)
        self.norm.epsc = self.epsc
        self.stg = [P.sb("stg%d" % i, [128, 1024], F32) for i in range(2)]
        self.Bstg = [P.buf(), P.buf()]
        self.xs = [P.sb("xs%d" % i, [128, KC, TT], F32) for i in range(2)]
        self.Bxs = [P.buf(), P.buf()]
        self.hT = [P.sb("hT%d" % i, [128, KC, TT], BF16) for i in range(2)]
        self.BhT = [P.buf(), P.buf()]
        self.rstd = [P.sb("rstd%d" % i, [128, TT], F32) for i in range(2)]
        self.Brstd = [P.buf(), P.buf()]
        self.f32r = Rot(P, "f32r", [128, TT], F32, 4)
        self.bf16r = Rot(P, "bf16r", [128, TT], BF16, 4)
        self.pp = [P.ps("pp%d" % i, [128, TT]) for i in range(6)]
        self.Bpp = [P.buf() for _ in range(6)]
        self.ppi = 0

    def psum(self):
        k = self.ppi % len(self.pp)
        self.ppi += 1
        return self.pp[k], self.Bpp[k]

    def make_h(self, s, n=TT):
        P = self.P
        x_, Bx_, r_, Br_, h_, Bh_ = self.xs[s], self.Bxs[s], self.rstd[s], self.Brstd[s], self.hT[s], self.BhT[s]
        self.norm.run(x_, Bx_, n, r_, Br_)
        for kc in range(KC):
            eng = "dve" if kc % 2 == 0 else "pool"
            P.op(eng, lambda e, kc=kc: e.tensor_tensor(out=h_[:, kc, 0:n], in0=x_[:, kc, 0:n], in1=r_[:, 0:n],
                                                        op=ALU.mult), reads=[Bx_, Br_], writes=[Bh_])


def load_small(P, dram, shape, name, dt=F32):
    t = P.sb(name, shape, dt)
    B = P.buf()
    P.dma("sp", t[:], dram, writes=[B])
    return t, B


class StageC:
    def __init__(self, P, cm, io, NTOK):
        self.P, self.cm = P, cm
        v = lambda a: a.rearrange("(c p) t -> p c t", p=128)
        self.szT, self.xT, self.w_out, self.xo = v(io["c_szT"]), v(io["c_xT"]), io["c_w_out"], v(io["xo"])
        SUB = NTOK // 8
        self.SUB = SUB
        self.oTs = [v(io["c_oT"][s_ * D:(s_ + 1) * D, :]) for s_ in range(8)]
        self.Bo = io["c_oT_bufs"]
        self.wob = P.sb("wob", [128, KC, D], BF16)
        self.Bwob = P.buf()
        self.og = P.sb("og", [128, KC, TT], BF16)
        self.Bog = P.buf()
        self.lo = Rot(P, "c_lo", [128, TT], F32, 2)
        self.ls = Rot(P, "c_ls", [128, TT], F32, 2)
        self.lx = Rot(P, "c_lx", [128, TT], F32, 2)

    def load_weights(self):
        load_weight_bf16(self.P, self.w_out, D, self.wob, self.Bwob, self.cm.stg, self.cm.Bstg)

    def tile(self, i, s):
        P, cm = self.P, self.cm
        sl = slice(i * TT, (i + 1) * TT)
        for ci in range(KC):
            to, Bo = self.lo.next()
            ts, Bs = self.ls.next()
            s_ = (i * TT) // self.SUB
            lo_ = i * TT - s_ * self.SUB
            P.dma("sp", to[:], self.oTs[s_][:, ci, lo_:lo_ + TT], reads=[self.Bo[s_]], writes=[Bo])
            P.dma("sp", ts[:], self.szT[:, ci, sl], writes=[Bs])
            eng = "dve" if ci % 2 == 0 else "pool"
            P.op(eng, lambda e, ci=ci, to=to, ts=ts: e.tensor_tensor(out=self.og[:, ci, :], in0=to[:], in1=ts[:], op=ALU.mult),
                 reads=[Bo, Bs], writes=[self.Bog])
        for dc in range(KC):
            ps, Bps = cm.psum()
            tx, Bx = self.lx.next()
            P.dma("sp", tx[:], self.xT[:, dc, sl], writes=[Bx])
            for ci in range(KC):
                P.op("pe", lambda e, ci=ci, dc=dc, ps=ps: e.matmul(ps[:], lhsT=self.wob[:, ci, dc * 128:(dc + 1) * 128],
                                                                  rhs=self.og[:, ci, :], start=(ci == 0), stop=(ci == KC - 1)),
                     reads=[self.Bwob, self.Bog], writes=[Bps])
            P.op("dve", lambda e, dc=dc, ps=ps, tx=tx: e.tensor_tensor(out=cm.xs[s][:, dc, :], in0=ps[:], in1=tx[:], op=ALU.add),
                 reads=[Bps, Bx], writes=[cm.Bxs[s]])
        P.dma("sp", self.xo[:, :, sl], cm.xs[s][:], reads=[cm.Bxs[s]], store=True, chan=cm.Bxs[s])


class StageLoadX:
    def __init__(self, P, cm, io, NTOK):
        self.P, self.cm = P, cm
        self.xT = io["c_xT"].rearrange("(c p) t -> p c t", p=128)

    def load_weights(self):
        pass

    def tile(self, i, s):
        self.P.dma("sp", self.cm.xs[s][:], self.xT[:, :, i * TT:(i + 1) * TT], writes=[self.cm.Bxs[s]])


class StageAFox:
    NCOL = 4104

    def __init__(self, P, cm, io, NTOK):
        self.P, self.cm = P, cm
        self.NTOK = NTOK
        self.nw, self.w_in = io["a_nw"], io["a_w_in"]
        self.xb, self.xf = io["xb"], io["xf"]
        self.szT = io["szT"].rearrange("(c p) t -> p c t", p=128)
        self.wib = P.sb("wib", [128, KC, self.NCOL], BF16)
        self.Bwib = P.buf()
        self.flr = Rot(P, "a_fl", [8, TT], F32, 2)

    def load_weights(self):
        self.nws, self.Bnw = load_small(self.P, self.nw[:, :], [128, KC], "a_nws")
        load_weight_bf16(self.P, self.w_in, self.NCOL, self.wib, self.Bwib, self.cm.stg, self.cm.Bstg, scale_col=self.nws, Bscale=self.Bnw)

    def fm_proj(self, col0, s, M=128):
        P, cm = self.P, self.cm
        ps, Bps = cm.psum()
        for kc in range(KC):
            P.op("pe", lambda e, kc=kc, ps=ps: e.matmul(ps[0:M, :], lhsT=self.wib[:, kc, col0:col0 + M], rhs=cm.hT[s][:, kc, :],
                                                        start=(kc == 0), stop=(kc == KC - 1)),
                 reads=[self.Bwib, cm.BhT[s]], writes=[Bps])
        return ps, Bps

    def tm_proj(self, col0, ncol, blk, s):
        P, cm = self.P, self.cm
        ps, Bps = cm.psum()
        for kc in range(KC):
            P.op("pe", lambda e, kc=kc, ps=ps: e.matmul(ps[:, 0:ncol], lhsT=cm.hT[s][:, kc, blk * 128:(blk + 1) * 128],
                                                        rhs=self.wib[:, kc, col0:col0 + ncol],
                                                        start=(kc == 0), stop=(kc == KC - 1)),
                 reads=[self.Bwib, cm.BhT[s]], writes=[Bps])
        return ps, Bps

    def tile(self, i, s):
        P, cm = self.P, self.cm
        sl = slice(i * TT, (i + 1) * TT)
        S_ = 4 * self.NTOK
        xb, xf = self.xb, self.xf
        n = 0
        for which in (0, 1):
            for c in range(KC):
                ps, Bps = self.fm_proj(which * 1024 + c * 128, s)
                t, Bt = cm.bf16r.next()
                if n % 2 == 0:
                    P.op("act", lambda e, t=t, ps=ps: e.activation(out=t[:], in_=ps[:], func=AF.Copy), reads=[Bps], writes=[Bt])
                else:
                    P.op("dve", lambda e, t=t, ps=ps: e.tensor_copy(out=t[:], in_=ps[:]), reads=[Bps], writes=[Bt])
                n += 1
                row0 = (c // 2) * 768 + which * 256 + (c % 2) * 128
                P.dma("sp", xb[row0:row0 + 128, sl], t[:], reads=[Bt], store=True)
        for c in range(KC):
            ps, Bps = self.fm_proj(3072 + c * 128, s)
            t, Bt = cm.f32r.next()
            P.op("act", lambda e, t=t, ps=ps: e.activation(out=t[:], in_=ps[:], func=AF.Silu), reads=[Bps], writes=[Bt])
            P.dma("sp", self.szT[:, c, sl], t[:], reads=[Bt], store=True)
        for blk in range(TT // 128):
            for half in range(2):
                ps, Bps = self.tm_proj(2048 + half * 512, 512, blk, s)
                t, Bt = cm.bf16r.next()
                P.op("dve", lambda e, t=t, ps=ps: e.tensor_copy(out=t[:], in_=ps[:]), reads=[Bps], writes=[Bt])
                for pr in range(2):
                    g = half * 2 + pr
                    tok0 = i * TT + blk * 128
                    r0 = g * 768 + 512
                    vview = xb[r0:r0 + 256, :].rearrange("(h r) (q d) -> h (r q) d", h=2, d=128)
                    P.dma("sp", vview[:, tok0:tok0 + 128, :].rearrange("h t d -> t h d"),
                          t[:, pr * 256:(pr + 1) * 256].rearrange("p (h d) -> p h d", d=128), reads=[Bt], store=True)
        ps, Bps = cm.psum()
        for kc in range(KC):
            P.op("pe", lambda e, kc=kc, ps=ps: e.matmul(ps[0:8, :], lhsT=self.wib[:, kc, 4096:4104], rhs=cm.hT[s][:, kc, :],
                                                        start=(kc == 0), stop=(kc == KC - 1)), reads=[self.Bwib, cm.BhT[s]], writes=[Bps])
        t, Bt = self.flr.next()
        P.op("dve", lambda e, t=t, ps=ps: e.tensor_copy(out=t[:], in_=ps[0:8, :]), reads=[Bps], writes=[Bt])
        P.dma("sp", xf[0:8, sl], t[:], reads=[Bt], store=True)


def build_CA(P, io, Ccls, Acls, NTOK=4096):
    cm = Common(P)
    C = Ccls(P, cm, io, NTOK)
    A = Acls(P, cm, io, NTOK)
    C.load_weights()
    A.load_weights()
    NT = NTOK // TT

    def prep(i):
        C.tile(i, i % 2)
        if Acls is not StageANull:
            cm.make_h(i % 2)

    prep(0)
    for i in range(NT):
        P.fill_step(3)
        if i + 1 < NT:
            prep(i + 1)
        A.tile(i, i % 2)
    return P


class StageAGdn(StageAFox):
    NCOL = 4112

    def __init__(self, P, cm, io, NTOK):
        self.P, self.cm = P, cm
        self.NTOK = NTOK
        self.nw, self.w_in = io["a_nw"], io["a_w_in"]
        self.xb, self.xf = io["xb"], io["xf"]
        self.szT = io["szT"].rearrange("(c p) t -> p c t", p=128)
        self.wib = P.sb("wib", [128, KC, self.NCOL], BF16)
        self.Bwib = P.buf()
        self.flr = Rot(P, "a_fl", [16, TT], F32, 2)

    def load_weights(self):
        StageAFox.load_weights(self)
        P = self.P
        self.wba = P.sb("wba", [128, KC, 16], BF16)
        P.op("dve", lambda e: e.tensor_copy(out=self.wba[:].rearrange("p k (g h w) -> p k g h w", g=4, h=2, w=2),
                                            in_=self.wib[:, :, 4096:4112].rearrange("p k (w g h) -> p k g h w", w=2, g=4, h=2)),
             reads=[self.Bwib], writes=[self.Bwib])

    def tile(self, i, s):
        P, cm = self.P, self.cm
        sl = slice(i * TT, (i + 1) * TT)
        xb, xf = self.xb, self.xf
        for c in range(24):
            ps, Bps = self.fm_proj(c * 128, s)
            t, Bt = cm.f32r.next()
            if c % 2 == 0:
                P.op("act", lambda e, t=t, ps=ps: e.activation(out=t[:], in_=ps[:], func=AF.Copy), reads=[Bps], writes=[Bt])
            else:
                P.op("dve", lambda e, t=t, ps=ps: e.tensor_copy(out=t[:], in_=ps[:]), reads=[Bps], writes=[Bt])
            which, h = c // 8, c % 8
            row0 = (h // 2) * 768 + ((h % 2) * 3 + which) * 128
            P.dma("sp", xb[row0:row0 + 128, sl], t[:], reads=[Bt], store=True)
        for c in range(KC):
            ps, Bps = self.fm_proj(3072 + c * 128, s)
            t, Bt = cm.f32r.next()
            P.op("act", lambda e, t=t, ps=ps: e.activation(out=t[:], in_=ps[:], func=AF.Silu), reads=[Bps], writes=[Bt])
            P.dma("sp", self.szT[:, c, sl], t[:], reads=[Bt], store=True)
        ps, Bps = cm.psum()
        for kc in range(KC):
            P.op("pe", lambda e, kc=kc, ps=ps: e.matmul(ps[0:16, :], lhsT=self.wba[:, kc, :], rhs=cm.hT[s][:, kc, :],
                                                        start=(kc == 0), stop=(kc == KC - 1)), reads=[self.Bwib, cm.BhT[s]], writes=[Bps])
        t, Bt = self.flr.next()
        P.op("dve", lambda e, t=t, ps=ps: e.tensor_copy(out=t[:], in_=ps[0:16, :]), reads=[Bps], writes=[Bt])
        P.dma("sp", xf[0:16, sl], t[:], reads=[Bt], store=True)


class StageANull:
    def __init__(self, P, cm, io, NTOK):
        pass

    def load_weights(self):
        pass

    def tile(self, i, s):
        pass


import math
import numpy as np
from concourse.bass import ds

S = 16384
NB = S // 128
QT = 512
NEG = -30000.0


def fox_consts():
    k = np.arange(128)
    U = (k[:, None] <= k[None, :]).astype(np.float32)
    SU = (k[:, None] < k[None, :]).astype(np.float32)
    E127 = np.zeros((128, 128), np.float32)
    E127[127, :] = 1.0
    ident = np.eye(128, dtype=np.float32)
    masks = np.zeros((4, 128, 512), np.float32)
    for r in range(4):
        for rp in range(4):
            blk = masks[r][:, rp * 128:(rp + 1) * 128]
            if rp < r:
                blk[:] = NEG
            elif rp == r:
                blk[:] = np.where(k[:, None] <= k[None, :], 0.0, NEG)
    return {"cU": U, "cSU": SU, "cE127": E127, "cI": ident, "cMask": np.ascontiguousarray(masks.transpose(1, 0, 2).reshape(128, 2048))}


def build_foxB(P, io, NH=2, S_=S):
    NBk = S_ // 128
    NQ = S_ // QT
    NTOK = S_ // 4
    scale = 1.0 / math.sqrt(128.0)
    yb = io["yb"]
    By = io["yb_bufs"]
    qTq = [yb[c_ * 768:c_ * 768 + 256, :].rearrange("(h p) s -> h p s", p=128) for c_ in range(4)]
    kTq = [yb[c_ * 768 + 256:c_ * 768 + 512, :].rearrange("(h p) s -> h p s", p=128) for c_ in range(4)]
    vvq = [yb[c_ * 768 + 512:c_ * 768 + 768, :].rearrange("(h r) (q d) -> h (r q) d", h=2, d=128) for c_ in range(4)]
    fl = io["yf"]
    bfd = io["fox_bf"]
    cdr = {n: io[n] for n in ("cU", "cSU", "cE127", "cI", "cMask")}
    xo = io["xo"]

    def const(name, w):
        t = P.sb("k" + name, [128, w], F32)
        B = P.buf()
        P.dma("sp", t[:], cdr[name][:, :], writes=[B])
        return t, B

    U, BU = const("cU", 128)
    SU, BSU = const("cSU", 128)
    E127, BE = const("cE127", 128)
    I32f, BI = const("cI", 128)
    Mf, BM = const("cMask", 2048)
    Bk = P.buf()
    ones_f = P.sb("ones_f", [128, 128], F32)
    ones_b = P.sb("ones_b", [128, 128], BF16)
    ident_b = P.sb("ident_b", [128, 128], BF16)
    mask_b = P.sb("mask_b", [128, 2048], BF16)
    P.op("dve", lambda e: e.memset(ones_f[:], 1.0), writes=[Bk])
    P.op("dve", lambda e: e.memset(ones_b[:], 1.0), writes=[Bk])
    P.op("dve", lambda e: e.tensor_copy(out=ident_b[:], in_=I32f[:]), reads=[BI], writes=[Bk])
    P.op("dve", lambda e: e.tensor_copy(out=mask_b[:], in_=Mf[:]), reads=[BM], writes=[Bk])
    bfs = P.sb("bfs", [128, NH], F32)
    nbf = P.sb("nbf", [128, NH], F32)
    Bbf = P.buf()
    P.dma("sp", bfs[:], bfd[:, 0:NH], writes=[Bbf])
    P.op("dve", lambda e: e.tensor_scalar(out=nbf[:], in0=bfs[:], scalar1=-1.0, scalar2=None, op0=ALU.mult),
         reads=[Bbf], writes=[Bbf])
    flr = P.sb("flr", [128, NH, 128], F32)
    Bflr = P.buf()
    P.dma("sp", flr[:], fl.rearrange("h (b s) -> b h s", s=128), writes=[Bflr])
    fls = P.sb("fls", [128, NH * NBk], F32)
    Bfl = P.buf()

    ks = [P.sb("ks%d" % i, [128, S_], BF16) for i in range(2)]
    Bks = [[P.buf() for _ in range(4)] for _ in range(2)]
    vs = [P.sb("vs%d" % i, [128, NBk, 128], BF16) for i in range(2)]
    Bvs = [[P.buf() for _ in range(4)] for _ in range(2)]
    cpos = [P.sb("cpos%d" % i, [128, NBk], F32) for i in range(2)]
    clast = [P.sb("clast%d" % i, [128, NBk], F32) for i in range(2)]
    Bcp = [P.buf(), P.buf()]
    l1 = P.sb("l1", [128, NBk], F32)
    Bl1 = P.buf()
    totT = P.sb("totT", [128, 128], F32)
    Btot = P.buf()
    qs = [P.sb("qs%d" % i, [128, QT], BF16) for i in range(2)]
    Bqs = [P.buf(), P.buf()]
    biasM = [P.sb("biasM%d" % i, [128, NBk], F32) for i in range(2)]
    BbM = [P.buf(), P.buf()]
    Rm = [P.sb("Rm%d" % i, [128, QT], BF16) for i in range(2)]
    BRm = [P.buf(), P.buf()]
    NS = 3
    pst = [P.ps("pst%d" % i, [128, QT]) for i in range(NS)]
    Bpst = [P.buf() for _ in range(NS)]
    pts = [P.sb("pts%d" % i, [128, QT], BF16) for i in range(NS)]
    Bpts = [P.buf() for _ in range(NS)]
    po = [P.ps("po%d" % i, [128, QT]) for i in range(2)]
    Bpo = [P.buf(), P.buf()]
    pr = [P.ps("pr%d" % i, [128, QT]) for i in range(2)]
    Bpr = [P.buf(), P.buf()]
    racc = [[P.sb("racc%d%d" % (i, z), [128, QT], F32) for z in range(2)] for i in range(2)]
    Bracc = [[P.buf(), P.buf()] for _ in range(2)]
    rinv = P.sb("rinv", [128, QT], F32)
    Brinv = P.buf()
    osb = [P.sb("osb%d" % i, [128, QT], F32) for i in range(2)]
    Bosb = [P.buf(), P.buf()]
    pmisc = P.ps("pmisc", [128, 512])
    Bpm = P.buf()

    def load_kv(h, c_):
        hs = h % 2
        P.dma("sp", ks[hs][:, c_ * NTOK:(c_ + 1) * NTOK], kTq[c_][h, :, :], reads=[By[c_]], writes=[Bks[hs][c_]])
        P.dma("sp", vs[hs][:, c_ * (NTOK // 128):(c_ + 1) * (NTOK // 128), :], vvq[c_][h].rearrange("(b s) d -> s b d", s=128),
              reads=[By[c_]], writes=[Bvs[hs][c_]])

    def head_prep(h):
        hs = h % 2
        for c_ in range(4 if h > 0 else 1):
            load_kv(h, c_)
        P.op("pe", lambda e: e.transpose(pmisc[:, 384:384 + NBk], flr[:, h, :], I32f[:]), reads=[Bflr, BI], writes=[Bpm])
        P.op("dve", lambda e: e.tensor_copy(out=fls[:, h * NBk:(h + 1) * NBk], in_=pmisc[:, 384:384 + NBk]), reads=[Bpm], writes=[Bfl])
        f_h = fls[:, h * NBk:(h + 1) * NBk]
        P.op("act", lambda e: e.activation(out=l1[:], in_=f_h, func=AF.Exp, scale=-1.0, bias=nbf[:, h:h + 1]),
             reads=[Bfl, Bbf], writes=[Bl1])
        P.op("act", lambda e: e.activation(out=l1[:], in_=l1[:], func=AF.Ln, scale=1.0, bias=ones_f[:, 0:1]),
             reads=[Bl1, Bk], writes=[Bl1])
        P.op("pe", lambda e: e.matmul(pmisc[0:NBk, 0:128], lhsT=l1[:, 0:NBk], rhs=ones_f[:], start=True, stop=True),
             reads=[Bl1, Bk], writes=[Bpm])
        P.op("dve", lambda e: e.tensor_copy(out=totT[0:NBk, :], in_=pmisc[0:NBk, 0:128]), reads=[Bpm], writes=[Btot])
        P.op("pe", lambda e: e.matmul(pmisc[:, 128:128 + NBk], lhsT=U[:], rhs=l1[:, 0:NBk], start=True, stop=False),
             reads=[Bl1, BU], writes=[Bpm])
        P.op("pe", lambda e: e.matmul(pmisc[:, 128:128 + NBk], lhsT=totT[0:NBk, :], rhs=SU[0:NBk, 0:NBk], start=False, stop=True),
             reads=[Btot, BSU], writes=[Bpm])
        P.op("dve", lambda e: e.tensor_copy(out=cpos[hs][:], in_=pmisc[:, 128:128 + NBk]), reads=[Bpm], writes=[Bcp[hs]])
        P.op("pe", lambda e: e.matmul(pmisc[:, 256:256 + NBk], lhsT=E127[:], rhs=cpos[hs][:], start=True, stop=True),
             reads=[Bcp[hs], BE], writes=[Bpm])
        P.op("dve", lambda e: e.tensor_copy(out=clast[hs][:], in_=pmisc[:, 256:256 + NBk]), reads=[Bpm], writes=[Bcp[hs]])

    items = []
    for h in range(NH):
        for j in range(NQ):
            nkb = 4 * j + 4
            for kb in range(nkb):
                items.append((h, j, kb, nkb))
    qcount = [0]

    def qtile_prep(h, j):
        P.fill_step(2 if j >= 8 else 0, q="sp", maxkey=3)
        hs = h % 2
        s = qcount[0] % 2
        qcount[0] += 1
        nkb = 4 * j + 4
        cq = (j * QT) // NTOK
        if h == 0 and cq > 0 and (j * QT) % NTOK == 0:
            load_kv(h, cq)
        P.dma("sp", qs[s][:], qTq[cq][h, :, j * QT - cq * NTOK:(j + 1) * QT - cq * NTOK], reads=[By[cq]], writes=[Bqs[s]])
        P.op("dve", lambda e: e.tensor_scalar(out=biasM[s][:, 0:nkb], in0=cpos[hs][:, 0:nkb],
                                              scalar1=clast[hs][:, nkb - 1:nkb], scalar2=None, op0=ALU.subtract),
             reads=[Bcp[hs]], writes=[BbM[s]])
        for r in range(4):
            P.op("dve", lambda e, r=r: e.tensor_scalar(out=Rm[s][:, r * 128:(r + 1) * 128], in0=I32f[:],
                                                         scalar1=biasM[s][:, 4 * j + r:4 * j + r + 1], scalar2=-math.sqrt(128.0),
                                                         op0=ALU.mult, op1=ALU.mult),
                 reads=[BI, BbM[s]], writes=[BRm[s]])
        return s

    qslot = {}

    def QK(n):
        h, j, kb, nkb = items[n]
        hs = h % 2
        if kb == 0:
            if j == 0:
                head_prep(h)
            qslot[(h, j)] = qtile_prep(h, j)
        s = qslot[(h, j)]
        b = n % NS
        diag = kb >= 4 * j
        P.op("pe", lambda e: e.matmul(pst[b][:], lhsT=ks[hs][:, kb * 128:(kb + 1) * 128], rhs=qs[s][:], start=True, stop=False),
             reads=[Bks[hs][(kb * 128) // NTOK], Bqs[s]], writes=[Bpst[b]])
        P.op("pe", lambda e: e.matmul(pst[b][:], lhsT=ones_b[:], rhs=Rm[s][:], start=False, stop=not diag),
             reads=[Bk, BRm[s]], writes=[Bpst[b]])
        if diag:
            r = kb - 4 * j
            P.op("pe", lambda e: e.matmul(pst[b][:], lhsT=ident_b[:], rhs=mask_b[:, r * 512:(r + 1) * 512], start=False, stop=True),
                 reads=[Bk], writes=[Bpst[b]])

    def PV(n):
        h, j, kb, nkb = items[n]
        hs = h % 2
        s = qslot[(h, j)]
        b = n % NS
        a = (h * NQ + j) % 2
        P.op("act", lambda e: e.activation(out=pts[b][:], in_=pst[b][:], func=AF.Exp, scale=scale, bias=biasM[s][:, kb:kb + 1]),
             reads=[Bpst[b], BbM[s]], writes=[Bpts[b]])
        P.op("pe", lambda e: e.matmul(po[a][:], lhsT=vs[hs][:, kb, :], rhs=pts[b][:], start=(kb == 0), stop=(kb == nkb - 1)),
             reads=[Bvs[hs][(kb * 128) // NTOK], Bpts[b]], writes=[Bpo[a]])
        ra, Bra = racc[a][kb % 2], Bracc[a][kb % 2]
        if kb < 2:
            P.op("dve", lambda e: e.tensor_copy(out=ra[:], in_=pts[b][:]), reads=[Bpts[b]], writes=[Bra])
        else:
            P.op("dve", lambda e: e.tensor_tensor(out=ra[:], in0=ra[:], in1=pts[b][:], op=ALU.add), reads=[Bpts[b], Bra], writes=[Bra])
        if kb == nkb - 1:
            for z_ in range(2):
                P.op("pe", lambda e, z_=z_: e.matmul(pr[a][:], lhsT=ones_f[:], rhs=racc[a][z_][:], start=(z_ == 0), stop=(z_ == 1)),
                     reads=[Bk, Bracc[a][z_]], writes=[Bpr[a]])
            P.op("dve", lambda e: e.reciprocal(out=rinv[:], in_=pr[a][:]), reads=[Bpr[a]], writes=[Brinv])
            P.op("dve", lambda e: e.tensor_tensor(out=osb[a][:], in0=po[a][:], in1=rinv[:], op=ALU.mult),
                 reads=[Bpo[a], Brinv], writes=[Bosb[a]])
            t0 = j * QT
            r0_ = (t0 // NTOK) * 256 + h * 128
            sub_ = (t0 % NTOK) // (NTOK // 8)
            P.dma("sp", xo[r0_:r0_ + 128, (t0 % NTOK):(t0 % NTOK) + QT], osb[a][:], reads=[Bosb[a]], writes=[io["lo_bufs"][sub_]], store=True, chan=Bosb[a])
            if h == NH - 1 and t0 >= 3 * NTOK and ((t0 + QT) % (NTOK // 8)) == 0:
                io["sub_done"](sub_)

    LA = 2
    for n in range(len(items) + LA):
        if n < len(items):
            QK(n)
        if n - LA >= 0:
            PV(n - LA)
    return P


import math
import numpy as np
from concourse.bass import ds

NEG = -30000.0
TT = 512


def gdn_consts():
    k = np.arange(128)
    same = (k[:, None] // 64) == (k[None, :] // 64)
    c = {}
    c["gU"] = ((k[:, None] <= k[None, :]) & same).astype(np.float32)
    c["gSL"] = ((k[:, None] > k[None, :]) & same).astype(np.float32)
    c["gSame"] = same.astype(np.float32)
    h0 = np.zeros((128, 128), np.float32)
    h0[:64, :] = 1.0
    c["gH0"] = h0
    c["gH1"] = 1.0 - h0
    c["gI"] = np.eye(128, dtype=np.float32)
    c["gMS"] = np.where((k[:, None] > k[None, :]) & same, 0.0, NEG).astype(np.float32)
    c["gMT"] = np.where((k[None, :] >= k[:, None]) & same, 0.0, NEG).astype(np.float32)
    return c


CN = ("gU", "gSL", "gSame", "gH0", "gH1", "gI", "gMS", "gMT")


def build_gdnB(P, io, NH=2, S_=16384, dbg=False):
    NB = S_ // 128
    NT = S_ // TT
    NTOK = S_ // 4
    qkvq = [io["yb"][c_ * 768:(c_ + 1) * 768, :].rearrange("(h w p) s -> h w p s", h=NH, w=3) for c_ in range(16)]
    By = io["yb_bufs"]
    wcv = io["gdn_wcv"]
    ba = io["yf"]
    hp = io["gdn_hp"]
    onw = io["gdn_onw"]
    cdr = {n: io[n] for n in CN}
    xo = io["xo"]

    def ld(name, dram, w):
        t = P.sb(name, [128, w], F32)
        B = P.buf()
        P.dma("sp", t[:], dram, writes=[B])
        return t, B

    K = {}
    BK = P.buf()
    for n in CN:
        K[n] = P.sb("k" + n, [128, 128], F32)
        P.dma("sp", K[n][:], cdr[n][:, :], writes=[BK])
    wcs, Bw = ld("wcs", wcv[:, 0:NH * 12], NH * 12)
    hps, Bhp = ld("hps", hp[:, 0:NH * 2], NH * 2)
    bar = P.sb("bar", [128, NH * 2, 128], F32)
    Bbar = P.buf()
    P.dma("sp", bar[:], ba.rearrange("r (b s) -> b r s", s=128), writes=[Bbar])
    bas = P.sb("bas", [128, NH * 2 * NB], F32)
    Bba = P.buf()
    onws, Bon = ld("onws", onw[:, :], 128)
    ones_b = P.sb("ones_b", [128, 128], BF16)
    ident_b = P.sb("ident_b", [128, 128], BF16)
    epsc = P.sb("epsc", [128, 1], F32)
    one_c = P.sb("one_c", [128, 1], F32)
    P.op("dve", lambda e: e.memset(ones_b[:], 1.0), writes=[BK])
    P.op("dve", lambda e: e.memset(epsc[:], 1e-6), writes=[BK])
    P.op("dve", lambda e: e.memset(one_c[:], 1.0), writes=[BK])
    P.op("dve", lambda e: e.tensor_copy(out=ident_b[:], in_=K["gI"][:]), reads=[BK], writes=[BK])

    pf = [P.ps("pf%d" % i, [128, 512]) for i in range(6)]
    Bpf = [P.buf() for _ in range(6)]
    for b_ in Bpf:
        b_.excl = True
    pfslots = [(pf[i % 6][:, ((i // 6) % 4) * 128:((i // 6) % 4 + 1) * 128], Bpf[i % 6]) for i in range(24)]
    pbt = P.ps("pbt", [128, 1024], BF16)
    Bpbt = P.buf()
    Bpbt.excl = True
    pbslots = [(pbt[:, i * 128:(i + 1) * 128], Bpbt) for i in range(8)]
    pbig = P.ps("pbig", [128, 512])
    Bpbig = P.buf()
    Bpbig.excl = True
    for r_ in range(NH * 2):
        P.op("pe", lambda e, r_=r_: e.transpose(pbig[:, (r_ % 4) * 128:(r_ % 4) * 128 + NB], bar[:, r_, :], K["gI"][:]), reads=[Bbar, BK], writes=[Bpbig])
        P.op("dve", lambda e, r_=r_: e.tensor_copy(out=bas[:, r_ * NB:(r_ + 1) * NB], in_=pbig[:, (r_ % 4) * 128:(r_ % 4) * 128 + NB]),
             reads=[Bpbig], writes=[Bba])
    cnt = {"f": 0, "b": 0, 0: 0, 1: 0}
    cur_head = [0]

    def psf():
        h_ = cur_head[0]
        cnt[h_] += 1
        k_ = cnt[h_] % 12
        bank = h_ * 3 + (k_ % 3)
        return pf[bank][:, (k_ // 3) * 128:(k_ // 3 + 1) * 128], Bpf[bank]

    def psb():
        cnt["b"] += 1
        return pbslots[cnt["b"] % 8]

    class Head:
        pass

    heads = []
    for hh in range(NH):
        H = Head()
        H.hh = hh
        n = "h%d_" % hh
        mk = lambda nm, w, dt=F32: P.sb(n + nm, [128, w], dt)
        H.tab = {t: mk("t" + t, NB) for t in ("beta", "nbeta", "g", "gc", "egc", "ekd", "bgc", "eg0", "eg1", "tmp")}
        H.Btab = P.buf()
        H.raw = [P.sb(n + "raw%d" % i_, [128, 3, TT + 3], F32) for i_ in range(2)]
        H.Braw = [P.buf(), P.buf()]
        H.cv = [mk("cv%d" % w, TT) for w in range(3)]
        H.Bcv = [P.buf() for _ in range(3)]
        H.sq = mk("sq", TT, BF16)
        H.Bsq = P.buf()
        H.rs = mk("rs", TT)
        H.Brs = P.buf()
        H.qn = mk("qn", TT, BF16)
        H.kn = mk("kn", TT, BF16)
        H.Bqn, H.Bkn = P.buf(), P.buf()
        H.S32 = mk("S32", 128)
        H.Sb = [mk("Sb%d" % i, 128, BF16) for i in range(2)]
        H.BS32 = P.buf()
        H.BSb = [P.buf(), P.buf()]
        H.scur = 0
        H.osb = mk("osb", TT)
        H.Bosb = P.buf()
        H.t = {}
        H.B = {}
        for nm, dt in (("kbg", BF16), ("kdec", BF16), ("vb32", F32), ("vb16", BF16), ("gU", F32), ("Ds", F32), ("DT", F32),
                       ("X", BF16), ("XT", BF16), ("X2", BF16), ("XT2", BF16), ("N", BF16), ("N2", BF16),
                       ("AqkT", BF16), ("u32", F32), ("wT", BF16), ("vnew", BF16), ("aq", F32), ("o32", F32),
                       ("on32", F32), ("junk", F32)):
            H.t[nm] = mk("b_" + nm, 128, dt)
            H.B[nm] = P.buf()
        H.ss = mk("ss", 1)
        H.rstd = mk("rstd", 1)
        H.Bss = P.buf()
        heads.append(H)

    def head_tables(H):
        hh = H.hh
        T = H.tab
        bl = bas[:, (hh * 2) * NB:(hh * 2 + 1) * NB]
        al = bas[:, (hh * 2 + 1) * NB:(hh * 2 + 2) * NB]
        rw = [Bba, Bhp, BK, H.Btab]
        P.op("act", lambda e: e.activation(out=T["beta"][:], in_=bl, func=AF.Sigmoid), reads=rw, writes=[H.Btab])
        P.op("dve", lambda e: e.tensor_scalar(out=T["nbeta"][:], in0=T["beta"][:], scalar1=-1.0, scalar2=None, op0=ALU.mult),
             reads=rw, writes=[H.Btab])
        P.op("act", lambda e: e.activation(out=T["tmp"][:], in_=al, func=AF.Exp, bias=hps[:, hh * 2 + 1:hh * 2 + 2], scale=1.0),
             reads=rw, writes=[H.Btab])
        P.op("act", lambda e: e.activation(out=T["tmp"][:], in_=T["tmp"][:], func=AF.Ln, bias=one_c[:, 0:1], scale=1.0),
             reads=rw, writes=[H.Btab])
        P.op("act", lambda e: e.activation(out=H.ss[:], in_=hps[:, hh * 2:hh * 2 + 1], func=AF.Exp), reads=rw, writes=[H.Bss])
        P.op("dve", lambda e: e.tensor_scalar(out=T["g"][:], in0=T["tmp"][:], scalar1=H.ss[:, 0:1], scalar2=-1.0,
                                              op0=ALU.mult, op1=ALU.mult), reads=rw + [H.Bss], writes=[H.Btab])
        P.op("pe", lambda e: e.matmul(pbig[:, 0:NB], lhsT=K["gU"][:], rhs=T["g"][:], start=True, stop=True), reads=rw, writes=[Bpbig])
        P.op("pe", lambda e: e.matmul(pbig[:, 128:128 + NB], lhsT=K["gSame"][:], rhs=T["g"][:], start=True, stop=True), reads=rw, writes=[Bpbig])
        P.op("pe", lambda e: e.matmul(pbig[:, 256:256 + NB], lhsT=K["gH0"][:], rhs=T["g"][:], start=True, stop=True), reads=rw, writes=[Bpbig])
        P.op("pe", lambda e: e.matmul(pbig[:, 384:384 + NB], lhsT=K["gH1"][:], rhs=T["g"][:], start=True, stop=True), reads=rw, writes=[Bpbig])
        P.op("dve", lambda e: e.tensor_copy(out=T["gc"][:], in_=pbig[:, 0:NB]), reads=[Bpbig], writes=[H.Btab])
        P.op("act", lambda e: e.activation(out=T["egc"][:], in_=pbig[:, 0:NB], func=AF.Exp), reads=[Bpbig], writes=[H.Btab])
        P.op("dve", lambda e: e.tensor_tensor(out=T["tmp"][:], in0=pbig[:, 128:128 + NB], in1=T["gc"][:], op=ALU.subtract),
             reads=[Bpbig, H.Btab], writes=[H.Btab])
        P.op("act", lambda e: e.activation(out=T["ekd"][:], in_=T["tmp"][:], func=AF.Exp), reads=[H.Btab], writes=[H.Btab])
        P.op("act", lambda e: e.activation(out=T["eg0"][:], in_=pbig[:, 256:256 + NB], func=AF.Exp), reads=[Bpbig], writes=[H.Btab])
        P.op("act", lambda e: e.activation(out=T["eg1"][:], in_=pbig[:, 384:384 + NB], func=AF.Exp), reads=[Bpbig], writes=[H.Btab])
        P.op("dve", lambda e: e.tensor_tensor(out=T["bgc"][:], in0=T["beta"][:], in1=T["egc"][:], op=ALU.mult),
             reads=[H.Btab], writes=[H.Btab])
        P.op("dve", lambda e: e.memset(H.S32[:], 0.0), writes=[H.BS32])
        P.op("dve", lambda e: e.memset(H.Sb[0][:], 0.0), writes=[H.BSb[0]])
        P.op("dve", lambda e: e.memset(H.raw[0][:, :, 0:3], 0.0), writes=[H.Braw[0]])

    def load_raw(H, ti):
        hh = H.hh
        s_ = ti % 2
        c_ = (ti * TT) // (NTOK // 4)
        lo = ti * TT - c_ * (NTOK // 4)
        P.dma("sp", H.raw[s_][:, :, 3:TT + 3], qkvq[c_][hh, :, :, lo:lo + TT].rearrange("w p t -> p w t"), reads=[By[c_]], writes=[H.Braw[s_]])

    def carry(H, ti):
        s_ = ti % 2
        if ti > 0:
            P.op("dve", lambda e: e.tensor_copy(out=H.raw[s_][:, :, 0:3], in_=H.raw[1 - s_][:, :, TT:TT + 3]), reads=[H.Braw[1 - s_]], writes=[H.Braw[s_]])

    def conv_phase(H, ti):
        hh = H.hh
        s_ = ti % 2
        raw, Braw = H.raw[s_], H.Braw[s_]
        for w in range(3):
            cv, Bc = H.cv[w], H.Bcv[w]
            wcl = [wcs[:, hh * 12 + w * 4 + k:hh * 12 + w * 4 + k + 1] for k in range(4)]
            P.op("dve", lambda e, cv=cv, w=w, wcl=wcl: e.tensor_scalar(out=cv[:], in0=raw[:, w, 0:TT], scalar1=wcl[0], scalar2=None, op0=ALU.mult),
                 reads=[Braw, Bw], writes=[Bc])
            for k in (1, 2, 3):
                P.op("dve", lambda e, cv=cv, k=k, w=w, wcl=wcl: e.scalar_tensor_tensor(out=cv[:], in0=raw[:, w, k:k + TT], scalar=wcl[k], in1=cv[:],
                                                                                    op0=ALU.mult, op1=ALU.add), reads=[Braw, Bw, Bc], writes=[Bc])
            P.op("act", lambda e, cv=cv: e.activation(out=cv[:], in_=cv[:], func=AF.Silu), reads=[Bc], writes=[Bc])
        for w, dst, Bd, sc in ((0, H.qn, H.Bqn, 1.0 / math.sqrt(128.0)), (1, H.kn, H.Bkn, 1.0)):
            cv, Bc = H.cv[w], H.Bcv[w]
            P.op("act", lambda e, cv=cv: e.activation(out=H.sq[:], in_=cv[:], func=AF.Square), reads=[Bc], writes=[H.Bsq])
            P.op("pe", lambda e: e.matmul(pbig[:], lhsT=ones_b[:], rhs=H.sq[:], start=True, stop=True), reads=[H.Bsq, BK], writes=[Bpbig])
            P.op("act", lambda e: e.activation(out=H.rs[:], in_=pbig[:], func=AF.Ln, bias=epsc[:, 0:1], scale=1.0), reads=[Bpbig, BK], writes=[H.Brs])
            P.op("act", lambda e: e.activation(out=H.rs[:], in_=H.rs[:], func=AF.Exp, scale=-0.5), reads=[H.Brs], writes=[H.Brs])
            P.op("dve", lambda e, cv=cv, dst=dst, sc=sc: e.scalar_tensor_tensor(out=dst[:], in0=cv[:], scalar=sc, in1=H.rs[:],
                                                                              op0=ALU.mult, op1=ALU.mult), reads=[Bc, H.Brs], writes=[Bd])

    def evac(eng, dst, Bdst, src, Bsrc, extra_reads=()):
        if eng == "act":
            P.op("act", lambda e: e.activation(out=dst, in_=src, func=AF.Copy), reads=[Bsrc] + list(extra_reads), writes=[Bdst])
        else:
            P.op("dve", lambda e: e.tensor_copy(out=dst, in_=src), reads=[Bsrc] + list(extra_reads), writes=[Bdst])

    def pre_scan(H, blk, bi):
        T, t, B = H.tab, H.t, H.B
        cur_head[0] = H.hh
        cs = slice(bi * 128, (bi + 1) * 128)
        col = lambda nm: T[nm][:, blk:blk + 1]
        kn, qn = H.kn[:, cs], H.qn[:, cs]
        pk, Bpk = psb()
        P.op("pe", lambda e: e.transpose(pk, kn, ident_b[:]), reads=[H.Bkn, BK], writes=[Bpk])
        yield
        P.op("act", lambda e: e.activation(out=t["kbg"][:], in_=pk, func=AF.Copy, scale=col("bgc")), reads=[Bpk, H.Btab], writes=[B["kbg"]])
        yield
        P.op("dve", lambda e: e.tensor_scalar(out=t["kdec"][:], in0=pk, scalar1=col("ekd"), scalar2=None, op0=ALU.mult),
             reads=[Bpk, H.Btab], writes=[B["kdec"]])
        yield
        pv, Bpv = psf()
        P.op("pe", lambda e: e.transpose(pv, H.cv[2][:, cs], K["gI"][:]), reads=[H.Bcv[2], BK], writes=[Bpv])
        yield
        P.op("dve", lambda e: e.tensor_scalar(out=t["vb32"][:], in0=pv, scalar1=col("beta"), scalar2=None, op0=ALU.mult),
             reads=[Bpv, H.Btab], writes=[B["vb32"]])
        yield
        P.op("act", lambda e: e.activation(out=t["vb16"][:], in_=pv, func=AF.Copy, scale=col("beta")), reads=[Bpv, H.Btab], writes=[B["vb16"]])
        yield
        P.op("act", lambda e: e.activation(out=t["gU"][:], in_=K["gU"][:], func=AF.Copy, scale=col("g")),
             reads=[BK, H.Btab], writes=[B["gU"]])
        yield
        pd, Bpd = psf()
        P.op("pe", lambda e: e.matmul(pd, lhsT=t["gU"][:], rhs=K["gSL"][:], start=True, stop=False), reads=[B["gU"], BK], writes=[Bpd])
        P.op("pe", lambda e: e.matmul(pd, lhsT=K["gI"][:], rhs=K["gMS"][:], start=False, stop=True), reads=[BK], writes=[Bpd])
        yield
        P.op("act", lambda e: e.activation(out=t["Ds"][:], in_=pd, func=AF.Exp), reads=[Bpd], writes=[B["Ds"]])
        yield
        pdt, Bpdt = psf()
        P.op("pe", lambda e: e.matmul(pdt, lhsT=K["gSL"][:], rhs=t["gU"][:], start=True, stop=False), reads=[B["gU"], BK], writes=[Bpdt])
        P.op("pe", lambda e: e.matmul(pdt, lhsT=K["gI"][:], rhs=K["gMT"][:], start=False, stop=True), reads=[BK], writes=[Bpdt])
        yield
        P.op("act", lambda e: e.activation(out=t["DT"][:], in_=pdt, func=AF.Exp), reads=[Bpdt], writes=[B["DT"]])
        yield
        pg, Bpg = psf()
        P.op("pe", lambda e: e.matmul(pg, lhsT=kn, rhs=kn, start=True, stop=True), reads=[H.Bkn], writes=[Bpg])
        yield
        P.op("dve", lambda e: e.scalar_tensor_tensor(out=t["X"][:], in0=pg, scalar=col("nbeta"), in1=t["Ds"][:], op0=ALU.mult, op1=ALU.mult),
             reads=[Bpg, H.Btab, B["Ds"]], writes=[B["X"]])
        yield
        pq, Bpq = psf()
        P.op("pe", lambda e: e.matmul(pq, lhsT=kn, rhs=qn, start=True, stop=True), reads=[H.Bkn, H.Bqn], writes=[Bpq])
        yield
        P.op("dve", lambda e: e.tensor_tensor(out=t["AqkT"][:], in0=pq, in1=t["DT"][:], op=ALU.mult), reads=[Bpq, B["DT"]], writes=[B["AqkT"]])
        yield
        px, Bpx = psb()
        P.op("pe", lambda e: e.transpose(px, t["X"][:], ident_b[:]), reads=[B["X"], BK], writes=[Bpx])
        yield
        evac("act", t["XT"][:], B["XT"], px, Bpx)
        yield
        evac("dve", t["N"][:], B["N"], px, Bpx)
        yield
        X, XT, X2, XT2, N, N2 = "X", "XT", "X2", "XT2", "N", "N2"
        for lvl in range(5):
            p1, Bp1 = psf()
            P.op("pe", lambda e, p1=p1, X=X, XT=XT: e.matmul(p1, lhsT=t[XT][:], rhs=t[X][:], start=True, stop=True),
                 reads=[B[X], B[XT]], writes=[Bp1])
            yield
            p2, Bp2 = psf()
            P.op("pe", lambda e, p2=p2, X=X, XT=XT: e.matmul(p2, lhsT=t[X][:], rhs=t[XT][:], start=True, stop=True),
                 reads=[B[X], B[XT]], writes=[Bp2])
            yield
            evac("act", t[X2][:], B[X2], p1, Bp1)
            yield
            evac("dve", t[XT2][:], B[XT2], p2, Bp2)
            yield
            p3, Bp3 = psf()
            P.op("pe", lambda e, p3=p3, X2=X2, N=N: e.matmul(p3, lhsT=t[X2][:], rhs=t[N][:], start=True, stop=False),
                 reads=[B[X2], B[N]], writes=[Bp3])
            P.op("pe", lambda e, p3=p3, N=N: e.matmul(p3, lhsT=ident_b[:], rhs=t[N][:], start=False, stop=False),
                 reads=[BK, B[N]], writes=[Bp3])
            P.op("pe", lambda e, p3=p3, XT2=XT2: e.matmul(p3, lhsT=ident_b[:], rhs=t[XT2][:], start=False, stop=True),
                 reads=[BK, B[XT2]], writes=[Bp3])
            yield
            evac("act" if lvl % 2 else "dve", t[N2][:], B[N2], p3, Bp3)
            yield
            X, X2 = X2, X
            XT, XT2 = XT2, XT
            N, N2 = N2, N
        pu, Bpu = psf()
        P.op("pe", lambda e, N=N: e.matmul(pu, lhsT=t[N][:], rhs=t["vb16"][:], start=True, stop=True), reads=[B[N], B["vb16"]], writes=[Bpu])
        yield
        P.op("dve", lambda e: e.tensor_tensor(out=t["u32"][:], in0=pu, in1=t["vb32"][:], op=ALU.add), reads=[Bpu, B["vb32"]], writes=[B["u32"]])
        yield
        pw, Bpw = psf()
        P.op("pe", lambda e, N=N: e.matmul(pw, lhsT=t["kbg"][:], rhs=t[N][:], start=True, stop=False), reads=[B["kbg"], B[N]], writes=[Bpw])
        P.op("pe", lambda e: e.matmul(pw, lhsT=t["kbg"][:], rhs=ident_b[:], start=False, stop=True), reads=[B["kbg"], BK], writes=[Bpw])
        yield
        evac("act", t["wT"][:], B["wT"], pw, Bpw)
        yield

    def scan_steps(H, blk, bi):
        T, t, B = H.tab, H.t, H.B
        cs0 = bi * 128
        steps = []
        pws, Bpws = psf()
        pqs, Bpqs = psf()
        for c in (0, 1):
            r = slice(c * 64, (c + 1) * 64)
            egl = T["eg%d" % c][:, blk:blk + 1]

            def s1(c=c, r=r):
                cur = H.scur
                P.op("pe", lambda e: e.matmul(pws[r, :], lhsT=t["wT"][:, r], rhs=H.Sb[cur][:], start=True, stop=True),
                     reads=[B["wT"], H.BSb[cur]], writes=[Bpws])
                P.op("pe", lambda e: e.matmul(pqs[r, :], lhsT=H.qn[:, cs0 + c * 64:cs0 + (c + 1) * 64], rhs=H.Sb[cur][:], start=True, stop=True),
                     reads=[H.Bqn, H.BSb[cur]], writes=[Bpqs])

            def s2(c=c, r=r):
                P.op("dve", lambda e: e.tensor_tensor(out=t["vnew"][r, :], in0=t["u32"][r, :], in1=pws[r, :], op=ALU.subtract),
                     reads=[B["u32"], Bpws], writes=[B["vnew"]])

            def s3(c=c, r=r, egl=egl):
                cur = H.scur
                nxt = 1 - cur
                pds, Bpds = psf()
                P.op("pe", lambda e: e.matmul(pds, lhsT=t["kdec"][r, :], rhs=t["vnew"][r, :], start=True, stop=True),
                     reads=[B["kdec"], B["vnew"]], writes=[Bpds])
                P.op("dve", lambda e: e.scalar_tensor_tensor(out=H.Sb[nxt][:], in0=H.S32[:], scalar=egl, in1=pds, op0=ALU.mult, op1=ALU.add),
                     reads=[H.BS32, H.Btab, Bpds], writes=[H.BSb[nxt]])
                P.op("dve", lambda e: e.scalar_tensor_tensor(out=H.S32[:], in0=H.S32[:], scalar=egl, in1=pds, op0=ALU.mult, op1=ALU.add),
                     reads=[H.BS32, H.Btab, Bpds], writes=[H.BS32])
                H.scur = nxt

            steps += [s1, s2, s3]

        def fin():
            pa, Bpa = psf()
            P.op("pe", lambda e: e.matmul(pa, lhsT=t["AqkT"][:], rhs=t["vnew"][:], start=True, stop=True), reads=[B["AqkT"], B["vnew"]], writes=[Bpa])
            evac("act", t["aq"][:], B["aq"], pa, Bpa)
            P.op("dve", lambda e: e.scalar_tensor_tensor(out=t["o32"][:], in0=pqs, scalar=T["egc"][:, blk:blk + 1], in1=t["aq"][:],
                                                         op0=ALU.mult, op1=ALU.add), reads=[Bpqs, H.Btab, B["aq"]], writes=[B["o32"]])
            P.op("act", lambda e: e.activation(out=t["junk"][:], in_=t["o32"][:], func=AF.Square, accum_out=H.ss[:, 0:1]),
                 reads=[B["o32"]], writes=[B["junk"], H.Bss])
            P.op("act", lambda e: e.activation(out=H.rstd[:], in_=H.ss[:], func=AF.Sqrt, scale=1.0 / 128.0, bias=epsc[:, 0:1]),
                 reads=[H.Bss, BK], writes=[H.Bss])
            P.op("dve", lambda e: e.reciprocal(out=H.rstd[:], in_=H.rstd[:]), reads=[H.Bss], writes=[H.Bss])
            P.op("dve", lambda e: e.scalar_tensor_tensor(out=t["on32"][:], in0=t["o32"][:], scalar=H.rstd[:, 0:1], in1=onws[:],
                                                         op0=ALU.mult, op1=ALU.mult), reads=[B["o32"], H.Bss, Bon], writes=[B["on32"]])
            po, Bpo = psf()
            P.op("pe", lambda e: e.transpose(po, t["on32"][:], K["gI"][:]), reads=[B["on32"], BK], writes=[Bpo])
            evac("act", H.osb[:, bi * 128:(bi + 1) * 128], H.Bosb, po, Bpo)

        steps.append(fin)
        return steps

    dbgn = []
    if dbg:
        dbgT = nc.dram_tensor("dbg", [128, 40 * 128], F32, kind="ExternalOutput").ap()
        dstg = P.sb("dstg", [128, 128], F32)
        Bdstg = P.buf()

        def dump(name, ap, Bs, w=128):
            i = len(dbgn)
            dbgn.append(name)
            P.op("dve", lambda e: e.tensor_copy(out=dstg[:, 0:w], in_=ap), reads=Bs, writes=[Bdstg])
            P.dma("sp", dbgT[:, i * 128:i * 128 + w], dstg[:, 0:w], reads=[Bdstg], store=True, final=True)
    P.dbgn = dbgn
    for H in heads:
        head_tables(H)
    for H in heads:
        load_raw(H, 0)
    for ti in range(NT):
        P.fill_step(1, q="sp")
        P.tag("ph5_gdnB_q%d" % (ti // 8))
        for H in heads:
            carry(H, ti)
        if ti + 1 < NT:
            for H in heads:
                load_raw(H, ti + 1)
        for H in heads:
            conv_phase(H, ti)
        for bi in range(TT // 128):
            blk = ti * 4 + bi
            gens = [pre_scan(H, blk, bi) for H in heads]
            gen_head = {id(gn): H.hh for gn, H in zip(gens, heads)}
            alive = list(gens)
            while alive:
                for gi_, gen in enumerate(list(alive)):
                    try:
                        cur_head[0] = gen_head[id(gen)]
                        next(gen)
                    except StopIteration:
                        alive.remove(gen)
            lists = []
            for H in heads:
                cur_head[0] = H.hh
                lists.append(scan_steps(H, blk, bi))
            for k in range(len(lists[0])):
                for hi_, L in enumerate(lists):
                    cur_head[0] = heads[hi_].hh
                    L[k]()
            if dbg and blk == 0:
                H = heads[0]
                for nm in ("g", "gc", "beta", "egc", "ekd", "eg0", "eg1", "bgc"):
                    dump("t_" + nm, H.tab[nm][:, 0:NB], [H.Btab], w=NB)
                dump("kn", H.kn[:, 0:128], [H.Bkn])
                dump("qn", H.qn[:, 0:128], [H.Bqn])
                dump("v", H.cv[2][:, 0:128], [H.Bcv[2]])
                for nm in H.t:
                    dump(nm, H.t[nm][:], [H.B[nm]])
                dump("S32", H.S32[:], [H.BS32])
        for H in heads:
            t0 = ti * TT
            r0_ = (t0 // NTOK) * 256 + H.hh * 128
            sub_ = (t0 % NTOK) // (NTOK // 8)
            P.dma("sp", xo[r0_:r0_ + 128, (t0 % NTOK):(t0 % NTOK) + TT], H.osb[:], reads=[H.Bosb], writes=[io["lo_bufs"][sub_]], store=True, chan=H.Bosb)
        if t0 >= 3 * NTOK and ((t0 + TT) % (NTOK // 8)) == 0:
            io["sub_done"]((t0 % NTOK) // (NTOK // 8))
    return P


import math
import numpy as np

HL = 128
NEG = -30000.0
TWO_PI = 2.0 * math.pi
C1 = 6.28125
C2 = TWO_PI - C1
MAGIC = 12582912.0
QA, QR, KA, KR, VV, ZZ, NCOL = 0, 1024, 2048, 2560, 3072, 3328, 4352


def swa_consts():
    k = np.arange(128)
    mprev = np.where(k[:, None] > k[None, :], 0.0, NEG).astype(np.float32)
    mcur = np.where(k[:, None] <= k[None, :], 0.0, NEG).astype(np.float32)
    mask = np.concatenate([mprev, mprev, mcur, mcur], axis=1)
    invf = (np.float32(10000.0) ** (-(np.arange(32, dtype=np.float32)) / np.float32(32))).astype(np.float32)
    invf = np.tile(invf, 4).reshape(128, 1)
    onesP = np.zeros((128, 2, 128), np.float32)
    onesP[:, 0, 0:64] = 1.0
    onesP[:, 1, 64:128] = 1.0
    return {"sMask": mask, "sInvf": invf, "sI": np.eye(128, dtype=np.float32), "sOnesP": onesP.reshape(128, 256)}


def build_swa(P, io, NTOK=4096):
    NT = NTOK // TT
    NC_ = HL + NTOK
    fmv = lambda a: a.rearrange("(c p) t -> p c t", p=128)
    x3T = fmv(io["x3T"])
    xH = fmv(io["yh"])
    posd, nw, fnw, w_in, w_out = io["swa_pos"], io["swa_nw"], io["swa_fnw"], io["swa_w_in"], io["swa_w_out"]
    skd, hmd, cM, cF, cI, cO = io["swa_sk"], io["swa_hmask"], io["sMask"], io["sInvf"], io["sI"], io["sOnesP"]
    outT = fmv(io["outT"])

    BK = P.buf()
    ones = P.sb("ones", [128, 128], BF16)
    epsc = P.sb("epsc", [128, 1], F32)
    P.op("dve", lambda e: e.memset(ones[:], 1.0), writes=[BK])
    P.op("dve", lambda e: e.memset(epsc[:], 1e-6), writes=[BK])
    norm = Norm(P, ones, BK)
    norm.epsc = epsc
    stg = [P.sb("stg%d" % i, [128, 1024], F32) for i in range(2)]
    Bstg = [P.buf(), P.buf()]
    nws, Bnw = load_small(P, nw[:, :], [128, KC], "nws")
    fnws, Bfnw = load_small(P, fnw[:, :], [128, KC], "fnws")
    sks, Bsk = load_small(P, skd[:, :], [128, 8], "sks")
    hms, Bhm = load_small(P, hmd[:, :], [128, 1], "hms")
    invf, Binvf = load_small(P, cF[:, :], [128, 1], "invf")
    esk = P.sb("esk", [128, 8], F32)
    P.op("act", lambda e: e.activation(out=esk[:], in_=sks[:], func=AF.Exp), reads=[Bsk], writes=[Bsk])
    mask_b = P.sb("mask_b", [128, 512], BF16)
    ident_b = P.sb("ident_b", [128, 128], BF16)
    onesP = P.sb("onesP", [128, 2, 128], BF16)
    P.dma("sp", stg[0][:, 0:512], cM[:, :], writes=[Bstg[0]])
    P.op("dve", lambda e: e.tensor_copy(out=mask_b[:], in_=stg[0][:, 0:512]), reads=[Bstg[0]], writes=[BK])
    P.dma("sp", stg[1][:, 0:128], cI[:, :], writes=[Bstg[1]])
    P.op("dve", lambda e: e.tensor_copy(out=ident_b[:], in_=stg[1][:, 0:128]), reads=[Bstg[1]], writes=[BK])
    P.dma("sp", stg[0][:, 0:256], cO[:, :], writes=[Bstg[0]])
    P.op("dve", lambda e: e.tensor_copy(out=onesP[:].rearrange("p a b -> p (a b)"), in_=stg[0][:, 0:256]), reads=[Bstg[0]], writes=[BK])

    wib = P.sb("wib", [128, KC, NCOL], BF16)
    Bwib = P.buf()
    wob = P.sb("wob", [128, KC, D], BF16)
    Bwob = P.buf()
    nld = [0]

    def stage_load(src, ncols):
        s = nld[0] % 2
        nld[0] += 1
        P.dma("sp", stg[s][:, 0:ncols], src, writes=[Bstg[s]])
        return stg[s], Bstg[s]

    def cast(dst, src, Bs, kc, sign=1.0, eng=None):
        eng = eng or ("dve", "pool")[nld[0] % 2]
        P.op(eng, lambda e: e.tensor_scalar(out=dst, in0=src, scalar1=nws[:, kc:kc + 1], scalar2=sign, op0=ALU.mult, op1=ALU.mult),
             reads=[Bs, Bnw], writes=[Bwib])

    for kc in range(KC):
        rows = slice(kc * 128, (kc + 1) * 128)
        s_, Bs = stage_load(w_in[rows, 0:1024], 1024)
        cast(wib[:, kc, QA:QA + 1024], s_[:, 0:1024], Bs, kc)
        sv = s_[:, 0:1024].rearrange("p (h t d) -> p h t d", t=2, d=32)
        dv = wib[:, kc, QR:QR + 1024].rearrange("p (h t d) -> p h t d", t=2, d=32)
        cast(dv[:, :, 0, :], sv[:, :, 1, :], Bs, kc, sign=-1.0, eng="dve")
        cast(dv[:, :, 1, :], sv[:, :, 0, :], Bs, kc, eng="pool")
        s_, Bs = stage_load(w_in[rows, 1024:1536], 512)
        ksrc = s_[:, 0:256].rearrange("p (g d) -> p g d", d=64)
        ksr2 = s_[:, 0:256].rearrange("p (g t d) -> p g t d", t=2, d=32)
        kad = wib[:, kc, KA:KA + 512].rearrange("p (g u d) -> p g u d", u=2, d=64)
        krd = wib[:, kc, KR:KR + 512].rearrange("p (g u t d) -> p g u t d", u=2, t=2, d=32)
        for u in range(2):
            cast(kad[:, :, u, :], ksrc, Bs, kc, eng=("dve", "pool")[u])
            cast(krd[:, :, u, 0, :], ksr2[:, :, 1, :], Bs, kc, sign=-1.0, eng="dve")
            cast(krd[:, :, u, 1, :], ksr2[:, :, 0, :], Bs, kc, eng="pool")
        cast(wib[:, kc, VV:VV + 256], s_[:, 256:512], Bs, kc, eng="dve")
        s_, Bs = stage_load(w_in[rows, 1536:2560], 1024)
        cast(wib[:, kc, ZZ:ZZ + 1024], s_[:, 0:1024], Bs, kc)
    for kc in range(KC):
        s_, Bs = stage_load(w_out[kc * 128:(kc + 1) * 128, :], 1024)
        P.op(("dve", "pool")[kc % 2], lambda e, kc=kc, s_=s_: e.tensor_copy(out=wob[:, kc, :], in_=s_[:, 0:1024]), reads=[Bs], writes=[Bwob])

    xs = P.sb("xs", [128, KC, TT], F32)
    Bxs = P.buf()
    hT = P.sb("hT", [128, KC, TT], BF16)
    BhT = P.buf()
    rstd = P.sb("rstd", [128, TT], F32)
    Brstd = P.buf()
    posi = P.sb("posi", [128, TT], I32)
    ang = P.sb("ang", [128, TT], F32)
    kf = P.sb("kf", [128, TT], F32)
    cosT = P.sb("cosT", [128, TT], F32)
    sinT = P.sb("sinT", [128, TT], F32)
    Brope = P.buf()
    Bcs = P.buf()
    QP = P.sb("QP", [128, 8, TT], BF16)
    BQP = P.buf()
    KP = P.sb("KP", [128, 4, HL + TT], BF16)
    BKP = P.buf()
    Vp = P.sb("Vp", [128, 5, 4, 2, 128], BF16)
    BVp = P.buf()
    P.op("pool", lambda e: e.memset(Vp[:].rearrange("p a b c d -> p (a b c d)"), 0.0), writes=[BVp])
    szs = P.sb("szs", [128, 8, TT], F32)
    Bszs = P.buf()
    og = P.sb("og", [128, 8, TT], BF16)
    Bog = P.buf()
    t1r = Rot(P, "t1r", [128, TT], F32, 2)
    ptr = Rot(P, "ptr", [128, TT], BF16, 4)
    smr = Rot(P, "smr", [128, 128], F32, 4)
    f32r = Rot(P, "f32r", [128, TT], F32, 2)
    pp = [P.ps("pp%d" % i, [128, TT]) for i in range(7)]
    Bpp = [P.buf() for _ in range(7)]
    ppi = [0]

    def psum():
        k = ppi[0] % 7
        ppi[0] += 1
        return pp[k], Bpp[k]

    def fm_proj(col0, n):
        ps, Bps = psum()
        for kc in range(KC):
            P.op("pe", lambda e, kc=kc: e.matmul(ps[:, 0:n], lhsT=wib[:, kc, col0:col0 + 128], rhs=hT[:, kc, 0:n],
                                                  start=(kc == 0), stop=(kc == KC - 1)), reads=[Bwib, BhT], writes=[Bps])
        return ps, Bps

    def rope_tables(c0, n):
        P.dma("sp", posi[:, 0:n], posd[:, c0:c0 + n], writes=[Brope])
        P.op("dve", lambda e: e.tensor_copy(out=ang[:, 0:n], in_=posi[:, 0:n]), reads=[Brope], writes=[Brope])
        P.op("dve", lambda e: e.tensor_scalar(out=ang[:, 0:n], in0=ang[:, 0:n], scalar1=invf[:, 0:1], scalar2=None, op0=ALU.mult),
             reads=[Brope, Binvf], writes=[Brope])
        P.op("dve", lambda e: e.tensor_scalar(out=kf[:, 0:n], in0=ang[:, 0:n], scalar1=1.0 / TWO_PI, scalar2=MAGIC, op0=ALU.mult, op1=ALU.add),
             reads=[Brope], writes=[Brope])
        P.op("dve", lambda e: e.tensor_scalar(out=kf[:, 0:n], in0=kf[:, 0:n], scalar1=MAGIC, scalar2=None, op0=ALU.subtract),
             reads=[Brope], writes=[Brope])
        P.op("dve", lambda e: e.scalar_tensor_tensor(out=ang[:, 0:n], in0=kf[:, 0:n], scalar=-C1, in1=ang[:, 0:n], op0=ALU.mult, op1=ALU.add),
             reads=[Brope], writes=[Brope])
        P.op("dve", lambda e: e.scalar_tensor_tensor(out=ang[:, 0:n], in0=kf[:, 0:n], scalar=-C2, in1=ang[:, 0:n], op0=ALU.mult, op1=ALU.add),
             reads=[Brope], writes=[Brope])
        P.op("dve", lambda e: e.tensor_scalar(out=ang[:, 0:n], in0=ang[:, 0:n], scalar1=3.14159, scalar2=-3.14159, op0=ALU.min, op1=ALU.max),
             reads=[Brope], writes=[Brope])
        P.op("act", lambda e: e.activation(out=sinT[:, 0:n], in_=ang[:, 0:n], func=AF.Sin), reads=[Brope], writes=[Bcs])
        P.op("act", lambda e: e.activation(out=kf[:, 0:n], in_=ang[:, 0:n], func=AF.Sin, scale=0.5), reads=[Brope], writes=[Brope])
        P.op("dve", lambda e: e.tensor_tensor(out=kf[:, 0:n], in0=kf[:, 0:n], in1=kf[:, 0:n], op=ALU.mult), reads=[Brope], writes=[Brope])
        P.op("dve", lambda e: e.tensor_scalar(out=cosT[:, 0:n], in0=kf[:, 0:n], scalar1=-2.0, scalar2=1.0, op0=ALU.mult, op1=ALU.add),
             reads=[Brope], writes=[Bcs])

    def roped(colA, colR, n, dst, Bdst):
        psA, BA = fm_proj(colA, n)
        psR, BR = fm_proj(colR, n)
        t1, Bt1 = t1r.next()
        P.op("dve", lambda e: e.tensor_tensor(out=t1[:, 0:n], in0=psA[:, 0:n], in1=cosT[:, 0:n], op=ALU.mult), reads=[BA, Bcs], writes=[Bt1])
        t2, Bt2 = t1r.next()
        P.op("dve", lambda e: e.tensor_tensor(out=t2[:, 0:n], in0=psR[:, 0:n], in1=sinT[:, 0:n], op=ALU.mult), reads=[BR, Bcs], writes=[Bt2])
        P.op("pool", lambda e: e.tensor_tensor(out=dst, in0=t1[:, 0:n], in1=t2[:, 0:n], op=ALU.add), reads=[Bt1, Bt2], writes=[Bdst])

    def kv_part(c0, n, koff, vslot0):
        for g in range(4):
            roped(KA + g * 128, KR + g * 128, n, KP[:, g, koff:koff + n], BKP)
        for blk in range(n // 128):
            ps, Bps = psum()
            for kc in range(KC):
                P.op("pe", lambda e, kc=kc, blk=blk, ps=ps: e.matmul(ps[:, 0:256], lhsT=hT[:, kc, blk * 128:(blk + 1) * 128], rhs=wib[:, kc, VV:VV + 256],
                                                              start=(kc == 0), stop=(kc == KC - 1)), reads=[Bwib, BhT], writes=[Bps])
            src = ps[:, 0:256].rearrange("p (g d) -> p g d", d=64)
            P.op("dve", lambda e, blk=blk, src=src: e.tensor_copy(out=Vp[:, vslot0 + blk, :, 0, 0:64], in_=src), reads=[Bps], writes=[BVp])
            P.op("dve", lambda e, blk=blk, src=src: e.tensor_copy(out=Vp[:, vslot0 + blk, :, 1, 64:128], in_=src), reads=[Bps], writes=[BVp])

    def load_and_norm(c0, n):
        if c0 == 0:
            P.dma("sp", xs[:, :, 0:n], xH[:, :, 0:n], writes=[Bxs])
        else:
            P.dma("sp", xs[:, :, 0:n], x3T[:, :, c0 - HL:c0 - HL + n], writes=[Bxs])
        norm.run(xs, Bxs, n, rstd, Brstd)
        for kc in range(KC):
            eng = "dve" if kc % 2 == 0 else "pool"
            P.op(eng, lambda e, kc=kc: e.tensor_tensor(out=hT[:, kc, 0:n], in0=xs[:, kc, 0:n], in1=rstd[:, 0:n], op=ALU.mult),
                 reads=[Bxs, Brstd], writes=[BhT])

    scale = 1.0 / math.sqrt(64.0)

    load_and_norm(0, HL)
    rope_tables(0, HL)
    kv_part(0, HL, 0, 0)

    for i in range(NT):
        c0 = HL + i * TT
        load_and_norm(c0, TT)
        rope_tables(c0, TT)
        kv_part(c0, TT, HL, 1)
        for p in range(8):
            roped(QA + p * 128, QR + p * 128, TT, QP[:, p, :], BQP)
        for c in range(8):
            ps, Bps = fm_proj(ZZ + c * 128, TT)
            P.op("act", lambda e, c=c, ps=ps: e.activation(out=szs[:, c, :], in_=ps[:], func=AF.Silu), reads=[Bps], writes=[Bszs])
        def attn(i, n, g):
            qc = slice(n * 128, (n + 1) * 128)
            kprev = slice(n * 128, (n + 1) * 128)
            kcur = slice((n + 1) * 128, (n + 2) * 128)
            pts = []
            for half in range(2):
                pr = slice(half * 64, half * 64 + 64)
                pst, Bpst = psum()
                rhs = QP[pr, 2 * g:2 * g + 2, qc]
                P.op("pe", lambda e, pst=pst, pr=pr, rhs=rhs: e.matmul(pst[:, 0:256], lhsT=KP[pr, g, kprev], rhs=rhs, start=True, stop=False),
                     reads=[BKP, BQP], writes=[Bpst])
                P.op("pe", lambda e, pst=pst, pr=pr, rhs=rhs: e.matmul(pst[:, 256:512], lhsT=KP[pr, g, kcur], rhs=rhs, start=False, stop=False),
                     reads=[BKP, BQP], writes=[Bpst])
                P.op("pe", lambda e, pst=pst: e.matmul(pst[:, :], lhsT=ident_b[:], rhs=mask_b[:], start=False, stop=True),
                     reads=[BK], writes=[Bpst])
                pt, Bpt = ptr.next()
                if i == 0 and n == 0:
                    P.op("act", lambda e, pt=pt, pst=pst: e.activation(out=pt[:, 0:256], in_=pst[:, 0:256], func=AF.Exp, scale=scale, bias=hms[:, 0:1]),
                         reads=[Bpst, Bhm], writes=[Bpt])
                    P.op("act", lambda e, pt=pt, pst=pst: e.activation(out=pt[:, 256:512], in_=pst[:, 256:512], func=AF.Exp, scale=scale),
                         reads=[Bpst], writes=[Bpt])
                else:
                    P.op("act", lambda e, pt=pt, pst=pst: e.activation(out=pt[:], in_=pst[:], func=AF.Exp, scale=scale), reads=[Bpst], writes=[Bpt])
                pts.append((pt, Bpt))
            po, Bpo = psum()
            prs, Bprs = psum()
            seq = [(0, n, slice(0, 256)), (0, n + 1, slice(256, 512)), (1, n, slice(0, 256)), (1, n + 1, slice(256, 512))]
            for idx, (half, vs_, cols) in enumerate(seq):
                pt, Bpt = pts[half]
                P.op("pe", lambda e, half=half, vs_=vs_, cols=cols, pt=pt, idx=idx, po=po: e.matmul(
                    po[:, 0:256], lhsT=Vp[:, vs_, g, half, :], rhs=pt[:, cols], start=(idx == 0), stop=(idx == 3)),
                    reads=[BVp, Bpt], writes=[Bpo])
            for idx, (half, vs_, cols) in enumerate(seq):
                pt, Bpt = pts[half]
                P.op("pe", lambda e, half=half, cols=cols, pt=pt, idx=idx, prs=prs: e.matmul(
                    prs[:, 0:256], lhsT=onesP[:, half, :], rhs=pt[:, cols], start=(idx == 0), stop=(idx == 3)),
                    reads=[BK, Bpt], writes=[Bprs])
            for c in range(2):
                pair = 2 * g + c
                cc = slice(c * 128, (c + 1) * 128)
                sm, Bsm = smr.next()
                P.op("act", lambda e, sm=sm, prs=prs, cc=cc, pair=pair: e.activation(out=sm[:], in_=prs[:, cc], func=AF.Ln, bias=esk[:, pair:pair + 1], scale=1.0),
                     reads=[Bprs, Bsk], writes=[Bsm])
                P.op("act", lambda e, sm=sm: e.activation(out=sm[:], in_=sm[:], func=AF.Exp, scale=-1.0), reads=[Bsm], writes=[Bsm])
                sm2, Bsm2 = smr.next()
                P.op("dve", lambda e, sm=sm, sm2=sm2, po=po, cc=cc: e.tensor_tensor(out=sm2[:], in0=po[:, cc], in1=sm[:], op=ALU.mult),
                     reads=[Bpo, Bsm], writes=[Bsm2])
                P.op("pool", lambda e, sm2=sm2, pair=pair: e.tensor_tensor(out=og[:, pair, qc], in0=sm2[:], in1=szs[:, pair, qc], op=ALU.mult),
                     reads=[Bsm2, Bszs], writes=[Bog])

        for n in range(4):
            for g in range(4):
                attn(i, n, g)
        P.op("pool", lambda e: e.tensor_copy(out=KP[:, :, 0:HL], in_=KP[:, :, TT:TT + HL]), reads=[BKP], writes=[BKP])
        P.op("pool", lambda e: e.tensor_copy(out=Vp[:, 0].rearrange("p a b c -> p (a b c)"), in_=Vp[:, 4].rearrange("p a b c -> p (a b c)")),
             reads=[BVp], writes=[BVp])
        for dc in range(KC):
            ps, Bps = psum()
            for pr_ in range(8):
                P.op("pe", lambda e, pr_=pr_, dc=dc, ps=ps: e.matmul(ps[:], lhsT=wob[:, pr_, dc * 128:(dc + 1) * 128], rhs=og[:, pr_, :],
                                                                    start=(pr_ == 0), stop=(pr_ == 7)), reads=[Bwob, Bog], writes=[Bps])
            P.op("dve", lambda e, dc=dc, ps=ps: e.tensor_tensor(out=xs[:, dc, :], in0=ps[:], in1=xs[:, dc, :], op=ALU.add),
                 reads=[Bps, Bxs], writes=[Bxs])
        norm.run(xs, Bxs, TT, rstd, Brstd)
        for dc in range(KC):
            t, Bt = f32r.next()
            P.op("dve", lambda e, dc=dc, t=t: e.scalar_tensor_tensor(out=t[:], in0=xs[:, dc, :], scalar=fnws[:, dc:dc + 1], in1=rstd[:],
                                                                    op0=ALU.mult, op1=ALU.mult), reads=[Bxs, Bfnw, Brstd], writes=[Bt])
            P.dma("sp", outT[:, dc, i * TT:(i + 1) * TT], t[:], reads=[Bt], store=True, final=True)
    return P


from concourse.bass import ds

SEQ = 16384
TOKC = 4096
GROUPS = [[0, 1, 2, 3], [4, 5, 6, 7]]


def build_fused(nc, st):
    P = Prog(nc, st)
    P.need_rank = True
    io = {}

    def ext(name, shape, dt=F32):
        io[name] = nc.dram_tensor(name, list(shape), dt, kind="ExternalInput").ap()

    def itn(name, shape, dt=F32):
        io[name] = nc.dram_tensor(name, list(shape), dt).ap()

    ext("l0_xT", [D, TOKC + 2]); ext("l0_nw", [128, KC]); ext("l0_w_in", [D, 4096]); ext("l0_w_cv", [128, 24]); ext("l0_w_out", [D, D])
    ext("f_nw", [128, KC]); ext("f_w_in", [D, 4104]); ext("f_w_out", [D, D]); ext("fox_bf", [128, 2])
    for n, w in (("cU", 128), ("cSU", 128), ("cE127", 128), ("cI", 128), ("cMask", 2048)):
        ext(n, [128, w])
    ext("g_nw", [128, KC]); ext("g_w_in", [D, 4112]); ext("g_w_out", [D, D]); ext("gdn_wcv", [128, 24]); ext("gdn_hp", [128, 4]); ext("gdn_onw", [128, 128])
    for n in CN:
        ext(n, [128, 128])
    ext("swa_pos", [128, 128 + TOKC], I32); ext("swa_nw", [128, KC]); ext("swa_fnw", [128, KC]); ext("swa_w_in", [D, 2560]); ext("swa_w_out", [D, D])
    ext("swa_sk", [128, 8]); ext("swa_hmask", [128, 1]); ext("sMask", [128, 512]); ext("sInvf", [128, 1]); ext("sI", [128, 128]); ext("sOnesP", [128, 256])
    io["outT"] = nc.dram_tensor("outT", [D, TOKC], F32, kind="ExternalOutput").ap()
    for n in ("x1T", "x2T", "x3T", "szT1", "szT2"):
        itn(n, [D, TOKC])
    itn("XB1", [4 * 3072, TOKC], BF16); itn("YB1", [4 * 768, TOKC], BF16); itn("XF1", [8, SEQ]); itn("YF1", [2, SEQ])
    itn("XO1", [8 * 4 * D, TOKC // 8]); itn("YO1", [8 * D, TOKC // 8])
    itn("XB2", [16 * 3072, TOKC // 4]); itn("YB2", [16 * 768, TOKC // 4]); itn("XF2", [16, SEQ]); itn("YF2", [4, SEQ])
    itn("XO2", [8 * 4 * D, TOKC // 8]); itn("YO2", [8 * D, TOKC // 8])
    itn("XH", [4 * D, 128]); itn("YH", [D, 128])
    itn("L1", [3072, TOKC], BF16); itn("LF1", [8, TOKC]); itn("LO1", [D, TOKC])
    itn("L2", [3072, TOKC]); itn("LF2", [16, TOKC]); itn("LO2", [D, TOKC])

    ZW = 1024
    zt32 = P.sb("zt32", [128, ZW], F32)
    zt16 = P.sb("zt16", [128, ZW], BF16)
    Bzt = P.buf()
    P.op("pool", lambda e: e.memset(zt32[:], 0.0), writes=[Bzt])
    P.op("pool", lambda e: e.memset(zt16[:], 0.0), writes=[Bzt])
    fills = {}

    def zfill(name, key):
        ap = io[name]
        rows, cols = ap.shape
        zt = zt16 if name == "XB1" else zt32
        tot = rows * cols
        per = 128 * ZW
        flat = ap.rearrange("r c -> (r c)")
        KB = 8
        o = 0
        while o < tot:
            k = min(KB, (tot - o) // per)
            if k >= 1:
                n = k * per
                src = bass.AP(zt[:].tensor, 0, [[ZW, 128], [0, k], [1, ZW]])
                P.fill_queue.append((key, lambda q, dst=flat[o:o + n].rearrange("(k p w) -> p k w", p=128, w=ZW), src=src, key=key:
                                     P.dma(q, dst, src, reads=[Bzt], writes=[fills.setdefault((key, q), P.buf())], chan=fills[(key, q)], defer=True)))
            else:
                n = tot - o
                w = n // 128
                P.fill_queue.append((key, lambda q, dst=flat[o:o + n].rearrange("(p w) -> p w", p=128), src=zt[:, 0:w], key=key:
                                     P.dma(q, dst, src, reads=[Bzt], writes=[fills.setdefault((key, q), P.buf())], chan=fills[(key, q)], defer=True)))
            o += n

    def exchange(xin, yout):
        P.coll("ReduceScatter", GROUPS, io[xin], io[yout])

    def dyn_copy(q, dst, dst_pat, dst_off, src, src_pat, src_off, jscale, fkey=None, extra=()):
        b = P.buf()
        P.dma(q, (lambda: bass.AP(io[dst].tensor, P.jx[q] * jscale + dst_off, [list(p) for p in dst_pat])),
              bass.AP(io[src].tensor, src_off, [list(p) for p in src_pat]), reads=[fb for (k_, q_), fb in fills.items() if k_ == fkey] + list(extra), writes=[b], chan=b)
        return b

    def place_tok(lname, xname, fl_l, fl_x, nfl, with_v, fkey):
        if True:
            for h_ in range(2):
                q = ("sp", "pool")[h_]
                dyn_copy(q, xname, [[TOKC, 1536], [1, TOKC]], h_ * 1536 * TOKC, lname, [[TOKC, 1536], [1, TOKC]], h_ * 1536 * TOKC, 3072 * TOKC, fkey=fkey)
        dyn_copy("sp", fl_x, [[SEQ, nfl], [1, TOKC]], 0, fl_l, [[TOKC, nfl], [1, TOKC]], 0, TOKC, fkey=fkey)

    def o_exchange_hooks(lname, xname, yname, fkey):
        SUB = TOKC // 8
        Blo = [P.buf() for _ in range(8)]

        def done(s_):
            P.fill_until(fkey)
            b = dyn_copy("pool", xname, [[D * SUB, 4], [SUB, 256], [1, SUB]], s_ * 4 * D * SUB,
                         lname, [[256 * TOKC, 4], [TOKC, 256], [1, SUB]], s_ * SUB, 256 * SUB, fkey=fkey, extra=[Blo[s_]])
            P.coll("ReduceScatter", GROUPS, io[xname][s_ * 4 * D:(s_ + 1) * 4 * D, :], io[yname][s_ * D:(s_ + 1) * D, :], reads=[b])
        return Blo, done

    P.tag('fill')
    zfill("XB1", 1); zfill("XF1", 1); zfill("XO1", 2); zfill("XB2", 3); zfill("XF2", 3); zfill("XO2", 4); zfill("XH", 5)
    P.tag('ph1_conv')
    P.phase_begin()
    build_stage0(P, io)
    P.phase_end()

    P.tag('ph2_foxA')
    P.phase_begin()
    io2 = dict(io, c_xT=io["x1T"], a_nw=io["f_nw"], a_w_in=io["f_w_in"], xb=io["L1"], xf=io["LF1"], szT=io["szT1"])
    build_CA(P, io2, StageLoadX, StageAFox)
    P.phase_end()
    P.tag("x1_place")
    P.fill_until(1)
    place_tok("L1", "XB1", "LF1", "XF1", 8, True, 1)
    P.barrier()
    P.tag("x1_rs")
    exchange("XF1", "YF1")
    P.barrier()

    P.tag('ph3_foxB')
    P.phase_begin()
    By1 = [P.buf() for _ in range(4)]
    for c_ in range(4):
        P.coll("ReduceScatter", GROUPS, io["XB1"][c_ * 3072:(c_ + 1) * 3072, :], io["YB1"][c_ * 768:(c_ + 1) * 768, :], writes=[By1[c_]])
    Blo1, done1 = o_exchange_hooks("LO1", "XO1", "YO1", 2)
    build_foxB(P, dict(io, yb=io["YB1"], yb_bufs=By1, yf=io["YF1"], xo=io["LO1"], lo_bufs=Blo1, sub_done=done1), NH=2, S_=SEQ)
    P.phase_end()

    P.tag('ph4_foxC_gdnA')
    P.phase_begin()
    io4 = dict(io, c_oT=io["YO1"], c_oT_bufs=[P.buf() for _ in range(8)], c_szT=io["szT1"], c_xT=io["x1T"], c_w_out=io["f_w_out"], xo=io["x2T"],
               a_nw=io["g_nw"], a_w_in=io["g_w_in"], xb=io["L2"], xf=io["LF2"], szT=io["szT2"])
    build_CA(P, io4, StageC, StageAGdn)
    P.phase_end()
    P.tag("x3_place")
    P.fill_until(3)
    HQ = TOKC // 4
    for h_ in range(4):
        dyn_copy(("sp", "pool")[h_ % 2], "XB2", [[HQ, 3072], [1, HQ]], h_ * 3072 * HQ, "L2", [[TOKC, 3072], [1, HQ]], h_ * HQ, 4 * 3072 * HQ, fkey=3)
    dyn_copy("sp", "XF2", [[SEQ, 16], [1, TOKC]], 0, "LF2", [[TOKC, 16], [1, TOKC]], 0, TOKC, fkey=3)
    P.barrier()
    P.tag("x3_rs")
    exchange("XF2", "YF2")
    P.barrier()

    P.tag('ph5_gdnB')
    P.phase_begin()
    By = [P.buf() for _ in range(16)]
    for c_ in range(16):
        P.coll("ReduceScatter", GROUPS, io["XB2"][c_ * 3072:(c_ + 1) * 3072, :], io["YB2"][c_ * 768:(c_ + 1) * 768, :], writes=[By[c_]])
    Blo2, done2 = o_exchange_hooks("LO2", "XO2", "YO2", 4)
    build_gdnB(P, dict(io, yb=io["YB2"], yb_bufs=By, yf=io["YF2"], xo=io["LO2"], lo_bufs=Blo2, sub_done=done2), NH=2, S_=SEQ)
    P.phase_end()

    P.tag('ph6_gdnC')
    P.phase_begin()
    io6 = dict(io, c_oT=io["YO2"], c_oT_bufs=[P.buf() for _ in range(8)], c_szT=io["szT2"], c_xT=io["x2T"], c_w_out=io["g_w_out"], xo=io["x3T"])
    build_CA(P, io6, StageC, StageANull)
    P.phase_end()
    P.tag('x5_halo')
    P.fill_until(5)
    Bh = P.buf()
    P.dma("sp", (lambda: io["XH"][ds(((P.jx["sp"] + 1) % 4) * D, D), :]), io["x3T"][:, TOKC - 128:TOKC], reads=[fb for (k_, q_), fb in fills.items() if k_ == 5], writes=[Bh], chan=Bh)
    P.barrier()
    exchange("XH", "YH")
    P.barrier()

    P.tag('ph7_swa')
    P.phase_begin()
    build_swa(P, dict(io, yh=io["YH"]))
    P.phase_end()
    return P


from concourse.bass_utils import run_bass_kernel_spmd

NCORE = 8


def _nwT(v):
    return np.ascontiguousarray(np.asarray(v, np.float32).reshape(8, 128).T)


def kernel(x, positions, norm_w, final_norm_w, conv_w_in, conv_w_conv, conv_w_out,
           fox_w_in, fox_b_f, fox_w_out, gdn_w_in, gdn_w_conv, gdn_a_log, gdn_dt_bias,
           gdn_norm_w, gdn_w_out, swa_w_in, swa_sinks, swa_w_out):
    x = np.asarray(x, np.float32)
    positions = np.asarray(positions, np.int32)
    f = lambda a: np.ascontiguousarray(np.asarray(a, np.float32))
    nc = bass.Bass("TRN2", target_bir_lowering=False)
    with ExitStack() as st:
        P = build_fused(nc, st)
        P.emit()
    shared = {"l0_nw": _nwT(norm_w[0]), "l0_w_in": f(conv_w_in[0]),
              "l0_w_cv": np.ascontiguousarray(f(conv_w_conv[0]).reshape(3, 8, 128).transpose(2, 1, 0).reshape(128, 24)),
              "l0_w_out": f(conv_w_out[0]),
              "f_nw": _nwT(norm_w[1]), "f_w_in": f(fox_w_in[0]), "f_w_out": f(fox_w_out[0]),
              "g_nw": _nwT(norm_w[2]), "g_w_in": f(gdn_w_in[0]), "g_w_out": f(gdn_w_out[0]),
              "gdn_onw": np.ascontiguousarray(np.broadcast_to(f(gdn_norm_w[0])[None, :], (128, 128))),
              "swa_nw": _nwT(norm_w[3]), "swa_fnw": _nwT(final_norm_w), "swa_w_in": f(swa_w_in[0]), "swa_w_out": f(swa_w_out[0])}
    wc = f(gdn_w_conv[0])
    wcv_all = np.stack([wc[k, w * 1024 + h * 128: w * 1024 + (h + 1) * 128]
                        for h in range(8) for w in range(3) for k in range(4)], axis=1)
    hp_all = np.stack([f(gdn_a_log[0]), f(gdn_dt_bias[0])], axis=1).reshape(1, 16)
    sinks = f(swa_sinks[0])
    sk = np.zeros((128, 8), np.float32)
    for p in range(8):
        sk[:64, p] = sinks[2 * p]
        sk[64:, p] = sinks[2 * p + 1]
    shared["swa_sk"] = sk
    shared.update(fox_consts())
    shared.update(gdn_consts())
    shared.update(swa_consts())
    ins = []
    for c in range(NCORE):
        b, t0 = c // 4, (c % 4) * TOKC
        xt = np.zeros((1024, TOKC + 2), np.float32)
        pos = np.zeros((128 + TOKC,), np.int32)
        if t0 > 0:
            xt[:, 0:2] = x[b, t0 - 2:t0].T
            pos[:128] = positions[b, t0 - 128:t0]
        xt[:, 2:] = x[b, t0:t0 + TOKC].T
        pos[128:] = positions[b, t0:t0 + TOKC]
        d = dict(shared)
        d["l0_xT"] = xt
        d["swa_pos"] = np.ascontiguousarray(np.broadcast_to(pos[None, :], (128, pos.shape[0])))
        d["swa_hmask"] = np.full((128, 1), -30000.0 if t0 == 0 else 0.0, np.float32)
        g = c % 4
        d["fox_bf"] = np.ascontiguousarray(np.broadcast_to(f(fox_b_f[0])[None, 2 * g:2 * g + 2], (128, 2)))
        d["gdn_wcv"] = np.ascontiguousarray(wcv_all[:, g * 24:(g + 1) * 24])
        d["gdn_hp"] = np.ascontiguousarray(np.broadcast_to(hp_all[:, g * 4:(g + 1) * 4], (128, 4)))
        ins.append(d)
    res = run_bass_kernel_spmd(nc, ins, core_ids=list(range(NCORE)))
    out = np.zeros((2, SEQ, 1024), np.float32)
    for c in range(NCORE):
        b, t0 = c // 4, (c % 4) * TOKC
        out[b, t0:t0 + TOKC] = np.asarray(res.results[c]["outT"]).T
    return out
```
